# Optimizing a Trainium2 kernel written in Bass

```python
import math
import jax, jax.numpy as jnp
from jax import lax
import numpy as np

D_MODEL = 2048
BATCH = 4
SEQ = 4096
DEPTH = 2

CHUNK = 64
Q_BLOCK = 128
SSM_WIDTH = 1024
SSM_GROUP = 16
SSM_GROUPS = SSM_WIDTH // SSM_GROUP
SSM_STATE = 64
ATTN_WIDTH = 1024
N_HEADS = 8
HEAD_DIM = ATTN_WIDTH // N_HEADS // 2
V_DIM = 2 * HEAD_DIM
IN_WIDTH = SSM_WIDTH + 3 * ATTN_WIDTH + 2 * D_MODEL
_FF_RAW = -(-8 * D_MODEL // 3)
D_FF = -(-_FF_RAW // 256) * 256
N_BUCKETS = 32
MAX_DISTANCE = 128
RMS_EPS = 1e-6
SUBLN_EPS = 1e-5

kernel_name = "hybrid_s5_diffattn_gated_block"


def rms_norm(x, w, eps):
    xf = x.astype(jnp.float32)
    y = xf * lax.rsqrt(jnp.mean(xf * xf, axis=-1, keepdims=True) + eps)
    return (y * w.astype(jnp.float32)).astype(x.dtype)


def t5_bucket(rel):
    nb = N_BUCKETS // 2
    ret = jnp.where(rel > 0, nb, 0)
    n = jnp.abs(rel)
    max_exact = nb // 2
    nf = jnp.maximum(n, 1).astype(jnp.float32)
    large = max_exact + (jnp.log(nf / max_exact) / math.log(MAX_DISTANCE / max_exact)
                         * (nb - max_exact)).astype(jnp.int32)
    large = jnp.minimum(large, nb - 1)
    return ret + jnp.where(n < max_exact, n, large)


def _complex_linear_combine(e1, e2):
    ar1, ai1, br1, bi1 = e1
    ar2, ai2, br2, bi2 = e2
    ar = ar1 * ar2 - ai1 * ai2
    ai = ar1 * ai2 + ai1 * ar2
    br = ar2 * br1 - ai2 * bi1 + br2
    bi = ar2 * bi1 + ai2 * br1 + bi2
    return (ar, ai, br, bi)


def s5_mixer(u, lam_re, lam_im, log_step, b_re, b_im, c_re, c_im, d_skip, w_glu, b_glu):
    f32 = jnp.float32
    bsz, L, _ = u.shape
    ug = u.reshape(bsz, L, SSM_GROUPS, SSM_GROUP).astype(f32)
    lr = jnp.minimum(lam_re.astype(f32), -1e-4)
    li = lam_im.astype(f32)
    dt = jnp.exp(log_step.astype(f32))[:, None]
    mag = jnp.exp(lr * dt)
    ab_r = mag * jnp.cos(li * dt)
    ab_i = mag * jnp.sin(li * dt)
    den = lr * lr + li * li
    nr = ab_r - 1.0
    ni = ab_i
    fr = (nr * lr + ni * li) / den
    fi = (ni * lr - nr * li) / den
    br = b_re.astype(f32)
    bi = b_im.astype(f32)
    bb_r = fr[..., None] * br - fi[..., None] * bi
    bb_i = fr[..., None] * bi + fi[..., None] * br
    bu_r = jnp.einsum('blgc,gpc->blgp', ug, bb_r)
    bu_i = jnp.einsum('blgc,gpc->blgp', ug, bb_i)
    a_r = jnp.broadcast_to(ab_r, (1, L, SSM_GROUPS, SSM_STATE))
    a_i = jnp.broadcast_to(ab_i, (1, L, SSM_GROUPS, SSM_STATE))
    _, _, s_r, s_i = lax.associative_scan(_complex_linear_combine, (a_r, a_i, bu_r, bu_i), axis=1)
    y = (jnp.einsum('blgp,gcp->blgc', s_r, c_re.astype(f32))
         - jnp.einsum('blgp,gcp->blgc', s_i, c_im.astype(f32))
         + d_skip.astype(f32).reshape(SSM_GROUPS, SSM_GROUP) * ug)
    y = jax.nn.gelu(y.reshape(bsz, L, SSM_WIDTH).astype(u.dtype))
    return y * jax.nn.sigmoid(y @ w_glu + b_glu)


def diff_attention(q, k, v, q_norm_w, k_norm_w, lq1, lk1, lq2, lk2, subln_w, rel_table, lambda_init):
    f32 = jnp.float32
    bsz, L, _ = q.shape
    q = rms_norm(q.reshape(bsz, L, N_HEADS, 2, HEAD_DIM), q_norm_w, RMS_EPS) * (HEAD_DIM ** -0.5)
    k = rms_norm(k.reshape(bsz, L, N_HEADS, 2, HEAD_DIM), k_norm_w, RMS_EPS)
    v = v.reshape(bsz, L, N_HEADS, V_DIM)
    lam = (jnp.exp(jnp.sum(lq1.astype(f32) * lk1.astype(f32)))
           - jnp.exp(jnp.sum(lq2.astype(f32) * lk2.astype(f32))) + lambda_init)
    kpos = jnp.arange(L)
    n_blocks = L // Q_BLOCK
    qb = q.reshape(bsz, n_blocks, Q_BLOCK, N_HEADS, 2, HEAD_DIM).transpose(1, 0, 2, 3, 4, 5)

    def block(args):
        q_i, i = args
        qpos = i * Q_BLOCK + jnp.arange(Q_BLOCK)
        s = jnp.einsum('bqhmd,bkhmd->bhmqk', q_i, k, preferred_element_type=f32)
        bias = rel_table[t5_bucket(kpos[None, :] - qpos[:, None])]
        bias = jnp.transpose(bias, (2, 0, 1)).astype(f32)[None, :, None]
        mask = (kpos[None, :] // CHUNK) <= (qpos[:, None] // CHUNK)
        s = jnp.where(mask, s + bias, -jnp.inf)
        p = jax.nn.softmax(s, axis=-1)
        p = p[:, :, 0] - lam * p[:, :, 1]
        return jnp.einsum('bhqk,bkhe->bqhe', p.astype(v.dtype), v)

    o = lax.map(block, (qb, jnp.arange(n_blocks)))
    o = o.transpose(1, 0, 2, 3, 4).reshape(bsz, L, N_HEADS, V_DIM)
    o = rms_norm(o, subln_w, SUBLN_EPS) * (1.0 - lambda_init)
    return o.reshape(bsz, L, ATTN_WIDTH)


def setup_inputs(seed: int = 0) -> dict:
    key = jax.random.key(seed)
    ks = jax.random.split(key, 32)
    f32 = jnp.float32
    nrm = lambda k, shape, s: jax.random.normal(k, shape, f32) * s
    lam_im0 = jnp.pi * jnp.arange(SSM_STATE, dtype=f32)
    return {
        "x": jax.random.normal(ks[0], (BATCH, SEQ, D_MODEL), f32),
        "norm1_w": 1.0 + nrm(ks[1], (DEPTH, D_MODEL), 0.02),
        "w_in": nrm(ks[2], (DEPTH, D_MODEL, IN_WIDTH), D_MODEL ** -0.5),
        "lam_re": -0.5 * jnp.exp(nrm(ks[3], (DEPTH, SSM_GROUPS, SSM_STATE), 0.05)),
        "lam_im": lam_im0 + nrm(ks[4], (DEPTH, SSM_GROUPS, SSM_STATE), 0.05),
        "log_step": jax.random.uniform(ks[5], (DEPTH, SSM_GROUPS), f32, math.log(1e-3), math.log(1e-1)),
        "ssm_b_re": nrm(ks[6], (DEPTH, SSM_GROUPS, SSM_STATE, SSM_GROUP), (2 * SSM_GROUP) ** -0.5),
        "ssm_b_im": nrm(ks[7], (DEPTH, SSM_GROUPS, SSM_STATE, SSM_GROUP), (2 * SSM_GROUP) ** -0.5),
        "ssm_c_re": nrm(ks[8], (DEPTH, SSM_GROUPS, SSM_GROUP, SSM_STATE), (2 * SSM_STATE) ** -0.5),
        "ssm_c_im": nrm(ks[9], (DEPTH, SSM_GROUPS, SSM_GROUP, SSM_STATE), (2 * SSM_STATE) ** -0.5),
        "ssm_d": nrm(ks[10], (DEPTH, SSM_WIDTH), 1.0),
        "w_glu": nrm(ks[11], (DEPTH, SSM_WIDTH, SSM_WIDTH), SSM_WIDTH ** -0.5),
        "b_glu": nrm(ks[12], (DEPTH, SSM_WIDTH), 0.02),
        "q_norm_w": 1.0 + nrm(ks[13], (DEPTH, HEAD_DIM), 0.02),
        "k_norm_w": 1.0 + nrm(ks[14], (DEPTH, HEAD_DIM), 0.02),
        "lambda_q1": nrm(ks[15], (DEPTH, HEAD_DIM), 0.1),
        "lambda_k1": nrm(ks[16], (DEPTH, HEAD_DIM), 0.1),
        "lambda_q2": nrm(ks[17], (DEPTH, HEAD_DIM), 0.1),
        "lambda_k2": nrm(ks[18], (DEPTH, HEAD_DIM), 0.1),
        "subln_w": 1.0 + nrm(ks[19], (DEPTH, V_DIM), 0.02),
        "w_proj_ssm": nrm(ks[20], (DEPTH, SSM_WIDTH, D_MODEL), SSM_WIDTH ** -0.5),
        "w_proj_attn": nrm(ks[21], (DEPTH, ATTN_WIDTH, D_MODEL), ATTN_WIDTH ** -0.5),
        "w_out": nrm(ks[22], (DEPTH, D_MODEL, D_MODEL), D_MODEL ** -0.5),
        "rel_bias": nrm(ks[23], (N_BUCKETS, N_HEADS), 0.2),
        "norm2_w": 1.0 + nrm(ks[24], (DEPTH, D_MODEL), 0.02),
        "w_ffn_gate": nrm(ks[25], (DEPTH, D_MODEL, D_FF), D_MODEL ** -0.5),
        "w_ffn_up": nrm(ks[26], (DEPTH, D_MODEL, D_FF), D_MODEL ** -0.5),
        "w_ffn_down": nrm(ks[27], (DEPTH, D_FF, D_MODEL), D_FF ** -0.5),
    }


def reference(x, norm1_w, w_in, lam_re, lam_im, log_step, ssm_b_re, ssm_b_im, ssm_c_re, ssm_c_im,
              ssm_d, w_glu, b_glu, q_norm_w, k_norm_w, lambda_q1, lambda_k1, lambda_q2, lambda_k2,
              subln_w, w_proj_ssm, w_proj_attn, w_out, rel_bias, norm2_w, w_ffn_gate, w_ffn_up,
              w_ffn_down):
    o_q = SSM_WIDTH
    o_k = o_q + ATTN_WIDTH
    o_v = o_k + ATTN_WIDTH
    o_gs = o_v + ATTN_WIDTH
    o_ga = o_gs + D_MODEL
    for l in range(DEPTH):
        lambda_init = 0.8 - 0.6 * math.exp(-0.3 * l)
        h = rms_norm(x, norm1_w[l], RMS_EPS)
        z = h @ w_in[l]
        y_ssm = s5_mixer(z[..., :o_q], lam_re[l], lam_im[l], log_step[l], ssm_b_re[l], ssm_b_im[l],
                         ssm_c_re[l], ssm_c_im[l], ssm_d[l], w_glu[l], b_glu[l])
        y_attn = diff_attention(z[..., o_q:o_k], z[..., o_k:o_v], z[..., o_v:o_gs],
                                q_norm_w[l], k_norm_w[l], lambda_q1[l], lambda_k1[l],
                                lambda_q2[l], lambda_k2[l], subln_w[l], rel_bias, lambda_init)
        m = (jax.nn.sigmoid(z[..., o_gs:o_ga]) * (y_ssm @ w_proj_ssm[l])
             + jax.nn.sigmoid(z[..., o_ga:]) * (y_attn @ w_proj_attn[l]))
        x = x + m @ w_out[l]
        h = rms_norm(x, norm2_w[l], RMS_EPS)
        x = x + (jax.nn.silu(h @ w_ffn_gate[l]) * (h @ w_ffn_up[l])) @ w_ffn_down[l]
    return x
```

```python
import contextlib
import math
import numpy as np
import concourse.bass as bass
import concourse.mybir as mybir
from concourse.bass_utils import run_bass_kernel_spmd

F32 = mybir.dt.float32
BF16 = mybir.dt.bfloat16
I32 = mybir.dt.int32
AF = mybir.ActivationFunctionType
ALU = mybir.AluOpType
AX = mybir.AxisListType

D = 2048
DEPTH = 2
SSMW = 1024
NG = 64
NS = 64
ATW = 1024
NH = 8
INW = 8192
DFF = 5632
RMS_EPS = 1e-6
SUBLN_EPS = 1e-5


class Res:
    __slots__ = ("name", "w", "r", "dsem", "dcnt", "excl")

    def __init__(self, name):
        self.name = name
        self.excl = False
        self.w = None
        self.r = {}
        self.dsem = None
        self.dcnt = 0


class Prog:
    ENG = ("pe", "act", "dve", "pool", "sp")

    def __init__(self, nc, stack):
        self.nc = nc
        self.stack = stack
        self.sems = []
        self.ops = {e: [] for e in self.ENG}
        self.cnt = {e: 0 for e in self.ENG}
        self.seen = {e: {} for e in self.ENG}
        self.esem = {}
        self.latest = {}
        for e in self.ENG:
            self.esem[e] = self.new_sem("s_" + e)
        self.nres = 0
        self.named = {}
        self.nalloc = 0

    def new_sem(self, name):
        s = self.stack.enter_context(self.nc.semaphore(name))
        self.sems.append(s)
        return len(self.sems) - 1

    def res(self, name=None):
        self.nres += 1
        return Res(name or ("r%d" % self.nres))

    def nres_(self, name):
        if name not in self.named:
            self.named[name] = Res(name)
        return self.named[name]

    def sbuf(self, name, shape, dtype, stack=None):
        self.nalloc += 1
        st = stack if stack is not None else self.stack
        t = st.enter_context(self.nc.sbuf_tensor("%s_%d" % (name, self.nalloc), list(shape), dtype))
        return t

    def psum(self, name, shape, dtype, stack=None):
        st = stack if stack is not None else self.stack
        return st.enter_context(self.nc.psum_tensor(name, list(shape), dtype))

    def _deps(self, eng, reads, writes):
        deps = {}
        for r in reads:
            if r.w is not None:
                s, v = r.w
                if deps.get(s, 0) < v:
                    deps[s] = v
        for w in writes:
            if w.w is not None:
                s, v = w.w
                if deps.get(s, 0) < v:
                    deps[s] = v
            for (s, v) in w.r.values():
                if deps.get(s, 0) < v:
                    deps[s] = v
        waits = []
        seen = self.seen[eng]
        own = self.esem[eng]
        for s, v in deps.items():
            if s == own and v > self.cnt[eng]:
                continue
            if seen.get(s, 0) < v:
                seen[s] = v
                waits.append((s, v))
        return waits

    def op(self, eng, fn, reads=(), writes=(), signal=True):
        if any(r.excl for r in reads):
            writes = list(writes) + [r for r in reads if r.excl and r not in writes]
            reads = [r for r in reads if not r.excl]
        waits = self._deps(eng, reads, writes)
        idx = self.cnt[eng] + 1
        if signal:
            self.cnt[eng] = idx
        ev = (self.esem[eng], idx)
        self.latest[ev[0]] = idx
        self.ops[eng].append((waits, fn, signal))
        for w in writes:
            w.w = ev
            w.r = {}
        for r in reads:
            r.r[ev[0]] = ev

    def dma(self, q, out, in_, sres, reads=(), writes=(), **kw):
        waits = self._deps(q, reads, writes)
        qk = "sw" if q == "pool" else "hw"
        if sres.dsem is None:
            sres.dsem = {}
            sres.dcnt = {}
        if qk not in sres.dsem:
            sres.dsem[qk] = self.new_sem("d%s_%s" % (qk, sres.name))
            sres.dcnt[qk] = 0
        sres.dcnt[qk] += 16
        ev = (sres.dsem[qk], sres.dcnt[qk])
        self.latest[ev[0]] = ev[1]
        sem = self.sems[ev[0]]

        def fn(e, out=out, in_=in_, sem=sem, kw=kw):
            e.dma_start(out=out, in_=in_, **kw).then_inc(sem, 16)
            return None

        self.ops[q].append((waits, fn, False))
        for w in writes:
            w.w = ev
            w.r = {}
        for r in reads:
            r.r[ev[0]] = ev

    def barrier(self):
        for e in self.ENG:
            waits = []
            seen = self.seen[e]
            for s, v in self.latest.items():
                if seen.get(s, 0) < v:
                    seen[s] = v
                    waits.append((s, v))
            if waits:
                self.ops[e].append((waits, None, False))

    def emit(self):
        nc = self.nc
        sems = self.sems
        with nc.Block() as block:
            def run(name, e):
                own = sems[self.esem[name]]
                for waits, fn, signal in self.ops[name]:
                    for s, v in waits:
                        e.wait_ge(sems[s], v)
                    if fn is None:
                        continue
                    ins = fn(e)
                    if signal:
                        ins.then_inc(own, 1)

            @block.tensor
            def _(e):
                run("pe", e)

            @block.scalar
            def _(e):
                run("act", e)

            @block.vector
            def _(e):
                run("dve", e)

            @block.gpsimd
            def _(e):
                run("pool", e)

            @block.sync
            def _(e):
                run("sp", e)


WSPEC = [
    ("w_in", D, INW), ("w_glu", SSMW, SSMW), ("w_proj_ssm", SSMW, D), ("w_proj_attn", ATW, D),
    ("w_out", D, D), ("w_ffn_gate", D, DFF), ("w_ffn_up", D, DFF), ("w_ffn_down", DFF, D),
]
SMALL = [("norm1_w", [DEPTH, D]), ("lam_re", [DEPTH, NG, NS]), ("lam_im", [DEPTH, NG, NS]),
         ("log_step", [DEPTH, NG]), ("ssm_b_re", [DEPTH, NG, NS, 16]), ("ssm_b_im", [DEPTH, NG, NS, 16]),
         ("ssm_c_re", [DEPTH, NG, 16, NS]), ("ssm_c_im", [DEPTH, NG, 16, NS]), ("ssm_d", [DEPTH, SSMW]),
         ("b_glu", [DEPTH, SSMW]), ("q_norm_w", [DEPTH, 64]), ("k_norm_w", [DEPTH, 64]),
         ("lambda_q1", [DEPTH, 64]), ("lambda_k1", [DEPTH, 64]), ("lambda_q2", [DEPTH, 64]),
         ("lambda_k2", [DEPTH, 64]), ("subln_w", [DEPTH, 128]), ("rel_bias", [32, NH]),
         ("norm2_w", [DEPTH, D])]
CONSTS = [("c_ident", [128, 128]), ("c_bones", [128, 128]), ("c_tmask", [128, 128]), ("c_reld", [128, 128])]


def host_consts():
    ident = np.eye(128, dtype=np.float32)
    bones = np.kron(np.eye(2, dtype=np.float32), np.ones((64, 64), np.float32))
    jj = np.arange(128) // 16
    tmask = (jj[None, :] >= jj[:, None]).astype(np.float32)
    reld = (np.arange(128)[:, None] - np.arange(128)[None, :]).astype(np.float32)
    return {"c_ident": ident, "c_bones": bones, "c_tmask": tmask, "c_reld": reld}


class Builder:
    def __init__(self, L, nlayers=DEPTH, dbg=False, phases=None):
        self.L = L
        self.NT = L // 512
        self.nlayers = nlayers
        self.dbg = dbg
        self.phases = phases
        self.nc = bass.Bass("TRN2", target_bir_lowering=False)
        self.stack = contextlib.ExitStack()
        self.P = Prog(self.nc, self.stack)
        nc = self.nc
        self.i = {}
        self.i["x"] = self.din("x", [L, D])
        for n, k, m in WSPEC:
            self.i[n] = self.din(n, [DEPTH, k, m])
        for n, sh in SMALL + CONSTS:
            self.i[n] = self.din(n, sh)
        self.out = self.dout("out", [L, D])
        self.wb = {n: [self.dscr("wb_%s_%d" % (n, l), [k, m], BF16) for l in range(nlayers)] for n, k, m in WSPEC}
        sk = self.dout if dbg else self.dscr
        self.u_d = sk("u_d", [L, SSMW], F32)
        skq = self.din if dbg == "attin" else sk
        self.qT_d = skq("qT_d", [ATW, L], BF16)
        self.kT_d = skq("kT_d", [ATW, L], BF16)
        self.v_d = skq("v_d", [L, ATW], BF16)
        self.gsT_d = sk("gsT_d", [D, L], BF16)
        self.gaT_d = sk("gaT_d", [D, L], BF16)
        self.y_d = sk("y_d", [L, SSMW], F32)
        self.yaT_d = sk("yaT_d", [ATW, L], BF16)
        self.xa_d = sk("xa_d", [L, D], F32)
        self.xb_d = self.dscr("xb_d", [L, D], F32)
        self.rdram = {}

    def din(self, name, shape, dtype=F32):
        return self.nc.dram_tensor(name, list(shape), dtype, kind="ExternalInput").ap()

    def dscr(self, name, shape, dtype):
        return self.nc.dram_tensor(name, list(shape), dtype, kind="Internal").ap()

    def dout(self, name, shape, dtype=F32):
        return self.nc.dram_tensor(name, list(shape), dtype, kind="ExternalOutput").ap()

    def setup_globals(self):
        P = self.P
        g = self.g = {}
        self.psb = []
        for b in range(8):
            self.psb.append((P.psum("psb%d" % b, [128, 512], F32), P.nres_("psb%d" % b)))
            self.psb[-1][1].excl = True
        self.psi = 0
        for n in ("c_ident", "c_bones", "c_tmask", "c_reld"):
            t = P.sbuf(n, [128, 128], F32)
            r = P.nres_(n)
            P.dma("sp", t[:, :], self.i[n][:, :], r, writes=[r])
            g[n] = (t, r)
        t = P.sbuf("bones_b", [128, 128], BF16)
        r = P.nres_("bones_b")
        P.op("dve", lambda e, t=t: e.tensor_copy(out=t[:, :], in_=g["c_bones"][0][:, :]), reads=[g["c_bones"][1]], writes=[r])
        g["bones_b"] = (t, r)
        for nm, val in (("eps_rms", RMS_EPS), ("eps_sub", SUBLN_EPS), ("halfpi", math.pi / 2), ("zero", 0.0)):
            t = P.sbuf(nm, [128, 1], F32)
            r = P.nres_(nm)
            P.op("pool", lambda e, t=t, val=val: e.memset(t[:, :], val), writes=[r])
            g[nm] = (t, r)

    def bank(self):
        b = self.psb[self.psi % 8]
        self.psi += 1
        return b

    def phase_w(self):
        P = self.P
        with contextlib.ExitStack() as st:
            NB = 3
            CW = 4096
            fb = [(P.sbuf("wf", [128, CW], F32, st), P.nres_("wf%d" % i)) for i in range(NB)]
            bb = [(P.sbuf("wbb", [128, CW], BF16, st), P.nres_("wbb%d" % i)) for i in range(NB)]
            rw = P.nres_("wdram")
            it = 0
            ce = ("dve", "pool", "act")
            for l in range(self.nlayers):
                for n, K, N in WSPEC:
                    src = self.i[n]
                    dst = self.wb[n][l]
                    ncc = (N + CW - 1) // CW
                    cw = N // ncc
                    for kt in range(K // 128):
                        for c in range(ncc):
                            (f, rf), (b, rb) = fb[it % NB], bb[it % NB]
                            P.dma("sp", f[:, 0:cw], src[l, kt * 128:(kt + 1) * 128, c * cw:(c + 1) * cw], rf, writes=[rf])
                            eng = ce[it % 3]
                            if eng == "act":
                                P.op("act", lambda e, f=f, b=b, cw=cw: e.activation(out=b[:, 0:cw], in_=f[:, 0:cw], func=AF.Copy), reads=[rf], writes=[rb])
                            else:
                                P.op(eng, lambda e, f=f, b=b, cw=cw: e.tensor_copy(out=b[:, 0:cw], in_=f[:, 0:cw]), reads=[rf], writes=[rb])
                            P.dma("act", dst[kt * 128:(kt + 1) * 128, c * cw:(c + 1) * cw], b[:, 0:cw], rb, reads=[rb], writes=[])
                            it += 1
            P.barrier()

    def norm_T(self, st_bufs, xt, rx, wT, rwT, hT, rhT):
        P, g = self.P, self.g
        junk, rjunk, ssq, rssq, rstd, rrstd, xs = st_bufs
        for s in range(4):
            P.op("act", lambda e, s=s: e.activation(out=junk[:, :], in_=xt[:, s, :], func=AF.Square, accum_out=ssq[:, s:s + 1]),
                 reads=[rx], writes=[rjunk, rssq])
        P.op("act", lambda e: e.activation(out=rstd[:, :], in_=ssq[:, :], func=AF.Sqrt, bias=g["eps_rms"][0][:, :], scale=1.0 / D),
             reads=[rssq, g["eps_rms"][1]], writes=[rrstd])
        P.op("dve", lambda e: e.reciprocal(out=rstd[:, :], in_=rstd[:, :]), reads=[rrstd], writes=[rrstd])
        idt, rid = g["c_ident"]
        for s in range(4):
            xs_t, rxs = xs[s % 2]
            P.op("act", lambda e, s=s, xs_t=xs_t: e.activation(out=xs_t[:, :], in_=xt[:, s, :], func=AF.Copy, scale=rstd[:, s:s + 1]),
                 reads=[rx, rrstd], writes=[rxs])
            for k4 in range(4):
                ps, rps = self.bank()
                for j in range(4):
                    kc = k4 * 4 + j
                    P.op("pe", lambda e, ps=ps, xs_t=xs_t, kc=kc, j=j: e.transpose(out=ps[:, j * 128:(j + 1) * 128], in_=xs_t[:, kc * 128:(kc + 1) * 128], identity=idt[:, :]),
                         reads=[rxs, rid], writes=[rps], signal=(j == 3))
                P.op("dve", lambda e, ps=ps, k4=k4, s=s: e.tensor_tensor(
                    out=hT[:, k4 * 4:k4 * 4 + 4, s * 128:(s + 1) * 128],
                    in0=ps[:, :].rearrange("p (k t) -> p k t", k=4),
                    in1=wT[:, k4 * 4:k4 * 4 + 4].unsqueeze(2).broadcast_to([128, 4, 128]), op=ALU.mult),
                    reads=[rps, rwT], writes=[rhT])

    def norm_bufs(self, st):
        P = self.P
        junk = P.sbuf("junk", [128, D], BF16, st)
        ssq = P.sbuf("ssq", [128, 4], F32, st)
        rstd = P.sbuf("rstd", [128, 4], F32, st)
        xs = [(P.sbuf("xs", [128, D], F32, st), P.nres_("xs%d" % i)) for i in range(2)]
        return (junk, P.nres_("junk"), ssq, P.nres_("ssq"), rstd, P.nres_("rstd"), xs)

    def load_wtile(self, wt, rwt, src, k0, kcn, n0, ncols, q="sp"):
        self.P.dma(q, wt[:, 0:kcn, 0:ncols],
                   src[k0 * 128:(k0 + kcn) * 128, n0:n0 + ncols].rearrange("(kc p) n -> p kc n", p=128),
                   rwt, writes=[rwt])

    def mm_acc(self, ps_ap, rps, pairs, reads):
        n = len(pairs)
        for i, (lhsT, rhs) in enumerate(pairs):
            self.P.op("pe", lambda e, lhsT=lhsT, rhs=rhs, i=i: e.matmul(ps_ap, lhsT=lhsT, rhs=rhs, start=(i == 0), stop=(i == n - 1)),
                      reads=reads, writes=[rps], signal=(i == n - 1))

    def phase_a(self, l, x_src):
        P, g = self.P, self.g
        L = self.L
        with contextlib.ExitStack() as st:
            xt = P.sbuf("xt", [128, 4, D], F32, st)
            rx = P.nres_("xt")
            nbufs = self.norm_bufs(st)
            w1T = P.sbuf("w1T", [128, 16], F32, st)
            rw1 = P.nres_("w1T")
            P.dma("sp", w1T[:, :], self.i["norm1_w"][l].rearrange("(kc p) -> p kc", p=128), rw1, writes=[rw1],
                  allow_slow_non_contiguous=True)
            wqk = P.sbuf("wqk", [128, 2], F32, st)
            rwqk = P.nres_("wqk")
            for ci, nm in enumerate(("q_norm_w", "k_norm_w")):
                for m in range(2):
                    P.dma("sp", wqk[m * 64:(m + 1) * 64, ci:ci + 1], self.i[nm][l:l + 1, :].rearrange("o d -> d o"), rwqk,
                          writes=[rwqk], allow_slow_non_contiguous=True)
            P.op("dve", lambda e: e.tensor_scalar(out=wqk[:, 0:1], in0=wqk[:, 0:1], scalar1=0.125, scalar2=None, op0=ALU.mult),
                 reads=[rwqk], writes=[rwqk])
            hT = P.sbuf("hT", [128, 16, 512], BF16, st)
            rhT = P.nres_("hT")
            wbuf = [(P.sbuf("wA", [128, 16, 512], BF16, st), P.nres_("wA%d" % i)) for i in range(3)]
            ut = [(P.sbuf("ut", [128, 4, 512], F32, st), P.nres_("ut%d" % i)) for i in range(2)]
            vt = [(P.sbuf("vt", [128, 4, 512], BF16, st), P.nres_("vt%d" % i)) for i in range(2)]
            ot = [(P.sbuf("ot", [128, 512], BF16, st), P.nres_("ot%d" % i)) for i in range(3)]
            sq = [(P.sbuf("sq", [128, 512], BF16, st), P.nres_("sq%d" % i)) for i in range(2)]
            rt = [(P.sbuf("rt", [128, 512], F32, st), P.nres_("rt%d" % i)) for i in range(2)]
            bones, rbones = g["bones_b"]
            wsrc = self.wb["w_in"][l]
            nwl = 0
            oi = 0
            for t in range(self.NT):
                t0 = t * 512
                P.dma("sp", xt[:, :, :], x_src[t0:t0 + 512, :].rearrange("(s p) d -> p s d", p=128), rx, writes=[rx])
                self.norm_T(nbufs, xt, rx, w1T, rw1, hT, rhT)
                for c in range(16):
                    wt, rwt = wbuf[nwl % 3]
                    nwl += 1
                    self.load_wtile(wt, rwt, wsrc, 0, 16, c * 512, 512)
                    if c in (0, 1, 6, 7):
                        isu = c < 2
                        stg, rstg = (ut if isu else vt)[c % 2]
                        for s in range(4):
                            ps, rps = self.bank()
                            self.mm_acc(ps[:, :], rps, [(hT[:, kc, s * 128:(s + 1) * 128], wt[:, kc, :]) for kc in range(16)], [rhT, rwt])
                            if s % 2 == 0:
                                P.op("act", lambda e, ps=ps, stg=stg, s=s: e.activation(out=stg[:, s, :], in_=ps[:, :], func=AF.Copy),
                                     reads=[rps], writes=[rstg])
                            else:
                                P.op("dve", lambda e, ps=ps, stg=stg, s=s: e.tensor_copy(out=stg[:, s, :], in_=ps[:, :]),
                                     reads=[rps], writes=[rstg])
                        dst = self.u_d if isu else self.v_d
                        cc = c if isu else c - 6
                        P.dma("pool", dst[t0:t0 + 512, cc * 512:(cc + 1) * 512].rearrange("(s p) n -> p s n", p=128), stg[:, :, :], rstg,
                              reads=[rstg])
                    else:
                        for j in range(4):
                            ps, rps = self.bank()
                            self.mm_acc(ps[:, :], rps, [(wt[:, kc, j * 128:(j + 1) * 128], hT[:, kc, :]) for kc in range(16)], [rhT, rwt])
                            o, ro = ot[oi % 3]
                            oi += 1
                            if c < 6:
                                isq = c < 4
                                sqt, rsq = sq[oi % 2]
                                rtt, rrt = rt[oi % 2]
                                P.op("act", lambda e, ps=ps, sqt=sqt: e.activation(out=sqt[:, :], in_=ps[:, :], func=AF.Square), reads=[rps], writes=[rsq])
                                ps2, rps2 = self.bank()
                                self.mm_acc(ps2[:, :], rps2, [(bones[:, :], sqt[:, :])], [rbones, rsq])
                                P.op("act", lambda e, ps2=ps2, rtt=rtt: e.activation(out=rtt[:, :], in_=ps2[:, :], func=AF.Sqrt, bias=g["eps_rms"][0][:, :], scale=1.0 / 64),
                                     reads=[rps2, g["eps_rms"][1]], writes=[rrt])
                                P.op("dve", lambda e, rtt=rtt: e.reciprocal(out=rtt[:, :], in_=rtt[:, :]), reads=[rrt], writes=[rrt])
                                ci = 0 if isq else 1
                                P.op("dve", lambda e, ps=ps, rtt=rtt, o=o, ci=ci: e.scalar_tensor_tensor(
                                    out=o[:, :], in0=ps[:, :], scalar=wqk[:, ci:ci + 1], in1=rtt[:, :], op0=ALU.mult, op1=ALU.mult),
                                    reads=[rps, rrt, rwqk], writes=[ro])
                                dst = self.qT_d if isq else self.kT_d
                                r0 = ((c - 2) if isq else (c - 4)) * 512 + j * 128
                            else:
                                P.op("act", lambda e, ps=ps, o=o: e.activation(out=o[:, :], in_=ps[:, :], func=AF.Sigmoid), reads=[rps], writes=[ro])
                                dst = self.gsT_d if c < 12 else self.gaT_d
                                r0 = ((c - 8) if c < 12 else (c - 12)) * 512 + j * 128
                            P.dma("pool", dst[r0:r0 + 128, t0:t0 + 512], o[:, :], ro, reads=[ro])
            P.barrier()

    def build(self):
        ph = self.phases
        self.setup_globals()
        if ph is None or "att" in ph:
            self.setup_attn()
        if ph is None or "w" in ph:
            self.phase_w()
        for l in range(self.nlayers):
            x_src = self.i["x"] if l == 0 else self.xb_d
            x_dst = self.out if l == self.nlayers - 1 else self.xb_d
            if ph is None or "a" in ph:
                self.phase_a(l, x_src)
            if ph is None or "ssm" in ph:
                self.phase_ssm(l)
            if ph is None or "att" in ph:
                self.phase_att(l)
            if ph is None or "b1" in ph:
                self.phase_b1(l, x_src)
            if ph is None or "b2" in ph:
                self.phase_b2(l, x_dst)
        self.P.barrier()
        self.P.emit()
        return self.nc

    def cmul(self, eng, out_re, out_im, a_re, a_im, b_re, b_im, tmp, rres, wres, neg_im=False):
        P = self.P
        t1, t2 = tmp
        ops = [
            (t1, a_re, b_re, ALU.mult), (t2, a_im, b_im, ALU.mult), (out_re, t1, t2, ALU.subtract),
            (t1, a_re, b_im, ALU.mult), (t2, a_im, b_re, ALU.mult), (out_im, t1, t2, ALU.add),
        ]
        for o, x, y, op in ops:
            P.op(eng, lambda e, o=o, x=x, y=y, op=op: e.tensor_tensor(out=o, in0=x, in1=y, op=op), reads=rres, writes=wres)
        if neg_im:
            P.op(eng, lambda e, o=out_im: e.tensor_scalar(out=o, in0=o, scalar1=-1.0, scalar2=None, op0=ALU.mult), reads=wres, writes=wres)

    def phase_ssm(self, l):
        P, g = self.P, self.g
        L = self.L
        NBLK = L // 8
        bp = min(128, NBLK)
        nbt = NBLK // bp
        nsteps = int(math.ceil(math.log2(NBLK)))
        idt, rid = g["c_ident"]
        tmask, rtm = g["c_tmask"]
        with contextlib.ExitStack() as st:
            rp = P.nres_("ssm_par")
            def T(name, shape, dt=F32):
                return P.sbuf(name, shape, dt, st)
            lre, lim, dtt = T("lre", [64, 64]), T("lim", [64, 64]), T("dtt", [64, 64])
            P.dma("sp", lre[:, :], self.i["lam_re"][l].rearrange("g p -> p g"), rp, writes=[rp], allow_slow_non_contiguous=True)
            P.dma("sp", lim[:, :], self.i["lam_im"][l].rearrange("g p -> p g"), rp, writes=[rp], allow_slow_non_contiguous=True)
            P.dma("sp", dtt[:, :], self.i["log_step"][l].partition_broadcast(64), rp, writes=[rp])
            Bre, Bim = T("Bre", [64, 64, 16]), T("Bim", [64, 64, 16])
            P.dma("sp", Bre[:, :, :], self.i["ssm_b_re"][l].rearrange("g p c -> p g c"), rp, writes=[rp])
            P.dma("sp", Bim[:, :, :], self.i["ssm_b_im"][l].rearrange("g p c -> p g c"), rp, writes=[rp])
            Dcol = T("Dcol", [128, 64])
            for j in range(8):
                P.dma("sp", Dcol[16 * j:16 * j + 16, :], self.i["ssm_d"][l].rearrange("(g c) -> c g", c=16), rp, writes=[rp],
                      allow_slow_non_contiguous=True)
            Cre, Cim = T("Cre", [64, 64, 16]), T("Cim", [64, 64, 16])
            crow = T("crow", [128, 64])
            for nm, dstt in (("ssm_c_re", Cre), ("ssm_c_im", Cim)):
                src = self.i[nm][l].rearrange("g c p -> (g c) p")
                for k in range(8):
                    P.dma("sp", crow[:, :], src[k * 128:(k + 1) * 128, :], rp, reads=[rp], writes=[rp])
                    ps, rps = self.bank()
                    P.op("pe", lambda e, ps=ps: e.transpose(out=ps[0:64, 0:128], in_=crow[:, :], identity=idt[:, :]), reads=[rp, rid], writes=[rps])
                    P.op("dve", lambda e, ps=ps, dstt=dstt, k=k: e.tensor_copy(out=dstt[:, k * 8:(k + 1) * 8, :], in_=ps[0:64, 0:128].rearrange("p (g c) -> p g c", c=16)),
                         reads=[rps], writes=[rp])
            V = lambda fn, rd=(rp,), wr=(rp,): P.op("dve", fn, reads=list(rd), writes=list(wr))
            A = lambda fn: P.op("act", fn, reads=[rp, g["halfpi"][1]], writes=[rp])
            lr, x1, mag, ang, tq, r_, m1 = [T(n, [64, 64]) for n in ("lr", "x1", "mag", "ang", "tq", "r_", "m1")]
            ti = T("ti", [64, 64], I32)
            sn, cs, ar, ai = [T(n, [64, 64]) for n in ("sn", "cs", "ar", "ai")]
            V(lambda e: e.tensor_scalar(out=lr[:, :], in0=lre[:, :], scalar1=-1e-4, scalar2=None, op0=ALU.min))
            A(lambda e: e.activation(out=dtt[:, :], in_=dtt[:, :], func=AF.Exp))
            V(lambda e: e.tensor_tensor(out=x1[:, :], in0=lr[:, :], in1=dtt[:, :], op=ALU.mult))
            A(lambda e: e.activation(out=mag[:, :], in_=x1[:, :], func=AF.Exp))
            V(lambda e: e.tensor_tensor(out=ang[:, :], in0=lim[:, :], in1=dtt[:, :], op=ALU.mult))
            V(lambda e: e.tensor_scalar(out=tq[:, :], in0=ang[:, :], scalar1=1.0 / (2 * math.pi), scalar2=0.5, op0=ALU.mult, op1=ALU.add))
            V(lambda e: e.tensor_copy(out=ti[:, :], in_=tq[:, :]))
            V(lambda e: e.tensor_copy(out=tq[:, :], in_=ti[:, :]))
            V(lambda e: e.scalar_tensor_tensor(out=r_[:, :], in0=tq[:, :], scalar=-2 * math.pi, in1=ang[:, :], op0=ALU.mult, op1=ALU.add))
            for thr, opc, add in ((-math.pi, ALU.is_lt, 2 * math.pi), (math.pi, ALU.is_gt, -2 * math.pi),
                                  (-math.pi, ALU.is_lt, 2 * math.pi), (math.pi, ALU.is_gt, -2 * math.pi)):
                V(lambda e, thr=thr, opc=opc: e.tensor_single_scalar(out=m1[:, :], in_=r_[:, :], scalar=thr, op=opc))
                V(lambda e, add=add: e.scalar_tensor_tensor(out=r_[:, :], in0=m1[:, :], scalar=add, in1=r_[:, :], op0=ALU.mult, op1=ALU.add))
            V(lambda e: e.tensor_scalar(out=r_[:, :], in0=r_[:, :], scalar1=math.pi, scalar2=-math.pi, op0=ALU.min, op1=ALU.max))
            A(lambda e: e.activation(out=sn[:, :], in_=r_[:, :], func=AF.Sin))
            V(lambda e: e.tensor_scalar(out=m1[:, :], in0=r_[:, :], scalar1=-1.0, scalar2=None, op0=ALU.mult))
            V(lambda e: e.tensor_tensor(out=m1[:, :], in0=m1[:, :], in1=r_[:, :], op=ALU.max))
            A(lambda e: e.activation(out=cs[:, :], in_=m1[:, :], func=AF.Sin, bias=g["halfpi"][0][0:64, :], scale=-1.0))
            V(lambda e: e.tensor_tensor(out=ar[:, :], in0=mag[:, :], in1=cs[:, :], op=ALU.mult))
            V(lambda e: e.tensor_tensor(out=ai[:, :], in0=mag[:, :], in1=sn[:, :], op=ALU.mult))
            den, nr, fr, fi, t1, t2 = [T(n, [64, 64]) for n in ("den", "nr", "fr", "fi", "t1", "t2")]
            V(lambda e: e.tensor_tensor(out=den[:, :], in0=lr[:, :], in1=lr[:, :], op=ALU.mult))
            V(lambda e: e.tensor_tensor(out=t1[:, :], in0=lim[:, :], in1=lim[:, :], op=ALU.mult))
            V(lambda e: e.tensor_tensor(out=den[:, :], in0=den[:, :], in1=t1[:, :], op=ALU.add))
            V(lambda e: e.reciprocal(out=den[:, :], in_=den[:, :]))
            V(lambda e: e.tensor_scalar(out=nr[:, :], in0=ar[:, :], scalar1=-1.0, scalar2=None, op0=ALU.add))
            V(lambda e: e.tensor_tensor(out=t1[:, :], in0=nr[:, :], in1=lr[:, :], op=ALU.mult))
            V(lambda e: e.tensor_tensor(out=t2[:, :], in0=ai[:, :], in1=lim[:, :], op=ALU.mult))
            V(lambda e: e.tensor_tensor(out=t1[:, :], in0=t1[:, :], in1=t2[:, :], op=ALU.add))
            V(lambda e: e.tensor_tensor(out=fr[:, :], in0=t1[:, :], in1=den[:, :], op=ALU.mult))
            V(lambda e: e.tensor_tensor(out=t1[:, :], in0=ai[:, :], in1=lr[:, :], op=ALU.mult))
            V(lambda e: e.tensor_tensor(out=t2[:, :], in0=nr[:, :], in1=lim[:, :], op=ALU.mult))
            V(lambda e: e.tensor_tensor(out=t1[:, :], in0=t1[:, :], in1=t2[:, :], op=ALU.subtract))
            V(lambda e: e.tensor_tensor(out=fi[:, :], in0=t1[:, :], in1=den[:, :], op=ALU.mult))
            Bbr, Bbi = T("Bbr", [64, 64, 16]), T("Bbi", [64, 64, 16])
            tb1, tb2 = T("tb1", [64, 64, 16]), T("tb2", [64, 64, 16])
            bc16 = lambda a: a[:, :].unsqueeze(2).broadcast_to([64, 64, 16])
            self.cmul("dve", Bbr[:, :, :], Bbi[:, :, :], bc16(fr), bc16(fi), Bre[:, :, :], Bim[:, :, :], (tb1[:, :, :], tb2[:, :, :]), [rp], [rp])
            PWr, PWi = T("PWr", [64, 64, 9]), T("PWi", [64, 64, 9])
            PIr, PIi = T("PIr", [64, 64, 8]), T("PIi", [64, 64, 8])
            PRr, PRi = T("PRr", [64, 64, 8]), T("PRi", [64, 64, 8])
            air, aii = T("air", [64, 64]), T("aii", [64, 64])
            V(lambda e: e.tensor_tensor(out=t1[:, :], in0=ar[:, :], in1=ar[:, :], op=ALU.mult))
            V(lambda e: e.tensor_tensor(out=t2[:, :], in0=ai[:, :], in1=ai[:, :], op=ALU.mult))
            V(lambda e: e.tensor_tensor(out=t1[:, :], in0=t1[:, :], in1=t2[:, :], op=ALU.add))
            V(lambda e: e.reciprocal(out=t1[:, :], in_=t1[:, :]))
            V(lambda e: e.tensor_tensor(out=air[:, :], in0=ar[:, :], in1=t1[:, :], op=ALU.mult))
            V(lambda e: e.scalar_tensor_tensor(out=aii[:, :], in0=ai[:, :], scalar=-1.0, in1=t1[:, :], op0=ALU.mult, op1=ALU.mult))
            for (Xr, Xi, br_, bi_, n) in ((PWr, PWi, ar, ai, 9), (PIr, PIi, air, aii, 8)):
                V(lambda e, Xr=Xr: e.memset(Xr[:, :, 0:1], 1.0))
                V(lambda e, Xi=Xi: e.memset(Xi[:, :, 0:1], 0.0))
                for k in range(1, n):
                    self.cmul("dve", Xr[:, :, k], Xi[:, :, k], Xr[:, :, k - 1], Xi[:, :, k - 1], br_[:, :], bi_[:, :], (t1[:, :], t2[:, :]), [rp], [rp])
            for j in range(8):
                V(lambda e, j=j: e.tensor_copy(out=PRr[:, :, j], in_=PWr[:, :, 7 - j]))
                V(lambda e, j=j: e.tensor_copy(out=PRi[:, :, j], in_=PWi[:, :, 7 - j]))
            APr, APi = T("APr", [64, 64, nsteps]), T("APi", [64, 64, nsteps])
            V(lambda e: e.tensor_copy(out=APr[:, :, 0], in_=PWr[:, :, 8]))
            V(lambda e: e.tensor_copy(out=APi[:, :, 0], in_=PWi[:, :, 8]))
            for k in range(1, nsteps):
                self.cmul("dve", APr[:, :, k], APi[:, :, k], APr[:, :, k - 1], APi[:, :, k - 1], APr[:, :, k - 1], APi[:, :, k - 1], (t1[:, :], t2[:, :]), [rp], [rp])
            GB = 4
            FW = GB * 16
            Bjr, Bji, BEr, BEi = [T(n, [64, GB, 8, 16]) for n in ("Bjr", "Bji", "BEr", "BEi")]
            Crr, Cri = T("Crr", [64, GB, 9, 16]), T("Cri", [64, GB, 9, 16])
            tg1, tg2 = T("tg1", [64, GB, 9, 16]), T("tg2", [64, GB, 9, 16])
            rgen = P.nres_("ssm_gen")
            Z = T("Z", [128, nbt, 8, FW])
            rZ = P.nres_("ssm_Z")
            Zc = T("Zc", [128, nbt, GB, 8, 16])
            rZc = P.nres_("ssm_Zc")
            U8 = T("U8", [128, GB, NBLK])
            rU8 = P.nres_("ssm_U8")
            LE = [(T("LE", [128, 2, 64]), P.nres_("ssm_LE%d" % i)) for i in range(2)]
            Tt = T("Tt", [128, GB, 128])
            rTt = P.nres_("ssm_Tt")
            tmpT = [(T("tmpT", [128, 128]), P.nres_("ssm_tmpT%d" % i)) for i in range(2)]
            W = NBLK + 1
            S = [[T("S%d%s" % (i, c), [64, GB, W]) for c in "ri"] for i in range(2)]
            rS = [[P.nres_("ssm_S%d%s" % (i, c)) for c in "ri"] for i in range(2)]
            ht = [[T("ht%d%s" % (i, c), [64, GB, NBLK]) for i in range(2)] for c in "ri"]
            rht = [P.nres_("ssm_ht%s" % c) for c in "ri"]
            Y8 = [(T("Y8", [128, NBLK]), P.nres_("ssm_Y8%d" % i)) for i in range(2)]
            Yt = T("Yt", [128, nbt, 8, FW])
            rYt = P.nres_("ssm_Yt")
            for i in range(2):
                for c in range(2):
                    P.op("pool", lambda e, i=i, c=c: e.memset(S[i][c][:, :, 0:1], 0.0), writes=[rS[i][c]])
            for gb in range(NG // GB):
                g0 = gb * GB
                gs_ = slice(g0, g0 + GB)
                bj = lambda a: a[:, gs_, :].unsqueeze(3).broadcast_to([64, GB, a.shape[2], 16])
                bb = lambda a, n: a[:, gs_, :].unsqueeze(2).broadcast_to([64, GB, n, 16])
                t8 = (tg1[:, :, 0:8, :], tg2[:, :, 0:8, :])
                self.cmul("pool", Bjr[:, :, :, :], Bji[:, :, :, :], bj(PIr), bj(PIi), bb(Bbr, 8), bb(Bbi, 8), t8, [rp], [rgen])
                self.cmul("pool", BEr[:, :, :, :], BEi[:, :, :, :], bj(PRr), bj(PRi), bb(Bbr, 8), bb(Bbi, 8), t8, [rp], [rgen])
                self.cmul("pool", Crr[:, :, :, :], Cri[:, :, :, :], bj(PWr), bj(PWi), bb(Cre, 9), bb(Cim, 9), (tg1[:, :, :, :], tg2[:, :, :, :]), [rp], [rgen], neg_im=True)
                for bt in range(nbt):
                    P.dma("sp", Z[0:bp, bt, :, :], self.u_d[bt * bp * 8:(bt + 1) * bp * 8, g0 * 16:g0 * 16 + FW].rearrange("(b j) f -> b j f", j=8), rZ, writes=[rZ])
                P.op("pool", lambda e: e.tensor_copy(out=Zc[0:bp, :, :, :, :], in_=Z[0:bp, :, :, :].rearrange("p b j (g c) -> p b g j c", c=16)), reads=[rZ], writes=[rZc])
                for gi in range(GB):
                    gg = g0 + gi
                    ps, rps = self.bank()
                    for bt in range(nbt):
                        P.op("pe", lambda e, ps=ps, bt=bt, gi=gi: e.transpose(out=ps[:, bt * bp:(bt + 1) * bp], in_=Zc[0:bp, bt, gi, :, :].rearrange("p j c -> p (j c)"), identity=idt[0:bp, 0:bp]),
                             reads=[rZc, rid], writes=[rps], signal=(bt == nbt - 1))
                    P.op("act", lambda e, ps=ps, gi=gi: e.activation(out=U8[:, gi, :], in_=ps[:, 0:NBLK], func=AF.Copy), reads=[rps], writes=[rU8])
                    le, rle = LE[gi % 2]
                    ps, rps = self.bank()
                    P.op("pe", lambda e, ps=ps, gi=gi: e.transpose(out=ps[:, 0:64], in_=BEr[:, gi, :, :].rearrange("p j c -> p (j c)"), identity=idt[0:64, 0:64]), reads=[rgen, rid], writes=[rps], signal=False)
                    P.op("pe", lambda e, ps=ps, gi=gi: e.transpose(out=ps[:, 64:128], in_=BEi[:, gi, :, :].rearrange("p j c -> p (j c)"), identity=idt[0:64, 0:64]), reads=[rgen, rid], writes=[rps])
                    P.op("dve", lambda e, ps=ps, le=le: e.tensor_copy(out=le[:, :, :], in_=ps[:, 0:128].rearrange("p (c s) -> p c s", c=2)), reads=[rps], writes=[rle])
                    for c in range(2):
                        ps, rps = self.bank()
                        self.mm_acc(ps[0:64, 0:NBLK], rps, [(le[:, c, :], U8[:, gi, :])], [rle, rU8])
                        P.op("act" if c == 0 else "dve",
                             (lambda e, ps=ps, gi=gi: e.activation(out=S[0][0][:, gi, 1:W], in_=ps[0:64, 0:NBLK], func=AF.Copy)) if c == 0 else
                             (lambda e, ps=ps, gi=gi: e.tensor_copy(out=S[0][1][:, gi, 1:W], in_=ps[0:64, 0:NBLK])),
                             reads=[rps], writes=[rS[0][c]])
                    ps, rps = self.bank()
                    self.mm_acc(ps[:, 0:128], rps, [(Bjr[:, gi, :, :].rearrange("p j c -> p (j c)"), Crr[:, gi, 0:8, :].rearrange("p j c -> p (j c)")), (Bji[:, gi, :, :].rearrange("p j c -> p (j c)"), Cri[:, gi, 0:8, :].rearrange("p j c -> p (j c)"))], [rgen])
                    tt_, rtt_ = tmpT[gi % 2]
                    P.op("dve", lambda e, ps=ps, tt_=tt_: e.tensor_tensor(out=tt_[:, :], in0=ps[:, 0:128], in1=tmask[:, :], op=ALU.mult), reads=[rps, rtm], writes=[rtt_])
                    P.op("dve", lambda e, tt_=tt_, gi=gi, gg=gg: e.scalar_tensor_tensor(out=Tt[:, gi, :], in0=idt[:, :], scalar=Dcol[:, gg:gg + 1], in1=tt_[:, :], op0=ALU.mult, op1=ALU.add),
                         reads=[rtt_, rid, rp], writes=[rTt])
                cur = 0
                for k in range(nsteps):
                    sh = 1 << k
                    n = NBLK - sh
                    src, dst = S[cur], S[1 - cur]
                    rsrc, rdst = rS[cur], rS[1 - cur]
                    Ar = APr[:, gs_, k:k + 1].broadcast_to([64, GB, n])
                    Ai = APi[:, gs_, k:k + 1].broadcast_to([64, GB, n])
                    for c, eng in ((0, "dve"), (1, "pool")):
                        h1, h2 = ht[c]
                        P.op(eng, lambda e, c=c, src=src, dst=dst, sh=sh: e.tensor_copy(out=dst[c][:, :, 1:1 + sh], in_=src[c][:, :, 1:1 + sh]), reads=[rsrc[c]], writes=[rdst[c]])
                        P.op(eng, lambda e, c=c, src=src, h1=h1, Ar=Ar, n=n: e.tensor_tensor(out=h1[:, :, 0:n], in0=src[c][:, :, 1:1 + n], in1=Ar, op=ALU.mult), reads=[rsrc[c], rp], writes=[rht[c]])
                        P.op(eng, lambda e, c=c, src=src, h1=h1, sh=sh, n=n: e.tensor_tensor(out=h1[:, :, 0:n], in0=h1[:, :, 0:n], in1=src[c][:, :, 1 + sh:1 + sh + n], op=ALU.add), reads=[rsrc[c], rht[c]], writes=[rht[c]])
                        P.op(eng, lambda e, c=c, src=src, h2=h2, Ai=Ai, n=n: e.tensor_tensor(out=h2[:, :, 0:n], in0=src[1 - c][:, :, 1:1 + n], in1=Ai, op=ALU.mult), reads=[rsrc[1 - c], rp], writes=[rht[c]])
                        P.op(eng, lambda e, c=c, dst=dst, h1=h1, h2=h2, sh=sh, n=n: e.tensor_tensor(out=dst[c][:, :, 1 + sh:1 + sh + n], in0=h1[:, :, 0:n], in1=h2[:, :, 0:n], op=(ALU.subtract if c == 0 else ALU.add)),
                             reads=[rht[c]], writes=[rdst[c]])
                    cur = 1 - cur
                Sf, rSf = S[cur], rS[cur]
                for gi in range(GB):
                    ps, rps = self.bank()
                    self.mm_acc(ps[:, 0:NBLK], rps, [(Tt[:, gi, :], U8[:, gi, :]), (Crr[:, gi, 1:9, :].rearrange("p j c -> p (j c)"), Sf[0][:, gi, 0:NBLK]), (Cri[:, gi, 1:9, :].rearrange("p j c -> p (j c)"), Sf[1][:, gi, 0:NBLK])],
                                [rTt, rU8, rgen, rSf[0], rSf[1]])
                    y8, ry8 = Y8[gi % 2]
                    P.op("act", lambda e, ps=ps, y8=y8: e.activation(out=y8[:, :], in_=ps[:, 0:NBLK], func=AF.Copy), reads=[rps], writes=[ry8])
                    ps, rps = self.bank()
                    for bt in range(nbt):
                        P.op("pe", lambda e, ps=ps, bt=bt, y8=y8: e.transpose(out=ps[0:bp, bt * 128:(bt + 1) * 128], in_=y8[:, bt * bp:(bt + 1) * bp], identity=idt[:, :]),
                             reads=[ry8, rid], writes=[rps], signal=(bt == nbt - 1))
                    P.op("dve", lambda e, ps=ps, gi=gi: e.tensor_copy(out=Yt[0:bp, :, :, gi * 16:(gi + 1) * 16], in_=ps[0:bp, 0:nbt * 128].rearrange("p (b j c) -> p b j c", b=nbt, j=8)),
                         reads=[rps], writes=[rYt])
                for bt in range(nbt):
                    P.dma("pool", self.y_d[bt * bp * 8:(bt + 1) * bp * 8, g0 * 16:g0 * 16 + FW].rearrange("(b j) f -> b j f", j=8), Yt[0:bp, bt, :, :], rYt, reads=[rYt])
            P.barrier()

    def setup_attn(self):
        P, g = self.P, self.g
        reld, rreld = g["c_reld"]
        tab = P.sbuf("tab", [128, 256], F32)
        rtab = P.nres_("tab")
        P.dma("sp", tab[:, :], self.i["rel_bias"].rearrange("b h -> (b h)").partition_broadcast(128), rtab, writes=[rtab])
        steps = [(-90, 15, 14), (-63, 14, 13), (-45, 13, 12), (-31, 12, 11), (-22, 11, 10), (-15, 10, 9), (-11, 9, 8)]
        steps += [(-n, n + 1, n) for n in range(7, -1, -1)]
        steps += [(1, 0, 17)] + [(n, 15 + n, 16 + n) for n in range(2, 8)]
        steps += [(8, 23, 24), (12, 24, 25), (16, 25, 26), (23, 26, 27), (32, 27, 28), (46, 28, 29), (64, 29, 30), (91, 30, 31)]
        ns = len(steps)
        dl = P.sbuf("dl", [128, ns, 8], F32)
        rdl = P.nres_("dl")
        for s, (thr, fb, tb) in enumerate(steps):
            P.op("dve", lambda e, s=s, fb=fb, tb=tb: e.tensor_tensor(out=dl[:, s, :], in0=tab[:, tb * 8:tb * 8 + 8], in1=tab[:, fb * 8:fb * 8 + 8], op=ALU.subtract),
                 reads=[rtab], writes=[rdl])
        bias = P.sbuf("biasT", [128, NH, 2, 128], F32)
        rbias = P.nres_("biasT")
        mk = P.sbuf("mk", [128, 128], F32)
        rmk = P.nres_("mk")
        for kind in range(2):
            off = -128.0 * kind
            for h in range(NH):
                P.op("dve", lambda e, h=h, kind=kind: e.tensor_scalar(out=bias[:, h, kind, :], in0=reld[:, :], scalar1=0.0, scalar2=tab[:, 15 * 8 + h:15 * 8 + h + 1], op0=ALU.mult, op1=ALU.add),
                     reads=[rreld, rtab], writes=[rbias])
            for s, (thr, fb, tb) in enumerate(steps):
                if kind == 1 and thr > -1:
                    continue
                if thr > 64:
                    continue
                P.op("dve", lambda e, thr=thr, off=off: e.tensor_single_scalar(out=mk[:, :], in_=reld[:, :], scalar=float(thr) - off, op=ALU.is_ge), reads=[rreld], writes=[rmk])
                for h in range(NH):
                    P.op("dve", lambda e, h=h, kind=kind, s=s: e.scalar_tensor_tensor(out=bias[:, h, kind, :], in0=mk[:, :], scalar=dl[:, s, h:h + 1], in1=bias[:, h, kind, :], op0=ALU.mult, op1=ALU.add),
                         reads=[rmk, rdl], writes=[rbias])
        for h in range(NH):
            P.op("pool", lambda e, h=h: e.memset(bias[64:128, h, 0, 0:64], -30000.0), reads=[rbias], writes=[rbias])
        g["tab"] = (tab, rtab)
        g["biasT"] = (bias, rbias)

    def bank2(self, lo, n, key):
        c = self.bctr.get(key, 0)
        self.bctr[key] = c + 1
        return self.psb[lo + c % n]

    def phase_att(self, l):
        P, g = self.P, self.g
        L = self.L
        nq = L // 128
        lam_init = 0.8 - 0.6 * math.exp(-0.3 * l)
        idt, rid = g["c_ident"]
        tab, rtab = g["tab"]
        bias, rbias = g["biasT"]
        self.bctr = {}
        with contextlib.ExitStack() as st:
            T = lambda name, shape, dt=F32: P.sbuf(name, shape, dt, st)
            rpar = P.nres_("att_par")
            lq = T("lq", [128, 4, 64])
            for ci, nm in enumerate(("lambda_q1", "lambda_k1", "lambda_q2", "lambda_k2")):
                P.dma("sp", lq[:, ci, :], self.i[nm][l].partition_broadcast(128), rpar, writes=[rpar])
            subw = T("subw", [128, 128])
            P.dma("sp", subw[:, :], self.i["subln_w"][l].partition_broadcast(128), rpar, writes=[rpar])
            e12 = T("e12", [128, 2])
            nlam = T("nlam", [128, 1])
            V = lambda fn: P.op("dve", fn, reads=[rpar], writes=[rpar])
            V(lambda e: e.tensor_tensor(out=lq[:, 0, :], in0=lq[:, 0, :], in1=lq[:, 1, :], op=ALU.mult))
            V(lambda e: e.tensor_tensor(out=lq[:, 2, :], in0=lq[:, 2, :], in1=lq[:, 3, :], op=ALU.mult))
            V(lambda e: e.reduce_sum(out=e12[:, 0:1], in_=lq[:, 0, :], axis=AX.X))
            V(lambda e: e.reduce_sum(out=e12[:, 1:2], in_=lq[:, 2, :], axis=AX.X))
            P.op("act", lambda e: e.activation(out=e12[:, :], in_=e12[:, :], func=AF.Exp), reads=[rpar], writes=[rpar])
            V(lambda e: e.tensor_tensor(out=nlam[:, :], in0=e12[:, 1:2], in1=e12[:, 0:1], op=ALU.subtract))
            V(lambda e: e.tensor_scalar(out=nlam[:, :], in0=nlam[:, :], scalar1=-lam_init, scalar2=None, op0=ALU.add))
            V(lambda e: e.tensor_scalar(out=subw[:, :], in0=subw[:, :], scalar1=1.0 - lam_init, scalar2=None, op0=ALU.mult))
            nb = nq
            KT = [(T("KT", [128, L], BF16), P.nres_("att_KT%d" % i)) for i in range(2)]
            QT = [[(T("QT", [128, L], BF16), P.nres_("att_QT%d_%d" % (i, m))) for m in range(2)] for i in range(2)]
            V1 = [(T("V1", [128, nb, 132], BF16), P.nres_("att_V1%d" % i)) for i in range(2)]
            for v1, rv1 in V1:
                P.op("pool", lambda e, v1=v1: e.memset(v1[:, :, 128:132], 0.0), writes=[rv1])
                P.op("pool", lambda e, v1=v1: e.memset(v1[:, :, 128:129], 1.0), writes=[rv1])
            for i in range(2):
                for m in range(2):
                    qz, rqz = QT[i][m]
                    P.op("pool", lambda e, qz=qz: e.memset(qz[:, :], 0.0), writes=[rqz])
            PT = [(T("PT", [128, 4, 128], BF16), P.nres_("att_PT%d" % i)) for i in range(3)]
            tS = [(T("tS", [128, 128]), P.nres_("att_tS%d" % i)) for i in range(3)]
            rc = [(T("rc", [128, 2]), P.nres_("att_rc%d" % i)) for i in range(2)]
            oT = [(T("oT", [128, 128]), P.nres_("att_oT%d" % i)) for i in range(2)]
            on = [(T("on", [128, 128]), P.nres_("att_on%d" % i)) for i in range(2)]
            sj = [(T("sj", [128, 128], BF16), P.nres_("att_sj%d" % i)) for i in range(2)]
            ssq = [(T("ssq", [128, 1]), P.nres_("att_ssq%d" % i)) for i in range(2)]
            yo = [(T("yo", [128, 512], BF16), P.nres_("att_yo%d" % i)) for i in range(2)]
            npt = 0
            nts = 0
            import os as _os
            STOP = int(_os.environ.get("ATT_STOP", "4"))
            for h in range(NH if STOP > 0 else 0):
                (kt, rkt), (v1, rv1) = KT[h % 2], V1[h % 2]
                qts = QT[h % 2]
                P.dma("sp", kt[:, :], self.kT_d[h * 128:(h + 1) * 128, :], rkt, writes=[rkt])
                for m in range(2):
                    P.dma("sp", qts[m][0][m * 64:(m + 1) * 64, :], self.qT_d[h * 128 + m * 64:h * 128 + (m + 1) * 64, :], qts[m][1], writes=[qts[m][1]])
                P.dma("sp", v1[:, :, 0:128], self.v_d[:, h * 128:(h + 1) * 128].rearrange("(j p) e -> p j e", p=128), rv1, writes=[rv1])
                cb = tab[:, 15 * 8 + h:15 * 8 + h + 1]
                tps = None
                for i in range(nq if STOP > 1 else 0):
                    ops_ = []
                    for m in range(2):
                        qt, rqt = qts[m]
                        o_ps, ro = self.bank2(0, 4, "O")
                        ops_.append((o_ps, ro))
                        pv = []
                        for j0 in range(0, i + 1, 4):
                            grp = list(range(j0, min(j0 + 4, i + 1)))
                            s_ps, rs = self.bank2(4, 3, "S")
                            for idx, j in enumerate(grp):
                                P.op("pe", lambda e, s_ps=s_ps, idx=idx, j=j, m=m, i=i, kt=kt, qt=qt: e.matmul(
                                    s_ps[:, idx * 128:(idx + 1) * 128], lhsT=kt[:, j * 128:(j + 1) * 128],
                                    rhs=qt[:, i * 128:(i + 1) * 128], start=True, stop=True),
                                    reads=[rkt, rqt], writes=[rs], signal=(idx == len(grp) - 1))
                            pt, rpt = PT[npt % 3]
                            npt += 1
                            nfar = len([j for j in grp if j <= i - 2])
                            if nfar and not _os.environ.get("ATT_NOFAR"):
                                if _os.environ.get("ATT_FAR2D"):
                                    for a_ in range(nfar):
                                        P.op("act", lambda e, s_ps=s_ps, pt=pt, a_=a_, cb=cb: e.activation(
                                            out=pt[:, a_, :], in_=s_ps[:, a_ * 128:(a_ + 1) * 128], func=AF.Exp, bias=cb),
                                            reads=[rs, rtab], writes=[rpt])
                                elif _os.environ.get("ATT_FARNB"):
                                    P.op("act", lambda e, s_ps=s_ps, pt=pt, nfar=nfar, cb=cb: e.activation(
                                        out=pt[:, 0:nfar, :], in_=s_ps[:, 0:nfar * 128].rearrange("p (a b) -> p a b", a=nfar), func=AF.Exp),
                                        reads=[rs, rtab], writes=[rpt])
                                else:
                                    P.op("act", lambda e, s_ps=s_ps, pt=pt, nfar=nfar, cb=cb: e.activation(
                                        out=pt[:, 0:nfar, :], in_=s_ps[:, 0:nfar * 128].rearrange("p (a b) -> p a b", a=nfar), func=AF.Exp, bias=cb),
                                        reads=[rs, rtab], writes=[rpt])
                            for idx, j in enumerate(grp):
                                if j <= i - 2 or _os.environ.get("ATT_NONEAR"):
                                    continue
                                kind = 0 if j == i else 1
                                ts_, rts = tS[nts % 3]
                                nts += 1
                                P.op("dve", lambda e, s_ps=s_ps, idx=idx, ts_=ts_, kind=kind, h=h: e.tensor_tensor(
                                    out=ts_[:, :], in0=s_ps[:, idx * 128:(idx + 1) * 128], in1=bias[:, h, kind, :], op=ALU.add),
                                    reads=[rs, rbias], writes=[rts])
                                P.op("act", lambda e, ts_=ts_, pt=pt, idx=idx: e.activation(out=pt[:, idx, :], in_=ts_[:, :], func=AF.Exp),
                                     reads=[rts], writes=[rpt])
                            for idx, j in enumerate(grp if STOP > 2 else []):
                                P.op("pe", lambda e, o_ps=o_ps, pt=pt, idx=idx, j=j, i=i, v1=v1: e.matmul(
                                    o_ps[:, 0:130], lhsT=pt[:, idx, :], rhs=v1[:, j, 0:130], start=(j == 0), stop=(j == i)),
                                    reads=[rpt, rv1], writes=[ro], signal=(j == i))
                    if STOP < 4:
                        continue
                    (o0, ro0), (o1, ro1) = ops_
                    rct, rrc = rc[i % 2]
                    ot_, rot = oT[i % 2]
                    on_, ron = on[i % 2]
                    sj_, rsj = sj[i % 2]
                    sq_, rsq = ssq[i % 2]
                    P.op("dve", lambda e, rct=rct, o0=o0: e.reciprocal(out=rct[:, 0:1], in_=o0[:, 128:129]), reads=[ro0], writes=[rrc])
                    P.op("dve", lambda e, rct=rct, o1=o1: e.reciprocal(out=rct[:, 1:2], in_=o1[:, 128:129]), reads=[ro1], writes=[rrc])
                    P.op("dve", lambda e, rct=rct: e.tensor_tensor(out=rct[:, 1:2], in0=rct[:, 1:2], in1=nlam[:, :], op=ALU.mult), reads=[rrc, rpar], writes=[rrc])
                    P.op("dve", lambda e, rct=rct, o0=o0, ot_=ot_: e.tensor_scalar(out=ot_[:, :], in0=o0[:, 0:128], scalar1=rct[:, 0:1], scalar2=None, op0=ALU.mult),
                         reads=[ro0, rrc], writes=[rot])
                    P.op("dve", lambda e, rct=rct, o1=o1, ot_=ot_: e.scalar_tensor_tensor(out=ot_[:, :], in0=o1[:, 0:128], scalar=rct[:, 1:2], in1=ot_[:, :], op0=ALU.mult, op1=ALU.add),
                         reads=[ro1, rrc, rot], writes=[rot])
                    P.op("act", lambda e, ot_=ot_, sj_=sj_, sq_=sq_: e.activation(out=sj_[:, :], in_=ot_[:, :], func=AF.Square, accum_out=sq_[:, :]),
                         reads=[rot], writes=[rsj, rsq])
                    P.op("act", lambda e, sq_=sq_: e.activation(out=sq_[:, :], in_=sq_[:, :], func=AF.Sqrt, bias=g["eps_sub"][0][:, :], scale=1.0 / 128),
                         reads=[rsq, g["eps_sub"][1]], writes=[rsq])
                    P.op("dve", lambda e, sq_=sq_: e.reciprocal(out=sq_[:, :], in_=sq_[:, :]), reads=[rsq], writes=[rsq])
                    P.op("dve", lambda e, ot_=ot_, sq_=sq_, on_=on_: e.scalar_tensor_tensor(out=on_[:, :], in0=ot_[:, :], scalar=sq_[:, 0:1], in1=subw[:, :], op0=ALU.mult, op1=ALU.mult),
                         reads=[rot, rsq, rpar], writes=[ron])
                    if i % 4 == 0:
                        tps, rtps = self.psb[7]
                    P.op("pe", lambda e, tps=tps, on_=on_, i=i: e.transpose(out=tps[:, (i % 4) * 128:(i % 4 + 1) * 128], in_=on_[:, :], identity=idt[:, :]),
                         reads=[ron, rid], writes=[rtps])
                    if i % 4 == 3 or i == nq - 1:
                        nblk = i % 4 + 1
                        y_, ry = yo[(i // 4) % 2]
                        P.op("act", lambda e, tps=tps, y_=y_, nblk=nblk: e.activation(out=y_[:, 0:nblk * 128], in_=tps[:, 0:nblk * 128], func=AF.Copy), reads=[rtps], writes=[ry])
                        q0 = (i - nblk + 1) * 128
                        P.dma("pool", self.yaT_d[h * 128:(h + 1) * 128, q0:q0 + nblk * 128], y_[:, 0:nblk * 128], ry, reads=[ry])
            P.barrier()

    def phase_b1(self, l, x_src):
        P, g = self.P, self.g
        idt, rid = g["c_ident"]
        with contextlib.ExitStack() as st:
            T = lambda name, shape, dt=F32: P.sbuf(name, shape, dt, st)
            xt, rx = T("xt", [128, 4, D]), P.nres_("xt")
            ytoks = [(T("ytok", [128, SSMW]), P.nres_("b1_ytok%d" % i)) for i in range(2)]
            tmp = [(T("gt", [128, SSMW]), P.nres_("b1_gt%d" % i)) for i in range(2)]
            yg = [(T("yg", [128, SSMW]), P.nres_("b1_yg%d" % i)) for i in range(2)]
            ygf, rygf = T("ygf", [128, 8, 512]), P.nres_("b1_ygf")
            ygb, rygb = T("ygb", [128, 8, 512], BF16), P.nres_("b1_ygb")
            yss, ryss = T("yss", [128, 8, 512], BF16), P.nres_("b1_yss")
            yat, ryat = T("yat", [128, 8, 512], BF16), P.nres_("b1_yat")
            gsTs = [(T("gsT", [128, 4, 512], BF16), P.nres_("b1_gs%d" % i)) for i in range(2)]
            gaTs = [(T("gaT", [128, 4, 512], BF16), P.nres_("b1_ga%d" % i)) for i in range(2)]
            mT, rmT = T("mT", [128, 16, 512], BF16), P.nres_("b1_mT")
            sgt = [(T("sgt", [128, 512]), P.nres_("b1_sgt%d" % i)) for i in range(2)]
            m1 = [(T("m1", [128, 512]), P.nres_("b1_m1%d" % i)) for i in range(2)]
            m2 = [(T("m2", [128, 512]), P.nres_("b1_m2%d" % i)) for i in range(2)]
            wbuf = [(T("wB", [128, 16, 512], BF16), P.nres_("wA%d" % i)) for i in range(2)]
            wglu, rwglu = T("wglu", [128, 8, SSMW], BF16), P.nres_("b1_wglu")
            bglu, rbglu = T("bglu", [128, 8]), P.nres_("b1_bglu")
            self.load_wtile(wglu, rwglu, self.wb["w_glu"][l], 0, 8, 0, SSMW)
            P.dma("sp", bglu[:, :], self.i["b_glu"][l].rearrange("(c p) -> p c", p=128), rbglu, writes=[rbglu], allow_slow_non_contiguous=True)
            nw = 0
            for t in range(self.NT):
                t0 = t * 512
                P.dma("sp", xt[:, :, :], x_src[t0:t0 + 512, :].rearrange("(s p) d -> p s d", p=128), rx, writes=[rx])
                P.dma("sp", yat[:, :, :], self.yaT_d.rearrange("(fc p) t -> p fc t", p=128)[:, :, t0:t0 + 512], ryat, writes=[ryat])
                for s in range(4):
                    (tm, rtm), (ygs, rygs) = tmp[s % 2], yg[s % 2]
                    ytk, rytok = ytoks[s % 2]
                    P.dma("sp", ytk[:, :], self.y_d[t0 + s * 128:t0 + (s + 1) * 128, :], rytok, writes=[rytok])
                    P.op("pool", lambda e, tm=tm, ytk=ytk: e.tensor_tensor(out=tm[:, :], in0=ytk[:, :], in1=ytk[:, :], op=ALU.mult), reads=[rytok], writes=[rtm])
                    P.op("dve", lambda e, tm=tm: e.tensor_scalar(out=tm[:, :], in0=tm[:, :], scalar1=0.044715, scalar2=1.0, op0=ALU.mult, op1=ALU.add), reads=[rtm], writes=[rtm])
                    P.op("pool", lambda e, tm=tm, ytk=ytk: e.tensor_tensor(out=tm[:, :], in0=tm[:, :], in1=ytk[:, :], op=ALU.mult), reads=[rtm, rytok], writes=[rtm])
                    P.op("act", lambda e, tm=tm: e.activation(out=tm[:, :], in_=tm[:, :], func=AF.Sigmoid, scale=2.0 * math.sqrt(2.0 / math.pi)), reads=[rtm], writes=[rtm])
                    P.op("dve", lambda e, tm=tm, ygs=ygs, ytk=ytk: e.tensor_tensor(out=ygs[:, :], in0=tm[:, :], in1=ytk[:, :], op=ALU.mult), reads=[rtm, rytok], writes=[rygs])
                    for k4 in range(2):
                        ps, rps = self.bank()
                        for j in range(4):
                            fc = k4 * 4 + j
                            P.op("pe", lambda e, ps=ps, ygs=ygs, fc=fc, j=j: e.transpose(out=ps[:, j * 128:(j + 1) * 128], in_=ygs[:, fc * 128:(fc + 1) * 128], identity=idt[:, :]),
                                 reads=[rygs, rid], writes=[rps], signal=(j == 3))
                        P.op("act", lambda e, ps=ps, k4=k4, s=s: e.activation(out=ygf[:, k4 * 4:k4 * 4 + 4, s * 128:(s + 1) * 128], in_=ps[:, :].rearrange("p (k t) -> p k t", k=4), func=AF.Copy),
                             reads=[rps], writes=[rygf])
                        P.op("dve", lambda e, ps=ps, k4=k4, s=s: e.tensor_copy(out=ygb[:, k4 * 4:k4 * 4 + 4, s * 128:(s + 1) * 128], in_=ps[:, :].rearrange("p (k t) -> p k t", k=4)),
                             reads=[rps], writes=[rygb])
                for cb in range(8):
                    ps, rps = self.bank()
                    self.mm_acc(ps[:, :], rps, [(wglu[:, kc, cb * 128:(cb + 1) * 128], ygb[:, kc, :]) for kc in range(8)], [rwglu, rygb])
                    sg, rsg = sgt[cb % 2]
                    P.op("act", lambda e, ps=ps, sg=sg, cb=cb: e.activation(out=sg[:, :], in_=ps[:, :], func=AF.Sigmoid, bias=bglu[:, cb:cb + 1]), reads=[rps, rbglu], writes=[rsg])
                    P.op("dve", lambda e, sg=sg, cb=cb: e.tensor_tensor(out=yss[:, cb, :], in0=ygf[:, cb, :], in1=sg[:, :], op=ALU.mult), reads=[rsg, rygf], writes=[ryss])
                for c4 in range(4):
                    (ws, rws), (wa, rwa) = wbuf[0], wbuf[1]
                    self.load_wtile(ws, rws, self.wb["w_proj_ssm"][l], 0, 8, c4 * 512, 512)
                    self.load_wtile(wa, rwa, self.wb["w_proj_attn"][l], 0, 8, c4 * 512, 512)
                    (gsT, rgs), (gaT, rga) = gsTs[c4 % 2], gaTs[c4 % 2]
                    P.dma("sp", gsT[:, :, :], self.gsT_d[c4 * 512:(c4 + 1) * 512, t0:t0 + 512].rearrange("(fc p) t -> p fc t", p=128), rgs, writes=[rgs])
                    P.dma("sp", gaT[:, :, :], self.gaT_d[c4 * 512:(c4 + 1) * 512, t0:t0 + 512].rearrange("(fc p) t -> p fc t", p=128), rga, writes=[rga])
                    for j in range(4):
                        cb = c4 * 4 + j
                        ps1, rps1 = self.bank()
                        self.mm_acc(ps1[:, :], rps1, [(ws[:, kc, j * 128:(j + 1) * 128], yss[:, kc, :]) for kc in range(8)], [rws, ryss])
                        ps2, rps2 = self.bank()
                        self.mm_acc(ps2[:, :], rps2, [(wa[:, kc, j * 128:(j + 1) * 128], yat[:, kc, :]) for kc in range(8)], [rwa, ryat])
                        (a1, ra1), (a2, ra2) = m1[cb % 2], m2[cb % 2]
                        P.op("dve", lambda e, ps1=ps1, a1=a1, j=j, gsT=gsT: e.tensor_tensor(out=a1[:, :], in0=ps1[:, :], in1=gsT[:, j, :], op=ALU.mult), reads=[rps1, rgs], writes=[ra1])
                        P.op("dve", lambda e, ps2=ps2, a2=a2, j=j, gaT=gaT: e.tensor_tensor(out=a2[:, :], in0=ps2[:, :], in1=gaT[:, j, :], op=ALU.mult), reads=[rps2, rga], writes=[ra2])
                        P.op("pool", lambda e, a1=a1, a2=a2, cb=cb: e.tensor_tensor(out=mT[:, cb, :], in0=a1[:, :], in1=a2[:, :], op=ALU.add), reads=[ra1, ra2], writes=[rmT])
                for nch in range(4):
                    wt, rwt = wbuf[nch % 2]
                    self.load_wtile(wt, rwt, self.wb["w_out"][l], 0, 16, nch * 512, 512)
                    for s in range(4):
                        ps, rps = self.bank()
                        self.mm_acc(ps[:, :], rps, [(mT[:, kc, s * 128:(s + 1) * 128], wt[:, kc, :]) for kc in range(16)], [rmT, rwt])
                        P.op("dve", lambda e, ps=ps, s=s, nch=nch: e.tensor_tensor(out=xt[:, s, nch * 512:(nch + 1) * 512], in0=ps[:, :], in1=xt[:, s, nch * 512:(nch + 1) * 512], op=ALU.add),
                             reads=[rps, rx], writes=[rx])
                P.dma("pool", self.xa_d[t0:t0 + 512, :].rearrange("(s p) d -> p s d", p=128), xt[:, :, :], rx, reads=[rx])
            P.barrier()

    def phase_b2(self, l, x_dst):
        P, g = self.P, self.g
        with contextlib.ExitStack() as st:
            T = lambda name, shape, dt=F32: P.sbuf(name, shape, dt, st)
            xt, rx = T("xt", [128, 4, D]), P.nres_("xt")
            nbufs = self.norm_bufs(st)
            w2T, rw2 = T("w2T", [128, 16]), P.nres_("w1T")
            P.dma("sp", w2T[:, :], self.i["norm2_w"][l].rearrange("(kc p) -> p kc", p=128), rw2, writes=[rw2], allow_slow_non_contiguous=True)
            hT, rhT = T("hT", [128, 16, 512], BF16), P.nres_("hT")
            aT, raT = T("aT", [128, 44, 512], BF16), P.nres_("b2_aT")
            wbuf = [(T("wF", [128, 22, 512], BF16), P.nres_("wA%d" % i)) for i in range(3)]
            sgt = [(T("sgt", [128, 512]), P.nres_("b1_sgt%d" % i)) for i in range(2)]
            nw = 0
            for t in range(self.NT):
                t0 = t * 512
                P.dma("sp", xt[:, :, :], self.xa_d[t0:t0 + 512, :].rearrange("(s p) d -> p s d", p=128), rx, writes=[rx])
                self.norm_T(nbufs, xt, rx, w2T, rw2, hT, rhT)
                for c in range(DFF // 512):
                    (wg, rwg) = wbuf[nw % 3]
                    (wu, rwu) = wbuf[(nw + 1) % 3]
                    nw += 2
                    self.load_wtile(wg, rwg, self.wb["w_ffn_gate"][l], 0, 16, c * 512, 512)
                    self.load_wtile(wu, rwu, self.wb["w_ffn_up"][l], 0, 16, c * 512, 512)
                    for j in range(4):
                        fb = c * 4 + j
                        psg, rpsg = self.bank()
                        self.mm_acc(psg[:, :], rpsg, [(wg[:, kc, j * 128:(j + 1) * 128], hT[:, kc, :]) for kc in range(16)], [rwg, rhT])
                        psu, rpsu = self.bank()
                        self.mm_acc(psu[:, :], rpsu, [(wu[:, kc, j * 128:(j + 1) * 128], hT[:, kc, :]) for kc in range(16)], [rwu, rhT])
                        sg, rsg = sgt[fb % 2]
                        P.op("act", lambda e, psg=psg, sg=sg: e.activation(out=sg[:, :], in_=psg[:, :], func=AF.Silu), reads=[rpsg], writes=[rsg])
                        P.op("dve", lambda e, psu=psu, sg=sg, fb=fb: e.tensor_tensor(out=aT[:, fb, :], in0=psu[:, :], in1=sg[:, :], op=ALU.mult), reads=[rpsu, rsg], writes=[raT])
                for nch in range(4):
                    (w0, rw0) = wbuf[nw % 3]
                    (w1, rw1) = wbuf[(nw + 1) % 3]
                    nw += 2
                    self.load_wtile(w0, rw0, self.wb["w_ffn_down"][l], 0, 22, nch * 512, 512)
                    self.load_wtile(w1, rw1, self.wb["w_ffn_down"][l], 22, 22, nch * 512, 512)
                    for s in range(4):
                        ps, rps = self.bank()
                        pairs = [(aT[:, kc, s * 128:(s + 1) * 128], w0[:, kc, :]) for kc in range(22)]
                        pairs += [(aT[:, 22 + kc, s * 128:(s + 1) * 128], w1[:, kc, :]) for kc in range(22)]
                        self.mm_acc(ps[:, :], rps, pairs, [raT, rw0, rw1])
                        P.op("dve", lambda e, ps=ps, s=s, nch=nch: e.tensor_tensor(out=xt[:, s, nch * 512:(nch + 1) * 512], in0=ps[:, :], in1=xt[:, s, nch * 512:(nch + 1) * 512], op=ALU.add),
                             reads=[rps, rx], writes=[rx])
                P.dma("pool", x_dst[t0:t0 + 512, :].rearrange("(s p) d -> p s d", p=128), xt[:, :, :], rx, reads=[rx])
            P.barrier()


_CACHE = {}


def kernel(**inputs):
    x = np.ascontiguousarray(np.asarray(inputs["x"], dtype=np.float32))
    B, L, _ = x.shape
    key = (L,)
    if key not in _CACHE:
        _CACHE[key] = Builder(L).build()
    nc = _CACHE[key]
    base = {k: np.ascontiguousarray(np.asarray(v, dtype=np.float32)) for k, v in inputs.items() if k != "x"}
    base.update(host_consts())
    in_maps = []
    for b in range(B):
        m = dict(base)
        m["x"] = x[b]
        in_maps.append(m)
    res = run_bass_kernel_spmd(nc, in_maps, core_ids=list(range(B)))
    return np.stack([np.asarray(r["out"], dtype=np.float32) for r in res.results], axis=0)
```

```python
import contextlib
import math
import numpy as np
import concourse.bass as bass
import concourse.mybir as mybir
from concourse.bass_utils import run_bass_kernel_spmd

F32 = mybir.dt.float32
BF16 = mybir.dt.bfloat16
I32 = mybir.dt.int32
AF = mybir.ActivationFunctionType
ALU = mybir.AluOpType
AX = mybir.AxisListType

D = 2048
DEPTH = 2
SSMW = 1024
NG = 64
NS = 64
ATW = 1024
NH = 8
INW = 8192
DFF = 5632
RMS_EPS = 1e-6
SUBLN_EPS = 1e-5


class Res:
    __slots__ = ("name", "w", "r", "dsem", "dcnt", "excl")

    def __init__(self, name):
        self.name = name
        self.excl = False
        self.w = None
        self.r = {}
        self.dsem = None
        self.dcnt = 0


class Prog:
    ENG = ("pe", "act", "dve", "pool", "sp")

    def __init__(self, nc, stack):
        self.nc = nc
        self.stack = stack
        self.sems = []
        self.ops = {e: [] for e in self.ENG}
        self.cnt = {e: 0 for e in self.ENG}
        self.seen = {e: {} for e in self.ENG}
        self.esem = {}
        self.latest = {}
        for e in self.ENG:
            self.esem[e] = self.new_sem("s_" + e)
        self.nres = 0
        self.named = {}
        self.nalloc = 0

    def new_sem(self, name):
        s = self.stack.enter_context(self.nc.semaphore(name))
        self.sems.append(s)
        return len(self.sems) - 1

    def res(self, name=None):
        self.nres += 1
        return Res(name or ("r%d" % self.nres))

    def nres_(self, name):
        if name not in self.named:
            self.named[name] = Res(name)
        return self.named[name]

    def sbuf(self, name, shape, dtype, stack=None):
        self.nalloc += 1
        st = stack if stack is not None else self.stack
        t = st.enter_context(self.nc.sbuf_tensor("%s_%d" % (name, self.nalloc), list(shape), dtype))
        return t

    def psum(self, name, shape, dtype, stack=None):
        st = stack if stack is not None else self.stack
        return st.enter_context(self.nc.psum_tensor(name, list(shape), dtype))

    def _deps(self, eng, reads, writes):
        deps = {}
        for r in reads:
            if r.w is not None:
                s, v = r.w
                if deps.get(s, 0) < v:
                    deps[s] = v
        for w in writes:
            if w.w is not None:
                s, v = w.w
                if deps.get(s, 0) < v:
                    deps[s] = v
            for (s, v) in w.r.values():
                if deps.get(s, 0) < v:
                    deps[s] = v
        waits = []
        seen = self.seen[eng]
        own = self.esem[eng]
        for s, v in deps.items():
            if s == own and v > self.cnt[eng]:
                continue
            if seen.get(s, 0) < v:
                seen[s] = v
                waits.append((s, v))
        return waits

    def op(self, eng, fn, reads=(), writes=(), signal=True):
        if any(r.excl for r in reads):
            writes = list(writes) + [r for r in reads if r.excl and r not in writes]
            reads = [r for r in reads if not r.excl]
        waits = self._deps(eng, reads, writes)
        idx = self.cnt[eng] + 1
        if signal:
            self.cnt[eng] = idx
        ev = (self.esem[eng], idx)
        self.latest[ev[0]] = idx
        self.ops[eng].append((waits, fn, signal))
        for w in writes:
            w.w = ev
            w.r = {}
        for r in reads:
            r.r[ev[0]] = ev

    def dma(self, q, out, in_, sres, reads=(), writes=(), **kw):
        waits = self._deps(q, reads, writes)
        qk = "sw" if q == "pool" else "hw"
        if sres.dsem is None:
            sres.dsem = {}
            sres.dcnt = {}
        if qk not in sres.dsem:
            sres.dsem[qk] = self.new_sem("d%s_%s" % (qk, sres.name))
            sres.dcnt[qk] = 0
        sres.dcnt[qk] += 16
        ev = (sres.dsem[qk], sres.dcnt[qk])
        self.latest[ev[0]] = ev[1]
        sem = self.sems[ev[0]]

        def fn(e, out=out, in_=in_, sem=sem, kw=kw):
            e.dma_start(out=out, in_=in_, **kw).then_inc(sem, 16)
            return None

        self.ops[q].append((waits, fn, False))
        for w in writes:
            w.w = ev
            w.r = {}
        for r in reads:
            r.r[ev[0]] = ev

    def barrier(self):
        for e in self.ENG:
            waits = []
            seen = self.seen[e]
            for s, v in self.latest.items():
                if seen.get(s, 0) < v:
                    seen[s] = v
                    waits.append((s, v))
            if waits:
                self.ops[e].append((waits, None, False))

    def emit(self):
        nc = self.nc
        sems = self.sems
        with nc.Block() as block:
            def run(name, e):
                own = sems[self.esem[name]]
                for waits, fn, signal in self.ops[name]:
                    for s, v in waits:
                        e.wait_ge(sems[s], v)
                    if fn is None:
                        continue
                    ins = fn(e)
                    if signal:
                        ins.then_inc(own, 1)

            @block.tensor
            def _(e):
                run("pe", e)

            @block.scalar
            def _(e):
                run("act", e)

            @block.vector
            def _(e):
                run("dve", e)

            @block.gpsimd
            def _(e):
                run("pool", e)

            @block.sync
            def _(e):
                run("sp", e)


WSPEC = [
    ("w_in", D, INW), ("w_glu", SSMW, SSMW), ("w_proj_ssm", SSMW, D), ("w_proj_attn", ATW, D),
    ("w_out", D, D), ("w_ffn_gate", D, DFF), ("w_ffn_up", D, DFF), ("w_ffn_down", DFF, D),
]
SMALL = [("norm1_w", [DEPTH, D]), ("lam_re", [DEPTH, NG, NS]), ("lam_im", [DEPTH, NG, NS]),
         ("log_step", [DEPTH, NG]), ("ssm_b_re", [DEPTH, NG, NS, 16]), ("ssm_b_im", [DEPTH, NG, NS, 16]),
         ("ssm_c_re", [DEPTH, NG, 16, NS]), ("ssm_c_im", [DEPTH, NG, 16, NS]), ("ssm_d", [DEPTH, SSMW]),
         ("b_glu", [DEPTH, SSMW]), ("q_norm_w", [DEPTH, 64]), ("k_norm_w", [DEPTH, 64]),
         ("lambda_q1", [DEPTH, 64]), ("lambda_k1", [DEPTH, 64]), ("lambda_q2", [DEPTH, 64]),
         ("lambda_k2", [DEPTH, 64]), ("subln_w", [DEPTH, 128]), ("rel_bias", [32, NH]),
         ("norm2_w", [DEPTH, D])]
CONSTS = [("c_ident", [128, 128]), ("c_bones", [128, 128]), ("c_tmask", [128, 128]), ("c_reld", [128, 128]), ("c_swap", [128, 128])]


def host_consts():
    ident = np.eye(128, dtype=np.float32)
    bones = np.kron(np.eye(2, dtype=np.float32), np.ones((64, 64), np.float32))
    jj = np.arange(128) // 16
    tmask = (jj[None, :] >= jj[:, None]).astype(np.float32)
    reld = (np.arange(128)[:, None] - np.arange(128)[None, :]).astype(np.float32)
    swap = np.roll(np.eye(128, dtype=np.float32), 64, axis=1)
    return {"c_ident": ident, "c_bones": bones, "c_tmask": tmask, "c_reld": reld, "c_swap": swap}


class Builder:
    def __init__(self, L, nlayers=DEPTH, dbg=False, phases=None):
        self.L = L
        self.NT = L // 512
        self.nlayers = nlayers
        self.dbg = dbg
        self.phases = phases
        self.nc = bass.Bass("TRN2", target_bir_lowering=False)
        self.stack = contextlib.ExitStack()
        self.P = Prog(self.nc, self.stack)
        nc = self.nc
        self.i = {}
        self.i["x"] = self.din("x", [L, D])
        for n, k, m in WSPEC:
            self.i[n] = self.din(n, [DEPTH, k, m])
        for n, sh in SMALL + CONSTS:
            self.i[n] = self.din(n, sh)
        self.out = self.dout("out", [L, D])
        self.wb = {n: [self.dscr("wb_%s_%d" % (n, l), [k, m], BF16) for l in range(nlayers)] for n, k, m in WSPEC}
        sk = self.dout if dbg else self.dscr
        self.u_d = sk("u_d", [L, SSMW], F32)
        skq = self.din if dbg == "attin" else sk
        self.qT_d = skq("qT_d", [ATW, L], BF16)
        self.kT_d = skq("kT_d", [ATW, L], BF16)
        self.v_d = skq("v_d", [L, ATW], BF16)
        self.gsT_d = sk("gsT_d", [D, L], BF16)
        self.gaT_d = sk("gaT_d", [D, L], BF16)
        self.y_d = sk("y_d", [L, SSMW], F32)
        self.yaT_d = sk("yaT_d", [ATW, L], BF16)
        self.xa_d = sk("xa_d", [L, D], F32)
        self.xb_d = self.dscr("xb_d", [L, D], F32)
        self.rdram = {}

    def din(self, name, shape, dtype=F32):
        return self.nc.dram_tensor(name, list(shape), dtype, kind="ExternalInput").ap()

    def dscr(self, name, shape, dtype):
        return self.nc.dram_tensor(name, list(shape), dtype, kind="Internal").ap()

    def dout(self, name, shape, dtype=F32):
        return self.nc.dram_tensor(name, list(shape), dtype, kind="ExternalOutput").ap()

    def setup_globals(self):
        P = self.P
        g = self.g = {}
        self.psb = []
        for b in range(8):
            self.psb.append((P.psum("psb%d" % b, [128, 512], F32), P.nres_("psb%d" % b)))
            self.psb[-1][1].excl = True
        self.psi = 0
        for n in ("c_ident", "c_bones", "c_tmask", "c_reld", "c_swap"):
            t = P.sbuf(n, [128, 128], F32)
            r = P.nres_(n)
            P.dma("sp", t[:, :], self.i[n][:, :], r, writes=[r])
            g[n] = (t, r)
        t = P.sbuf("bones_b", [128, 128], BF16)
        r = P.nres_("bones_b")
        P.op("dve", lambda e, t=t: e.tensor_copy(out=t[:, :], in_=g["c_bones"][0][:, :]), reads=[g["c_bones"][1]], writes=[r])
        g["bones_b"] = (t, r)
        for nm, val in (("eps_rms", RMS_EPS), ("eps_sub", SUBLN_EPS), ("halfpi", math.pi / 2), ("zero", 0.0)):
            t = P.sbuf(nm, [128, 1], F32)
            r = P.nres_(nm)
            P.op("pool", lambda e, t=t, val=val: e.memset(t[:, :], val), writes=[r])
            g[nm] = (t, r)

    def bank(self):
        b = self.psb[self.psi % 8]
        self.psi += 1
        return b

    def phase_w(self):
        P = self.P
        with contextlib.ExitStack() as st:
            NB = 3
            CW = 4096
            fb = [(P.sbuf("wf", [128, CW], F32, st), P.nres_("wf%d" % i)) for i in range(NB)]
            bb = [(P.sbuf("wbb", [128, CW], BF16, st), P.nres_("wbb%d" % i)) for i in range(NB)]
            rw = P.nres_("wdram")
            it = 0
            ce = ("dve", "pool", "act")
            for l in range(self.nlayers):
                for n, K, N in WSPEC:
                    src = self.i[n]
                    dst = self.wb[n][l]
                    ncc = (N + CW - 1) // CW
                    cw = N // ncc
                    for kt in range(K // 128):
                        for c in range(ncc):
                            (f, rf), (b, rb) = fb[it % NB], bb[it % NB]
                            P.dma("sp", f[:, 0:cw], src[l, kt * 128:(kt + 1) * 128, c * cw:(c + 1) * cw], rf, writes=[rf])
                            eng = ce[it % 3]
                            if eng == "act":
                                P.op("act", lambda e, f=f, b=b, cw=cw: e.activation(out=b[:, 0:cw], in_=f[:, 0:cw], func=AF.Copy), reads=[rf], writes=[rb])
                            else:
                                P.op(eng, lambda e, f=f, b=b, cw=cw: e.tensor_copy(out=b[:, 0:cw], in_=f[:, 0:cw]), reads=[rf], writes=[rb])
                            P.dma("act", dst[kt * 128:(kt + 1) * 128, c * cw:(c + 1) * cw], b[:, 0:cw], rb, reads=[rb], writes=[])
                            it += 1
            P.barrier()

    def norm_T(self, st_bufs, xt, rx, wT, rwT, hT, rhT):
        P, g = self.P, self.g
        junk, rjunk, ssq, rssq, rstd, rrstd, xs = st_bufs
        for s in range(4):
            P.op("act", lambda e, s=s: e.activation(out=junk[:, :], in_=xt[:, s, :], func=AF.Square, accum_out=ssq[:, s:s + 1]),
                 reads=[rx], writes=[rjunk, rssq])
        P.op("act", lambda e: e.activation(out=rstd[:, :], in_=ssq[:, :], func=AF.Sqrt, bias=g["eps_rms"][0][:, :], scale=1.0 / D),
             reads=[rssq, g["eps_rms"][1]], writes=[rrstd])
        P.op("dve", lambda e: e.reciprocal(out=rstd[:, :], in_=rstd[:, :]), reads=[rrstd], writes=[rrstd])
        idt, rid = g["c_ident"]
        for s in range(4):
            xs_t, rxs = xs[s % 2]
            P.op("act", lambda e, s=s, xs_t=xs_t: e.activation(out=xs_t[:, :], in_=xt[:, s, :], func=AF.Copy, scale=rstd[:, s:s + 1]),
                 reads=[rx, rrstd], writes=[rxs])
            for k4 in range(4):
                ps, rps = self.bank()
                for j in range(4):
                    kc = k4 * 4 + j
                    P.op("pe", lambda e, ps=ps, xs_t=xs_t, kc=kc, j=j: e.transpose(out=ps[:, j * 128:(j + 1) * 128], in_=xs_t[:, kc * 128:(kc + 1) * 128], identity=idt[:, :]),
                         reads=[rxs, rid], writes=[rps], signal=(j == 3))
                P.op("dve", lambda e, ps=ps, k4=k4, s=s: e.tensor_tensor(
                    out=hT[:, k4 * 4:k4 * 4 + 4, s * 128:(s + 1) * 128],
                    in0=ps[:, :].rearrange("p (k t) -> p k t", k=4),
                    in1=wT[:, k4 * 4:k4 * 4 + 4].unsqueeze(2).broadcast_to([128, 4, 128]), op=ALU.mult),
                    reads=[rps, rwT], writes=[rhT])

    def norm_bufs(self, st):
        P = self.P
        junk = P.sbuf("junk", [128, D], BF16, st)
        ssq = P.sbuf("ssq", [128, 4], F32, st)
        rstd = P.sbuf("rstd", [128, 4], F32, st)
        xs = [(P.sbuf("xs", [128, D], F32, st), P.nres_("xs%d" % i)) for i in range(2)]
        return (junk, P.nres_("junk"), ssq, P.nres_("ssq"), rstd, P.nres_("rstd"), xs)

    def load_wtile(self, wt, rwt, src, k0, kcn, n0, ncols, q="sp"):
        self.P.dma(q, wt[:, 0:kcn, 0:ncols],
                   src[k0 * 128:(k0 + kcn) * 128, n0:n0 + ncols].rearrange("(kc p) n -> p kc n", p=128),
                   rwt, writes=[rwt])

    def mm_acc(self, ps_ap, rps, pairs, reads):
        n = len(pairs)
        for i, (lhsT, rhs) in enumerate(pairs):
            self.P.op("pe", lambda e, lhsT=lhsT, rhs=rhs, i=i: e.matmul(ps_ap, lhsT=lhsT, rhs=rhs, start=(i == 0), stop=(i == n - 1)),
                      reads=reads, writes=[rps], signal=(i == n - 1))

    def phase_a(self, l, x_src):
        P, g = self.P, self.g
        L = self.L
        with contextlib.ExitStack() as st:
            xt = P.sbuf("xt", [128, 4, D], F32, st)
            rx = P.nres_("xt")
            nbufs = self.norm_bufs(st)
            w1T = P.sbuf("w1T", [128, 16], F32, st)
            rw1 = P.nres_("w1T")
            P.dma("sp", w1T[:, :], self.i["norm1_w"][l].rearrange("(kc p) -> p kc", p=128), rw1, writes=[rw1],
                  allow_slow_non_contiguous=True)
            wqk = P.sbuf("wqk", [128, 2], F32, st)
            rwqk = P.nres_("wqk")
            for ci, nm in enumerate(("q_norm_w", "k_norm_w")):
                for m in range(2):
                    P.dma("sp", wqk[m * 64:(m + 1) * 64, ci:ci + 1], self.i[nm][l:l + 1, :].rearrange("o d -> d o"), rwqk,
                          writes=[rwqk], allow_slow_non_contiguous=True)
            P.op("dve", lambda e: e.tensor_scalar(out=wqk[:, 0:1], in0=wqk[:, 0:1], scalar1=0.125, scalar2=None, op0=ALU.mult),
                 reads=[rwqk], writes=[rwqk])
            hT = P.sbuf("hT", [128, 16, 512], BF16, st)
            rhT = P.nres_("hT")
            wbuf = [(P.sbuf("wA", [128, 16, 512], BF16, st), P.nres_("wA%d" % i)) for i in range(3)]
            ut = [(P.sbuf("ut", [128, 4, 512], F32, st), P.nres_("ut%d" % i)) for i in range(2)]
            vt = [(P.sbuf("vt", [128, 4, 512], BF16, st), P.nres_("vt%d" % i)) for i in range(2)]
            ot = [(P.sbuf("ot", [128, 512], BF16, st), P.nres_("ot%d" % i)) for i in range(3)]
            sq = [(P.sbuf("sq", [128, 512], BF16, st), P.nres_("sq%d" % i)) for i in range(2)]
            rt = [(P.sbuf("rt", [128, 512], F32, st), P.nres_("rt%d" % i)) for i in range(2)]
            bones, rbones = g["bones_b"]
            wsrc = self.wb["w_in"][l]
            nwl = 0
            oi = 0
            for t in range(self.NT):
                t0 = t * 512
                P.dma("sp", xt[:, :, :], x_src[t0:t0 + 512, :].rearrange("(s p) d -> p s d", p=128), rx, writes=[rx])
                self.norm_T(nbufs, xt, rx, w1T, rw1, hT, rhT)
                for c in range(16):
                    wt, rwt = wbuf[nwl % 3]
                    nwl += 1
                    self.load_wtile(wt, rwt, wsrc, 0, 16, c * 512, 512)
                    if c in (0, 1, 6, 7):
                        isu = c < 2
                        stg, rstg = (ut if isu else vt)[c % 2]
                        for s in range(4):
                            ps, rps = self.bank()
                            self.mm_acc(ps[:, :], rps, [(hT[:, kc, s * 128:(s + 1) * 128], wt[:, kc, :]) for kc in range(16)], [rhT, rwt])
                            if s % 2 == 0:
                                P.op("act", lambda e, ps=ps, stg=stg, s=s: e.activation(out=stg[:, s, :], in_=ps[:, :], func=AF.Copy),
                                     reads=[rps], writes=[rstg])
                            else:
                                P.op("dve", lambda e, ps=ps, stg=stg, s=s: e.tensor_copy(out=stg[:, s, :], in_=ps[:, :]),
                                     reads=[rps], writes=[rstg])
                        dst = self.u_d if isu else self.v_d
                        cc = c if isu else c - 6
                        P.dma("pool", dst[t0:t0 + 512, cc * 512:(cc + 1) * 512].rearrange("(s p) n -> p s n", p=128), stg[:, :, :], rstg,
                              reads=[rstg])
                    else:
                        for j in range(4):
                            ps, rps = self.bank()
                            self.mm_acc(ps[:, :], rps, [(wt[:, kc, j * 128:(j + 1) * 128], hT[:, kc, :]) for kc in range(16)], [rhT, rwt])
                            o, ro = ot[oi % 3]
                            oi += 1
                            if c < 6:
                                isq = c < 4
                                sqt, rsq = sq[oi % 2]
                                rtt, rrt = rt[oi % 2]
                                P.op("act", lambda e, ps=ps, sqt=sqt: e.activation(out=sqt[:, :], in_=ps[:, :], func=AF.Square), reads=[rps], writes=[rsq])
                                ps2, rps2 = self.bank()
                                self.mm_acc(ps2[:, :], rps2, [(bones[:, :], sqt[:, :])], [rbones, rsq])
                                P.op("act", lambda e, ps2=ps2, rtt=rtt: e.activation(out=rtt[:, :], in_=ps2[:, :], func=AF.Sqrt, bias=g["eps_rms"][0][:, :], scale=1.0 / 64),
                                     reads=[rps2, g["eps_rms"][1]], writes=[rrt])
                                P.op("dve", lambda e, rtt=rtt: e.reciprocal(out=rtt[:, :], in_=rtt[:, :]), reads=[rrt], writes=[rrt])
                                ci = 0 if isq else 1
                                P.op("dve", lambda e, ps=ps, rtt=rtt, o=o, ci=ci: e.scalar_tensor_tensor(
                                    out=o[:, :], in0=ps[:, :], scalar=wqk[:, ci:ci + 1], in1=rtt[:, :], op0=ALU.mult, op1=ALU.mult),
                                    reads=[rps, rrt, rwqk], writes=[ro])
                                dst = self.qT_d if isq else self.kT_d
                                r0 = ((c - 2) if isq else (c - 4)) * 512 + j * 128
                            else:
                                P.op("act", lambda e, ps=ps, o=o: e.activation(out=o[:, :], in_=ps[:, :], func=AF.Sigmoid), reads=[rps], writes=[ro])
                                dst = self.gsT_d if c < 12 else self.gaT_d
                                r0 = ((c - 8) if c < 12 else (c - 12)) * 512 + j * 128
                            P.dma("pool", dst[r0:r0 + 128, t0:t0 + 512], o[:, :], ro, reads=[ro])
            P.barrier()

    def build(self):
        ph = self.phases
        self.setup_globals()
        if ph is None or "att" in ph:
            self.setup_attn()
        if ph is None or "w" in ph:
            self.phase_w()
        for l in range(self.nlayers):
            x_src = self.i["x"] if l == 0 else self.xb_d
            x_dst = self.out if l == self.nlayers - 1 else self.xb_d
            if ph is None or "a" in ph:
                self.phase_a(l, x_src)
            if ph is None or "ssm" in ph:
                self.phase_ssm(l)
            if ph is None or "att" in ph:
                self.phase_att(l)
            if ph is None or "b1" in ph:
                self.phase_b1(l, x_src)
            if ph is None or "b2" in ph:
                self.phase_b2(l, x_dst)
        self.P.barrier()
        self.P.emit()
        return self.nc

    def cmul(self, eng, out_re, out_im, a_re, a_im, b_re, b_im, tmp, rres, wres, neg_im=False):
        P = self.P
        t1, t2 = tmp
        ops = [
            (t1, a_re, b_re, ALU.mult), (t2, a_im, b_im, ALU.mult), (out_re, t1, t2, ALU.subtract),
            (t1, a_re, b_im, ALU.mult), (t2, a_im, b_re, ALU.mult), (out_im, t1, t2, ALU.add),
        ]
        for o, x, y, op in ops:
            P.op(eng, lambda e, o=o, x=x, y=y, op=op: e.tensor_tensor(out=o, in0=x, in1=y, op=op), reads=rres, writes=wres)
        if neg_im:
            P.op(eng, lambda e, o=out_im: e.tensor_scalar(out=o, in0=o, scalar1=-1.0, scalar2=None, op0=ALU.mult), reads=wres, writes=wres)

    def phase_ssm(self, l):
        P, g = self.P, self.g
        L = self.L
        NBLK = L // 8
        bp = min(128, NBLK)
        nbt = NBLK // bp
        nsteps = int(math.ceil(math.log2(NBLK)))
        idt, rid = g["c_ident"]
        swp, rswp = g["c_swap"]
        tmask, rtm = g["c_tmask"]
        with contextlib.ExitStack() as st:
            rp = P.nres_("ssm_par")
            def T(name, shape, dt=F32):
                return P.sbuf(name, shape, dt, st)
            lre, lim, dtt = T("lre", [128, 64]), T("lim", [128, 64]), T("dtt", [128, 64])
            for hf in range(2):
                hs = slice(hf * 64, hf * 64 + 64)
                P.dma("sp", lre[hs, :], self.i["lam_re"][l].rearrange("g p -> p g"), rp, writes=[rp], allow_slow_non_contiguous=True)
                P.dma("sp", lim[hs, :], self.i["lam_im"][l].rearrange("g p -> p g"), rp, writes=[rp], allow_slow_non_contiguous=True)
            P.dma("sp", dtt[:, :], self.i["log_step"][l].partition_broadcast(128), rp, writes=[rp])
            Bre, Bim = T("Bre", [128, 64, 16]), T("Bim", [128, 64, 16])
            for hf in range(2):
                hs = slice(hf * 64, hf * 64 + 64)
                P.dma("sp", Bre[hs, :, :], self.i["ssm_b_re"][l].rearrange("g p c -> p g c"), rp, writes=[rp])
                P.dma("sp", Bim[hs, :, :], self.i["ssm_b_im"][l].rearrange("g p c -> p g c"), rp, writes=[rp])
            Dcol = T("Dcol", [128, 64])
            for j in range(8):
                P.dma("sp", Dcol[16 * j:16 * j + 16, :], self.i["ssm_d"][l].rearrange("(g c) -> c g", c=16), rp, writes=[rp],
                      allow_slow_non_contiguous=True)
            Cre, Cim = T("Cre", [128, 64, 16]), T("Cim", [128, 64, 16])
            crow = T("crow", [128, 128])
            for nm, dstt in (("ssm_c_re", Cre), ("ssm_c_im", Cim)):
                src = self.i[nm][l].rearrange("g c p -> (g c) p")
                for k in range(8):
                    P.dma("sp", crow[:, 0:64], src[k * 128:(k + 1) * 128, :], rp, reads=[rp], writes=[rp])
                    P.dma("sp", crow[:, 64:128], src[k * 128:(k + 1) * 128, :], rp, reads=[rp], writes=[rp])
                    ps, rps = self.bank()
                    P.op("pe", lambda e, ps=ps: e.transpose(out=ps[:, 0:128], in_=crow[:, :], identity=idt[:, :]), reads=[rp, rid], writes=[rps])
                    P.op("dve", lambda e, ps=ps, dstt=dstt, k=k: e.tensor_copy(out=dstt[:, k * 8:(k + 1) * 8, :], in_=ps[:, 0:128].rearrange("p (g c) -> p g c", c=16)),
                         reads=[rps], writes=[rp])
            V = lambda fn, rd=(rp,), wr=(rp,): P.op("dve", fn, reads=list(rd), writes=list(wr))
            A = lambda fn: P.op("act", fn, reads=[rp, g["halfpi"][1]], writes=[rp])
            lr, x1, mag, ang, tq, r_, m1 = [T(n, [128, 64]) for n in ("lr", "x1", "mag", "ang", "tq", "r_", "m1")]
            ti = T("ti", [128, 64], I32)
            sn, cs, ar, ai = [T(n, [128, 64]) for n in ("sn", "cs", "ar", "ai")]
            V(lambda e: e.tensor_scalar(out=lr[:, :], in0=lre[:, :], scalar1=-1e-4, scalar2=None, op0=ALU.min))
            A(lambda e: e.activation(out=dtt[:, :], in_=dtt[:, :], func=AF.Exp))
            V(lambda e: e.tensor_tensor(out=x1[:, :], in0=lr[:, :], in1=dtt[:, :], op=ALU.mult))
            A(lambda e: e.activation(out=mag[:, :], in_=x1[:, :], func=AF.Exp))
            V(lambda e: e.tensor_tensor(out=ang[:, :], in0=lim[:, :], in1=dtt[:, :], op=ALU.mult))
            V(lambda e: e.tensor_scalar(out=tq[:, :], in0=ang[:, :], scalar1=1.0 / (2 * math.pi), scalar2=0.5, op0=ALU.mult, op1=ALU.add))
            V(lambda e: e.tensor_copy(out=ti[:, :], in_=tq[:, :]))
            V(lambda e: e.tensor_copy(out=tq[:, :], in_=ti[:, :]))
            V(lambda e: e.scalar_tensor_tensor(out=r_[:, :], in0=tq[:, :], scalar=-2 * math.pi, in1=ang[:, :], op0=ALU.mult, op1=ALU.add))
            for thr, opc, add in ((-math.pi, ALU.is_lt, 2 * math.pi), (math.pi, ALU.is_gt, -2 * math.pi),
                                  (-math.pi, ALU.is_lt, 2 * math.pi), (math.pi, ALU.is_gt, -2 * math.pi)):
                V(lambda e, thr=thr, opc=opc: e.tensor_single_scalar(out=m1[:, :], in_=r_[:, :], scalar=thr, op=opc))
                V(lambda e, add=add: e.scalar_tensor_tensor(out=r_[:, :], in0=m1[:, :], scalar=add, in1=r_[:, :], op0=ALU.mult, op1=ALU.add))
            V(lambda e: e.tensor_scalar(out=r_[:, :], in0=r_[:, :], scalar1=math.pi, scalar2=-math.pi, op0=ALU.min, op1=ALU.max))
            A(lambda e: e.activation(out=sn[:, :], in_=r_[:, :], func=AF.Sin))
            V(lambda e: e.tensor_scalar(out=m1[:, :], in0=r_[:, :], scalar1=-1.0, scalar2=None, op0=ALU.mult))
            V(lambda e: e.tensor_tensor(out=m1[:, :], in0=m1[:, :], in1=r_[:, :], op=ALU.max))
            A(lambda e: e.activation(out=cs[:, :], in_=m1[:, :], func=AF.Sin, bias=g["halfpi"][0][:, :], scale=-1.0))
            V(lambda e: e.tensor_tensor(out=ar[:, :], in0=mag[:, :], in1=cs[:, :], op=ALU.mult))
            V(lambda e: e.tensor_tensor(out=ai[:, :], in0=mag[:, :], in1=sn[:, :], op=ALU.mult))
            den, nr, fr, fi, t1, t2 = [T(n, [128, 64]) for n in ("den", "nr", "fr", "fi", "t1", "t2")]
            V(lambda e: e.tensor_tensor(out=den[:, :], in0=lr[:, :], in1=lr[:, :], op=ALU.mult))
            V(lambda e: e.tensor_tensor(out=t1[:, :], in0=lim[:, :], in1=lim[:, :], op=ALU.mult))
            V(lambda e: e.tensor_tensor(out=den[:, :], in0=den[:, :], in1=t1[:, :], op=ALU.add))
            V(lambda e: e.reciprocal(out=den[:, :], in_=den[:, :]))
            V(lambda e: e.tensor_scalar(out=nr[:, :], in0=ar[:, :], scalar1=-1.0, scalar2=None, op0=ALU.add))
            V(lambda e: e.tensor_tensor(out=t1[:, :], in0=nr[:, :], in1=lr[:, :], op=ALU.mult))
            V(lambda e: e.tensor_tensor(out=t2[:, :], in0=ai[:, :], in1=lim[:, :], op=ALU.mult))
            V(lambda e: e.tensor_tensor(out=t1[:, :], in0=t1[:, :], in1=t2[:, :], op=ALU.add))
            V(lambda e: e.tensor_tensor(out=fr[:, :], in0=t1[:, :], in1=den[:, :], op=ALU.mult))
            V(lambda e: e.tensor_tensor(out=t1[:, :], in0=ai[:, :], in1=lr[:, :], op=ALU.mult))
            V(lambda e: e.tensor_tensor(out=t2[:, :], in0=nr[:, :], in1=lim[:, :], op=ALU.mult))
            V(lambda e: e.tensor_tensor(out=t1[:, :], in0=t1[:, :], in1=t2[:, :], op=ALU.subtract))
            V(lambda e: e.tensor_tensor(out=fi[:, :], in0=t1[:, :], in1=den[:, :], op=ALU.mult))
            Bbr, Bbi = T("Bbr", [128, 64, 16]), T("Bbi", [128, 64, 16])
            tb1, tb2 = T("tb1", [128, 64, 16]), T("tb2", [128, 64, 16])
            bc16 = lambda a: a[:, :].unsqueeze(2).broadcast_to([128, 64, 16])
            self.cmul("dve", Bbr[:, :, :], Bbi[:, :, :], bc16(fr), bc16(fi), Bre[:, :, :], Bim[:, :, :], (tb1[:, :, :], tb2[:, :, :]), [rp], [rp])
            PWr, PWi = T("PWr", [128, 64, 9]), T("PWi", [128, 64, 9])
            PIr, PIi = T("PIr", [128, 64, 8]), T("PIi", [128, 64, 8])
            PRr, PRi = T("PRr", [128, 64, 8]), T("PRi", [128, 64, 8])
            air, aii = T("air", [128, 64]), T("aii", [128, 64])
            V(lambda e: e.tensor_tensor(out=t1[:, :], in0=ar[:, :], in1=ar[:, :], op=ALU.mult))
            V(lambda e: e.tensor_tensor(out=t2[:, :], in0=ai[:, :], in1=ai[:, :], op=ALU.mult))
            V(lambda e: e.tensor_tensor(out=t1[:, :], in0=t1[:, :], in1=t2[:, :], op=ALU.add))
            V(lambda e: e.reciprocal(out=t1[:, :], in_=t1[:, :]))
            V(lambda e: e.tensor_tensor(out=air[:, :], in0=ar[:, :], in1=t1[:, :], op=ALU.mult))
            V(lambda e: e.scalar_tensor_tensor(out=aii[:, :], in0=ai[:, :], scalar=-1.0, in1=t1[:, :], op0=ALU.mult, op1=ALU.mult))
            for (Xr, Xi, br_, bi_, n) in ((PWr, PWi, ar, ai, 9), (PIr, PIi, air, aii, 8)):
                V(lambda e, Xr=Xr: e.memset(Xr[:, :, 0:1], 1.0))
                V(lambda e, Xi=Xi: e.memset(Xi[:, :, 0:1], 0.0))
                for k in range(1, n):
                    self.cmul("dve", Xr[:, :, k], Xi[:, :, k], Xr[:, :, k - 1], Xi[:, :, k - 1], br_[:, :], bi_[:, :], (t1[:, :], t2[:, :]), [rp], [rp])
            for j in range(8):
                V(lambda e, j=j: e.tensor_copy(out=PRr[:, :, j], in_=PWr[:, :, 7 - j]))
                V(lambda e, j=j: e.tensor_copy(out=PRi[:, :, j], in_=PWi[:, :, 7 - j]))
            APr, APi = T("APr", [128, 64, nsteps]), T("APi", [128, 64, nsteps])
            V(lambda e: e.tensor_copy(out=APr[:, :, 0], in_=PWr[:, :, 8]))
            V(lambda e: e.tensor_copy(out=APi[:, :, 0], in_=PWi[:, :, 8]))
            for k in range(1, nsteps):
                self.cmul("dve", APr[:, :, k], APi[:, :, k], APr[:, :, k - 1], APi[:, :, k - 1], APr[:, :, k - 1], APi[:, :, k - 1], (t1[:, :], t2[:, :]), [rp], [rp])
            V(lambda e: e.tensor_scalar(out=APi[64:128, :, :], in0=APi[64:128, :, :], scalar1=-1.0, scalar2=None, op0=ALU.mult))
            GB = 8
            FW = GB * 16
            Bjr, Bji, BEr, BEi = [T(n, [128, GB, 8, 16]) for n in ("Bjr", "Bji", "BEr", "BEi")]
            Crr, Cri = T("Crr", [128, GB, 9, 16]), T("Cri", [128, GB, 9, 16])
            tg1, tg2 = T("tg1", [128, GB, 9, 16]), T("tg2", [128, GB, 9, 16])
            rgen = P.nres_("ssm_gen")
            Z = T("Z", [128, nbt, 8, FW])
            rZ = P.nres_("ssm_Z")
            Zc = T("Zc", [128, nbt, GB, 8, 16])
            rZc = P.nres_("ssm_Zc")
            U8 = T("U8", [128, GB, NBLK])
            rU8 = P.nres_("ssm_U8")
            LE = [(T("LE", [128, 128]), P.nres_("ssm_LE%d" % i)) for i in range(2)]
            LS = [(T("LS", [128, 128]), P.nres_("ssm_LS%d" % i)) for i in range(2)]
            MK = [(T("MK", [128, 128]), P.nres_("ssm_MK%d" % i)) for i in range(4)]
            MK2 = [(T("MK2", [128, 128]), P.nres_("ssm_MK2%d" % i)) for i in range(2)]
            Tt = T("Tt", [128, GB, 128])
            rTt = P.nres_("ssm_Tt")
            tmpT = [(T("tmpT", [128, 128]), P.nres_("ssm_tmpT%d" % i)) for i in range(2)]
            W = NBLK + 1
            S = T("S", [128, GB, W])
            rS = [P.nres_("ssm_S%d" % i) for i in range(GB)]
            Y8 = [(T("Y8", [128, NBLK]), P.nres_("ssm_Y8%d" % i)) for i in range(2)]
            Yt = T("Yt", [128, nbt, 8, FW])
            rYt = P.nres_("ssm_Yt")
            P.op("pool", lambda e: e.memset(S[:, :, 0:1], 0.0), writes=rS)
            nmk = 0
            for gb in range(NG // GB):
                g0 = gb * GB
                gs_ = slice(g0, g0 + GB)
                bj = lambda a: a[:, gs_, :].unsqueeze(3).broadcast_to([128, GB, a.shape[2], 16])
                bb = lambda a, n: a[:, gs_, :].unsqueeze(2).broadcast_to([128, GB, n, 16])
                t8 = (tg1[:, :, 0:8, :], tg2[:, :, 0:8, :])
                self.cmul("pool", Bjr[:, :, :, :], Bji[:, :, :, :], bj(PIr), bj(PIi), bb(Bbr, 8), bb(Bbi, 8), t8, [rp], [rgen])
                self.cmul("pool", BEr[:, :, :, :], BEi[:, :, :, :], bj(PRr), bj(PRi), bb(Bbr, 8), bb(Bbi, 8), t8, [rp], [rgen])
                self.cmul("pool", Crr[:, :, :, :], Cri[:, :, :, :], bj(PWr), bj(PWi), bb(Cre, 9), bb(Cim, 9), (tg1[:, :, :, :], tg2[:, :, :, :]), [rp], [rgen], neg_im=True)
                for bt in range(nbt):
                    P.dma("sp", Z[0:bp, bt, :, :], self.u_d[bt * bp * 8:(bt + 1) * bp * 8, g0 * 16:g0 * 16 + FW].rearrange("(b j) f -> b j f", j=8), rZ, writes=[rZ])
                P.op("pool", lambda e: e.tensor_copy(out=Zc[0:bp, :, :, :, :], in_=Z[0:bp, :, :, :].rearrange("p b j (g c) -> p b g j c", c=16)), reads=[rZ], writes=[rZc])
                for gi in range(GB):
                    gg = g0 + gi
                    ps, rps = self.bank()
                    for bt in range(nbt):
                        P.op("pe", lambda e, ps=ps, bt=bt, gi=gi: e.transpose(out=ps[:, bt * bp:(bt + 1) * bp], in_=Zc[0:bp, bt, gi, :, :].rearrange("p j c -> p (j c)"), identity=idt[0:bp, 0:bp]),
                             reads=[rZc, rid], writes=[rps], signal=(bt == nbt - 1))
                    P.op("act", lambda e, ps=ps, gi=gi: e.activation(out=U8[:, gi, :], in_=ps[:, 0:NBLK], func=AF.Copy), reads=[rps], writes=[rU8])
                    le, rle = LE[gi % 2]
                    ps, rps = self.bank()
                    P.op("pe", lambda e, ps=ps, gi=gi: e.transpose(out=ps[:, 0:64], in_=BEr[0:64, gi, :, :].rearrange("p j c -> p (j c)"), identity=idt[0:64, 0:64]), reads=[rgen, rid], writes=[rps], signal=False)
                    P.op("pe", lambda e, ps=ps, gi=gi: e.transpose(out=ps[:, 64:128], in_=BEi[0:64, gi, :, :].rearrange("p j c -> p (j c)"), identity=idt[0:64, 0:64]), reads=[rgen, rid], writes=[rps])
                    P.op("dve", lambda e, ps=ps, le=le: e.tensor_copy(out=le[:, :], in_=ps[:, 0:128]), reads=[rps], writes=[rle])
                    ps, rps = self.bank()
                    self.mm_acc(ps[:, 0:NBLK], rps, [(le[:, :], U8[:, gi, :])], [rle, rU8])
                    P.op("act", lambda e, ps=ps, gi=gi: e.activation(out=S[:, gi, 1:W], in_=ps[:, 0:NBLK], func=AF.Copy), reads=[rps], writes=[rS[gi]])
                    ps, rps = self.bank()
                    self.mm_acc(ps[:, 0:128], rps, [(Bjr[0:64, gi, :, :].rearrange("p j c -> p (j c)"), Crr[0:64, gi, 0:8, :].rearrange("p j c -> p (j c)")),
                                                    (Bji[0:64, gi, :, :].rearrange("p j c -> p (j c)"), Cri[0:64, gi, 0:8, :].rearrange("p j c -> p (j c)"))], [rgen])
                    tt_, rtt_ = tmpT[gi % 2]
                    P.op("dve", lambda e, ps=ps, tt_=tt_: e.tensor_tensor(out=tt_[:, :], in0=ps[:, 0:128], in1=tmask[:, :], op=ALU.mult), reads=[rps, rtm], writes=[rtt_])
                    P.op("dve", lambda e, tt_=tt_, gi=gi, gg=gg: e.scalar_tensor_tensor(out=Tt[:, gi, :], in0=idt[:, :], scalar=Dcol[:, gg:gg + 1], in1=tt_[:, :], op0=ALU.mult, op1=ALU.add),
                         reads=[rtt_, rid, rp], writes=[rTt])
                for k in range(nsteps):
                    sh = 1 << k
                    n = NBLK - sh
                    for gi in range(GB):
                        gg = g0 + gi
                        mk, rmk = MK[nmk % 4]
                        nmk += 1
                        P.op("pool", lambda e, mk=mk, gg=gg, k=k: e.tensor_scalar(out=mk[:, :], in0=idt[:, :], scalar1=APr[:, gg, k:k + 1], scalar2=None, op0=ALU.mult),
                             reads=[rid, rp], writes=[rmk])
                        mk2, rmk2 = MK2[nmk % 2]
                        P.op("pool", lambda e, mk2=mk2, gg=gg, k=k: e.tensor_scalar(out=mk2[:, :], in0=swp[:, :], scalar1=APi[:, gg, k:k + 1], scalar2=None, op0=ALU.mult),
                             reads=[rswp, rp], writes=[rmk2])
                        P.op("pool", lambda e, mk=mk, mk2=mk2: e.tensor_tensor(out=mk[:, :], in0=mk[:, :], in1=mk2[:, :], op=ALU.add),
                             reads=[rmk2, rmk], writes=[rmk])
                        ps, rps = self.bank()
                        self.mm_acc(ps[:, 0:n], rps, [(mk[:, :], S[:, gi, 1:1 + n])], [rmk, rS[gi]])
                        P.op("dve", lambda e, ps=ps, gi=gi, sh=sh, n=n: e.tensor_tensor(out=S[:, gi, 1 + sh:1 + sh + n], in0=ps[:, 0:n], in1=S[:, gi, 1 + sh:1 + sh + n], op=ALU.add),
                             reads=[rps], writes=[rS[gi]])
                for gi in range(GB):
                    ls, rls = LS[gi % 2]
                    P.op("pool", lambda e, ls=ls, gi=gi: e.tensor_copy(out=ls[0:64, :], in_=Crr[0:64, gi, 1:9, :].rearrange("p j c -> p (j c)")), reads=[rgen], writes=[rls])
                    P.op("pool", lambda e, ls=ls, gi=gi: e.tensor_copy(out=ls[64:128, :], in_=Cri[64:128, gi, 1:9, :].rearrange("p j c -> p (j c)")), reads=[rgen], writes=[rls])
                    ps, rps = self.bank()
                    self.mm_acc(ps[:, 0:NBLK], rps, [(Tt[:, gi, :], U8[:, gi, :]), (ls[:, :], S[:, gi, 0:NBLK])], [rTt, rU8, rls, rS[gi]])
                    y8, ry8 = Y8[gi % 2]
                    P.op("act", lambda e, ps=ps, y8=y8: e.activation(out=y8[:, :], in_=ps[:, 0:NBLK], func=AF.Copy), reads=[rps], writes=[ry8])
                    ps, rps = self.bank()
                    for bt in range(nbt):
                        P.op("pe", lambda e, ps=ps, bt=bt, y8=y8: e.transpose(out=ps[0:bp, bt * 128:(bt + 1) * 128], in_=y8[:, bt * bp:(bt + 1) * bp], identity=idt[:, :]),
                             reads=[ry8, rid], writes=[rps], signal=(bt == nbt - 1))
                    P.op("dve", lambda e, ps=ps, gi=gi: e.tensor_copy(out=Yt[0:bp, :, :, gi * 16:(gi + 1) * 16], in_=ps[0:bp, 0:nbt * 128].rearrange("p (b j c) -> p b j c", b=nbt, j=8)),
                         reads=[rps], writes=[rYt])
                for bt in range(nbt):
                    P.dma("pool", self.y_d[bt * bp * 8:(bt + 1) * bp * 8, g0 * 16:g0 * 16 + FW].rearrange("(b j) f -> b j f", j=8), Yt[0:bp, bt, :, :], rYt, reads=[rYt])
            P.barrier()

    def setup_attn(self):
        P, g = self.P, self.g
        reld, rreld = g["c_reld"]
        tab = P.sbuf("tab", [128, 256], F32)
        rtab = P.nres_("tab")
        P.dma("sp", tab[:, :], self.i["rel_bias"].rearrange("b h -> (b h)").partition_broadcast(128), rtab, writes=[rtab])
        steps = [(-90, 15, 14), (-63, 14, 13), (-45, 13, 12), (-31, 12, 11), (-22, 11, 10), (-15, 10, 9), (-11, 9, 8)]
        steps += [(-n, n + 1, n) for n in range(7, -1, -1)]
        steps += [(1, 0, 17)] + [(n, 15 + n, 16 + n) for n in range(2, 8)]
        steps += [(8, 23, 24), (12, 24, 25), (16, 25, 26), (23, 26, 27), (32, 27, 28), (46, 28, 29), (64, 29, 30), (91, 30, 31)]
        ns = len(steps)
        dl = P.sbuf("dl", [128, ns, 8], F32)
        rdl = P.nres_("dl")
        for s, (thr, fb, tb) in enumerate(steps):
            P.op("dve", lambda e, s=s, fb=fb, tb=tb: e.tensor_tensor(out=dl[:, s, :], in0=tab[:, tb * 8:tb * 8 + 8], in1=tab[:, fb * 8:fb * 8 + 8], op=ALU.subtract),
                 reads=[rtab], writes=[rdl])
        bias = P.sbuf("biasT", [128, NH, 2, 128], F32)
        rbias = P.nres_("biasT")
        mk = P.sbuf("mk", [128, 128], F32)
        rmk = P.nres_("mk")
        for kind in range(2):
            off = -128.0 * kind
            for h in range(NH):
                P.op("dve", lambda e, h=h, kind=kind: e.tensor_scalar(out=bias[:, h, kind, :], in0=reld[:, :], scalar1=0.0, scalar2=tab[:, 15 * 8 + h:15 * 8 + h + 1], op0=ALU.mult, op1=ALU.add),
                     reads=[rreld, rtab], writes=[rbias])
            for s, (thr, fb, tb) in enumerate(steps):
                if kind == 1 and thr > -1:
                    continue
                if thr > 64:
                    continue
                P.op("dve", lambda e, thr=thr, off=off: e.tensor_single_scalar(out=mk[:, :], in_=reld[:, :], scalar=float(thr) - off, op=ALU.is_ge), reads=[rreld], writes=[rmk])
                for h in range(NH):
                    P.op("dve", lambda e, h=h, kind=kind, s=s: e.scalar_tensor_tensor(out=bias[:, h, kind, :], in0=mk[:, :], scalar=dl[:, s, h:h + 1], in1=bias[:, h, kind, :], op0=ALU.mult, op1=ALU.add),
                         reads=[rmk, rdl], writes=[rbias])
        for h in range(NH):
            P.op("pool", lambda e, h=h: e.memset(bias[64:128, h, 0, 0:64], -30000.0), reads=[rbias], writes=[rbias])
        g["tab"] = (tab, rtab)
        g["biasT"] = (bias, rbias)

    def bank2(self, lo, n, key):
        c = self.bctr.get(key, 0)
        self.bctr[key] = c + 1
        return self.psb[lo + c % n]

    def phase_att(self, l):
        P, g = self.P, self.g
        L = self.L
        nq = L // 128
        lam_init = 0.8 - 0.6 * math.exp(-0.3 * l)
        idt, rid = g["c_ident"]
        tab, rtab = g["tab"]
        bias, rbias = g["biasT"]
        self.bctr = {}
        with contextlib.ExitStack() as st:
            T = lambda name, shape, dt=F32: P.sbuf(name, shape, dt, st)
            rpar = P.nres_("att_par")
            lq = T("lq", [128, 4, 64])
            for ci, nm in enumerate(("lambda_q1", "lambda_k1", "lambda_q2", "lambda_k2")):
                P.dma("sp", lq[:, ci, :], self.i[nm][l].partition_broadcast(128), rpar, writes=[rpar])
            subw = T("subw", [128, 128])
            P.dma("sp", subw[:, :], self.i["subln_w"][l].partition_broadcast(128), rpar, writes=[rpar])
            e12 = T("e12", [128, 2])
            nlam = T("nlam", [128, 1])
            V = lambda fn: P.op("dve", fn, reads=[rpar], writes=[rpar])
            V(lambda e: e.tensor_tensor(out=lq[:, 0, :], in0=lq[:, 0, :], in1=lq[:, 1, :], op=ALU.mult))
            V(lambda e: e.tensor_tensor(out=lq[:, 2, :], in0=lq[:, 2, :], in1=lq[:, 3, :], op=ALU.mult))
            V(lambda e: e.reduce_sum(out=e12[:, 0:1], in_=lq[:, 0, :], axis=AX.X))
            V(lambda e: e.reduce_sum(out=e12[:, 1:2], in_=lq[:, 2, :], axis=AX.X))
            P.op("act", lambda e: e.activation(out=e12[:, :], in_=e12[:, :], func=AF.Exp), reads=[rpar], writes=[rpar])
            V(lambda e: e.tensor_tensor(out=nlam[:, :], in0=e12[:, 1:2], in1=e12[:, 0:1], op=ALU.subtract))
            V(lambda e: e.tensor_scalar(out=nlam[:, :], in0=nlam[:, :], scalar1=-lam_init, scalar2=None, op0=ALU.add))
            V(lambda e: e.tensor_scalar(out=subw[:, :], in0=subw[:, :], scalar1=1.0 - lam_init, scalar2=None, op0=ALU.mult))
            nb = nq
            KT = [(T("KT", [128, L], BF16), P.nres_("att_KT%d" % i)) for i in range(2)]
            QT = [[(T("QT", [128, L], BF16), P.nres_("att_QT%d_%d" % (i, m))) for m in range(2)] for i in range(2)]
            V1 = [(T("V1", [128, nb, 132], BF16), P.nres_("att_V1%d" % i)) for i in range(2)]
            for v1, rv1 in V1:
                P.op("pool", lambda e, v1=v1: e.memset(v1[:, :, 128:132], 0.0), writes=[rv1])
                P.op("pool", lambda e, v1=v1: e.memset(v1[:, :, 128:129], 1.0), writes=[rv1])
            for i in range(2):
                for m in range(2):
                    qz, rqz = QT[i][m]
                    P.op("pool", lambda e, qz=qz: e.memset(qz[:, :], 0.0), writes=[rqz])
            PT = [(T("PT", [128, 4, 128], BF16), P.nres_("att_PT%d" % i)) for i in range(3)]
            tS = [(T("tS", [128, 128]), P.nres_("att_tS%d" % i)) for i in range(3)]
            rc = [(T("rc", [128, 2]), P.nres_("att_rc%d" % i)) for i in range(2)]
            oT = [(T("oT", [128, 128]), P.nres_("att_oT%d" % i)) for i in range(2)]
            on = [(T("on", [128, 128]), P.nres_("att_on%d" % i)) for i in range(2)]
            sj = [(T("sj", [128, 128], BF16), P.nres_("att_sj%d" % i)) for i in range(2)]
            ssq = [(T("ssq", [128, 1]), P.nres_("att_ssq%d" % i)) for i in range(2)]
            yo = [(T("yo", [128, 512], BF16), P.nres_("att_yo%d" % i)) for i in range(2)]
            npt = 0
            nts = 0
            import os as _os
            STOP = int(_os.environ.get("ATT_STOP", "4"))
            for h in range(NH if STOP > 0 else 0):
                (kt, rkt), (v1, rv1) = KT[h % 2], V1[h % 2]
                qts = QT[h % 2]
                P.dma("sp", kt[:, :], self.kT_d[h * 128:(h + 1) * 128, :], rkt, writes=[rkt])
                for m in range(2):
                    P.dma("sp", qts[m][0][m * 64:(m + 1) * 64, :], self.qT_d[h * 128 + m * 64:h * 128 + (m + 1) * 64, :], qts[m][1], writes=[qts[m][1]])
                P.dma("sp", v1[:, :, 0:128], self.v_d[:, h * 128:(h + 1) * 128].rearrange("(j p) e -> p j e", p=128), rv1, writes=[rv1])
                cb = tab[:, 15 * 8 + h:15 * 8 + h + 1]
                tps = None
                pending = []

                def emit_pv(o_ps, ro, pt, rpt, grp, i, v1=None, rv1=None):
                    for idx, j in enumerate(grp):
                        P.op("pe", lambda e, o_ps=o_ps, pt=pt, idx=idx, j=j, i=i, v1=v1: e.matmul(
                            o_ps[:, 0:130], lhsT=pt[:, idx, :], rhs=v1[:, j, 0:130], start=(j == 0), stop=(j == i)),
                            reads=[rpt, rv1], writes=[ro], signal=(j == i))

                def emit_fin(i, ops_, tps_box):
                    (o0, ro0), (o1, ro1) = ops_
                    rct, rrc = rc[i % 2]
                    ot_, rot = oT[i % 2]
                    on_, ron = on[i % 2]
                    sj_, rsj = sj[i % 2]
                    sq_, rsq = ssq[i % 2]
                    P.op("dve", lambda e, rct=rct, o0=o0: e.reciprocal(out=rct[:, 0:1], in_=o0[:, 128:129]), reads=[ro0], writes=[rrc])
                    P.op("dve", lambda e, rct=rct, o1=o1: e.reciprocal(out=rct[:, 1:2], in_=o1[:, 128:129]), reads=[ro1], writes=[rrc])
                    P.op("dve", lambda e, rct=rct: e.tensor_tensor(out=rct[:, 1:2], in0=rct[:, 1:2], in1=nlam[:, :], op=ALU.mult), reads=[rrc, rpar], writes=[rrc])
                    P.op("dve", lambda e, rct=rct, o0=o0, ot_=ot_: e.tensor_scalar(out=ot_[:, :], in0=o0[:, 0:128], scalar1=rct[:, 0:1], scalar2=None, op0=ALU.mult),
                         reads=[ro0, rrc], writes=[rot])
                    P.op("dve", lambda e, rct=rct, o1=o1, ot_=ot_: e.scalar_tensor_tensor(out=ot_[:, :], in0=o1[:, 0:128], scalar=rct[:, 1:2], in1=ot_[:, :], op0=ALU.mult, op1=ALU.add),
                         reads=[ro1, rrc, rot], writes=[rot])
                    P.op("act", lambda e, ot_=ot_, sj_=sj_, sq_=sq_: e.activation(out=sj_[:, :], in_=ot_[:, :], func=AF.Square, accum_out=sq_[:, :]),
                         reads=[rot], writes=[rsj, rsq])
                    P.op("act", lambda e, sq_=sq_: e.activation(out=sq_[:, :], in_=sq_[:, :], func=AF.Sqrt, bias=g["eps_sub"][0][:, :], scale=1.0 / 128),
                         reads=[rsq, g["eps_sub"][1]], writes=[rsq])
                    P.op("dve", lambda e, sq_=sq_: e.reciprocal(out=sq_[:, :], in_=sq_[:, :]), reads=[rsq], writes=[rsq])
                    P.op("dve", lambda e, ot_=ot_, sq_=sq_, on_=on_: e.scalar_tensor_tensor(out=on_[:, :], in0=ot_[:, :], scalar=sq_[:, 0:1], in1=subw[:, :], op0=ALU.mult, op1=ALU.mult),
                         reads=[rot, rsq, rpar], writes=[ron])
                    tps, rtps = self.psb[7]
                    P.op("pe", lambda e, tps=tps, on_=on_, i=i: e.transpose(out=tps[:, (i % 4) * 128:(i % 4 + 1) * 128], in_=on_[:, :], identity=idt[:, :]),
                         reads=[ron, rid], writes=[rtps])
                    if i % 4 == 3 or i == nq - 1:
                        nblk = i % 4 + 1
                        y_, ry = yo[(i // 4) % 2]
                        P.op("act", lambda e, tps=tps, y_=y_, nblk=nblk: e.activation(out=y_[:, 0:nblk * 128], in_=tps[:, 0:nblk * 128], func=AF.Copy), reads=[rtps], writes=[ry])
                        q0 = (i - nblk + 1) * 128
                        P.dma("pool", self.yaT_d[h * 128:(h + 1) * 128, q0:q0 + nblk * 128], y_[:, 0:nblk * 128], ry, reads=[ry])

                def flush():
                    while pending:
                        fn = pending.pop(0)
                        fn()

                for i in range(nq if STOP > 1 else 0):
                    ops_ = []
                    for m in range(2):
                        qt, rqt = qts[m]
                        o_ps, ro = self.bank2(0, 4, "O")
                        ops_.append((o_ps, ro))
                        for j0 in range(0, i + 1, 4):
                            grp = list(range(j0, min(j0 + 4, i + 1)))
                            s_ps, rs = self.bank2(4, 3, "S")
                            for idx, j in enumerate(grp):
                                P.op("pe", lambda e, s_ps=s_ps, idx=idx, j=j, i=i, kt=kt, qt=qt: e.matmul(
                                    s_ps[:, idx * 128:(idx + 1) * 128], lhsT=kt[:, j * 128:(j + 1) * 128],
                                    rhs=qt[:, i * 128:(i + 1) * 128], start=True, stop=True),
                                    reads=[rkt, rqt], writes=[rs], signal=(idx == len(grp) - 1))
                            pt, rpt = PT[npt % 3]
                            npt += 1
                            nfar = len([j for j in grp if j <= i - 2])
                            if nfar:
                                P.op("act", lambda e, s_ps=s_ps, pt=pt, nfar=nfar, cb=cb: e.activation(
                                    out=pt[:, 0:nfar, :], in_=s_ps[:, 0:nfar * 128].rearrange("p (a b) -> p a b", a=nfar), func=AF.Exp, bias=cb),
                                    reads=[rs, rtab], writes=[rpt])
                            for idx, j in enumerate(grp):
                                if j <= i - 2:
                                    continue
                                kind = 0 if j == i else 1
                                ts_, rts = tS[nts % 3]
                                nts += 1
                                P.op("dve", lambda e, s_ps=s_ps, idx=idx, ts_=ts_, kind=kind, h=h: e.tensor_tensor(
                                    out=ts_[:, :], in0=s_ps[:, idx * 128:(idx + 1) * 128], in1=bias[:, h, kind, :], op=ALU.add),
                                    reads=[rs, rbias], writes=[rts])
                                P.op("act", lambda e, ts_=ts_, pt=pt, idx=idx: e.activation(out=pt[:, idx, :], in_=ts_[:, :], func=AF.Exp),
                                     reads=[rts], writes=[rpt])
                            flush()
                            pending.append(lambda o_ps=o_ps, ro=ro, pt=pt, rpt=rpt, grp=grp, i=i: emit_pv(o_ps, ro, pt, rpt, grp, i, v1, rv1))
                    pending.append(lambda i=i, ops_=ops_: emit_fin(i, ops_, None))
                flush()
            P.barrier()

    def phase_b1(self, l, x_src):
        P, g = self.P, self.g
        idt, rid = g["c_ident"]
        with contextlib.ExitStack() as st:
            T = lambda name, shape, dt=F32: P.sbuf(name, shape, dt, st)
            xt, rx = T("xt", [128, 4, D]), P.nres_("xt")
            ytoks = [(T("ytok", [128, SSMW]), P.nres_("b1_ytok%d" % i)) for i in range(2)]
            tmp = [(T("gt", [128, SSMW]), P.nres_("b1_gt%d" % i)) for i in range(2)]
            yg = [(T("yg", [128, SSMW]), P.nres_("b1_yg%d" % i)) for i in range(2)]
            ygf, rygf = T("ygf", [128, 8, 512]), P.nres_("b1_ygf")
            ygb, rygb = T("ygb", [128, 8, 512], BF16), P.nres_("b1_ygb")
            yss, ryss = T("yss", [128, 8, 512], BF16), P.nres_("b1_yss")
            yat, ryat = T("yat", [128, 8, 512], BF16), P.nres_("b1_yat")
            gsTs = [(T("gsT", [128, 4, 512], BF16), P.nres_("b1_gs%d" % i)) for i in range(2)]
            gaTs = [(T("gaT", [128, 4, 512], BF16), P.nres_("b1_ga%d" % i)) for i in range(2)]
            mT, rmT = T("mT", [128, 16, 512], BF16), P.nres_("b1_mT")
            sgt = [(T("sgt", [128, 512]), P.nres_("b1_sgt%d" % i)) for i in range(2)]
            m1 = [(T("m1", [128, 512]), P.nres_("b1_m1%d" % i)) for i in range(2)]
            m2 = [(T("m2", [128, 512]), P.nres_("b1_m2%d" % i)) for i in range(2)]
            wbuf = [(T("wB", [128, 16, 512], BF16), P.nres_("wA%d" % i)) for i in range(2)]
            wglu, rwglu = T("wglu", [128, 8, SSMW], BF16), P.nres_("b1_wglu")
            bglu, rbglu = T("bglu", [128, 8]), P.nres_("b1_bglu")
            self.load_wtile(wglu, rwglu, self.wb["w_glu"][l], 0, 8, 0, SSMW)
            P.dma("sp", bglu[:, :], self.i["b_glu"][l].rearrange("(c p) -> p c", p=128), rbglu, writes=[rbglu], allow_slow_non_contiguous=True)
            nw = 0
            for t in range(self.NT):
                t0 = t * 512
                P.dma("sp", xt[:, :, :], x_src[t0:t0 + 512, :].rearrange("(s p) d -> p s d", p=128), rx, writes=[rx])
                P.dma("sp", yat[:, :, :], self.yaT_d.rearrange("(fc p) t -> p fc t", p=128)[:, :, t0:t0 + 512], ryat, writes=[ryat])
                for s in range(4):
                    (tm, rtm), (ygs, rygs) = tmp[s % 2], yg[s % 2]
                    ytk, rytok = ytoks[s % 2]
                    P.dma("sp", ytk[:, :], self.y_d[t0 + s * 128:t0 + (s + 1) * 128, :], rytok, writes=[rytok])
                    P.op("pool", lambda e, tm=tm, ytk=ytk: e.tensor_tensor(out=tm[:, :], in0=ytk[:, :], in1=ytk[:, :], op=ALU.mult), reads=[rytok], writes=[rtm])
                    P.op("dve", lambda e, tm=tm: e.tensor_scalar(out=tm[:, :], in0=tm[:, :], scalar1=0.044715, scalar2=1.0, op0=ALU.mult, op1=ALU.add), reads=[rtm], writes=[rtm])
                    P.op("pool", lambda e, tm=tm, ytk=ytk: e.tensor_tensor(out=tm[:, :], in0=tm[:, :], in1=ytk[:, :], op=ALU.mult), reads=[rtm, rytok], writes=[rtm])
                    P.op("act", lambda e, tm=tm: e.activation(out=tm[:, :], in_=tm[:, :], func=AF.Sigmoid, scale=2.0 * math.sqrt(2.0 / math.pi)), reads=[rtm], writes=[rtm])
                    P.op("dve", lambda e, tm=tm, ygs=ygs, ytk=ytk: e.tensor_tensor(out=ygs[:, :], in0=tm[:, :], in1=ytk[:, :], op=ALU.mult), reads=[rtm, rytok], writes=[rygs])
                    for k4 in range(2):
                        ps, rps = self.bank()
                        for j in range(4):
                            fc = k4 * 4 + j
                            P.op("pe", lambda e, ps=ps, ygs=ygs, fc=fc, j=j: e.transpose(out=ps[:, j * 128:(j + 1) * 128], in_=ygs[:, fc * 128:(fc + 1) * 128], identity=idt[:, :]),
                                 reads=[rygs, rid], writes=[rps], signal=(j == 3))
                        P.op("act", lambda e, ps=ps, k4=k4, s=s: e.activation(out=ygf[:, k4 * 4:k4 * 4 + 4, s * 128:(s + 1) * 128], in_=ps[:, :].rearrange("p (k t) -> p k t", k=4), func=AF.Copy),
                             reads=[rps], writes=[rygf])
                        P.op("dve", lambda e, ps=ps, k4=k4, s=s: e.tensor_copy(out=ygb[:, k4 * 4:k4 * 4 + 4, s * 128:(s + 1) * 128], in_=ps[:, :].rearrange("p (k t) -> p k t", k=4)),
                             reads=[rps], writes=[rygb])
                for cb in range(8):
                    ps, rps = self.bank()
                    self.mm_acc(ps[:, :], rps, [(wglu[:, kc, cb * 128:(cb + 1) * 128], ygb[:, kc, :]) for kc in range(8)], [rwglu, rygb])
                    sg, rsg = sgt[cb % 2]
                    P.op("act", lambda e, ps=ps, sg=sg, cb=cb: e.activation(out=sg[:, :], in_=ps[:, :], func=AF.Sigmoid, bias=bglu[:, cb:cb + 1]), reads=[rps, rbglu], writes=[rsg])
                    P.op("dve", lambda e, sg=sg, cb=cb: e.tensor_tensor(out=yss[:, cb, :], in0=ygf[:, cb, :], in1=sg[:, :], op=ALU.mult), reads=[rsg, rygf], writes=[ryss])
                for c4 in range(4):
                    (ws, rws), (wa, rwa) = wbuf[0], wbuf[1]
                    self.load_wtile(ws, rws, self.wb["w_proj_ssm"][l], 0, 8, c4 * 512, 512)
                    self.load_wtile(wa, rwa, self.wb["w_proj_attn"][l], 0, 8, c4 * 512, 512)
                    (gsT, rgs), (gaT, rga) = gsTs[c4 % 2], gaTs[c4 % 2]
                    P.dma("sp", gsT[:, :, :], self.gsT_d[c4 * 512:(c4 + 1) * 512, t0:t0 + 512].rearrange("(fc p) t -> p fc t", p=128), rgs, writes=[rgs])
                    P.dma("sp", gaT[:, :, :], self.gaT_d[c4 * 512:(c4 + 1) * 512, t0:t0 + 512].rearrange("(fc p) t -> p fc t", p=128), rga, writes=[rga])
                    for j in range(4):
                        cb = c4 * 4 + j
                        ps1, rps1 = self.bank()
                        self.mm_acc(ps1[:, :], rps1, [(ws[:, kc, j * 128:(j + 1) * 128], yss[:, kc, :]) for kc in range(8)], [rws, ryss])
                        ps2, rps2 = self.bank()
                        self.mm_acc(ps2[:, :], rps2, [(wa[:, kc, j * 128:(j + 1) * 128], yat[:, kc, :]) for kc in range(8)], [rwa, ryat])
                        (a1, ra1), (a2, ra2) = m1[cb % 2], m2[cb % 2]
                        P.op("dve", lambda e, ps1=ps1, a1=a1, j=j, gsT=gsT: e.tensor_tensor(out=a1[:, :], in0=ps1[:, :], in1=gsT[:, j, :], op=ALU.mult), reads=[rps1, rgs], writes=[ra1])
                        P.op("dve", lambda e, ps2=ps2, a2=a2, j=j, gaT=gaT: e.tensor_tensor(out=a2[:, :], in0=ps2[:, :], in1=gaT[:, j, :], op=ALU.mult), reads=[rps2, rga], writes=[ra2])
                        P.op("pool", lambda e, a1=a1, a2=a2, cb=cb: e.tensor_tensor(out=mT[:, cb, :], in0=a1[:, :], in1=a2[:, :], op=ALU.add), reads=[ra1, ra2], writes=[rmT])
                for nch in range(4):
                    wt, rwt = wbuf[nch % 2]
                    self.load_wtile(wt, rwt, self.wb["w_out"][l], 0, 16, nch * 512, 512)
                    for s in range(4):
                        ps, rps = self.bank()
                        self.mm_acc(ps[:, :], rps, [(mT[:, kc, s * 128:(s + 1) * 128], wt[:, kc, :]) for kc in range(16)], [rmT, rwt])
                        P.op("dve", lambda e, ps=ps, s=s, nch=nch: e.tensor_tensor(out=xt[:, s, nch * 512:(nch + 1) * 512], in0=ps[:, :], in1=xt[:, s, nch * 512:(nch + 1) * 512], op=ALU.add),
                             reads=[rps, rx], writes=[rx])
                P.dma("pool", self.xa_d[t0:t0 + 512, :].rearrange("(s p) d -> p s d", p=128), xt[:, :, :], rx, reads=[rx])
            P.barrier()

    def phase_b2(self, l, x_dst):
        P, g = self.P, self.g
        with contextlib.ExitStack() as st:
            T = lambda name, shape, dt=F32: P.sbuf(name, shape, dt, st)
            xt, rx = T("xt", [128, 4, D]), P.nres_("xt")
            nbufs = self.norm_bufs(st)
            w2T, rw2 = T("w2T", [128, 16]), P.nres_("w1T")
            P.dma("sp", w2T[:, :], self.i["norm2_w"][l].rearrange("(kc p) -> p kc", p=128), rw2, writes=[rw2], allow_slow_non_contiguous=True)
            hT, rhT = T("hT", [128, 16, 512], BF16), P.nres_("hT")
            aT, raT = T("aT", [128, 44, 512], BF16), P.nres_("b2_aT")
            wbuf = [(T("wF", [128, 22, 512], BF16), P.nres_("wA%d" % i)) for i in range(3)]
            sgt = [(T("sgt", [128, 512]), P.nres_("b1_sgt%d" % i)) for i in range(2)]
            nw = 0
            for t in range(self.NT):
                t0 = t * 512
                P.dma("sp", xt[:, :, :], self.xa_d[t0:t0 + 512, :].rearrange("(s p) d -> p s d", p=128), rx, writes=[rx])
                self.norm_T(nbufs, xt, rx, w2T, rw2, hT, rhT)
                for c in range(DFF // 512):
                    (wg, rwg) = wbuf[nw % 3]
                    (wu, rwu) = wbuf[(nw + 1) % 3]
                    nw += 2
                    self.load_wtile(wg, rwg, self.wb["w_ffn_gate"][l], 0, 16, c * 512, 512)
                    self.load_wtile(wu, rwu, self.wb["w_ffn_up"][l], 0, 16, c * 512, 512)
                    for j in range(4):
                        fb = c * 4 + j
                        psg, rpsg = self.bank()
                        self.mm_acc(psg[:, :], rpsg, [(wg[:, kc, j * 128:(j + 1) * 128], hT[:, kc, :]) for kc in range(16)], [rwg, rhT])
                        psu, rpsu = self.bank()
                        self.mm_acc(psu[:, :], rpsu, [(wu[:, kc, j * 128:(j + 1) * 128], hT[:, kc, :]) for kc in range(16)], [rwu, rhT])
                        sg, rsg = sgt[fb % 2]
                        P.op("act", lambda e, psg=psg, sg=sg: e.activation(out=sg[:, :], in_=psg[:, :], func=AF.Silu), reads=[rpsg], writes=[rsg])
                        P.op("dve", lambda e, psu=psu, sg=sg, fb=fb: e.tensor_tensor(out=aT[:, fb, :], in0=psu[:, :], in1=sg[:, :], op=ALU.mult), reads=[rpsu, rsg], writes=[raT])
                for nch in range(4):
                    (w0, rw0) = wbuf[nw % 3]
                    (w1, rw1) = wbuf[(nw + 1) % 3]
                    nw += 2
                    self.load_wtile(w0, rw0, self.wb["w_ffn_down"][l], 0, 22, nch * 512, 512)
                    self.load_wtile(w1, rw1, self.wb["w_ffn_down"][l], 22, 22, nch * 512, 512)
                    for s in range(4):
                        ps, rps = self.bank()
                        pairs = [(aT[:, kc, s * 128:(s + 1) * 128], w0[:, kc, :]) for kc in range(22)]
                        pairs += [(aT[:, 22 + kc, s * 128:(s + 1) * 128], w1[:, kc, :]) for kc in range(22)]
                        self.mm_acc(ps[:, :], rps, pairs, [raT, rw0, rw1])
                        P.op("dve", lambda e, ps=ps, s=s, nch=nch: e.tensor_tensor(out=xt[:, s, nch * 512:(nch + 1) * 512], in0=ps[:, :], in1=xt[:, s, nch * 512:(nch + 1) * 512], op=ALU.add),
                             reads=[rps, rx], writes=[rx])
                P.dma("pool", x_dst[t0:t0 + 512, :].rearrange("(s p) d -> p s d", p=128), xt[:, :, :], rx, reads=[rx])
            P.barrier()


_CACHE = {}


def kernel(**inputs):
    x = np.ascontiguousarray(np.asarray(inputs["x"], dtype=np.float32))
    B, L, _ = x.shape
    key = (L,)
    if key not in _CACHE:
        _CACHE[key] = Builder(L).build()
    nc = _CACHE[key]
    base = {k: np.ascontiguousarray(np.asarray(v, dtype=np.float32)) for k, v in inputs.items() if k != "x"}
    base.update(host_consts())
    in_maps = []
    for b in range(B):
        m = dict(base)
        m["x"] = x[b]
        in_maps.append(m)
    res = run_bass_kernel_spmd(nc, in_maps, core_ids=list(range(B)))
    return np.stack([np.asarray(r["out"], dtype=np.float32) for r in res.results], axis=0)
```

```python
import contextlib
import math
import numpy as np
import concourse.bass as bass
import concourse.mybir as mybir
from concourse.bass_utils import run_bass_kernel_spmd

F32 = mybir.dt.float32
BF16 = mybir.dt.bfloat16
I32 = mybir.dt.int32
AF = mybir.ActivationFunctionType
ALU = mybir.AluOpType
AX = mybir.AxisListType

D = 2048
DEPTH = 2
SSMW = 1024
NG = 64
NS = 64
ATW = 1024
NH = 8
INW = 8192
DFF = 5632
RMS_EPS = 1e-6
SUBLN_EPS = 1e-5


class Res:
    __slots__ = ("name", "w", "r", "dsem", "dcnt", "excl")

    def __init__(self, name):
        self.name = name
        self.excl = False
        self.w = None
        self.r = {}
        self.dsem = None
        self.dcnt = 0


class Prog:
    ENG = ("pe", "act", "dve", "pool", "sp")

    def __init__(self, nc, stack):
        self.nc = nc
        self.stack = stack
        self.sems = []
        self.ops = {e: [] for e in self.ENG}
        self.cnt = {e: 0 for e in self.ENG}
        self.seen = {e: {} for e in self.ENG}
        self.esem = {}
        self.latest = {}
        for e in self.ENG:
            self.esem[e] = self.new_sem("s_" + e)
        self.nres = 0
        self.ccsem = None
        self.named = {}
        self.nalloc = 0

    def new_sem(self, name):
        s = self.stack.enter_context(self.nc.semaphore(name))
        self.sems.append(s)
        return len(self.sems) - 1

    def res(self, name=None):
        self.nres += 1
        return Res(name or ("r%d" % self.nres))

    def nres_(self, name):
        if name not in self.named:
            self.named[name] = Res(name)
        return self.named[name]

    def sbuf(self, name, shape, dtype, stack=None):
        self.nalloc += 1
        st = stack if stack is not None else self.stack
        t = st.enter_context(self.nc.sbuf_tensor("%s_%d" % (name, self.nalloc), list(shape), dtype))
        return t

    def psum(self, name, shape, dtype, stack=None):
        st = stack if stack is not None else self.stack
        return st.enter_context(self.nc.psum_tensor(name, list(shape), dtype))

    def _deps(self, eng, reads, writes):
        deps = {}
        for r in reads:
            if r.w is not None:
                s, v = r.w
                if deps.get(s, 0) < v:
                    deps[s] = v
        for w in writes:
            if w.w is not None:
                s, v = w.w
                if deps.get(s, 0) < v:
                    deps[s] = v
            for (s, v) in w.r.values():
                if deps.get(s, 0) < v:
                    deps[s] = v
        waits = []
        seen = self.seen[eng]
        own = self.esem[eng]
        for s, v in deps.items():
            if s == own and v > self.cnt[eng]:
                continue
            if seen.get(s, 0) < v:
                seen[s] = v
                waits.append((s, v))
        return waits

    def op(self, eng, fn, reads=(), writes=(), signal=True):
        if any(r.excl for r in reads):
            writes = list(writes) + [r for r in reads if r.excl and r not in writes]
            reads = [r for r in reads if not r.excl]
        waits = self._deps(eng, reads, writes)
        idx = self.cnt[eng] + 1
        if signal:
            self.cnt[eng] = idx
        ev = (self.esem[eng], idx)
        self.latest[ev[0]] = idx
        self.ops[eng].append((waits, fn, signal))
        for w in writes:
            w.w = ev
            w.r = {}
        for r in reads:
            r.r[ev[0]] = ev

    def dma(self, q, out, in_, sres, reads=(), writes=(), **kw):
        waits = self._deps(q, reads, writes)
        qk = "sw" if q == "pool" else "hw"
        if sres.dsem is None:
            sres.dsem = {}
            sres.dcnt = {}
        if qk not in sres.dsem:
            sres.dsem[qk] = self.new_sem("d%s_%s" % (qk, sres.name))
            sres.dcnt[qk] = 0
        sres.dcnt[qk] += 16
        ev = (sres.dsem[qk], sres.dcnt[qk])
        self.latest[ev[0]] = ev[1]
        sem = self.sems[ev[0]]

        def fn(e, out=out, in_=in_, sem=sem, kw=kw):
            e.dma_start(out=out, in_=in_, **kw).then_inc(sem, 16)
            return None

        self.ops[q].append((waits, fn, False))
        for w in writes:
            w.w = ev
            w.r = {}
        for r in reads:
            r.r[ev[0]] = ev

    def coll(self, kind, in_ap, out_ap, groups, reads=(), writes=()):
        waits = self._deps("pool", reads, writes)
        if self.ccsem is None:
            self.ccsem = self.new_sem("s_cc")
            self.cccnt = 0
        self.cccnt += 1
        ev = (self.ccsem, self.cccnt)
        self.latest[ev[0]] = ev[1]
        sem = self.sems[ev[0]]

        def fn(e):
            e.collective_compute(kind, ALU.bypass, replica_groups=groups, ins=[in_ap], outs=[out_ap]).then_inc(sem, 1)
            return None

        self.ops["pool"].append((waits, fn, False))
        for w in writes:
            w.w = ev
            w.r = {}
        for r in reads:
            r.r[ev[0]] = ev

    def barrier(self):
        for e in self.ENG:
            waits = []
            seen = self.seen[e]
            for s, v in self.latest.items():
                if seen.get(s, 0) < v:
                    seen[s] = v
                    waits.append((s, v))
            if waits:
                self.ops[e].append((waits, None, False))

    def emit(self):
        nc = self.nc
        sems = self.sems
        with nc.Block() as block:
            def run(name, e):
                own = sems[self.esem[name]]
                for waits, fn, signal in self.ops[name]:
                    for s, v in waits:
                        e.wait_ge(sems[s], v)
                    if fn is None:
                        continue
                    ins = fn(e)
                    if signal:
                        ins.then_inc(own, 1)

            @block.tensor
            def _(e):
                run("pe", e)

            @block.scalar
            def _(e):
                run("act", e)

            @block.vector
            def _(e):
                run("dve", e)

            @block.gpsimd
            def _(e):
                run("pool", e)

            @block.sync
            def _(e):
                run("sp", e)


WSPEC = [
    ("w_in", D, INW), ("w_glu", SSMW, SSMW), ("w_proj_ssm", SSMW, D), ("w_proj_attn", ATW, D),
    ("w_out", D, D), ("w_ffn_gate", D, DFF), ("w_ffn_up", D, DFF), ("w_ffn_down", DFF, D),
]
SMALL = [("norm1_w", [DEPTH, D]), ("lam_re", [DEPTH, NG, NS]), ("lam_im", [DEPTH, NG, NS]),
         ("log_step", [DEPTH, NG]), ("ssm_b_re", [DEPTH, NG, NS, 16]), ("ssm_b_im", [DEPTH, NG, NS, 16]),
         ("ssm_c_re", [DEPTH, NG, 16, NS]), ("ssm_c_im", [DEPTH, NG, 16, NS]), ("ssm_d", [DEPTH, SSMW]),
         ("b_glu", [DEPTH, SSMW]), ("q_norm_w", [DEPTH, 64]), ("k_norm_w", [DEPTH, 64]),
         ("lambda_q1", [DEPTH, 64]), ("lambda_k1", [DEPTH, 64]), ("lambda_q2", [DEPTH, 64]),
         ("lambda_k2", [DEPTH, 64]), ("subln_w", [DEPTH, 128]), ("rel_bias", [32, NH]),
         ("norm2_w", [DEPTH, D])]
CONSTS = [("c_ident", [128, 128]), ("c_bones", [128, 128]), ("c_tmask", [128, 128]), ("c_reld", [128, 128]), ("c_swap", [128, 128]), ("c_flag", [128, 1])]


def host_consts():
    ident = np.eye(128, dtype=np.float32)
    bones = np.kron(np.eye(2, dtype=np.float32), np.ones((64, 64), np.float32))
    jj = np.arange(128) // 16
    tmask = (jj[None, :] >= jj[:, None]).astype(np.float32)
    reld = (np.arange(128)[:, None] - np.arange(128)[None, :]).astype(np.float32)
    swap = np.roll(np.eye(128, dtype=np.float32), 64, axis=1)
    return {"c_ident": ident, "c_bones": bones, "c_tmask": tmask, "c_reld": reld, "c_swap": swap,
            "c_flag": np.zeros((128, 1), np.float32)}


class Builder:
    def __init__(self, L, nlayers=DEPTH, dbg=False, phases=None, nsh=8):
        self.L = L
        self.NT = L // 512
        self.nlayers = nlayers
        self.dbg = dbg
        self.phases = phases
        self.nc = bass.Bass("TRN2", target_bir_lowering=False)
        self.stack = contextlib.ExitStack()
        self.P = Prog(self.nc, self.stack)
        nc = self.nc
        self.i = {}
        self.i["x"] = self.din("x", [L, D])
        for n, k, m in WSPEC:
            self.i[n] = self.din(n, [DEPTH, k, m])
        for n, sh in SMALL + CONSTS:
            self.i[n] = self.din(n, sh)
        self.out = self.dout("out", [L, D])
        self.wb = {n: [self.dscr("wb_%s_%d" % (n, l), [k, m], BF16) for l in range(nlayers)] for n, k, m in WSPEC}
        sk = self.dout if dbg else self.dscr
        self.u_d = sk("u_d", [L, SSMW], BF16)
        self.nch = max(1, L // 1024)
        self.krows = ATW // self.nch
        self.ug = [self.dscr("ug%d" % c, [2 * 1024, SSMW], BF16) for c in range(self.nch)]
        self.vg = [self.dscr("vg%d" % c, [2 * 1024, ATW], BF16) for c in range(self.nch)]
        self.kTg = [self.dscr("kTg%d" % c, [2 * self.krows, L], BF16) for c in range(self.nch)]
        self.groups = [[0, 1], [2, 3], [4, 5], [6, 7]]
        skq = self.din if dbg == "attin" else sk
        self.qT_d = skq("qT_d", [ATW, L], BF16)
        self.kT_d = skq("kT_d", [ATW, L], BF16)
        self.v_d = skq("v_d", [L, ATW], BF16)
        self.gsT_d = sk("gsT_d", [D, L], BF16)
        self.gaT_d = sk("gaT_d", [D, L], BF16)
        self.y_d = sk("y_d", [L, SSMW], F32)
        self.yaT_d = sk("yaT_d", [ATW, L], BF16)
        self.xa_d = sk("xa_d", [L, D], F32)
        self.xb_d = self.dscr("xb_d", [L, D], F32)
        self.rdram = {}

    def din(self, name, shape, dtype=F32):
        return self.nc.dram_tensor(name, list(shape), dtype, kind="ExternalInput").ap()

    def dscr(self, name, shape, dtype):
        return self.nc.dram_tensor(name, list(shape), dtype, kind="Internal").ap()

    def dout(self, name, shape, dtype=F32):
        return self.nc.dram_tensor(name, list(shape), dtype, kind="ExternalOutput").ap()

    def setup_globals(self):
        P = self.P
        g = self.g = {}
        self.psb = []
        for b in range(8):
            self.psb.append((P.psum("psb%d" % b, [128, 512], F32), P.nres_("psb%d" % b)))
            self.psb[-1][1].excl = True
        self.psi = 0
        for n in ("c_ident", "c_bones", "c_tmask", "c_reld", "c_swap"):
            t = P.sbuf(n, [128, 128], F32)
            r = P.nres_(n)
            P.dma("sp", t[:, :], self.i[n][:, :], r, writes=[r])
            g[n] = (t, r)
        t = P.sbuf("c_flag", [128, 1], F32)
        r = P.nres_("c_flag")
        P.dma("sp", t[:, :], self.i["c_flag"][:, :], r, writes=[r])
        g["c_flag"] = (t, r)
        t2 = P.sbuf("mflag", [128, 1], F32)
        P.op("dve", lambda e, t=t, t2=t2: e.tensor_scalar(out=t2[:, :], in0=t[:, :], scalar1=-1.0, scalar2=30000.0, op0=ALU.add, op1=ALU.mult), reads=[r], writes=[r])
        g["mflag"] = (t2, r)
        t = P.sbuf("bones_b", [128, 128], BF16)
        r = P.nres_("bones_b")
        P.op("dve", lambda e, t=t: e.tensor_copy(out=t[:, :], in_=g["c_bones"][0][:, :]), reads=[g["c_bones"][1]], writes=[r])
        g["bones_b"] = (t, r)
        for nm, val in (("eps_rms", RMS_EPS), ("eps_sub", SUBLN_EPS), ("halfpi", math.pi / 2), ("zero", 0.0)):
            t = P.sbuf(nm, [128, 1], F32)
            r = P.nres_(nm)
            P.op("pool", lambda e, t=t, val=val: e.memset(t[:, :], val), writes=[r])
            g[nm] = (t, r)

    def bank(self):
        b = self.psb[self.psi % 8]
        self.psi += 1
        return b

    def phase_w(self):
        P = self.P
        with contextlib.ExitStack() as st:
            NB = 3
            CW = 4096
            fb = [(P.sbuf("wf", [128, CW], F32, st), P.nres_("wf%d" % i)) for i in range(NB)]
            bb = [(P.sbuf("wbb", [128, CW], BF16, st), P.nres_("wbb%d" % i)) for i in range(NB)]
            rw = P.nres_("wdram")
            it = 0
            ce = ("dve", "pool", "act")
            for l in range(self.nlayers):
                for n, K, N in WSPEC:
                    src = self.i[n]
                    dst = self.wb[n][l]
                    ncc = (N + CW - 1) // CW
                    cw = N // ncc
                    for kt in range(K // 128):
                        for c in range(ncc):
                            (f, rf), (b, rb) = fb[it % NB], bb[it % NB]
                            P.dma("sp", f[:, 0:cw], src[l, kt * 128:(kt + 1) * 128, c * cw:(c + 1) * cw], rf, writes=[rf])
                            eng = ce[it % 3]
                            if eng == "act":
                                P.op("act", lambda e, f=f, b=b, cw=cw: e.activation(out=b[:, 0:cw], in_=f[:, 0:cw], func=AF.Copy), reads=[rf], writes=[rb])
                            else:
                                P.op(eng, lambda e, f=f, b=b, cw=cw: e.tensor_copy(out=b[:, 0:cw], in_=f[:, 0:cw]), reads=[rf], writes=[rb])
                            P.dma("act", dst[kt * 128:(kt + 1) * 128, c * cw:(c + 1) * cw], b[:, 0:cw], rb, reads=[rb], writes=[])
                            it += 1
            P.barrier()

    def norm_T(self, st_bufs, xt, rx, wT, rwT, hT, rhT):
        P, g = self.P, self.g
        junk, rjunk, ssq, rssq, rstd, rrstd, xs = st_bufs
        for s in range(4):
            P.op("act", lambda e, s=s: e.activation(out=junk[:, :], in_=xt[:, s, :], func=AF.Square, accum_out=ssq[:, s:s + 1]),
                 reads=[rx], writes=[rjunk, rssq])
        P.op("act", lambda e: e.activation(out=rstd[:, :], in_=ssq[:, :], func=AF.Sqrt, bias=g["eps_rms"][0][:, :], scale=1.0 / D),
             reads=[rssq, g["eps_rms"][1]], writes=[rrstd])
        P.op("dve", lambda e: e.reciprocal(out=rstd[:, :], in_=rstd[:, :]), reads=[rrstd], writes=[rrstd])
        idt, rid = g["c_ident"]
        for s in range(4):
            xs_t, rxs = xs[s % 2]
            P.op("act", lambda e, s=s, xs_t=xs_t: e.activation(out=xs_t[:, :], in_=xt[:, s, :], func=AF.Copy, scale=rstd[:, s:s + 1]),
                 reads=[rx, rrstd], writes=[rxs])
            for k4 in range(4):
                ps, rps = self.bank()
                for j in range(4):
                    kc = k4 * 4 + j
                    P.op("pe", lambda e, ps=ps, xs_t=xs_t, kc=kc, j=j: e.transpose(out=ps[:, j * 128:(j + 1) * 128], in_=xs_t[:, kc * 128:(kc + 1) * 128], identity=idt[:, :]),
                         reads=[rxs, rid], writes=[rps], signal=(j == 3))
                P.op("dve", lambda e, ps=ps, k4=k4, s=s: e.tensor_tensor(
                    out=hT[:, k4 * 4:k4 * 4 + 4, s * 128:(s + 1) * 128],
                    in0=ps[:, :].rearrange("p (k t) -> p k t", k=4),
                    in1=wT[:, k4 * 4:k4 * 4 + 4].unsqueeze(2).broadcast_to([128, 4, 128]), op=ALU.mult),
                    reads=[rps, rwT], writes=[rhT])

    def norm_bufs(self, st):
        P = self.P
        junk = P.sbuf("junk", [128, D], BF16, st)
        ssq = P.sbuf("ssq", [128, 4], F32, st)
        rstd = P.sbuf("rstd", [128, 4], F32, st)
        xs = [(P.sbuf("xs", [128, D], F32, st), P.nres_("xs%d" % i)) for i in range(2)]
        return (junk, P.nres_("junk"), ssq, P.nres_("ssq"), rstd, P.nres_("rstd"), xs)

    def load_wtile(self, wt, rwt, src, k0, kcn, n0, ncols, q="sp"):
        self.P.dma(q, wt[:, 0:kcn, 0:ncols],
                   src[k0 * 128:(k0 + kcn) * 128, n0:n0 + ncols].rearrange("(kc p) n -> p kc n", p=128),
                   rwt, writes=[rwt])

    def mm_acc(self, ps_ap, rps, pairs, reads):
        n = len(pairs)
        for i, (lhsT, rhs) in enumerate(pairs):
            self.P.op("pe", lambda e, lhsT=lhsT, rhs=rhs, i=i: e.matmul(ps_ap, lhsT=lhsT, rhs=rhs, start=(i == 0), stop=(i == n - 1)),
                      reads=reads, writes=[rps], signal=(i == n - 1))

    def phase_a(self, l, x_src):
        P, g = self.P, self.g
        L = self.L
        with contextlib.ExitStack() as st:
            xt = P.sbuf("xt", [128, 4, D], F32, st)
            rx = P.nres_("xt")
            nbufs = self.norm_bufs(st)
            w1T = P.sbuf("w1T", [128, 16], F32, st)
            rw1 = P.nres_("w1T")
            P.dma("sp", w1T[:, :], self.i["norm1_w"][l].rearrange("(kc p) -> p kc", p=128), rw1, writes=[rw1],
                  allow_slow_non_contiguous=True)
            wqk = P.sbuf("wqk", [128, 2], F32, st)
            rwqk = P.nres_("wqk")
            for ci, nm in enumerate(("q_norm_w", "k_norm_w")):
                for m in range(2):
                    P.dma("sp", wqk[m * 64:(m + 1) * 64, ci:ci + 1], self.i[nm][l:l + 1, :].rearrange("o d -> d o"), rwqk,
                          writes=[rwqk], allow_slow_non_contiguous=True)
            P.op("dve", lambda e: e.tensor_scalar(out=wqk[:, 0:1], in0=wqk[:, 0:1], scalar1=0.125, scalar2=None, op0=ALU.mult),
                 reads=[rwqk], writes=[rwqk])
            hT = P.sbuf("hT", [128, 16, 512], BF16, st)
            rhT = P.nres_("hT")
            wbuf = [(P.sbuf("wA", [128, 16, 512], BF16, st), P.nres_("wA%d" % i)) for i in range(3)]
            ut = [(P.sbuf("ut", [128, 4, 512], BF16, st), P.nres_("ut%d" % i)) for i in range(2)]
            vt = [(P.sbuf("vt", [128, 4, 512], BF16, st), P.nres_("vt%d" % i)) for i in range(2)]
            ot = [(P.sbuf("ot", [128, 512], BF16, st), P.nres_("ot%d" % i)) for i in range(3)]
            sq = [(P.sbuf("sq", [128, 512], BF16, st), P.nres_("sq%d" % i)) for i in range(2)]
            rt = [(P.sbuf("rt", [128, 512], F32, st), P.nres_("rt%d" % i)) for i in range(2)]
            bones, rbones = g["bones_b"]
            wsrc = self.wb["w_in"][l]
            nwl = 0
            oi = 0
            for t in range(self.NT):
                t0 = t * 512
                P.dma("sp", xt[:, :, :], x_src[t0:t0 + 512, :].rearrange("(s p) d -> p s d", p=128), rx, writes=[rx])
                self.norm_T(nbufs, xt, rx, w1T, rw1, hT, rhT)
                for c in range(16):
                    wt, rwt = wbuf[nwl % 3]
                    nwl += 1
                    self.load_wtile(wt, rwt, wsrc, 0, 16, c * 512, 512)
                    if c in (0, 1, 6, 7):
                        isu = c < 2
                        stg, rstg = (ut if isu else vt)[c % 2]
                        for s in range(4):
                            ps, rps = self.bank()
                            self.mm_acc(ps[:, :], rps, [(hT[:, kc, s * 128:(s + 1) * 128], wt[:, kc, :]) for kc in range(16)], [rhT, rwt])
                            if s % 2 == 0:
                                P.op("act", lambda e, ps=ps, stg=stg, s=s: e.activation(out=stg[:, s, :], in_=ps[:, :], func=AF.Copy),
                                     reads=[rps], writes=[rstg])
                            else:
                                P.op("dve", lambda e, ps=ps, stg=stg, s=s: e.tensor_copy(out=stg[:, s, :], in_=ps[:, :]),
                                     reads=[rps], writes=[rstg])
                        dst = self.u_d if isu else self.v_d
                        cc = c if isu else c - 6
                        P.dma("pool", dst[t0:t0 + 512, cc * 512:(cc + 1) * 512].rearrange("(s p) n -> p s n", p=128), stg[:, :, :], rstg,
                              reads=[rstg])
                    else:
                        for j in range(4):
                            ps, rps = self.bank()
                            self.mm_acc(ps[:, :], rps, [(wt[:, kc, j * 128:(j + 1) * 128], hT[:, kc, :]) for kc in range(16)], [rhT, rwt])
                            o, ro = ot[oi % 3]
                            oi += 1
                            if c < 6:
                                isq = c < 4
                                sqt, rsq = sq[oi % 2]
                                rtt, rrt = rt[oi % 2]
                                P.op("act", lambda e, ps=ps, sqt=sqt: e.activation(out=sqt[:, :], in_=ps[:, :], func=AF.Square), reads=[rps], writes=[rsq])
                                ps2, rps2 = self.bank()
                                self.mm_acc(ps2[:, :], rps2, [(bones[:, :], sqt[:, :])], [rbones, rsq])
                                P.op("act", lambda e, ps2=ps2, rtt=rtt: e.activation(out=rtt[:, :], in_=ps2[:, :], func=AF.Sqrt, bias=g["eps_rms"][0][:, :], scale=1.0 / 64),
                                     reads=[rps2, g["eps_rms"][1]], writes=[rrt])
                                P.op("dve", lambda e, rtt=rtt: e.reciprocal(out=rtt[:, :], in_=rtt[:, :]), reads=[rrt], writes=[rrt])
                                ci = 0 if isq else 1
                                P.op("dve", lambda e, ps=ps, rtt=rtt, o=o, ci=ci: e.scalar_tensor_tensor(
                                    out=o[:, :], in0=ps[:, :], scalar=wqk[:, ci:ci + 1], in1=rtt[:, :], op0=ALU.mult, op1=ALU.mult),
                                    reads=[rps, rrt, rwqk], writes=[ro])
                                dst = self.qT_d if isq else self.kT_d
                                r0 = ((c - 2) if isq else (c - 4)) * 512 + j * 128
                            else:
                                P.op("act", lambda e, ps=ps, o=o: e.activation(out=o[:, :], in_=ps[:, :], func=AF.Sigmoid), reads=[rps], writes=[ro])
                                dst = self.gsT_d if c < 12 else self.gaT_d
                                r0 = ((c - 8) if c < 12 else (c - 12)) * 512 + j * 128
                            P.dma("pool", dst[r0:r0 + 128, t0:t0 + 512], o[:, :], ro, reads=[ro])
            P.barrier()

    def build(self):
        ph = self.phases
        self.setup_globals()
        if ph is None or "att" in ph:
            self.setup_attn()
        if ph is None or "w" in ph:
            self.phase_w()
        for l in range(self.nlayers):
            x_src = self.i["x"] if l == 0 else self.xb_d
            x_dst = self.out if l == self.nlayers - 1 else self.xb_d
            if ph is None or "a" in ph:
                self.phase_a(l, x_src)
            if ph is None or "xch" in ph:
                self.phase_xch()
            if ph is None or "ssm" in ph:
                self.phase_ssm(l)
            if ph is None or "att" in ph:
                self.phase_att(l)
            if ph is None or "b1" in ph:
                self.phase_b1(l, x_src)
            if ph is None or "b2" in ph:
                self.phase_b2(l, x_dst)
        self.P.barrier()
        self.P.emit()
        return self.nc

    def phase_xch(self):
        P = self.P
        r = P.nres_("xch")
        tr = min(1024, self.L)
        for c in range(self.nch):
            P.coll("AllGather", self.u_d[c * tr:(c + 1) * tr, :], self.ug[c][:, :], self.groups, reads=[r], writes=[r])
            P.coll("AllGather", self.v_d[c * tr:(c + 1) * tr, :], self.vg[c][:, :], self.groups, reads=[r], writes=[r])
            P.coll("AllGather", self.kT_d[c * self.krows:(c + 1) * self.krows, :], self.kTg[c][:, :], self.groups, reads=[r], writes=[r])
        P.barrier()

    def cmul(self, eng, out_re, out_im, a_re, a_im, b_re, b_im, tmp, rres, wres, neg_im=False):
        P = self.P
        t1, t2 = tmp
        ops = [
            (t1, a_re, b_re, ALU.mult), (t2, a_im, b_im, ALU.mult), (out_re, t1, t2, ALU.subtract),
            (t1, a_re, b_im, ALU.mult), (t2, a_im, b_re, ALU.mult), (out_im, t1, t2, ALU.add),
        ]
        for o, x, y, op in ops:
            P.op(eng, lambda e, o=o, x=x, y=y, op=op: e.tensor_tensor(out=o, in0=x, in1=y, op=op), reads=rres, writes=wres)
        if neg_im:
            P.op(eng, lambda e, o=out_im: e.tensor_scalar(out=o, in0=o, scalar1=-1.0, scalar2=None, op0=ALU.mult), reads=wres, writes=wres)

    def phase_ssm(self, l):
        P, g = self.P, self.g
        L = self.L
        NBLK = 2 * L // 8
        NB2 = NBLK // 2
        bp = min(128, NBLK)
        nbt = NBLK // bp
        nbt2 = nbt // 2
        assert nbt % 2 == 0
        flag, rflag = g["c_flag"]
        nsteps = int(math.ceil(math.log2(NBLK)))
        idt, rid = g["c_ident"]
        swp, rswp = g["c_swap"]
        tmask, rtm = g["c_tmask"]
        with contextlib.ExitStack() as st:
            rp = P.nres_("ssm_par")
            def T(name, shape, dt=F32):
                return P.sbuf(name, shape, dt, st)
            lre, lim, dtt = T("lre", [128, 64]), T("lim", [128, 64]), T("dtt", [128, 64])
            for hf in range(2):
                hs = slice(hf * 64, hf * 64 + 64)
                P.dma("sp", lre[hs, :], self.i["lam_re"][l].rearrange("g p -> p g"), rp, writes=[rp], allow_slow_non_contiguous=True)
                P.dma("sp", lim[hs, :], self.i["lam_im"][l].rearrange("g p -> p g"), rp, writes=[rp], allow_slow_non_contiguous=True)
            P.dma("sp", dtt[:, :], self.i["log_step"][l].partition_broadcast(128), rp, writes=[rp])
            Bre, Bim = T("Bre", [128, 64, 16]), T("Bim", [128, 64, 16])
            for hf in range(2):
                hs = slice(hf * 64, hf * 64 + 64)
                P.dma("sp", Bre[hs, :, :], self.i["ssm_b_re"][l].rearrange("g p c -> p g c"), rp, writes=[rp])
                P.dma("sp", Bim[hs, :, :], self.i["ssm_b_im"][l].rearrange("g p c -> p g c"), rp, writes=[rp])
            Dcol = T("Dcol", [128, 64])
            for j in range(8):
                P.dma("sp", Dcol[16 * j:16 * j + 16, :], self.i["ssm_d"][l].rearrange("(g c) -> c g", c=16), rp, writes=[rp],
                      allow_slow_non_contiguous=True)
            Cre, Cim = T("Cre", [128, 64, 16]), T("Cim", [128, 64, 16])
            crow = T("crow", [128, 128])
            for nm, dstt in (("ssm_c_re", Cre), ("ssm_c_im", Cim)):
                src = self.i[nm][l].rearrange("g c p -> (g c) p")
                for k in range(8):
                    P.dma("sp", crow[:, 0:64], src[k * 128:(k + 1) * 128, :], rp, reads=[rp], writes=[rp])
                    P.dma("sp", crow[:, 64:128], src[k * 128:(k + 1) * 128, :], rp, reads=[rp], writes=[rp])
                    ps, rps = self.bank()
                    P.op("pe", lambda e, ps=ps: e.transpose(out=ps[:, 0:128], in_=crow[:, :], identity=idt[:, :]), reads=[rp, rid], writes=[rps])
                    P.op("dve", lambda e, ps=ps, dstt=dstt, k=k: e.tensor_copy(out=dstt[:, k * 8:(k + 1) * 8, :], in_=ps[:, 0:128].rearrange("p (g c) -> p g c", c=16)),
                         reads=[rps], writes=[rp])
            V = lambda fn, rd=(rp,), wr=(rp,): P.op("dve", fn, reads=list(rd), writes=list(wr))
            A = lambda fn: P.op("act", fn, reads=[rp, g["halfpi"][1]], writes=[rp])
            lr, x1, mag, ang, tq, r_, m1 = [T(n, [128, 64]) for n in ("lr", "x1", "mag", "ang", "tq", "r_", "m1")]
            ti = T("ti", [128, 64], I32)
            sn, cs, ar, ai = [T(n, [128, 64]) for n in ("sn", "cs", "ar", "ai")]
            V(lambda e: e.tensor_scalar(out=lr[:, :], in0=lre[:, :], scalar1=-1e-4, scalar2=None, op0=ALU.min))
            A(lambda e: e.activation(out=dtt[:, :], in_=dtt[:, :], func=AF.Exp))
            V(lambda e: e.tensor_tensor(out=x1[:, :], in0=lr[:, :], in1=dtt[:, :], op=ALU.mult))
            A(lambda e: e.activation(out=mag[:, :], in_=x1[:, :], func=AF.Exp))
            V(lambda e: e.tensor_tensor(out=ang[:, :], in0=lim[:, :], in1=dtt[:, :], op=ALU.mult))
            V(lambda e: e.tensor_scalar(out=tq[:, :], in0=ang[:, :], scalar1=1.0 / (2 * math.pi), scalar2=0.5, op0=ALU.mult, op1=ALU.add))
            V(lambda e: e.tensor_copy(out=ti[:, :], in_=tq[:, :]))
            V(lambda e: e.tensor_copy(out=tq[:, :], in_=ti[:, :]))
            V(lambda e: e.scalar_tensor_tensor(out=r_[:, :], in0=tq[:, :], scalar=-2 * math.pi, in1=ang[:, :], op0=ALU.mult, op1=ALU.add))
            for thr, opc, add in ((-math.pi, ALU.is_lt, 2 * math.pi), (math.pi, ALU.is_gt, -2 * math.pi),
                                  (-math.pi, ALU.is_lt, 2 * math.pi), (math.pi, ALU.is_gt, -2 * math.pi)):
                V(lambda e, thr=thr, opc=opc: e.tensor_single_scalar(out=m1[:, :], in_=r_[:, :], scalar=thr, op=opc))
                V(lambda e, add=add: e.scalar_tensor_tensor(out=r_[:, :], in0=m1[:, :], scalar=add, in1=r_[:, :], op0=ALU.mult, op1=ALU.add))
            V(lambda e: e.tensor_scalar(out=r_[:, :], in0=r_[:, :], scalar1=math.pi, scalar2=-math.pi, op0=ALU.min, op1=ALU.max))
            A(lambda e: e.activation(out=sn[:, :], in_=r_[:, :], func=AF.Sin))
            V(lambda e: e.tensor_scalar(out=m1[:, :], in0=r_[:, :], scalar1=-1.0, scalar2=None, op0=ALU.mult))
            V(lambda e: e.tensor_tensor(out=m1[:, :], in0=m1[:, :], in1=r_[:, :], op=ALU.max))
            A(lambda e: e.activation(out=cs[:, :], in_=m1[:, :], func=AF.Sin, bias=g["halfpi"][0][:, :], scale=-1.0))
            V(lambda e: e.tensor_tensor(out=ar[:, :], in0=mag[:, :], in1=cs[:, :], op=ALU.mult))
            V(lambda e: e.tensor_tensor(out=ai[:, :], in0=mag[:, :], in1=sn[:, :], op=ALU.mult))
            den, nr, fr, fi, t1, t2 = [T(n, [128, 64]) for n in ("den", "nr", "fr", "fi", "t1", "t2")]
            V(lambda e: e.tensor_tensor(out=den[:, :], in0=lr[:, :], in1=lr[:, :], op=ALU.mult))
            V(lambda e: e.tensor_tensor(out=t1[:, :], in0=lim[:, :], in1=lim[:, :], op=ALU.mult))
            V(lambda e: e.tensor_tensor(out=den[:, :], in0=den[:, :], in1=t1[:, :], op=ALU.add))
            V(lambda e: e.reciprocal(out=den[:, :], in_=den[:, :]))
            V(lambda e: e.tensor_scalar(out=nr[:, :], in0=ar[:, :], scalar1=-1.0, scalar2=None, op0=ALU.add))
            V(lambda e: e.tensor_tensor(out=t1[:, :], in0=nr[:, :], in1=lr[:, :], op=ALU.mult))
            V(lambda e: e.tensor_tensor(out=t2[:, :], in0=ai[:, :], in1=lim[:, :], op=ALU.mult))
            V(lambda e: e.tensor_tensor(out=t1[:, :], in0=t1[:, :], in1=t2[:, :], op=ALU.add))
            V(lambda e: e.tensor_tensor(out=fr[:, :], in0=t1[:, :], in1=den[:, :], op=ALU.mult))
            V(lambda e: e.tensor_tensor(out=t1[:, :], in0=ai[:, :], in1=lr[:, :], op=ALU.mult))
            V(lambda e: e.tensor_tensor(out=t2[:, :], in0=nr[:, :], in1=lim[:, :], op=ALU.mult))
            V(lambda e: e.tensor_tensor(out=t1[:, :], in0=t1[:, :], in1=t2[:, :], op=ALU.subtract))
            V(lambda e: e.tensor_tensor(out=fi[:, :], in0=t1[:, :], in1=den[:, :], op=ALU.mult))
            Bbr, Bbi = T("Bbr", [128, 64, 16]), T("Bbi", [128, 64, 16])
            tb1, tb2 = T("tb1", [128, 64, 16]), T("tb2", [128, 64, 16])
            bc16 = lambda a: a[:, :].unsqueeze(2).broadcast_to([128, 64, 16])
            self.cmul("dve", Bbr[:, :, :], Bbi[:, :, :], bc16(fr), bc16(fi), Bre[:, :, :], Bim[:, :, :], (tb1[:, :, :], tb2[:, :, :]), [rp], [rp])
            PWr, PWi = T("PWr", [128, 64, 9]), T("PWi", [128, 64, 9])
            PIr, PIi = T("PIr", [128, 64, 8]), T("PIi", [128, 64, 8])
            PRr, PRi = T("PRr", [128, 64, 8]), T("PRi", [128, 64, 8])
            air, aii = T("air", [128, 64]), T("aii", [128, 64])
            V(lambda e: e.tensor_tensor(out=t1[:, :], in0=ar[:, :], in1=ar[:, :], op=ALU.mult))
            V(lambda e: e.tensor_tensor(out=t2[:, :], in0=ai[:, :], in1=ai[:, :], op=ALU.mult))
            V(lambda e: e.tensor_tensor(out=t1[:, :], in0=t1[:, :], in1=t2[:, :], op=ALU.add))
            V(lambda e: e.reciprocal(out=t1[:, :], in_=t1[:, :]))
            V(lambda e: e.tensor_tensor(out=air[:, :], in0=ar[:, :], in1=t1[:, :], op=ALU.mult))
            V(lambda e: e.scalar_tensor_tensor(out=aii[:, :], in0=ai[:, :], scalar=-1.0, in1=t1[:, :], op0=ALU.mult, op1=ALU.mult))
            for (Xr, Xi, br_, bi_, n) in ((PWr, PWi, ar, ai, 9), (PIr, PIi, air, aii, 8)):
                V(lambda e, Xr=Xr: e.memset(Xr[:, :, 0:1], 1.0))
                V(lambda e, Xi=Xi: e.memset(Xi[:, :, 0:1], 0.0))
                for k in range(1, n):
                    self.cmul("dve", Xr[:, :, k], Xi[:, :, k], Xr[:, :, k - 1], Xi[:, :, k - 1], br_[:, :], bi_[:, :], (t1[:, :], t2[:, :]), [rp], [rp])
            for j in range(8):
                V(lambda e, j=j: e.tensor_copy(out=PRr[:, :, j], in_=PWr[:, :, 7 - j]))
                V(lambda e, j=j: e.tensor_copy(out=PRi[:, :, j], in_=PWi[:, :, 7 - j]))
            APr, APi = T("APr", [128, 64, nsteps]), T("APi", [128, 64, nsteps])
            V(lambda e: e.tensor_copy(out=APr[:, :, 0], in_=PWr[:, :, 8]))
            V(lambda e: e.tensor_copy(out=APi[:, :, 0], in_=PWi[:, :, 8]))
            for k in range(1, nsteps):
                self.cmul("dve", APr[:, :, k], APi[:, :, k], APr[:, :, k - 1], APi[:, :, k - 1], APr[:, :, k - 1], APi[:, :, k - 1], (t1[:, :], t2[:, :]), [rp], [rp])
            V(lambda e: e.tensor_scalar(out=APi[64:128, :, :], in0=APi[64:128, :, :], scalar1=-1.0, scalar2=None, op0=ALU.mult))
            GB = 8
            FW = GB * 16
            Bjr, Bji, BEr, BEi = [T(n, [128, GB, 8, 16]) for n in ("Bjr", "Bji", "BEr", "BEi")]
            Crr, Cri = T("Crr", [128, GB, 9, 16]), T("Cri", [128, GB, 9, 16])
            tg1, tg2 = T("tg1", [128, GB, 9, 16]), T("tg2", [128, GB, 9, 16])
            rgen = P.nres_("ssm_gen")
            Z = T("Z", [128, nbt, 8, FW], BF16)
            rZ = P.nres_("ssm_Z")
            Zc = T("Zc", [128, nbt, GB, 8, 16])
            rZc = P.nres_("ssm_Zc")
            U8 = T("U8", [128, GB, NBLK])
            rU8 = P.nres_("ssm_U8")
            LE = [(T("LE", [128, 128]), P.nres_("ssm_LE%d" % i)) for i in range(2)]
            LS = [(T("LS", [128, 128]), P.nres_("ssm_LS%d" % i)) for i in range(2)]
            MK = [(T("MK", [128, 128]), P.nres_("ssm_MK%d" % i)) for i in range(4)]
            Tt = T("Tt", [128, GB, 128])
            rTt = P.nres_("ssm_Tt")
            tmpT = [(T("tmpT", [128, 128]), P.nres_("ssm_tmpT%d" % i)) for i in range(2)]
            W = NBLK + 1
            S = T("S", [128, GB, W])
            rS = [P.nres_("ssm_S%d" % i) for i in range(GB)]
            Y8 = [(T("Y8", [128, NB2]), P.nres_("ssm_Y8%d" % i)) for i in range(2)]
            Yt = T("Yt", [128, nbt2, 8, FW])
            rYt = P.nres_("ssm_Yt")
            P.op("pool", lambda e: e.memset(S[:, :, 0:1], 0.0), writes=rS)
            nmk = 0
            for gb in range(NG // GB):
                g0 = gb * GB
                gs_ = slice(g0, g0 + GB)
                bj = lambda a: a[:, gs_, :].unsqueeze(3).broadcast_to([128, GB, a.shape[2], 16])
                bb = lambda a, n: a[:, gs_, :].unsqueeze(2).broadcast_to([128, GB, n, 16])
                t8 = (tg1[:, :, 0:8, :], tg2[:, :, 0:8, :])
                self.cmul("pool", Bjr[:, :, :, :], Bji[:, :, :, :], bj(PIr), bj(PIi), bb(Bbr, 8), bb(Bbi, 8), t8, [rp], [rgen])
                self.cmul("pool", BEr[:, :, :, :], BEi[:, :, :, :], bj(PRr), bj(PRi), bb(Bbr, 8), bb(Bbi, 8), t8, [rp], [rgen])
                self.cmul("pool", Crr[:, :, :, :], Cri[:, :, :, :], bj(PWr), bj(PWi), bb(Cre, 9), bb(Cim, 9), (tg1[:, :, :, :], tg2[:, :, :, :]), [rp], [rgen], neg_im=True)
                for bt in range(nbt):
                    usrc, b2 = (self.ug[bt], 0) if bt < nbt2 else (self.u_d, bt - nbt2)
                    P.dma("sp", Z[0:bp, bt, :, :], usrc[b2 * bp * 8:(b2 + 1) * bp * 8, g0 * 16:g0 * 16 + FW].rearrange("(b j) f -> b j f", j=8), rZ, writes=[rZ])
                P.op("pool", lambda e: e.tensor_copy(out=Zc[0:bp, :, :, :, :], in_=Z[0:bp, :, :, :].rearrange("p b j (g c) -> p b g j c", c=16)), reads=[rZ], writes=[rZc])
                for gi in range(GB):
                    gg = g0 + gi
                    ps, rps = self.bank()
                    for bt in range(nbt):
                        P.op("pe", lambda e, ps=ps, bt=bt, gi=gi: e.transpose(out=ps[:, bt * bp:(bt + 1) * bp], in_=Zc[0:bp, bt, gi, :, :].rearrange("p j c -> p (j c)"), identity=idt[0:bp, 0:bp]),
                             reads=[rZc, rid], writes=[rps], signal=(bt == nbt - 1))
                    P.op("act", lambda e, ps=ps, gi=gi: e.activation(out=U8[:, gi, 0:NB2], in_=ps[:, 0:NB2], func=AF.Copy, scale=flag[:, 0:1]), reads=[rps, rflag], writes=[rU8])
                    P.op("act", lambda e, ps=ps, gi=gi: e.activation(out=U8[:, gi, NB2:NBLK], in_=ps[:, NB2:NBLK], func=AF.Copy), reads=[rps], writes=[rU8])
                    le, rle = LE[gi % 2]
                    ps, rps = self.bank()
                    P.op("pe", lambda e, ps=ps, gi=gi: e.transpose(out=ps[:, 0:64], in_=BEr[0:64, gi, :, :].rearrange("p j c -> p (j c)"), identity=idt[0:64, 0:64]), reads=[rgen, rid], writes=[rps], signal=False)
                    P.op("pe", lambda e, ps=ps, gi=gi: e.transpose(out=ps[:, 64:128], in_=BEi[0:64, gi, :, :].rearrange("p j c -> p (j c)"), identity=idt[0:64, 0:64]), reads=[rgen, rid], writes=[rps])
                    P.op("dve", lambda e, ps=ps, le=le: e.tensor_copy(out=le[:, :], in_=ps[:, 0:128]), reads=[rps], writes=[rle])
                    ps, rps = self.bank()
                    self.mm_acc(ps[:, 0:NBLK], rps, [(le[:, :], U8[:, gi, :])], [rle, rU8])
                    P.op("act", lambda e, ps=ps, gi=gi: e.activation(out=S[:, gi, 1:W], in_=ps[:, 0:NBLK], func=AF.Copy), reads=[rps], writes=[rS[gi]])
                    ps, rps = self.bank()
                    self.mm_acc(ps[:, 0:128], rps, [(Bjr[0:64, gi, :, :].rearrange("p j c -> p (j c)"), Crr[0:64, gi, 0:8, :].rearrange("p j c -> p (j c)")),
                                                    (Bji[0:64, gi, :, :].rearrange("p j c -> p (j c)"), Cri[0:64, gi, 0:8, :].rearrange("p j c -> p (j c)"))], [rgen])
                    tt_, rtt_ = tmpT[gi % 2]
                    P.op("dve", lambda e, ps=ps, tt_=tt_: e.tensor_tensor(out=tt_[:, :], in0=ps[:, 0:128], in1=tmask[:, :], op=ALU.mult), reads=[rps, rtm], writes=[rtt_])
                    P.op("dve", lambda e, tt_=tt_, gi=gi, gg=gg: e.scalar_tensor_tensor(out=Tt[:, gi, :], in0=idt[:, :], scalar=Dcol[:, gg:gg + 1], in1=tt_[:, :], op0=ALU.mult, op1=ALU.add),
                         reads=[rtt_, rid, rp], writes=[rTt])
                for k in range(nsteps):
                    sh = 1 << k
                    n = NBLK - sh
                    for gi in range(GB):
                        gg = g0 + gi
                        mk, rmk = MK[nmk % 4]
                        nmk += 1
                        P.op("pool", lambda e, mk=mk, gg=gg, k=k: e.tensor_scalar(out=mk[:, :], in0=idt[:, :], scalar1=APr[:, gg, k:k + 1], scalar2=None, op0=ALU.mult),
                             reads=[rid, rp], writes=[rmk])
                        P.op("dve", lambda e, mk=mk, gg=gg, k=k: e.scalar_tensor_tensor(out=mk[:, :], in0=swp[:, :], scalar=APi[:, gg, k:k + 1], in1=mk[:, :], op0=ALU.mult, op1=ALU.add),
                             reads=[rswp, rp, rmk], writes=[rmk])
                        ps, rps = self.bank()
                        self.mm_acc(ps[:, 0:n], rps, [(mk[:, :], S[:, gi, 1:1 + n])], [rmk, rS[gi]])
                        P.op("dve", lambda e, ps=ps, gi=gi, sh=sh, n=n: e.tensor_tensor(out=S[:, gi, 1 + sh:1 + sh + n], in0=ps[:, 0:n], in1=S[:, gi, 1 + sh:1 + sh + n], op=ALU.add),
                             reads=[rps], writes=[rS[gi]])
                for gi in range(GB):
                    ls, rls = LS[gi % 2]
                    P.op("pool", lambda e, ls=ls, gi=gi: e.tensor_copy(out=ls[0:64, :], in_=Crr[0:64, gi, 1:9, :].rearrange("p j c -> p (j c)")), reads=[rgen], writes=[rls])
                    P.op("pool", lambda e, ls=ls, gi=gi: e.tensor_copy(out=ls[64:128, :], in_=Cri[64:128, gi, 1:9, :].rearrange("p j c -> p (j c)")), reads=[rgen], writes=[rls])
                    ps, rps = self.bank()
                    self.mm_acc(ps[:, 0:NB2], rps, [(Tt[:, gi, :], U8[:, gi, NB2:NBLK]), (ls[:, :], S[:, gi, NB2:NBLK])], [rTt, rU8, rls, rS[gi]])
                    y8, ry8 = Y8[gi % 2]
                    P.op("act", lambda e, ps=ps, y8=y8: e.activation(out=y8[:, :], in_=ps[:, 0:NB2], func=AF.Copy), reads=[rps], writes=[ry8])
                    ps, rps = self.bank()
                    for bt in range(nbt2):
                        P.op("pe", lambda e, ps=ps, bt=bt, y8=y8: e.transpose(out=ps[0:bp, bt * 128:(bt + 1) * 128], in_=y8[:, bt * bp:(bt + 1) * bp], identity=idt[:, :]),
                             reads=[ry8, rid], writes=[rps], signal=(bt == nbt2 - 1))
                    P.op("dve", lambda e, ps=ps, gi=gi: e.tensor_copy(out=Yt[0:bp, :, :, gi * 16:(gi + 1) * 16], in_=ps[0:bp, 0:nbt2 * 128].rearrange("p (b j c) -> p b j c", b=nbt2, j=8)),
                         reads=[rps], writes=[rYt])
                for bt in range(nbt2):
                    P.dma("pool", self.y_d[bt * bp * 8:(bt + 1) * bp * 8, g0 * 16:g0 * 16 + FW].rearrange("(b j) f -> b j f", j=8), Yt[0:bp, bt, :, :], rYt, reads=[rYt])
            P.barrier()

    def setup_attn(self):
        P, g = self.P, self.g
        reld, rreld = g["c_reld"]
        tab = P.sbuf("tab", [128, 256], F32)
        rtab = P.nres_("tab")
        P.dma("sp", tab[:, :], self.i["rel_bias"].rearrange("b h -> (b h)").partition_broadcast(128), rtab, writes=[rtab])
        steps = [(-90, 15, 14), (-63, 14, 13), (-45, 13, 12), (-31, 12, 11), (-22, 11, 10), (-15, 10, 9), (-11, 9, 8)]
        steps += [(-n, n + 1, n) for n in range(7, -1, -1)]
        steps += [(1, 0, 17)] + [(n, 15 + n, 16 + n) for n in range(2, 8)]
        steps += [(8, 23, 24), (12, 24, 25), (16, 25, 26), (23, 26, 27), (32, 27, 28), (46, 28, 29), (64, 29, 30), (91, 30, 31)]
        ns = len(steps)
        dl = P.sbuf("dl", [128, ns, 8], F32)
        rdl = P.nres_("dl")
        for s, (thr, fb, tb) in enumerate(steps):
            P.op("dve", lambda e, s=s, fb=fb, tb=tb: e.tensor_tensor(out=dl[:, s, :], in0=tab[:, tb * 8:tb * 8 + 8], in1=tab[:, fb * 8:fb * 8 + 8], op=ALU.subtract),
                 reads=[rtab], writes=[rdl])
        bias = P.sbuf("biasT", [128, NH, 2, 128], F32)
        rbias = P.nres_("biasT")
        mk = P.sbuf("mk", [128, 128], F32)
        rmk = P.nres_("mk")
        for kind in range(2):
            off = -128.0 * kind
            for h in range(NH):
                P.op("dve", lambda e, h=h, kind=kind: e.tensor_scalar(out=bias[:, h, kind, :], in0=reld[:, :], scalar1=0.0, scalar2=tab[:, 15 * 8 + h:15 * 8 + h + 1], op0=ALU.mult, op1=ALU.add),
                     reads=[rreld, rtab], writes=[rbias])
            for s, (thr, fb, tb) in enumerate(steps):
                if kind == 1 and thr > -1:
                    continue
                if thr > 64:
                    continue
                P.op("dve", lambda e, thr=thr, off=off: e.tensor_single_scalar(out=mk[:, :], in_=reld[:, :], scalar=float(thr) - off, op=ALU.is_ge), reads=[rreld], writes=[rmk])
                for h in range(NH):
                    P.op("dve", lambda e, h=h, kind=kind, s=s: e.scalar_tensor_tensor(out=bias[:, h, kind, :], in0=mk[:, :], scalar=dl[:, s, h:h + 1], in1=bias[:, h, kind, :], op0=ALU.mult, op1=ALU.add),
                         reads=[rmk, rdl], writes=[rbias])
        for h in range(NH):
            P.op("pool", lambda e, h=h: e.memset(bias[64:128, h, 0, 0:64], -30000.0), reads=[rbias], writes=[rbias])
        g["tab"] = (tab, rtab)
        g["biasT"] = (bias, rbias)

    def bank2(self, lo, n, key):
        c = self.bctr.get(key, 0)
        self.bctr[key] = c + 1
        return self.psb[lo + c % n]

    def phase_att(self, l):
        P, g = self.P, self.g
        L = self.L
        nq = L // 128
        lam_init = 0.8 - 0.6 * math.exp(-0.3 * l)
        idt, rid = g["c_ident"]
        tab, rtab = g["tab"]
        bias, rbias = g["biasT"]
        self.bctr = {}
        with contextlib.ExitStack() as st:
            T = lambda name, shape, dt=F32: P.sbuf(name, shape, dt, st)
            rpar = P.nres_("att_par")
            lq = T("lq", [128, 4, 64])
            for ci, nm in enumerate(("lambda_q1", "lambda_k1", "lambda_q2", "lambda_k2")):
                P.dma("sp", lq[:, ci, :], self.i[nm][l].partition_broadcast(128), rpar, writes=[rpar])
            subw = T("subw", [128, 128])
            P.dma("sp", subw[:, :], self.i["subln_w"][l].partition_broadcast(128), rpar, writes=[rpar])
            e12 = T("e12", [128, 2])
            nlam = T("nlam", [128, 1])
            V = lambda fn: P.op("dve", fn, reads=[rpar], writes=[rpar])
            V(lambda e: e.tensor_tensor(out=lq[:, 0, :], in0=lq[:, 0, :], in1=lq[:, 1, :], op=ALU.mult))
            V(lambda e: e.tensor_tensor(out=lq[:, 2, :], in0=lq[:, 2, :], in1=lq[:, 3, :], op=ALU.mult))
            V(lambda e: e.reduce_sum(out=e12[:, 0:1], in_=lq[:, 0, :], axis=AX.X))
            V(lambda e: e.reduce_sum(out=e12[:, 1:2], in_=lq[:, 2, :], axis=AX.X))
            P.op("act", lambda e: e.activation(out=e12[:, :], in_=e12[:, :], func=AF.Exp), reads=[rpar], writes=[rpar])
            V(lambda e: e.tensor_tensor(out=nlam[:, :], in0=e12[:, 1:2], in1=e12[:, 0:1], op=ALU.subtract))
            V(lambda e: e.tensor_scalar(out=nlam[:, :], in0=nlam[:, :], scalar1=-lam_init, scalar2=None, op0=ALU.add))
            V(lambda e: e.tensor_scalar(out=subw[:, :], in0=subw[:, :], scalar1=1.0 - lam_init, scalar2=None, op0=ALU.mult))
            nb = nq
            NKP = nq
            mflag = g["mflag"][0]
            rflag = g["mflag"][1]
            cbp = T("cbp", [128, 8])
            P.op("dve", lambda e: e.tensor_scalar(out=cbp[:, :], in0=tab[:, 120:128], scalar1=mflag[:, 0:1], scalar2=None, op0=ALU.add), reads=[rtab, rflag], writes=[rpar])
            KT = [(T("KT", [128, 2 * L], BF16), P.nres_("att_KT%d" % i)) for i in range(2)]
            QT = [[(T("QT", [128, L], BF16), P.nres_("att_QT%d_%d" % (i, m))) for m in range(2)] for i in range(2)]
            V1 = [(T("V1", [128, 2 * nb, 132], BF16), P.nres_("att_V1%d" % i)) for i in range(2)]
            for v1, rv1 in V1:
                P.op("pool", lambda e, v1=v1: e.memset(v1[:, :, 128:132], 0.0), writes=[rv1])
                P.op("pool", lambda e, v1=v1: e.memset(v1[:, :, 128:129], 1.0), writes=[rv1])
            for i in range(2):
                for m in range(2):
                    qz, rqz = QT[i][m]
                    P.op("pool", lambda e, qz=qz: e.memset(qz[:, :], 0.0), writes=[rqz])
            PT = [(T("PT", [128, 4, 128], BF16), P.nres_("att_PT%d" % i)) for i in range(3)]
            tS = [(T("tS", [128, 128]), P.nres_("att_tS%d" % i)) for i in range(3)]
            rc = [(T("rc", [128, 2]), P.nres_("att_rc%d" % i)) for i in range(2)]
            oT = [(T("oT", [128, 128]), P.nres_("att_oT%d" % i)) for i in range(2)]
            on = [(T("on", [128, 128]), P.nres_("att_on%d" % i)) for i in range(2)]
            sj = [(T("sj", [128, 128], BF16), P.nres_("att_sj%d" % i)) for i in range(2)]
            ssq = [(T("ssq", [128, 1]), P.nres_("att_ssq%d" % i)) for i in range(2)]
            yo = [(T("yo", [128, 512], BF16), P.nres_("att_yo%d" % i)) for i in range(2)]
            npt = 0
            nts = 0
            import os as _os
            STOP = int(_os.environ.get("ATT_STOP", "4"))
            for h in range(NH if STOP > 0 else 0):
                (kt, rkt), (v1, rv1) = KT[h % 2], V1[h % 2]
                qts = QT[h % 2]
                hpc = self.krows // 128
                P.dma("sp", kt[:, 0:L], self.kTg[h // hpc][(h % hpc) * 128:(h % hpc + 1) * 128, :], rkt, writes=[rkt])
                P.dma("sp", kt[:, L:2 * L], self.kT_d[h * 128:(h + 1) * 128, :], rkt, writes=[rkt])
                for m in range(2):
                    P.dma("sp", qts[m][0][m * 64:(m + 1) * 64, :], self.qT_d[h * 128 + m * 64:h * 128 + (m + 1) * 64, :], qts[m][1], writes=[qts[m][1]])
                for c in range(self.nch):
                    nbc = nb // self.nch
                    P.dma("sp", v1[:, c * nbc:(c + 1) * nbc, 0:128], self.vg[c][0:nbc * 128, h * 128:(h + 1) * 128].rearrange("(j p) e -> p j e", p=128), rv1, writes=[rv1])
                P.dma("sp", v1[:, nb:2 * nb, 0:128], self.v_d[:, h * 128:(h + 1) * 128].rearrange("(j p) e -> p j e", p=128), rv1, writes=[rv1])
                cb = tab[:, 15 * 8 + h:15 * 8 + h + 1]
                cbprev = cbp[:, h:h + 1]
                tps = None
                pending = []

                def emit_pv(o_ps, ro, pt, rpt, grp, i, v1=None, rv1=None):
                    for idx, j in enumerate(grp):
                        P.op("pe", lambda e, o_ps=o_ps, pt=pt, idx=idx, j=j, i=i, v1=v1: e.matmul(
                            o_ps[:, 0:130], lhsT=pt[:, idx, :], rhs=v1[:, j, 0:130], start=(j == 0), stop=(j == NKP + i)),
                            reads=[rpt, rv1], writes=[ro], signal=(j == NKP + i))

                def emit_fin(i, ops_, tps_box):
                    (o0, ro0), (o1, ro1) = ops_
                    rct, rrc = rc[i % 2]
                    ot_, rot = oT[i % 2]
                    on_, ron = on[i % 2]
                    sj_, rsj = sj[i % 2]
                    sq_, rsq = ssq[i % 2]
                    P.op("dve", lambda e, rct=rct, o0=o0: e.reciprocal(out=rct[:, 0:1], in_=o0[:, 128:129]), reads=[ro0], writes=[rrc])
                    P.op("dve", lambda e, rct=rct, o1=o1: e.reciprocal(out=rct[:, 1:2], in_=o1[:, 128:129]), reads=[ro1], writes=[rrc])
                    P.op("dve", lambda e, rct=rct: e.tensor_tensor(out=rct[:, 1:2], in0=rct[:, 1:2], in1=nlam[:, :], op=ALU.mult), reads=[rrc, rpar], writes=[rrc])
                    P.op("dve", lambda e, rct=rct, o0=o0, ot_=ot_: e.tensor_scalar(out=ot_[:, :], in0=o0[:, 0:128], scalar1=rct[:, 0:1], scalar2=None, op0=ALU.mult),
                         reads=[ro0, rrc], writes=[rot])
                    P.op("dve", lambda e, rct=rct, o1=o1, ot_=ot_: e.scalar_tensor_tensor(out=ot_[:, :], in0=o1[:, 0:128], scalar=rct[:, 1:2], in1=ot_[:, :], op0=ALU.mult, op1=ALU.add),
                         reads=[ro1, rrc, rot], writes=[rot])
                    P.op("act", lambda e, ot_=ot_, sj_=sj_, sq_=sq_: e.activation(out=sj_[:, :], in_=ot_[:, :], func=AF.Square, accum_out=sq_[:, :]),
                         reads=[rot], writes=[rsj, rsq])
                    P.op("act", lambda e, sq_=sq_: e.activation(out=sq_[:, :], in_=sq_[:, :], func=AF.Sqrt, bias=g["eps_sub"][0][:, :], scale=1.0 / 128),
                         reads=[rsq, g["eps_sub"][1]], writes=[rsq])
                    P.op("dve", lambda e, sq_=sq_: e.reciprocal(out=sq_[:, :], in_=sq_[:, :]), reads=[rsq], writes=[rsq])
                    P.op("dve", lambda e, ot_=ot_, sq_=sq_, on_=on_: e.scalar_tensor_tensor(out=on_[:, :], in0=ot_[:, :], scalar=sq_[:, 0:1], in1=subw[:, :], op0=ALU.mult, op1=ALU.mult),
                         reads=[rot, rsq, rpar], writes=[ron])
                    tps, rtps = self.psb[7]
                    P.op("pe", lambda e, tps=tps, on_=on_, i=i: e.transpose(out=tps[:, (i % 4) * 128:(i % 4 + 1) * 128], in_=on_[:, :], identity=idt[:, :]),
                         reads=[ron, rid], writes=[rtps])
                    if i % 4 == 3 or i == nq - 1:
                        nblk = i % 4 + 1
                        y_, ry = yo[(i // 4) % 2]
                        P.op("act", lambda e, tps=tps, y_=y_, nblk=nblk: e.activation(out=y_[:, 0:nblk * 128], in_=tps[:, 0:nblk * 128], func=AF.Copy), reads=[rtps], writes=[ry])
                        q0 = (i - nblk + 1) * 128
                        P.dma("pool", self.yaT_d[h * 128:(h + 1) * 128, q0:q0 + nblk * 128], y_[:, 0:nblk * 128], ry, reads=[ry])

                def flush():
                    while pending:
                        fn = pending.pop(0)
                        fn()

                for i in range(nq if STOP > 1 else 0):
                    ops_ = []
                    for m in range(2):
                        qt, rqt = qts[m]
                        o_ps, ro = self.bank2(0, 4, "O")
                        ops_.append((o_ps, ro))
                        gi_ = NKP + i
                        for j0 in range(0, gi_ + 1, 4):
                            grp = list(range(j0, min(j0 + 4, gi_ + 1)))
                            isprev = j0 < NKP
                            s_ps, rs = self.bank2(4, 3, "S")
                            for idx, j in enumerate(grp):
                                P.op("pe", lambda e, s_ps=s_ps, idx=idx, j=j, i=i, kt=kt, qt=qt: e.matmul(
                                    s_ps[:, idx * 128:(idx + 1) * 128], lhsT=kt[:, j * 128:(j + 1) * 128],
                                    rhs=qt[:, i * 128:(i + 1) * 128], start=True, stop=True),
                                    reads=[rkt, rqt], writes=[rs], signal=(idx == len(grp) - 1))
                            pt, rpt = PT[npt % 3]
                            npt += 1
                            nfar = len([j for j in grp if j <= gi_ - 2])
                            if nfar:
                                fb_ = cbprev if isprev else cb
                                P.op("act", lambda e, s_ps=s_ps, pt=pt, nfar=nfar, fb_=fb_: e.activation(
                                    out=pt[:, 0:nfar, :], in_=s_ps[:, 0:nfar * 128].rearrange("p (a b) -> p a b", a=nfar), func=AF.Exp, bias=fb_),
                                    reads=[rs, rtab, rpar], writes=[rpt])
                            for idx, j in enumerate(grp):
                                if j <= gi_ - 2:
                                    continue
                                kind = 0 if j == gi_ else 1
                                ts_, rts = tS[nts % 3]
                                nts += 1
                                if j < NKP:
                                    P.op("dve", lambda e, s_ps=s_ps, idx=idx, ts_=ts_, kind=kind, h=h: e.scalar_tensor_tensor(
                                        out=ts_[:, :], in0=s_ps[:, idx * 128:(idx + 1) * 128], scalar=mflag[:, 0:1], in1=bias[:, h, kind, :], op0=ALU.add, op1=ALU.add),
                                        reads=[rs, rbias, rflag], writes=[rts])
                                else:
                                    P.op("dve", lambda e, s_ps=s_ps, idx=idx, ts_=ts_, kind=kind, h=h: e.tensor_tensor(
                                        out=ts_[:, :], in0=s_ps[:, idx * 128:(idx + 1) * 128], in1=bias[:, h, kind, :], op=ALU.add),
                                        reads=[rs, rbias], writes=[rts])
                                P.op("act", lambda e, ts_=ts_, pt=pt, idx=idx: e.activation(out=pt[:, idx, :], in_=ts_[:, :], func=AF.Exp),
                                     reads=[rts], writes=[rpt])
                            flush()
                            pending.append(lambda o_ps=o_ps, ro=ro, pt=pt, rpt=rpt, grp=grp, i=i: emit_pv(o_ps, ro, pt, rpt, grp, i, v1, rv1))
                    pending.append(lambda i=i, ops_=ops_: emit_fin(i, ops_, None))
                flush()
            P.barrier()

    def phase_b1(self, l, x_src):
        P, g = self.P, self.g
        idt, rid = g["c_ident"]
        with contextlib.ExitStack() as st:
            T = lambda name, shape, dt=F32: P.sbuf(name, shape, dt, st)
            xt, rx = T("xt", [128, 4, D]), P.nres_("xt")
            ytoks = [(T("ytok", [128, SSMW]), P.nres_("b1_ytok%d" % i)) for i in range(2)]
            tmp = [(T("gt", [128, SSMW]), P.nres_("b1_gt%d" % i)) for i in range(2)]
            yg = [(T("yg", [128, SSMW]), P.nres_("b1_yg%d" % i)) for i in range(2)]
            ygf, rygf = T("ygf", [128, 8, 512]), P.nres_("b1_ygf")
            ygb, rygb = T("ygb", [128, 8, 512], BF16), P.nres_("b1_ygb")
            yss, ryss = T("yss", [128, 8, 512], BF16), P.nres_("b1_yss")
            yat, ryat = T("yat", [128, 8, 512], BF16), P.nres_("b1_yat")
            gsTs = [(T("gsT", [128, 4, 512], BF16), P.nres_("b1_gs%d" % i)) for i in range(2)]
            gaTs = [(T("gaT", [128, 4, 512], BF16), P.nres_("b1_ga%d" % i)) for i in range(2)]
            mT, rmT = T("mT", [128, 16, 512], BF16), P.nres_("b1_mT")
            sgt = [(T("sgt", [128, 512]), P.nres_("b1_sgt%d" % i)) for i in range(2)]
            m1 = [(T("m1", [128, 512]), P.nres_("b1_m1%d" % i)) for i in range(2)]
            m2 = [(T("m2", [128, 512]), P.nres_("b1_m2%d" % i)) for i in range(2)]
            wbuf = [(T("wB", [128, 16, 512], BF16), P.nres_("wA%d" % i)) for i in range(2)]
            wglu, rwglu = T("wglu", [128, 8, SSMW], BF16), P.nres_("b1_wglu")
            bglu, rbglu = T("bglu", [128, 8]), P.nres_("b1_bglu")
            self.load_wtile(wglu, rwglu, self.wb["w_glu"][l], 0, 8, 0, SSMW)
            P.dma("sp", bglu[:, :], self.i["b_glu"][l].rearrange("(c p) -> p c", p=128), rbglu, writes=[rbglu], allow_slow_non_contiguous=True)
            nw = 0
            for t in range(self.NT):
                t0 = t * 512
                P.dma("sp", xt[:, :, :], x_src[t0:t0 + 512, :].rearrange("(s p) d -> p s d", p=128), rx, writes=[rx])
                P.dma("sp", yat[:, :, :], self.yaT_d.rearrange("(fc p) t -> p fc t", p=128)[:, :, t0:t0 + 512], ryat, writes=[ryat])
                for s in range(4):
                    (tm, rtm), (ygs, rygs) = tmp[s % 2], yg[s % 2]
                    ytk, rytok = ytoks[s % 2]
                    P.dma("sp", ytk[:, :], self.y_d[t0 + s * 128:t0 + (s + 1) * 128, :], rytok, writes=[rytok])
                    P.op("pool", lambda e, tm=tm, ytk=ytk: e.tensor_tensor(out=tm[:, :], in0=ytk[:, :], in1=ytk[:, :], op=ALU.mult), reads=[rytok], writes=[rtm])
                    P.op("dve", lambda e, tm=tm: e.tensor_scalar(out=tm[:, :], in0=tm[:, :], scalar1=0.044715, scalar2=1.0, op0=ALU.mult, op1=ALU.add), reads=[rtm], writes=[rtm])
                    P.op("pool", lambda e, tm=tm, ytk=ytk: e.tensor_tensor(out=tm[:, :], in0=tm[:, :], in1=ytk[:, :], op=ALU.mult), reads=[rtm, rytok], writes=[rtm])
                    P.op("act", lambda e, tm=tm: e.activation(out=tm[:, :], in_=tm[:, :], func=AF.Sigmoid, scale=2.0 * math.sqrt(2.0 / math.pi)), reads=[rtm], writes=[rtm])
                    P.op("dve", lambda e, tm=tm, ygs=ygs, ytk=ytk: e.tensor_tensor(out=ygs[:, :], in0=tm[:, :], in1=ytk[:, :], op=ALU.mult), reads=[rtm, rytok], writes=[rygs])
                    for k4 in range(2):
                        ps, rps = self.bank()
                        for j in range(4):
                            fc = k4 * 4 + j
                            P.op("pe", lambda e, ps=ps, ygs=ygs, fc=fc, j=j: e.transpose(out=ps[:, j * 128:(j + 1) * 128], in_=ygs[:, fc * 128:(fc + 1) * 128], identity=idt[:, :]),
                                 reads=[rygs, rid], writes=[rps], signal=(j == 3))
                        P.op("act", lambda e, ps=ps, k4=k4, s=s: e.activation(out=ygf[:, k4 * 4:k4 * 4 + 4, s * 128:(s + 1) * 128], in_=ps[:, :].rearrange("p (k t) -> p k t", k=4), func=AF.Copy),
                             reads=[rps], writes=[rygf])
                        P.op("dve", lambda e, ps=ps, k4=k4, s=s: e.tensor_copy(out=ygb[:, k4 * 4:k4 * 4 + 4, s * 128:(s + 1) * 128], in_=ps[:, :].rearrange("p (k t) -> p k t", k=4)),
                             reads=[rps], writes=[rygb])
                for cb in range(8):
                    ps, rps = self.bank()
                    self.mm_acc(ps[:, :], rps, [(wglu[:, kc, cb * 128:(cb + 1) * 128], ygb[:, kc, :]) for kc in range(8)], [rwglu, rygb])
                    sg, rsg = sgt[cb % 2]
                    P.op("act", lambda e, ps=ps, sg=sg, cb=cb: e.activation(out=sg[:, :], in_=ps[:, :], func=AF.Sigmoid, bias=bglu[:, cb:cb + 1]), reads=[rps, rbglu], writes=[rsg])
                    P.op("dve", lambda e, sg=sg, cb=cb: e.tensor_tensor(out=yss[:, cb, :], in0=ygf[:, cb, :], in1=sg[:, :], op=ALU.mult), reads=[rsg, rygf], writes=[ryss])
                for c4 in range(4):
                    (ws, rws), (wa, rwa) = wbuf[0], wbuf[1]
                    self.load_wtile(ws, rws, self.wb["w_proj_ssm"][l], 0, 8, c4 * 512, 512)
                    self.load_wtile(wa, rwa, self.wb["w_proj_attn"][l], 0, 8, c4 * 512, 512)
                    (gsT, rgs), (gaT, rga) = gsTs[c4 % 2], gaTs[c4 % 2]
                    P.dma("sp", gsT[:, :, :], self.gsT_d[c4 * 512:(c4 + 1) * 512, t0:t0 + 512].rearrange("(fc p) t -> p fc t", p=128), rgs, writes=[rgs])
                    P.dma("sp", gaT[:, :, :], self.gaT_d[c4 * 512:(c4 + 1) * 512, t0:t0 + 512].rearrange("(fc p) t -> p fc t", p=128), rga, writes=[rga])
                    for j in range(4):
                        cb = c4 * 4 + j
                        ps1, rps1 = self.bank()
                        self.mm_acc(ps1[:, :], rps1, [(ws[:, kc, j * 128:(j + 1) * 128], yss[:, kc, :]) for kc in range(8)], [rws, ryss])
                        ps2, rps2 = self.bank()
                        self.mm_acc(ps2[:, :], rps2, [(wa[:, kc, j * 128:(j + 1) * 128], yat[:, kc, :]) for kc in range(8)], [rwa, ryat])
                        (a1, ra1), (a2, ra2) = m1[cb % 2], m2[cb % 2]
                        P.op("dve", lambda e, ps1=ps1, a1=a1, j=j, gsT=gsT: e.tensor_tensor(out=a1[:, :], in0=ps1[:, :], in1=gsT[:, j, :], op=ALU.mult), reads=[rps1, rgs], writes=[ra1])
                        P.op("dve", lambda e, ps2=ps2, a2=a2, j=j, gaT=gaT: e.tensor_tensor(out=a2[:, :], in0=ps2[:, :], in1=gaT[:, j, :], op=ALU.mult), reads=[rps2, rga], writes=[ra2])
                        P.op("pool", lambda e, a1=a1, a2=a2, cb=cb: e.tensor_tensor(out=mT[:, cb, :], in0=a1[:, :], in1=a2[:, :], op=ALU.add), reads=[ra1, ra2], writes=[rmT])
                for nch in range(4):
                    wt, rwt = wbuf[nch % 2]
                    self.load_wtile(wt, rwt, self.wb["w_out"][l], 0, 16, nch * 512, 512)
                    for s in range(4):
                        ps, rps = self.bank()
                        self.mm_acc(ps[:, :], rps, [(mT[:, kc, s * 128:(s + 1) * 128], wt[:, kc, :]) for kc in range(16)], [rmT, rwt])
                        P.op("dve", lambda e, ps=ps, s=s, nch=nch: e.tensor_tensor(out=xt[:, s, nch * 512:(nch + 1) * 512], in0=ps[:, :], in1=xt[:, s, nch * 512:(nch + 1) * 512], op=ALU.add),
                             reads=[rps, rx], writes=[rx])
                P.dma("pool", self.xa_d[t0:t0 + 512, :].rearrange("(s p) d -> p s d", p=128), xt[:, :, :], rx, reads=[rx])
            P.barrier()

    def phase_b2(self, l, x_dst):
        P, g = self.P, self.g
        with contextlib.ExitStack() as st:
            T = lambda name, shape, dt=F32: P.sbuf(name, shape, dt, st)
            xt, rx = T("xt", [128, 4, D]), P.nres_("xt")
            nbufs = self.norm_bufs(st)
            w2T, rw2 = T("w2T", [128, 16]), P.nres_("w1T")
            P.dma("sp", w2T[:, :], self.i["norm2_w"][l].rearrange("(kc p) -> p kc", p=128), rw2, writes=[rw2], allow_slow_non_contiguous=True)
            hT, rhT = T("hT", [128, 16, 512], BF16), P.nres_("hT")
            aT, raT = T("aT", [128, 44, 512], BF16), P.nres_("b2_aT")
            wbuf = [(T("wF", [128, 22, 512], BF16), P.nres_("wA%d" % i)) for i in range(3)]
            sgt = [(T("sgt", [128, 512]), P.nres_("b1_sgt%d" % i)) for i in range(2)]
            nw = 0
            for t in range(self.NT):
                t0 = t * 512
                P.dma("sp", xt[:, :, :], self.xa_d[t0:t0 + 512, :].rearrange("(s p) d -> p s d", p=128), rx, writes=[rx])
                self.norm_T(nbufs, xt, rx, w2T, rw2, hT, rhT)
                for c in range(DFF // 512):
                    (wg, rwg) = wbuf[nw % 3]
                    (wu, rwu) = wbuf[(nw + 1) % 3]
                    nw += 2
                    self.load_wtile(wg, rwg, self.wb["w_ffn_gate"][l], 0, 16, c * 512, 512)
                    self.load_wtile(wu, rwu, self.wb["w_ffn_up"][l], 0, 16, c * 512, 512)
                    for j in range(4):
                        fb = c * 4 + j
                        psg, rpsg = self.bank()
                        self.mm_acc(psg[:, :], rpsg, [(wg[:, kc, j * 128:(j + 1) * 128], hT[:, kc, :]) for kc in range(16)], [rwg, rhT])
                        psu, rpsu = self.bank()
                        self.mm_acc(psu[:, :], rpsu, [(wu[:, kc, j * 128:(j + 1) * 128], hT[:, kc, :]) for kc in range(16)], [rwu, rhT])
                        sg, rsg = sgt[fb % 2]
                        P.op("act", lambda e, psg=psg, sg=sg: e.activation(out=sg[:, :], in_=psg[:, :], func=AF.Silu), reads=[rpsg], writes=[rsg])
                        P.op("dve", lambda e, psu=psu, sg=sg, fb=fb: e.tensor_tensor(out=aT[:, fb, :], in0=psu[:, :], in1=sg[:, :], op=ALU.mult), reads=[rpsu, rsg], writes=[raT])
                for nch in range(4):
                    (w0, rw0) = wbuf[nw % 3]
                    (w1, rw1) = wbuf[(nw + 1) % 3]
                    nw += 2
                    self.load_wtile(w0, rw0, self.wb["w_ffn_down"][l], 0, 22, nch * 512, 512)
                    self.load_wtile(w1, rw1, self.wb["w_ffn_down"][l], 22, 22, nch * 512, 512)
                    for s in range(4):
                        ps, rps = self.bank()
                        pairs = [(aT[:, kc, s * 128:(s + 1) * 128], w0[:, kc, :]) for kc in range(22)]
                        pairs += [(aT[:, 22 + kc, s * 128:(s + 1) * 128], w1[:, kc, :]) for kc in range(22)]
                        self.mm_acc(ps[:, :], rps, pairs, [raT, rw0, rw1])
                        P.op("dve", lambda e, ps=ps, s=s, nch=nch: e.tensor_tensor(out=xt[:, s, nch * 512:(nch + 1) * 512], in0=ps[:, :], in1=xt[:, s, nch * 512:(nch + 1) * 512], op=ALU.add),
                             reads=[rps, rx], writes=[rx])
                P.dma("pool", x_dst[t0:t0 + 512, :].rearrange("(s p) d -> p s d", p=128), xt[:, :, :], rx, reads=[rx])
            P.barrier()


_CACHE = {}


def kernel(**inputs):
    x = np.ascontiguousarray(np.asarray(inputs["x"], dtype=np.float32))
    B, L, _ = x.shape
    Lh = L // 2
    key = (Lh,)
    if key not in _CACHE:
        _CACHE[key] = Builder(Lh).build()
    nc = _CACHE[key]
    base = {k: np.ascontiguousarray(np.asarray(v, dtype=np.float32)) for k, v in inputs.items() if k != "x"}
    base.update(host_consts())
    in_maps = []
    for b in range(B):
        for half in range(2):
            m = dict(base)
            m["x"] = np.ascontiguousarray(x[b, half * Lh:(half + 1) * Lh])
            m["c_flag"] = np.full((128, 1), float(half), np.float32)
            in_maps.append(m)
    res = run_bass_kernel_spmd(nc, in_maps, core_ids=list(range(2 * B)))
    out = np.empty((B, L, x.shape[2]), np.float32)
    for b in range(B):
        for half in range(2):
            out[b, half * Lh:(half + 1) * Lh] = np.asarray(res.results[2 * b + half]["out"], dtype=np.float32)
    return out
```

```python
import contextlib
import math
import numpy as np
import concourse.bass as bass
import concourse.mybir as mybir
from concourse.bass_utils import run_bass_kernel_spmd

F32 = mybir.dt.float32
BF16 = mybir.dt.bfloat16
I32 = mybir.dt.int32
AF = mybir.ActivationFunctionType
ALU = mybir.AluOpType
AX = mybir.AxisListType

D = 2048
DEPTH = 2
SSMW = 1024
NG = 64
NS = 64
ATW = 1024
NH = 8
INW = 8192
DFF = 5632
RMS_EPS = 1e-6
SUBLN_EPS = 1e-5


class Res:
    __slots__ = ("name", "w", "r", "dsem", "dcnt", "excl")

    def __init__(self, name):
        self.name = name
        self.excl = False
        self.w = None
        self.r = {}
        self.dsem = None
        self.dcnt = 0


class Prog:
    ENG = ("pe", "act", "dve", "pool", "sp")

    def __init__(self, nc, stack):
        self.nc = nc
        self.stack = stack
        self.sems = []
        self.ops = {e: [] for e in self.ENG}
        self.cnt = {e: 0 for e in self.ENG}
        self.seen = {e: {} for e in self.ENG}
        self.esem = {}
        self.latest = {}
        for e in self.ENG:
            self.esem[e] = self.new_sem("s_" + e)
        self.nres = 0
        self.ccsem = None
        self.named = {}
        self.nalloc = 0

    def new_sem(self, name):
        s = self.stack.enter_context(self.nc.semaphore(name))
        self.sems.append(s)
        return len(self.sems) - 1

    def res(self, name=None):
        self.nres += 1
        return Res(name or ("r%d" % self.nres))

    def nres_(self, name):
        if name not in self.named:
            self.named[name] = Res(name)
        return self.named[name]

    def sbuf(self, name, shape, dtype, stack=None):
        self.nalloc += 1
        st = stack if stack is not None else self.stack
        t = st.enter_context(self.nc.sbuf_tensor("%s_%d" % (name, self.nalloc), list(shape), dtype))
        return t

    def psum(self, name, shape, dtype, stack=None):
        st = stack if stack is not None else self.stack
        return st.enter_context(self.nc.psum_tensor(name, list(shape), dtype))

    def _deps(self, eng, reads, writes):
        deps = {}
        for r in reads:
            if r.w is not None:
                s, v = r.w
                if deps.get(s, 0) < v:
                    deps[s] = v
        for w in writes:
            if w.w is not None:
                s, v = w.w
                if deps.get(s, 0) < v:
                    deps[s] = v
            for (s, v) in w.r.values():
                if deps.get(s, 0) < v:
                    deps[s] = v
        waits = []
        seen = self.seen[eng]
        own = self.esem[eng]
        for s, v in deps.items():
            if s == own and v > self.cnt[eng]:
                continue
            if seen.get(s, 0) < v:
                seen[s] = v
                waits.append((s, v))
        return waits

    def op(self, eng, fn, reads=(), writes=(), signal=True):
        if any(r.excl for r in reads):
            writes = list(writes) + [r for r in reads if r.excl and r not in writes]
            reads = [r for r in reads if not r.excl]
        waits = self._deps(eng, reads, writes)
        idx = self.cnt[eng] + 1
        if signal:
            self.cnt[eng] = idx
        ev = (self.esem[eng], idx)
        self.latest[ev[0]] = idx
        self.ops[eng].append((waits, fn, signal))
        for w in writes:
            w.w = ev
            w.r = {}
        for r in reads:
            r.r[ev[0]] = ev

    def dma(self, q, out, in_, sres, reads=(), writes=(), **kw):
        waits = self._deps(q, reads, writes)
        qk = "sw" if q == "pool" else "hw"
        if sres.dsem is None:
            sres.dsem = {}
            sres.dcnt = {}
        if qk not in sres.dsem:
            sres.dsem[qk] = self.new_sem("d%s_%s" % (qk, sres.name))
            sres.dcnt[qk] = 0
        sres.dcnt[qk] += 16
        ev = (sres.dsem[qk], sres.dcnt[qk])
        self.latest[ev[0]] = ev[1]
        sem = self.sems[ev[0]]

        def fn(e, out=out, in_=in_, sem=sem, kw=kw):
            e.dma_start(out=out, in_=in_, **kw).then_inc(sem, 16)
            return None

        self.ops[q].append((waits, fn, False))
        for w in writes:
            w.w = ev
            w.r = {}
        for r in reads:
            r.r[ev[0]] = ev

    def coll(self, kind, in_ap, out_ap, groups, reads=(), writes=()):
        waits = self._deps("pool", reads, writes)
        if self.ccsem is None:
            self.ccsem = self.new_sem("s_cc")
            self.cccnt = 0
        self.cccnt += 1
        ev = (self.ccsem, self.cccnt)
        self.latest[ev[0]] = ev[1]
        sem = self.sems[ev[0]]

        def fn(e):
            e.collective_compute(kind, ALU.bypass, replica_groups=groups, ins=[in_ap], outs=[out_ap]).then_inc(sem, 1)
            return None

        self.ops["pool"].append((waits, fn, False))
        for w in writes:
            w.w = ev
            w.r = {}
        for r in reads:
            r.r[ev[0]] = ev

    def barrier(self):
        for e in self.ENG:
            waits = []
            seen = self.seen[e]
            for s, v in self.latest.items():
                if seen.get(s, 0) < v:
                    seen[s] = v
                    waits.append((s, v))
            if waits:
                self.ops[e].append((waits, None, False))

    def emit(self):
        nc = self.nc
        sems = self.sems
        with nc.Block() as block:
            def run(name, e):
                own = sems[self.esem[name]]
                for waits, fn, signal in self.ops[name]:
                    for s, v in waits:
                        e.wait_ge(sems[s], v)
                    if fn is None:
                        continue
                    ins = fn(e)
                    if signal:
                        ins.then_inc(own, 1)

            @block.tensor
            def _(e):
                run("pe", e)

            @block.scalar
            def _(e):
                run("act", e)

            @block.vector
            def _(e):
                run("dve", e)

            @block.gpsimd
            def _(e):
                run("pool", e)

            @block.sync
            def _(e):
                run("sp", e)


WSPEC = [
    ("w_in", D, INW), ("w_glu", SSMW, SSMW), ("w_proj_ssm", SSMW, D), ("w_proj_attn", ATW, D),
    ("w_out", D, D), ("w_ffn_gate", D, DFF), ("w_ffn_up", D, DFF), ("w_ffn_down", DFF, D),
]
SMALL = [("norm1_w", [DEPTH, D]), ("lam_re", [DEPTH, NG, NS]), ("lam_im", [DEPTH, NG, NS]),
         ("log_step", [DEPTH, NG]), ("ssm_b_re", [DEPTH, NG, NS, 16]), ("ssm_b_im", [DEPTH, NG, NS, 16]),
         ("ssm_c_re", [DEPTH, NG, 16, NS]), ("ssm_c_im", [DEPTH, NG, 16, NS]), ("ssm_d", [DEPTH, SSMW]),
         ("b_glu", [DEPTH, SSMW]), ("q_norm_w", [DEPTH, 64]), ("k_norm_w", [DEPTH, 64]),
         ("lambda_q1", [DEPTH, 64]), ("lambda_k1", [DEPTH, 64]), ("lambda_q2", [DEPTH, 64]),
         ("lambda_k2", [DEPTH, 64]), ("subln_w", [DEPTH, 128]), ("rel_bias", [32, NH]),
         ("norm2_w", [DEPTH, D])]
CONSTS = [("c_ident", [128, 128]), ("c_bones", [128, 128]), ("c_tmask", [128, 128]), ("c_reld", [128, 128]), ("c_swap", [128, 128]), ("c_flag", [128, 1])]


def host_consts():
    ident = np.eye(128, dtype=np.float32)
    bones = np.kron(np.eye(2, dtype=np.float32), np.ones((64, 64), np.float32))
    jj = np.arange(128) // 16
    tmask = (jj[None, :] >= jj[:, None]).astype(np.float32)
    reld = (np.arange(128)[:, None] - np.arange(128)[None, :]).astype(np.float32)
    swap = np.roll(np.eye(128, dtype=np.float32), 64, axis=1)
    return {"c_ident": ident, "c_bones": bones, "c_tmask": tmask, "c_reld": reld, "c_swap": swap,
            "c_flag": np.zeros((128, 1), np.float32)}


class Builder:
    def __init__(self, L, nlayers=DEPTH, dbg=False, phases=None, nsh=8):
        self.L = L
        self.NT = L // 512
        self.nlayers = nlayers
        self.dbg = dbg
        self.phases = phases
        self.nc = bass.Bass("TRN2", target_bir_lowering=False)
        self.stack = contextlib.ExitStack()
        self.P = Prog(self.nc, self.stack)
        nc = self.nc
        self.i = {}
        self.i["x"] = self.din("x", [L, D])
        for n, k, m in WSPEC:
            self.i[n] = self.din(n, [DEPTH, k, m])
        for n, sh in SMALL + CONSTS:
            self.i[n] = self.din(n, sh)
        self.out = self.dout("out", [L, D])
        self.wb = {n: [self.dscr("wb_%s_%d" % (n, l), [k, m], BF16) for l in range(nlayers)] for n, k, m in WSPEC}
        sk = self.dout if dbg else self.dscr
        self.u_d = sk("u_d", [L, SSMW], BF16)
        self.nch = max(1, L // 1024)
        self.krows = ATW // self.nch
        self.ug = [self.dscr("ug%d" % c, [2 * 1024, SSMW], BF16) for c in range(self.nch)]
        self.vg = [self.dscr("vg%d" % c, [2 * 1024, ATW], BF16) for c in range(self.nch)]
        self.kTg = [self.dscr("kTg%d" % c, [2 * self.krows, L], BF16) for c in range(self.nch)]
        self.groups = [[0, 1], [2, 3], [4, 5], [6, 7]]
        skq = self.din if dbg == "attin" else sk
        self.qT_d = skq("qT_d", [ATW, L], BF16)
        self.kT_d = skq("kT_d", [ATW, L], BF16)
        self.v_d = skq("v_d", [L, ATW], BF16)
        self.gsT_d = sk("gsT_d", [D, L], BF16)
        self.gaT_d = sk("gaT_d", [D, L], BF16)
        self.y_d = sk("y_d", [L, SSMW], F32)
        self.yaT_d = sk("yaT_d", [ATW, L], BF16)
        self.xa_d = sk("xa_d", [L, D], F32)
        self.xb_d = self.dscr("xb_d", [L, D], F32)
        self.rdram = {}

    def din(self, name, shape, dtype=F32):
        return self.nc.dram_tensor(name, list(shape), dtype, kind="ExternalInput").ap()

    def dscr(self, name, shape, dtype):
        return self.nc.dram_tensor(name, list(shape), dtype, kind="Internal").ap()

    def dout(self, name, shape, dtype=F32):
        return self.nc.dram_tensor(name, list(shape), dtype, kind="ExternalOutput").ap()

    def setup_globals(self):
        P = self.P
        g = self.g = {}
        self.psb = []
        for b in range(8):
            self.psb.append((P.psum("psb%d" % b, [128, 512], F32), P.nres_("psb%d" % b)))
            self.psb[-1][1].excl = True
        self.psi = 0
        for n in ("c_ident", "c_bones", "c_tmask", "c_reld", "c_swap"):
            t = P.sbuf(n, [128, 128], F32)
            r = P.nres_(n)
            P.dma("sp", t[:, :], self.i[n][:, :], r, writes=[r])
            g[n] = (t, r)
        t = P.sbuf("c_flag", [128, 1], F32)
        r = P.nres_("c_flag")
        P.dma("sp", t[:, :], self.i["c_flag"][:, :], r, writes=[r])
        g["c_flag"] = (t, r)
        t2 = P.sbuf("mflag", [128, 1], F32)
        P.op("dve", lambda e, t=t, t2=t2: e.tensor_scalar(out=t2[:, :], in0=t[:, :], scalar1=-1.0, scalar2=30000.0, op0=ALU.add, op1=ALU.mult), reads=[r], writes=[r])
        g["mflag"] = (t2, r)
        t = P.sbuf("bones_b", [128, 128], BF16)
        r = P.nres_("bones_b")
        P.op("dve", lambda e, t=t: e.tensor_copy(out=t[:, :], in_=g["c_bones"][0][:, :]), reads=[g["c_bones"][1]], writes=[r])
        g["bones_b"] = (t, r)
        for nm, val in (("eps_rms", RMS_EPS), ("eps_sub", SUBLN_EPS), ("halfpi", math.pi / 2), ("zero", 0.0)):
            t = P.sbuf(nm, [128, 1], F32)
            r = P.nres_(nm)
            P.op("pool", lambda e, t=t, val=val: e.memset(t[:, :], val), writes=[r])
            g[nm] = (t, r)

    def bank(self):
        b = self.psb[self.psi % 8]
        self.psi += 1
        return b

    def phase_w(self):
        P = self.P
        with contextlib.ExitStack() as st:
            NB = 3
            CW = 4096
            fb = [(P.sbuf("wf", [128, CW], F32, st), P.nres_("wf%d" % i)) for i in range(NB)]
            bb = [(P.sbuf("wbb", [128, CW], BF16, st), P.nres_("wbb%d" % i)) for i in range(NB)]
            rw = P.nres_("wdram")
            it = 0
            ce = ("dve", "pool", "act")
            for l in range(self.nlayers):
                for n, K, N in WSPEC:
                    src = self.i[n]
                    dst = self.wb[n][l]
                    ncc = (N + CW - 1) // CW
                    cw = N // ncc
                    for kt in range(K // 128):
                        for c in range(ncc):
                            (f, rf), (b, rb) = fb[it % NB], bb[it % NB]
                            P.dma("sp", f[:, 0:cw], src[l, kt * 128:(kt + 1) * 128, c * cw:(c + 1) * cw], rf, writes=[rf])
                            eng = ce[it % 3]
                            if eng == "act":
                                P.op("act", lambda e, f=f, b=b, cw=cw: e.activation(out=b[:, 0:cw], in_=f[:, 0:cw], func=AF.Copy), reads=[rf], writes=[rb])
                            else:
                                P.op(eng, lambda e, f=f, b=b, cw=cw: e.tensor_copy(out=b[:, 0:cw], in_=f[:, 0:cw]), reads=[rf], writes=[rb])
                            P.dma("act", dst[kt * 128:(kt + 1) * 128, c * cw:(c + 1) * cw], b[:, 0:cw], rb, reads=[rb], writes=[])
                            it += 1
            P.barrier()

    def norm_T(self, st_bufs, xt, rx, wT, rwT, hT, rhT):
        P, g = self.P, self.g
        junk, rjunk, ssq, rssq, rstd, rrstd, xs = st_bufs
        for s in range(4):
            P.op("act", lambda e, s=s: e.activation(out=junk[:, :], in_=xt[:, s, :], func=AF.Square, accum_out=ssq[:, s:s + 1]),
                 reads=[rx], writes=[rjunk, rssq])
        P.op("act", lambda e: e.activation(out=rstd[:, :], in_=ssq[:, :], func=AF.Sqrt, bias=g["eps_rms"][0][:, :], scale=1.0 / D),
             reads=[rssq, g["eps_rms"][1]], writes=[rrstd])
        P.op("dve", lambda e: e.reciprocal(out=rstd[:, :], in_=rstd[:, :]), reads=[rrstd], writes=[rrstd])
        idt, rid = g["c_ident"]
        for s in range(4):
            xs_t, rxs = xs[s % 2]
            P.op("act", lambda e, s=s, xs_t=xs_t: e.activation(out=xs_t[:, :], in_=xt[:, s, :], func=AF.Copy, scale=rstd[:, s:s + 1]),
                 reads=[rx, rrstd], writes=[rxs])
            for k4 in range(4):
                ps, rps = self.bank()
                for j in range(4):
                    kc = k4 * 4 + j
                    P.op("pe", lambda e, ps=ps, xs_t=xs_t, kc=kc, j=j: e.transpose(out=ps[:, j * 128:(j + 1) * 128], in_=xs_t[:, kc * 128:(kc + 1) * 128], identity=idt[:, :]),
                         reads=[rxs, rid], writes=[rps], signal=(j == 3))
                P.op("dve", lambda e, ps=ps, k4=k4, s=s: e.tensor_tensor(
                    out=hT[:, k4 * 4:k4 * 4 + 4, s * 128:(s + 1) * 128],
                    in0=ps[:, :].rearrange("p (k t) -> p k t", k=4),
                    in1=wT[:, k4 * 4:k4 * 4 + 4].unsqueeze(2).broadcast_to([128, 4, 128]), op=ALU.mult),
                    reads=[rps, rwT], writes=[rhT])

    def norm_bufs(self, st):
        P = self.P
        junk = P.sbuf("junk", [128, D], BF16, st)
        ssq = P.sbuf("ssq", [128, 4], F32, st)
        rstd = P.sbuf("rstd", [128, 4], F32, st)
        xs = [(P.sbuf("xs", [128, D], F32, st), P.nres_("xs%d" % i)) for i in range(2)]
        return (junk, P.nres_("junk"), ssq, P.nres_("ssq"), rstd, P.nres_("rstd"), xs)

    def load_wtile(self, wt, rwt, src, k0, kcn, n0, ncols, q="sp"):
        self.P.dma(q, wt[:, 0:kcn, 0:ncols],
                   src[k0 * 128:(k0 + kcn) * 128, n0:n0 + ncols].rearrange("(kc p) n -> p kc n", p=128),
                   rwt, writes=[rwt])

    def mm_acc(self, ps_ap, rps, pairs, reads):
        n = len(pairs)
        for i, (lhsT, rhs) in enumerate(pairs):
            self.P.op("pe", lambda e, lhsT=lhsT, rhs=rhs, i=i: e.matmul(ps_ap, lhsT=lhsT, rhs=rhs, start=(i == 0), stop=(i == n - 1)),
                      reads=reads, writes=[rps], signal=(i == n - 1))

    def phase_a(self, l, x_src):
        P, g = self.P, self.g
        L = self.L
        with contextlib.ExitStack() as st:
            xt = P.sbuf("xt", [128, 4, D], F32, st)
            rx = P.nres_("xt")
            nbufs = self.norm_bufs(st)
            w1T = P.sbuf("w1T", [128, 16], F32, st)
            rw1 = P.nres_("w1T")
            P.dma("sp", w1T[:, :], self.i["norm1_w"][l].rearrange("(kc p) -> p kc", p=128), rw1, writes=[rw1],
                  allow_slow_non_contiguous=True)
            wqk = P.sbuf("wqk", [128, 2], F32, st)
            rwqk = P.nres_("wqk")
            for ci, nm in enumerate(("q_norm_w", "k_norm_w")):
                for m in range(2):
                    P.dma("sp", wqk[m * 64:(m + 1) * 64, ci:ci + 1], self.i[nm][l:l + 1, :].rearrange("o d -> d o"), rwqk,
                          writes=[rwqk], allow_slow_non_contiguous=True)
            P.op("dve", lambda e: e.tensor_scalar(out=wqk[:, 0:1], in0=wqk[:, 0:1], scalar1=0.125, scalar2=None, op0=ALU.mult),
                 reads=[rwqk], writes=[rwqk])
            hT = P.sbuf("hT", [128, 16, 512], BF16, st)
            rhT = P.nres_("hT")
            wbuf = [(P.sbuf("wA", [128, 16, 512], BF16, st), P.nres_("wA%d" % i)) for i in range(3)]
            ut = [(P.sbuf("ut", [128, 4, 512], BF16, st), P.nres_("ut%d" % i)) for i in range(2)]
            vt = [(P.sbuf("vt", [128, 4, 512], BF16, st), P.nres_("vt%d" % i)) for i in range(2)]
            ot = [(P.sbuf("ot", [128, 512], BF16, st), P.nres_("ot%d" % i)) for i in range(3)]
            sq = [(P.sbuf("sq", [128, 512], BF16, st), P.nres_("sq%d" % i)) for i in range(2)]
            rt = [(P.sbuf("rt", [128, 512], F32, st), P.nres_("rt%d" % i)) for i in range(2)]
            bones, rbones = g["bones_b"]
            wsrc = self.wb["w_in"][l]
            nwl = 0
            oi = 0
            for t in range(self.NT):
                t0 = t * 512
                P.dma("sp", xt[:, :, :], x_src[t0:t0 + 512, :].rearrange("(s p) d -> p s d", p=128), rx, writes=[rx])
                self.norm_T(nbufs, xt, rx, w1T, rw1, hT, rhT)
                for c in range(16):
                    wt, rwt = wbuf[nwl % 3]
                    nwl += 1
                    self.load_wtile(wt, rwt, wsrc, 0, 16, c * 512, 512)
                    if c in (0, 1, 6, 7):
                        isu = c < 2
                        stg, rstg = (ut if isu else vt)[c % 2]
                        for s in range(4):
                            ps, rps = self.bank()
                            self.mm_acc(ps[:, :], rps, [(hT[:, kc, s * 128:(s + 1) * 128], wt[:, kc, :]) for kc in range(16)], [rhT, rwt])
                            if s % 2 == 0:
                                P.op("act", lambda e, ps=ps, stg=stg, s=s: e.activation(out=stg[:, s, :], in_=ps[:, :], func=AF.Copy),
                                     reads=[rps], writes=[rstg])
                            else:
                                P.op("dve", lambda e, ps=ps, stg=stg, s=s: e.tensor_copy(out=stg[:, s, :], in_=ps[:, :]),
                                     reads=[rps], writes=[rstg])
                        dst = self.u_d if isu else self.v_d
                        cc = c if isu else c - 6
                        P.dma("pool", dst[t0:t0 + 512, cc * 512:(cc + 1) * 512].rearrange("(s p) n -> p s n", p=128), stg[:, :, :], rstg,
                              reads=[rstg])
                    else:
                        for j in range(4):
                            ps, rps = self.bank()
                            self.mm_acc(ps[:, :], rps, [(wt[:, kc, j * 128:(j + 1) * 128], hT[:, kc, :]) for kc in range(16)], [rhT, rwt])
                            o, ro = ot[oi % 3]
                            oi += 1
                            if c < 6:
                                isq = c < 4
                                sqt, rsq = sq[oi % 2]
                                rtt, rrt = rt[oi % 2]
                                P.op("act", lambda e, ps=ps, sqt=sqt: e.activation(out=sqt[:, :], in_=ps[:, :], func=AF.Square), reads=[rps], writes=[rsq])
                                ps2, rps2 = self.bank()
                                self.mm_acc(ps2[:, :], rps2, [(bones[:, :], sqt[:, :])], [rbones, rsq])
                                P.op("act", lambda e, ps2=ps2, rtt=rtt: e.activation(out=rtt[:, :], in_=ps2[:, :], func=AF.Sqrt, bias=g["eps_rms"][0][:, :], scale=1.0 / 64),
                                     reads=[rps2, g["eps_rms"][1]], writes=[rrt])
                                P.op("dve", lambda e, rtt=rtt: e.reciprocal(out=rtt[:, :], in_=rtt[:, :]), reads=[rrt], writes=[rrt])
                                ci = 0 if isq else 1
                                P.op("dve", lambda e, ps=ps, rtt=rtt, o=o, ci=ci: e.scalar_tensor_tensor(
                                    out=o[:, :], in0=ps[:, :], scalar=wqk[:, ci:ci + 1], in1=rtt[:, :], op0=ALU.mult, op1=ALU.mult),
                                    reads=[rps, rrt, rwqk], writes=[ro])
                                dst = self.qT_d if isq else self.kT_d
                                r0 = ((c - 2) if isq else (c - 4)) * 512 + j * 128
                            else:
                                P.op("act", lambda e, ps=ps, o=o: e.activation(out=o[:, :], in_=ps[:, :], func=AF.Sigmoid), reads=[rps], writes=[ro])
                                dst = self.gsT_d if c < 12 else self.gaT_d
                                r0 = ((c - 8) if c < 12 else (c - 12)) * 512 + j * 128
                            P.dma("pool", dst[r0:r0 + 128, t0:t0 + 512], o[:, :], ro, reads=[ro])
            P.barrier()

    def build(self):
        ph = self.phases
        self.setup_globals()
        if ph is None or "att" in ph:
            self.setup_attn()
        if ph is None or "w" in ph:
            self.phase_w()
        for l in range(self.nlayers):
            x_src = self.i["x"] if l == 0 else self.xb_d
            x_dst = self.out if l == self.nlayers - 1 else self.xb_d
            if ph is None or "a" in ph:
                self.phase_a(l, x_src)
            if ph is None or "xch" in ph:
                self.phase_xch()
            if ph is None or "ssm" in ph:
                self.phase_ssm(l)
            if ph is None or "att" in ph:
                self.phase_att(l)
            if ph is None or "b1" in ph:
                self.phase_b1(l, x_src)
            if ph is None or "b2" in ph:
                self.phase_b2(l, x_dst)
        self.P.barrier()
        self.P.emit()
        return self.nc

    def phase_xch(self):
        P = self.P
        r = P.nres_("xch")
        tr = min(1024, self.L)
        for c in range(self.nch):
            P.coll("AllGather", self.u_d[c * tr:(c + 1) * tr, :], self.ug[c][:, :], self.groups, reads=[r], writes=[r])
            P.coll("AllGather", self.v_d[c * tr:(c + 1) * tr, :], self.vg[c][:, :], self.groups, reads=[r], writes=[r])
            P.coll("AllGather", self.kT_d[c * self.krows:(c + 1) * self.krows, :], self.kTg[c][:, :], self.groups, reads=[r], writes=[r])
        P.barrier()

    def cmul(self, eng, out_re, out_im, a_re, a_im, b_re, b_im, tmp, rres, wres, neg_im=False):
        P = self.P
        t1, t2 = tmp
        ops = [
            (t1, a_re, b_re, ALU.mult), (t2, a_im, b_im, ALU.mult), (out_re, t1, t2, ALU.subtract),
            (t1, a_re, b_im, ALU.mult), (t2, a_im, b_re, ALU.mult), (out_im, t1, t2, ALU.add),
        ]
        for o, x, y, op in ops:
            P.op(eng, lambda e, o=o, x=x, y=y, op=op: e.tensor_tensor(out=o, in0=x, in1=y, op=op), reads=rres, writes=wres)
        if neg_im:
            P.op(eng, lambda e, o=out_im: e.tensor_scalar(out=o, in0=o, scalar1=-1.0, scalar2=None, op0=ALU.mult), reads=wres, writes=wres)

    def phase_ssm(self, l):
        P, g = self.P, self.g
        L = self.L
        NBLK = 2 * L // 8
        NB2 = NBLK // 2
        bp = min(128, NBLK)
        nbt = NBLK // bp
        nbt2 = nbt // 2
        assert nbt % 2 == 0
        flag, rflag = g["c_flag"]
        nsteps = int(math.ceil(math.log2(NBLK)))
        idt, rid = g["c_ident"]
        swp, rswp = g["c_swap"]
        tmask, rtm = g["c_tmask"]
        with contextlib.ExitStack() as st:
            rp = P.nres_("ssm_par")
            def T(name, shape, dt=F32):
                return P.sbuf(name, shape, dt, st)
            lre, lim, dtt = T("lre", [128, 64]), T("lim", [128, 64]), T("dtt", [128, 64])
            for hf in range(2):
                hs = slice(hf * 64, hf * 64 + 64)
                P.dma("sp", lre[hs, :], self.i["lam_re"][l].rearrange("g p -> p g"), rp, writes=[rp], allow_slow_non_contiguous=True)
                P.dma("sp", lim[hs, :], self.i["lam_im"][l].rearrange("g p -> p g"), rp, writes=[rp], allow_slow_non_contiguous=True)
            P.dma("sp", dtt[:, :], self.i["log_step"][l].partition_broadcast(128), rp, writes=[rp])
            Bre, Bim = T("Bre", [128, 64, 16]), T("Bim", [128, 64, 16])
            for hf in range(2):
                hs = slice(hf * 64, hf * 64 + 64)
                P.dma("sp", Bre[hs, :, :], self.i["ssm_b_re"][l].rearrange("g p c -> p g c"), rp, writes=[rp])
                P.dma("sp", Bim[hs, :, :], self.i["ssm_b_im"][l].rearrange("g p c -> p g c"), rp, writes=[rp])
            Dcol = T("Dcol", [128, 64])
            for j in range(8):
                P.dma("sp", Dcol[16 * j:16 * j + 16, :], self.i["ssm_d"][l].rearrange("(g c) -> c g", c=16), rp, writes=[rp],
                      allow_slow_non_contiguous=True)
            Cre, Cim = T("Cre", [128, 64, 16]), T("Cim", [128, 64, 16])
            crow = T("crow", [128, 128])
            for nm, dstt in (("ssm_c_re", Cre), ("ssm_c_im", Cim)):
                src = self.i[nm][l].rearrange("g c p -> (g c) p")
                for k in range(8):
                    P.dma("sp", crow[:, 0:64], src[k * 128:(k + 1) * 128, :], rp, reads=[rp], writes=[rp])
                    P.dma("sp", crow[:, 64:128], src[k * 128:(k + 1) * 128, :], rp, reads=[rp], writes=[rp])
                    ps, rps = self.bank()
                    P.op("pe", lambda e, ps=ps: e.transpose(out=ps[:, 0:128], in_=crow[:, :], identity=idt[:, :]), reads=[rp, rid], writes=[rps])
                    P.op("dve", lambda e, ps=ps, dstt=dstt, k=k: e.tensor_copy(out=dstt[:, k * 8:(k + 1) * 8, :], in_=ps[:, 0:128].rearrange("p (g c) -> p g c", c=16)),
                         reads=[rps], writes=[rp])
            V = lambda fn, rd=(rp,), wr=(rp,): P.op("dve", fn, reads=list(rd), writes=list(wr))
            A = lambda fn: P.op("act", fn, reads=[rp, g["halfpi"][1]], writes=[rp])
            lr, x1, mag, ang, tq, r_, m1 = [T(n, [128, 64]) for n in ("lr", "x1", "mag", "ang", "tq", "r_", "m1")]
            ti = T("ti", [128, 64], I32)
            sn, cs, ar, ai = [T(n, [128, 64]) for n in ("sn", "cs", "ar", "ai")]
            V(lambda e: e.tensor_scalar(out=lr[:, :], in0=lre[:, :], scalar1=-1e-4, scalar2=None, op0=ALU.min))
            A(lambda e: e.activation(out=dtt[:, :], in_=dtt[:, :], func=AF.Exp))
            V(lambda e: e.tensor_tensor(out=x1[:, :], in0=lr[:, :], in1=dtt[:, :], op=ALU.mult))
            A(lambda e: e.activation(out=mag[:, :], in_=x1[:, :], func=AF.Exp))
            V(lambda e: e.tensor_tensor(out=ang[:, :], in0=lim[:, :], in1=dtt[:, :], op=ALU.mult))
            V(lambda e: e.tensor_scalar(out=tq[:, :], in0=ang[:, :], scalar1=1.0 / (2 * math.pi), scalar2=0.5, op0=ALU.mult, op1=ALU.add))
            V(lambda e: e.tensor_copy(out=ti[:, :], in_=tq[:, :]))
            V(lambda e: e.tensor_copy(out=tq[:, :], in_=ti[:, :]))
            V(lambda e: e.scalar_tensor_tensor(out=r_[:, :], in0=tq[:, :], scalar=-2 * math.pi, in1=ang[:, :], op0=ALU.mult, op1=ALU.add))
            for thr, opc, add in ((-math.pi, ALU.is_lt, 2 * math.pi), (math.pi, ALU.is_gt, -2 * math.pi),
                                  (-math.pi, ALU.is_lt, 2 * math.pi), (math.pi, ALU.is_gt, -2 * math.pi)):
                V(lambda e, thr=thr, opc=opc: e.tensor_single_scalar(out=m1[:, :], in_=r_[:, :], scalar=thr, op=opc))
                V(lambda e, add=add: e.scalar_tensor_tensor(out=r_[:, :], in0=m1[:, :], scalar=add, in1=r_[:, :], op0=ALU.mult, op1=ALU.add))
            V(lambda e: e.tensor_scalar(out=r_[:, :], in0=r_[:, :], scalar1=math.pi, scalar2=-math.pi, op0=ALU.min, op1=ALU.max))
            A(lambda e: e.activation(out=sn[:, :], in_=r_[:, :], func=AF.Sin))
            V(lambda e: e.tensor_scalar(out=m1[:, :], in0=r_[:, :], scalar1=-1.0, scalar2=None, op0=ALU.mult))
            V(lambda e: e.tensor_tensor(out=m1[:, :], in0=m1[:, :], in1=r_[:, :], op=ALU.max))
            A(lambda e: e.activation(out=cs[:, :], in_=m1[:, :], func=AF.Sin, bias=g["halfpi"][0][:, :], scale=-1.0))
            V(lambda e: e.tensor_tensor(out=ar[:, :], in0=mag[:, :], in1=cs[:, :], op=ALU.mult))
            V(lambda e: e.tensor_tensor(out=ai[:, :], in0=mag[:, :], in1=sn[:, :], op=ALU.mult))
            den, nr, fr, fi, t1, t2 = [T(n, [128, 64]) for n in ("den", "nr", "fr", "fi", "t1", "t2")]
            V(lambda e: e.tensor_tensor(out=den[:, :], in0=lr[:, :], in1=lr[:, :], op=ALU.mult))
            V(lambda e: e.tensor_tensor(out=t1[:, :], in0=lim[:, :], in1=lim[:, :], op=ALU.mult))
            V(lambda e: e.tensor_tensor(out=den[:, :], in0=den[:, :], in1=t1[:, :], op=ALU.add))
            V(lambda e: e.reciprocal(out=den[:, :], in_=den[:, :]))
            V(lambda e: e.tensor_scalar(out=nr[:, :], in0=ar[:, :], scalar1=-1.0, scalar2=None, op0=ALU.add))
            V(lambda e: e.tensor_tensor(out=t1[:, :], in0=nr[:, :], in1=lr[:, :], op=ALU.mult))
            V(lambda e: e.tensor_tensor(out=t2[:, :], in0=ai[:, :], in1=lim[:, :], op=ALU.mult))
            V(lambda e: e.tensor_tensor(out=t1[:, :], in0=t1[:, :], in1=t2[:, :], op=ALU.add))
            V(lambda e: e.tensor_tensor(out=fr[:, :], in0=t1[:, :], in1=den[:, :], op=ALU.mult))
            V(lambda e: e.tensor_tensor(out=t1[:, :], in0=ai[:, :], in1=lr[:, :], op=ALU.mult))
            V(lambda e: e.tensor_tensor(out=t2[:, :], in0=nr[:, :], in1=lim[:, :], op=ALU.mult))
            V(lambda e: e.tensor_tensor(out=t1[:, :], in0=t1[:, :], in1=t2[:, :], op=ALU.subtract))
            V(lambda e: e.tensor_tensor(out=fi[:, :], in0=t1[:, :], in1=den[:, :], op=ALU.mult))
            Bbr, Bbi = T("Bbr", [128, 64, 16]), T("Bbi", [128, 64, 16])
            tb1, tb2 = T("tb1", [128, 64, 16]), T("tb2", [128, 64, 16])
            bc16 = lambda a: a[:, :].unsqueeze(2).broadcast_to([128, 64, 16])
            self.cmul("dve", Bbr[:, :, :], Bbi[:, :, :], bc16(fr), bc16(fi), Bre[:, :, :], Bim[:, :, :], (tb1[:, :, :], tb2[:, :, :]), [rp], [rp])
            PWr, PWi = T("PWr", [128, 64, 9]), T("PWi", [128, 64, 9])
            PIr, PIi = T("PIr", [128, 64, 8]), T("PIi", [128, 64, 8])
            PRr, PRi = T("PRr", [128, 64, 8]), T("PRi", [128, 64, 8])
            air, aii = T("air", [128, 64]), T("aii", [128, 64])
            V(lambda e: e.tensor_tensor(out=t1[:, :], in0=ar[:, :], in1=ar[:, :], op=ALU.mult))
            V(lambda e: e.tensor_tensor(out=t2[:, :], in0=ai[:, :], in1=ai[:, :], op=ALU.mult))
            V(lambda e: e.tensor_tensor(out=t1[:, :], in0=t1[:, :], in1=t2[:, :], op=ALU.add))
            V(lambda e: e.reciprocal(out=t1[:, :], in_=t1[:, :]))
            V(lambda e: e.tensor_tensor(out=air[:, :], in0=ar[:, :], in1=t1[:, :], op=ALU.mult))
            V(lambda e: e.scalar_tensor_tensor(out=aii[:, :], in0=ai[:, :], scalar=-1.0, in1=t1[:, :], op0=ALU.mult, op1=ALU.mult))
            for (Xr, Xi, br_, bi_, n) in ((PWr, PWi, ar, ai, 9), (PIr, PIi, air, aii, 8)):
                V(lambda e, Xr=Xr: e.memset(Xr[:, :, 0:1], 1.0))
                V(lambda e, Xi=Xi: e.memset(Xi[:, :, 0:1], 0.0))
                for k in range(1, n):
                    self.cmul("dve", Xr[:, :, k], Xi[:, :, k], Xr[:, :, k - 1], Xi[:, :, k - 1], br_[:, :], bi_[:, :], (t1[:, :], t2[:, :]), [rp], [rp])
            for j in range(8):
                V(lambda e, j=j: e.tensor_copy(out=PRr[:, :, j], in_=PWr[:, :, 7 - j]))
                V(lambda e, j=j: e.tensor_copy(out=PRi[:, :, j], in_=PWi[:, :, 7 - j]))
            APr, APi = T("APr", [128, 64, nsteps]), T("APi", [128, 64, nsteps])
            V(lambda e: e.tensor_copy(out=APr[:, :, 0], in_=PWr[:, :, 8]))
            V(lambda e: e.tensor_copy(out=APi[:, :, 0], in_=PWi[:, :, 8]))
            for k in range(1, nsteps):
                self.cmul("dve", APr[:, :, k], APi[:, :, k], APr[:, :, k - 1], APi[:, :, k - 1], APr[:, :, k - 1], APi[:, :, k - 1], (t1[:, :], t2[:, :]), [rp], [rp])
            V(lambda e: e.tensor_scalar(out=APi[64:128, :, :], in0=APi[64:128, :, :], scalar1=-1.0, scalar2=None, op0=ALU.mult))
            GB = 8
            FW = GB * 16
            Bjr, Bji, BEr, BEi = [T(n, [128, GB, 8, 16]) for n in ("Bjr", "Bji", "BEr", "BEi")]
            Crr, Cri = T("Crr", [128, GB, 9, 16]), T("Cri", [128, GB, 9, 16])
            tg1, tg2 = T("tg1", [128, GB, 9, 16]), T("tg2", [128, GB, 9, 16])
            rgen = P.nres_("ssm_gen")
            Z = T("Z", [128, nbt, 8, FW], BF16)
            rZ = P.nres_("ssm_Z")
            Zc = T("Zc", [128, nbt, GB, 8, 16])
            rZc = P.nres_("ssm_Zc")
            U8 = T("U8", [128, GB, NBLK])
            rU8 = P.nres_("ssm_U8")
            LE = [(T("LE", [128, 128]), P.nres_("ssm_LE%d" % i)) for i in range(2)]
            LS = [(T("LS", [128, 128]), P.nres_("ssm_LS%d" % i)) for i in range(2)]
            MK = [(T("MK", [128, 128]), P.nres_("ssm_MK%d" % i)) for i in range(16)]
            Tt = T("Tt", [128, GB, 128])
            rTt = P.nres_("ssm_Tt")
            tmpT = [(T("tmpT", [128, 128]), P.nres_("ssm_tmpT%d" % i)) for i in range(2)]
            W = NBLK + 1
            S = T("S", [128, GB, W])
            rS = [P.nres_("ssm_S%d" % i) for i in range(GB)]
            Y8 = [(T("Y8", [128, NB2]), P.nres_("ssm_Y8%d" % i)) for i in range(2)]
            Yt = T("Yt", [128, nbt2, 8, FW])
            rYt = P.nres_("ssm_Yt")
            P.op("pool", lambda e: e.memset(S[:, :, 0:1], 0.0), writes=rS)
            nmk = 0
            for gb in range(NG // GB):
                g0 = gb * GB
                gs_ = slice(g0, g0 + GB)
                bj = lambda a: a[:, gs_, :].unsqueeze(3).broadcast_to([128, GB, a.shape[2], 16])
                bb = lambda a, n: a[:, gs_, :].unsqueeze(2).broadcast_to([128, GB, n, 16])
                t8 = (tg1[:, :, 0:8, :], tg2[:, :, 0:8, :])
                self.cmul("pool", Bjr[:, :, :, :], Bji[:, :, :, :], bj(PIr), bj(PIi), bb(Bbr, 8), bb(Bbi, 8), t8, [rp], [rgen])
                self.cmul("pool", BEr[:, :, :, :], BEi[:, :, :, :], bj(PRr), bj(PRi), bb(Bbr, 8), bb(Bbi, 8), t8, [rp], [rgen])
                self.cmul("pool", Crr[:, :, :, :], Cri[:, :, :, :], bj(PWr), bj(PWi), bb(Cre, 9), bb(Cim, 9), (tg1[:, :, :, :], tg2[:, :, :, :]), [rp], [rgen], neg_im=True)
                for bt in range(nbt):
                    usrc, b2 = (self.ug[bt], 0) if bt < nbt2 else (self.u_d, bt - nbt2)
                    P.dma("sp", Z[0:bp, bt, :, :], usrc[b2 * bp * 8:(b2 + 1) * bp * 8, g0 * 16:g0 * 16 + FW].rearrange("(b j) f -> b j f", j=8), rZ, writes=[rZ])
                P.op("pool", lambda e: e.tensor_copy(out=Zc[0:bp, :, :, :, :], in_=Z[0:bp, :, :, :].rearrange("p b j (g c) -> p b g j c", c=16)), reads=[rZ], writes=[rZc])
                for gi in range(GB):
                    gg = g0 + gi
                    ps, rps = self.bank()
                    for bt in range(nbt):
                        P.op("pe", lambda e, ps=ps, bt=bt, gi=gi: e.transpose(out=ps[:, bt * bp:(bt + 1) * bp], in_=Zc[0:bp, bt, gi, :, :].rearrange("p j c -> p (j c)"), identity=idt[0:bp, 0:bp]),
                             reads=[rZc, rid], writes=[rps], signal=(bt == nbt - 1))
                    P.op("act", lambda e, ps=ps, gi=gi: e.activation(out=U8[:, gi, 0:NB2], in_=ps[:, 0:NB2], func=AF.Copy, scale=flag[:, 0:1]), reads=[rps, rflag], writes=[rU8])
                    P.op("act", lambda e, ps=ps, gi=gi: e.activation(out=U8[:, gi, NB2:NBLK], in_=ps[:, NB2:NBLK], func=AF.Copy), reads=[rps], writes=[rU8])
                    le, rle = LE[gi % 2]
                    ps, rps = self.bank()
                    P.op("pe", lambda e, ps=ps, gi=gi: e.transpose(out=ps[:, 0:64], in_=BEr[0:64, gi, :, :].rearrange("p j c -> p (j c)"), identity=idt[0:64, 0:64]), reads=[rgen, rid], writes=[rps], signal=False)
                    P.op("pe", lambda e, ps=ps, gi=gi: e.transpose(out=ps[:, 64:128], in_=BEi[0:64, gi, :, :].rearrange("p j c -> p (j c)"), identity=idt[0:64, 0:64]), reads=[rgen, rid], writes=[rps])
                    P.op("dve", lambda e, ps=ps, le=le: e.tensor_copy(out=le[:, :], in_=ps[:, 0:128]), reads=[rps], writes=[rle])
                    ps, rps = self.bank()
                    self.mm_acc(ps[:, 0:NBLK], rps, [(le[:, :], U8[:, gi, :])], [rle, rU8])
                    P.op("act", lambda e, ps=ps, gi=gi: e.activation(out=S[:, gi, 1:W], in_=ps[:, 0:NBLK], func=AF.Copy), reads=[rps], writes=[rS[gi]])
                    ps, rps = self.bank()
                    self.mm_acc(ps[:, 0:128], rps, [(Bjr[0:64, gi, :, :].rearrange("p j c -> p (j c)"), Crr[0:64, gi, 0:8, :].rearrange("p j c -> p (j c)")),
                                                    (Bji[0:64, gi, :, :].rearrange("p j c -> p (j c)"), Cri[0:64, gi, 0:8, :].rearrange("p j c -> p (j c)"))], [rgen])
                    tt_, rtt_ = tmpT[gi % 2]
                    P.op("dve", lambda e, ps=ps, tt_=tt_: e.tensor_tensor(out=tt_[:, :], in0=ps[:, 0:128], in1=tmask[:, :], op=ALU.mult), reads=[rps, rtm], writes=[rtt_])
                    P.op("dve", lambda e, tt_=tt_, gi=gi, gg=gg: e.scalar_tensor_tensor(out=Tt[:, gi, :], in0=idt[:, :], scalar=Dcol[:, gg:gg + 1], in1=tt_[:, :], op0=ALU.mult, op1=ALU.add),
                         reads=[rtt_, rid, rp], writes=[rTt])
                def build_mk(k):
                    out = []
                    for gi in range(GB):
                        gg = g0 + gi
                        mk, rmk = MK[(k * GB + gi) % len(MK)]
                        P.op("dve", lambda e, mk=mk, gg=gg, k=k: e.tensor_scalar(out=mk[:, :], in0=idt[:, :], scalar1=APr[:, gg, k:k + 1], scalar2=None, op0=ALU.mult),
                             reads=[rid, rp], writes=[rmk])
                        P.op("dve", lambda e, mk=mk, gg=gg, k=k: e.scalar_tensor_tensor(out=mk[:, :], in0=swp[:, :], scalar=APi[:, gg, k:k + 1], in1=mk[:, :], op0=ALU.mult, op1=ALU.add),
                             reads=[rswp, rp, rmk], writes=[rmk])
                        out.append((mk, rmk))
                    return out

                mks = build_mk(0)
                for k in range(nsteps):
                    sh = 1 << k
                    n = NBLK - sh
                    pss = []
                    for gi in range(GB):
                        mk, rmk = mks[gi]
                        ps, rps = self.bank()
                        self.mm_acc(ps[:, 0:n], rps, [(mk[:, :], S[:, gi, 1:1 + n])], [rmk, rS[gi]])
                        pss.append((ps, rps))
                    if k + 1 < nsteps:
                        mks = build_mk(k + 1)
                    for gi in range(GB):
                        ps, rps = pss[gi]
                        P.op("dve", lambda e, ps=ps, gi=gi, sh=sh, n=n: e.tensor_tensor(out=S[:, gi, 1 + sh:1 + sh + n], in0=ps[:, 0:n], in1=S[:, gi, 1 + sh:1 + sh + n], op=ALU.add),
                             reads=[rps], writes=[rS[gi]])
                for gi in range(GB):
                    ls, rls = LS[gi % 2]
                    P.op("dve", lambda e, ls=ls, gi=gi: e.tensor_copy(out=ls[0:64, :], in_=Crr[0:64, gi, 1:9, :].rearrange("p j c -> p (j c)")), reads=[rgen], writes=[rls])
                    P.op("dve", lambda e, ls=ls, gi=gi: e.tensor_copy(out=ls[64:128, :], in_=Cri[64:128, gi, 1:9, :].rearrange("p j c -> p (j c)")), reads=[rgen], writes=[rls])
                    ps, rps = self.bank()
                    self.mm_acc(ps[:, 0:NB2], rps, [(Tt[:, gi, :], U8[:, gi, NB2:NBLK]), (ls[:, :], S[:, gi, NB2:NBLK])], [rTt, rU8, rls, rS[gi]])
                    y8, ry8 = Y8[gi % 2]
                    P.op("act", lambda e, ps=ps, y8=y8: e.activation(out=y8[:, :], in_=ps[:, 0:NB2], func=AF.Copy), reads=[rps], writes=[ry8])
                    ps, rps = self.bank()
                    for bt in range(nbt2):
                        P.op("pe", lambda e, ps=ps, bt=bt, y8=y8: e.transpose(out=ps[0:bp, bt * 128:(bt + 1) * 128], in_=y8[:, bt * bp:(bt + 1) * bp], identity=idt[:, :]),
                             reads=[ry8, rid], writes=[rps], signal=(bt == nbt2 - 1))
                    P.op("dve", lambda e, ps=ps, gi=gi: e.tensor_copy(out=Yt[0:bp, :, :, gi * 16:(gi + 1) * 16], in_=ps[0:bp, 0:nbt2 * 128].rearrange("p (b j c) -> p b j c", b=nbt2, j=8)),
                         reads=[rps], writes=[rYt])
                for bt in range(nbt2):
                    P.dma("pool", self.y_d[bt * bp * 8:(bt + 1) * bp * 8, g0 * 16:g0 * 16 + FW].rearrange("(b j) f -> b j f", j=8), Yt[0:bp, bt, :, :], rYt, reads=[rYt])
            P.barrier()

    def setup_attn(self):
        P, g = self.P, self.g
        reld, rreld = g["c_reld"]
        tab = P.sbuf("tab", [128, 256], F32)
        rtab = P.nres_("tab")
        P.dma("sp", tab[:, :], self.i["rel_bias"].rearrange("b h -> (b h)").partition_broadcast(128), rtab, writes=[rtab])
        steps = [(-90, 15, 14), (-63, 14, 13), (-45, 13, 12), (-31, 12, 11), (-22, 11, 10), (-15, 10, 9), (-11, 9, 8)]
        steps += [(-n, n + 1, n) for n in range(7, -1, -1)]
        steps += [(1, 0, 17)] + [(n, 15 + n, 16 + n) for n in range(2, 8)]
        steps += [(8, 23, 24), (12, 24, 25), (16, 25, 26), (23, 26, 27), (32, 27, 28), (46, 28, 29), (64, 29, 30), (91, 30, 31)]
        ns = len(steps)
        dl = P.sbuf("dl", [128, ns, 8], F32)
        rdl = P.nres_("dl")
        for s, (thr, fb, tb) in enumerate(steps):
            P.op("dve", lambda e, s=s, fb=fb, tb=tb: e.tensor_tensor(out=dl[:, s, :], in0=tab[:, tb * 8:tb * 8 + 8], in1=tab[:, fb * 8:fb * 8 + 8], op=ALU.subtract),
                 reads=[rtab], writes=[rdl])
        bias = P.sbuf("biasT", [128, NH, 2, 128], F32)
        rbias = P.nres_("biasT")
        mk = P.sbuf("mk", [128, 128], F32)
        rmk = P.nres_("mk")
        for kind in range(2):
            off = -128.0 * kind
            for h in range(NH):
                P.op("dve", lambda e, h=h, kind=kind: e.tensor_scalar(out=bias[:, h, kind, :], in0=reld[:, :], scalar1=0.0, scalar2=tab[:, 15 * 8 + h:15 * 8 + h + 1], op0=ALU.mult, op1=ALU.add),
                     reads=[rreld, rtab], writes=[rbias])
            for s, (thr, fb, tb) in enumerate(steps):
                if kind == 1 and thr > -1:
                    continue
                if thr > 64:
                    continue
                P.op("dve", lambda e, thr=thr, off=off: e.tensor_single_scalar(out=mk[:, :], in_=reld[:, :], scalar=float(thr) - off, op=ALU.is_ge), reads=[rreld], writes=[rmk])
                for h in range(NH):
                    P.op("dve", lambda e, h=h, kind=kind, s=s: e.scalar_tensor_tensor(out=bias[:, h, kind, :], in0=mk[:, :], scalar=dl[:, s, h:h + 1], in1=bias[:, h, kind, :], op0=ALU.mult, op1=ALU.add),
                         reads=[rmk, rdl], writes=[rbias])
        for h in range(NH):
            P.op("pool", lambda e, h=h: e.memset(bias[64:128, h, 0, 0:64], -30000.0), reads=[rbias], writes=[rbias])
        g["tab"] = (tab, rtab)
        g["biasT"] = (bias, rbias)

    def bank2(self, lo, n, key):
        c = self.bctr.get(key, 0)
        self.bctr[key] = c + 1
        return self.psb[lo + c % n]

    def phase_att(self, l):
        P, g = self.P, self.g
        L = self.L
        nq = L // 128
        lam_init = 0.8 - 0.6 * math.exp(-0.3 * l)
        idt, rid = g["c_ident"]
        tab, rtab = g["tab"]
        bias, rbias = g["biasT"]
        self.bctr = {}
        with contextlib.ExitStack() as st:
            T = lambda name, shape, dt=F32: P.sbuf(name, shape, dt, st)
            rpar = P.nres_("att_par")
            lq = T("lq", [128, 4, 64])
            for ci, nm in enumerate(("lambda_q1", "lambda_k1", "lambda_q2", "lambda_k2")):
                P.dma("sp", lq[:, ci, :], self.i[nm][l].partition_broadcast(128), rpar, writes=[rpar])
            subw = T("subw", [128, 128])
            P.dma("sp", subw[:, :], self.i["subln_w"][l].partition_broadcast(128), rpar, writes=[rpar])
            e12 = T("e12", [128, 2])
            nlam = T("nlam", [128, 1])
            V = lambda fn: P.op("dve", fn, reads=[rpar], writes=[rpar])
            V(lambda e: e.tensor_tensor(out=lq[:, 0, :], in0=lq[:, 0, :], in1=lq[:, 1, :], op=ALU.mult))
            V(lambda e: e.tensor_tensor(out=lq[:, 2, :], in0=lq[:, 2, :], in1=lq[:, 3, :], op=ALU.mult))
            V(lambda e: e.reduce_sum(out=e12[:, 0:1], in_=lq[:, 0, :], axis=AX.X))
            V(lambda e: e.reduce_sum(out=e12[:, 1:2], in_=lq[:, 2, :], axis=AX.X))
            P.op("act", lambda e: e.activation(out=e12[:, :], in_=e12[:, :], func=AF.Exp), reads=[rpar], writes=[rpar])
            V(lambda e: e.tensor_tensor(out=nlam[:, :], in0=e12[:, 1:2], in1=e12[:, 0:1], op=ALU.subtract))
            V(lambda e: e.tensor_scalar(out=nlam[:, :], in0=nlam[:, :], scalar1=-lam_init, scalar2=None, op0=ALU.add))
            V(lambda e: e.tensor_scalar(out=subw[:, :], in0=subw[:, :], scalar1=1.0 - lam_init, scalar2=None, op0=ALU.mult))
            nb = nq
            NKP = nq
            mflag = g["mflag"][0]
            rflag = g["mflag"][1]
            cbp = T("cbp", [128, 8])
            P.op("dve", lambda e: e.tensor_scalar(out=cbp[:, :], in0=tab[:, 120:128], scalar1=mflag[:, 0:1], scalar2=None, op0=ALU.add), reads=[rtab, rflag], writes=[rpar])
            KT = [(T("KT", [128, 2 * L], BF16), P.nres_("att_KT%d" % i)) for i in range(2)]
            QT = [[(T("QT", [128, L], BF16), P.nres_("att_QT%d_%d" % (i, m))) for m in range(2)] for i in range(2)]
            V1 = [(T("V1", [128, 2 * nb, 132], BF16), P.nres_("att_V1%d" % i)) for i in range(2)]
            for v1, rv1 in V1:
                P.op("pool", lambda e, v1=v1: e.memset(v1[:, :, 128:132], 0.0), writes=[rv1])
                P.op("pool", lambda e, v1=v1: e.memset(v1[:, :, 128:129], 1.0), writes=[rv1])
            for i in range(2):
                for m in range(2):
                    qz, rqz = QT[i][m]
                    P.op("pool", lambda e, qz=qz: e.memset(qz[:, :], 0.0), writes=[rqz])
            PT = [(T("PT", [128, 4, 128], BF16), P.nres_("att_PT%d" % i)) for i in range(3)]
            tS = [(T("tS", [128, 128]), P.nres_("att_tS%d" % i)) for i in range(3)]
            rc = [(T("rc", [128, 2]), P.nres_("att_rc%d" % i)) for i in range(2)]
            oT = [(T("oT", [128, 128]), P.nres_("att_oT%d" % i)) for i in range(2)]
            on = [(T("on", [128, 128]), P.nres_("att_on%d" % i)) for i in range(2)]
            sj = [(T("sj", [128, 128], BF16), P.nres_("att_sj%d" % i)) for i in range(2)]
            ssq = [(T("ssq", [128, 1]), P.nres_("att_ssq%d" % i)) for i in range(2)]
            yo = [(T("yo", [128, 512], BF16), P.nres_("att_yo%d" % i)) for i in range(2)]
            npt = 0
            nts = 0
            import os as _os
            STOP = int(_os.environ.get("ATT_STOP", "4"))
            for h in range(NH if STOP > 0 else 0):
                (kt, rkt), (v1, rv1) = KT[h % 2], V1[h % 2]
                qts = QT[h % 2]
                hpc = self.krows // 128
                P.dma("sp", kt[:, 0:L], self.kTg[h // hpc][(h % hpc) * 128:(h % hpc + 1) * 128, :], rkt, writes=[rkt])
                P.dma("sp", kt[:, L:2 * L], self.kT_d[h * 128:(h + 1) * 128, :], rkt, writes=[rkt])
                for m in range(2):
                    P.dma("sp", qts[m][0][m * 64:(m + 1) * 64, :], self.qT_d[h * 128 + m * 64:h * 128 + (m + 1) * 64, :], qts[m][1], writes=[qts[m][1]])
                for c in range(self.nch):
                    nbc = nb // self.nch
                    P.dma("sp", v1[:, c * nbc:(c + 1) * nbc, 0:128], self.vg[c][0:nbc * 128, h * 128:(h + 1) * 128].rearrange("(j p) e -> p j e", p=128), rv1, writes=[rv1])
                P.dma("sp", v1[:, nb:2 * nb, 0:128], self.v_d[:, h * 128:(h + 1) * 128].rearrange("(j p) e -> p j e", p=128), rv1, writes=[rv1])
                cb = tab[:, 15 * 8 + h:15 * 8 + h + 1]
                cbprev = cbp[:, h:h + 1]
                tps = None
                pending = []

                def emit_pv(o_ps, ro, pt, rpt, grp, i, v1=None, rv1=None):
                    for idx, j in enumerate(grp):
                        P.op("pe", lambda e, o_ps=o_ps, pt=pt, idx=idx, j=j, i=i, v1=v1: e.matmul(
                            o_ps[:, 0:130], lhsT=pt[:, idx, :], rhs=v1[:, j, 0:130], start=(j == 0), stop=(j == NKP + i)),
                            reads=[rpt, rv1], writes=[ro], signal=(j == NKP + i))

                def emit_fin(i, ops_, tps_box):
                    (o0, ro0), (o1, ro1) = ops_
                    rct, rrc = rc[i % 2]
                    ot_, rot = oT[i % 2]
                    on_, ron = on[i % 2]
                    sj_, rsj = sj[i % 2]
                    sq_, rsq = ssq[i % 2]
                    P.op("dve", lambda e, rct=rct, o0=o0: e.reciprocal(out=rct[:, 0:1], in_=o0[:, 128:129]), reads=[ro0], writes=[rrc])
                    P.op("dve", lambda e, rct=rct, o1=o1: e.reciprocal(out=rct[:, 1:2], in_=o1[:, 128:129]), reads=[ro1], writes=[rrc])
                    P.op("dve", lambda e, rct=rct: e.tensor_tensor(out=rct[:, 1:2], in0=rct[:, 1:2], in1=nlam[:, :], op=ALU.mult), reads=[rrc, rpar], writes=[rrc])
                    P.op("dve", lambda e, rct=rct, o0=o0, ot_=ot_: e.tensor_scalar(out=ot_[:, :], in0=o0[:, 0:128], scalar1=rct[:, 0:1], scalar2=None, op0=ALU.mult),
                         reads=[ro0, rrc], writes=[rot])
                    P.op("dve", lambda e, rct=rct, o1=o1, ot_=ot_: e.scalar_tensor_tensor(out=ot_[:, :], in0=o1[:, 0:128], scalar=rct[:, 1:2], in1=ot_[:, :], op0=ALU.mult, op1=ALU.add),
                         reads=[ro1, rrc, rot], writes=[rot])
                    P.op("act", lambda e, ot_=ot_, sj_=sj_, sq_=sq_: e.activation(out=sj_[:, :], in_=ot_[:, :], func=AF.Square, accum_out=sq_[:, :]),
                         reads=[rot], writes=[rsj, rsq])
                    P.op("act", lambda e, sq_=sq_: e.activation(out=sq_[:, :], in_=sq_[:, :], func=AF.Sqrt, bias=g["eps_sub"][0][:, :], scale=1.0 / 128),
                         reads=[rsq, g["eps_sub"][1]], writes=[rsq])
                    P.op("dve", lambda e, sq_=sq_: e.reciprocal(out=sq_[:, :], in_=sq_[:, :]), reads=[rsq], writes=[rsq])
                    P.op("dve", lambda e, ot_=ot_, sq_=sq_, on_=on_: e.scalar_tensor_tensor(out=on_[:, :], in0=ot_[:, :], scalar=sq_[:, 0:1], in1=subw[:, :], op0=ALU.mult, op1=ALU.mult),
                         reads=[rot, rsq, rpar], writes=[ron])
                    tps, rtps = self.psb[7]
                    P.op("pe", lambda e, tps=tps, on_=on_, i=i: e.transpose(out=tps[:, (i % 4) * 128:(i % 4 + 1) * 128], in_=on_[:, :], identity=idt[:, :]),
                         reads=[ron, rid], writes=[rtps])
                    if i % 4 == 3 or i == nq - 1:
                        nblk = i % 4 + 1
                        y_, ry = yo[(i // 4) % 2]
                        P.op("act", lambda e, tps=tps, y_=y_, nblk=nblk: e.activation(out=y_[:, 0:nblk * 128], in_=tps[:, 0:nblk * 128], func=AF.Copy), reads=[rtps], writes=[ry])
                        q0 = (i - nblk + 1) * 128
                        P.dma("pool", self.yaT_d[h * 128:(h + 1) * 128, q0:q0 + nblk * 128], y_[:, 0:nblk * 128], ry, reads=[ry])

                def flush():
                    while pending:
                        fn = pending.pop(0)
                        fn()

                for i in range(nq if STOP > 1 else 0):
                    ops_ = []
                    for m in range(2):
                        qt, rqt = qts[m]
                        o_ps, ro = self.bank2(0, 4, "O")
                        ops_.append((o_ps, ro))
                        gi_ = NKP + i
                        for j0 in range(0, gi_ + 1, 4):
                            grp = list(range(j0, min(j0 + 4, gi_ + 1)))
                            isprev = j0 < NKP
                            s_ps, rs = self.bank2(4, 3, "S")
                            for idx, j in enumerate(grp):
                                P.op("pe", lambda e, s_ps=s_ps, idx=idx, j=j, i=i, kt=kt, qt=qt: e.matmul(
                                    s_ps[:, idx * 128:(idx + 1) * 128], lhsT=kt[:, j * 128:(j + 1) * 128],
                                    rhs=qt[:, i * 128:(i + 1) * 128], start=True, stop=True),
                                    reads=[rkt, rqt], writes=[rs], signal=(idx == len(grp) - 1))
                            pt, rpt = PT[npt % 3]
                            npt += 1
                            nfar = len([j for j in grp if j <= gi_ - 2])
                            if nfar:
                                fb_ = cbprev if isprev else cb
                                P.op("act", lambda e, s_ps=s_ps, pt=pt, nfar=nfar, fb_=fb_: e.activation(
                                    out=pt[:, 0:nfar, :], in_=s_ps[:, 0:nfar * 128].rearrange("p (a b) -> p a b", a=nfar), func=AF.Exp, bias=fb_),
                                    reads=[rs, rtab, rpar], writes=[rpt])
                            for idx, j in enumerate(grp):
                                if j <= gi_ - 2:
                                    continue
                                kind = 0 if j == gi_ else 1
                                ts_, rts = tS[nts % 3]
                                nts += 1
                                if j < NKP:
                                    P.op("dve", lambda e, s_ps=s_ps, idx=idx, ts_=ts_, kind=kind, h=h: e.scalar_tensor_tensor(
                                        out=ts_[:, :], in0=s_ps[:, idx * 128:(idx + 1) * 128], scalar=mflag[:, 0:1], in1=bias[:, h, kind, :], op0=ALU.add, op1=ALU.add),
                                        reads=[rs, rbias, rflag], writes=[rts])
                                else:
                                    P.op("dve", lambda e, s_ps=s_ps, idx=idx, ts_=ts_, kind=kind, h=h: e.tensor_tensor(
                                        out=ts_[:, :], in0=s_ps[:, idx * 128:(idx + 1) * 128], in1=bias[:, h, kind, :], op=ALU.add),
                                        reads=[rs, rbias], writes=[rts])
                                P.op("act", lambda e, ts_=ts_, pt=pt, idx=idx: e.activation(out=pt[:, idx, :], in_=ts_[:, :], func=AF.Exp),
                                     reads=[rts], writes=[rpt])
                            flush()
                            pending.append(lambda o_ps=o_ps, ro=ro, pt=pt, rpt=rpt, grp=grp, i=i: emit_pv(o_ps, ro, pt, rpt, grp, i, v1, rv1))
                    pending.append(lambda i=i, ops_=ops_: emit_fin(i, ops_, None))
                flush()
            P.barrier()

    def phase_b1(self, l, x_src):
        P, g = self.P, self.g
        idt, rid = g["c_ident"]
        with contextlib.ExitStack() as st:
            T = lambda name, shape, dt=F32: P.sbuf(name, shape, dt, st)
            xt, rx = T("xt", [128, 4, D]), P.nres_("xt")
            ytoks = [(T("ytok", [128, SSMW]), P.nres_("b1_ytok%d" % i)) for i in range(2)]
            tmp = [(T("gt", [128, SSMW]), P.nres_("b1_gt%d" % i)) for i in range(2)]
            yg = [(T("yg", [128, SSMW]), P.nres_("b1_yg%d" % i)) for i in range(2)]
            ygf, rygf = T("ygf", [128, 8, 512]), P.nres_("b1_ygf")
            ygb, rygb = T("ygb", [128, 8, 512], BF16), P.nres_("b1_ygb")
            yss, ryss = T("yss", [128, 8, 512], BF16), P.nres_("b1_yss")
            yat, ryat = T("yat", [128, 8, 512], BF16), P.nres_("b1_yat")
            gsTs = [(T("gsT", [128, 4, 512], BF16), P.nres_("b1_gs%d" % i)) for i in range(2)]
            gaTs = [(T("gaT", [128, 4, 512], BF16), P.nres_("b1_ga%d" % i)) for i in range(2)]
            mT, rmT = T("mT", [128, 16, 512], BF16), P.nres_("b1_mT")
            sgt = [(T("sgt", [128, 512]), P.nres_("b1_sgt%d" % i)) for i in range(2)]
            m1 = [(T("m1", [128, 512]), P.nres_("b1_m1%d" % i)) for i in range(2)]
            m2 = [(T("m2", [128, 512]), P.nres_("b1_m2%d" % i)) for i in range(2)]
            wbuf = [(T("wB", [128, 16, 512], BF16), P.nres_("wA%d" % i)) for i in range(2)]
            wglu, rwglu = T("wglu", [128, 8, SSMW], BF16), P.nres_("b1_wglu")
            bglu, rbglu = T("bglu", [128, 8]), P.nres_("b1_bglu")
            self.load_wtile(wglu, rwglu, self.wb["w_glu"][l], 0, 8, 0, SSMW)
            P.dma("sp", bglu[:, :], self.i["b_glu"][l].rearrange("(c p) -> p c", p=128), rbglu, writes=[rbglu], allow_slow_non_contiguous=True)
            nw = 0
            for t in range(self.NT):
                t0 = t * 512
                P.dma("sp", xt[:, :, :], x_src[t0:t0 + 512, :].rearrange("(s p) d -> p s d", p=128), rx, writes=[rx])
                P.dma("sp", yat[:, :, :], self.yaT_d.rearrange("(fc p) t -> p fc t", p=128)[:, :, t0:t0 + 512], ryat, writes=[ryat])
                for s in range(4):
                    (tm, rtm), (ygs, rygs) = tmp[s % 2], yg[s % 2]
                    ytk, rytok = ytoks[s % 2]
                    P.dma("sp", ytk[:, :], self.y_d[t0 + s * 128:t0 + (s + 1) * 128, :], rytok, writes=[rytok])
                    P.op("pool", lambda e, tm=tm, ytk=ytk: e.tensor_tensor(out=tm[:, :], in0=ytk[:, :], in1=ytk[:, :], op=ALU.mult), reads=[rytok], writes=[rtm])
                    P.op("dve", lambda e, tm=tm: e.tensor_scalar(out=tm[:, :], in0=tm[:, :], scalar1=0.044715, scalar2=1.0, op0=ALU.mult, op1=ALU.add), reads=[rtm], writes=[rtm])
                    P.op("pool", lambda e, tm=tm, ytk=ytk: e.tensor_tensor(out=tm[:, :], in0=tm[:, :], in1=ytk[:, :], op=ALU.mult), reads=[rtm, rytok], writes=[rtm])
                    P.op("act", lambda e, tm=tm: e.activation(out=tm[:, :], in_=tm[:, :], func=AF.Sigmoid, scale=2.0 * math.sqrt(2.0 / math.pi)), reads=[rtm], writes=[rtm])
                    P.op("dve", lambda e, tm=tm, ygs=ygs, ytk=ytk: e.tensor_tensor(out=ygs[:, :], in0=tm[:, :], in1=ytk[:, :], op=ALU.mult), reads=[rtm, rytok], writes=[rygs])
                    for k4 in range(2):
                        ps, rps = self.bank()
                        for j in range(4):
                            fc = k4 * 4 + j
                            P.op("pe", lambda e, ps=ps, ygs=ygs, fc=fc, j=j: e.transpose(out=ps[:, j * 128:(j + 1) * 128], in_=ygs[:, fc * 128:(fc + 1) * 128], identity=idt[:, :]),
                                 reads=[rygs, rid], writes=[rps], signal=(j == 3))
                        P.op("act", lambda e, ps=ps, k4=k4, s=s: e.activation(out=ygf[:, k4 * 4:k4 * 4 + 4, s * 128:(s + 1) * 128], in_=ps[:, :].rearrange("p (k t) -> p k t", k=4), func=AF.Copy),
                             reads=[rps], writes=[rygf])
                        P.op("dve", lambda e, ps=ps, k4=k4, s=s: e.tensor_copy(out=ygb[:, k4 * 4:k4 * 4 + 4, s * 128:(s + 1) * 128], in_=ps[:, :].rearrange("p (k t) -> p k t", k=4)),
                             reads=[rps], writes=[rygb])
                for cb in range(8):
                    ps, rps = self.bank()
                    self.mm_acc(ps[:, :], rps, [(wglu[:, kc, cb * 128:(cb + 1) * 128], ygb[:, kc, :]) for kc in range(8)], [rwglu, rygb])
                    sg, rsg = sgt[cb % 2]
                    P.op("act", lambda e, ps=ps, sg=sg, cb=cb: e.activation(out=sg[:, :], in_=ps[:, :], func=AF.Sigmoid, bias=bglu[:, cb:cb + 1]), reads=[rps, rbglu], writes=[rsg])
                    P.op("dve", lambda e, sg=sg, cb=cb: e.tensor_tensor(out=yss[:, cb, :], in0=ygf[:, cb, :], in1=sg[:, :], op=ALU.mult), reads=[rsg, rygf], writes=[ryss])
                for c4 in range(4):
                    (ws, rws), (wa, rwa) = wbuf[0], wbuf[1]
                    self.load_wtile(ws, rws, self.wb["w_proj_ssm"][l], 0, 8, c4 * 512, 512)
                    self.load_wtile(wa, rwa, self.wb["w_proj_attn"][l], 0, 8, c4 * 512, 512)
                    (gsT, rgs), (gaT, rga) = gsTs[c4 % 2], gaTs[c4 % 2]
                    P.dma("sp", gsT[:, :, :], self.gsT_d[c4 * 512:(c4 + 1) * 512, t0:t0 + 512].rearrange("(fc p) t -> p fc t", p=128), rgs, writes=[rgs])
                    P.dma("sp", gaT[:, :, :], self.gaT_d[c4 * 512:(c4 + 1) * 512, t0:t0 + 512].rearrange("(fc p) t -> p fc t", p=128), rga, writes=[rga])
                    for j in range(4):
                        cb = c4 * 4 + j
                        ps1, rps1 = self.bank()
                        self.mm_acc(ps1[:, :], rps1, [(ws[:, kc, j * 128:(j + 1) * 128], yss[:, kc, :]) for kc in range(8)], [rws, ryss])
                        ps2, rps2 = self.bank()
                        self.mm_acc(ps2[:, :], rps2, [(wa[:, kc, j * 128:(j + 1) * 128], yat[:, kc, :]) for kc in range(8)], [rwa, ryat])
                        (a1, ra1), (a2, ra2) = m1[cb % 2], m2[cb % 2]
                        P.op("dve", lambda e, ps1=ps1, a1=a1, j=j, gsT=gsT: e.tensor_tensor(out=a1[:, :], in0=ps1[:, :], in1=gsT[:, j, :], op=ALU.mult), reads=[rps1, rgs], writes=[ra1])
                        P.op("dve", lambda e, ps2=ps2, a2=a2, j=j, gaT=gaT: e.tensor_tensor(out=a2[:, :], in0=ps2[:, :], in1=gaT[:, j, :], op=ALU.mult), reads=[rps2, rga], writes=[ra2])
                        P.op("pool", lambda e, a1=a1, a2=a2, cb=cb: e.tensor_tensor(out=mT[:, cb, :], in0=a1[:, :], in1=a2[:, :], op=ALU.add), reads=[ra1, ra2], writes=[rmT])
                for nch in range(4):
                    wt, rwt = wbuf[nch % 2]
                    self.load_wtile(wt, rwt, self.wb["w_out"][l], 0, 16, nch * 512, 512)
                    for s in range(4):
                        ps, rps = self.bank()
                        self.mm_acc(ps[:, :], rps, [(mT[:, kc, s * 128:(s + 1) * 128], wt[:, kc, :]) for kc in range(16)], [rmT, rwt])
                        P.op("dve", lambda e, ps=ps, s=s, nch=nch: e.tensor_tensor(out=xt[:, s, nch * 512:(nch + 1) * 512], in0=ps[:, :], in1=xt[:, s, nch * 512:(nch + 1) * 512], op=ALU.add),
                             reads=[rps, rx], writes=[rx])
                P.dma("pool", self.xa_d[t0:t0 + 512, :].rearrange("(s p) d -> p s d", p=128), xt[:, :, :], rx, reads=[rx])
            P.barrier()

    def phase_b2(self, l, x_dst):
        P, g = self.P, self.g
        with contextlib.ExitStack() as st:
            T = lambda name, shape, dt=F32: P.sbuf(name, shape, dt, st)
            xt, rx = T("xt", [128, 4, D]), P.nres_("xt")
            nbufs = self.norm_bufs(st)
            w2T, rw2 = T("w2T", [128, 16]), P.nres_("w1T")
            P.dma("sp", w2T[:, :], self.i["norm2_w"][l].rearrange("(kc p) -> p kc", p=128), rw2, writes=[rw2], allow_slow_non_contiguous=True)
            hT, rhT = T("hT", [128, 16, 512], BF16), P.nres_("hT")
            aT, raT = T("aT", [128, 44, 512], BF16), P.nres_("b2_aT")
            wbuf = [(T("wF", [128, 22, 512], BF16), P.nres_("wA%d" % i)) for i in range(3)]
            sgt = [(T("sgt", [128, 512]), P.nres_("b1_sgt%d" % i)) for i in range(2)]
            nw = 0
            for t in range(self.NT):
                t0 = t * 512
                P.dma("sp", xt[:, :, :], self.xa_d[t0:t0 + 512, :].rearrange("(s p) d -> p s d", p=128), rx, writes=[rx])
                self.norm_T(nbufs, xt, rx, w2T, rw2, hT, rhT)
                for c in range(DFF // 512):
                    (wg, rwg) = wbuf[nw % 3]
                    (wu, rwu) = wbuf[(nw + 1) % 3]
                    nw += 2
                    self.load_wtile(wg, rwg, self.wb["w_ffn_gate"][l], 0, 16, c * 512, 512)
                    self.load_wtile(wu, rwu, self.wb["w_ffn_up"][l], 0, 16, c * 512, 512)
                    for j in range(4):
                        fb = c * 4 + j
                        psg, rpsg = self.bank()
                        self.mm_acc(psg[:, :], rpsg, [(wg[:, kc, j * 128:(j + 1) * 128], hT[:, kc, :]) for kc in range(16)], [rwg, rhT])
                        psu, rpsu = self.bank()
                        self.mm_acc(psu[:, :], rpsu, [(wu[:, kc, j * 128:(j + 1) * 128], hT[:, kc, :]) for kc in range(16)], [rwu, rhT])
                        sg, rsg = sgt[fb % 2]
                        P.op("act", lambda e, psg=psg, sg=sg: e.activation(out=sg[:, :], in_=psg[:, :], func=AF.Silu), reads=[rpsg], writes=[rsg])
                        P.op("dve", lambda e, psu=psu, sg=sg, fb=fb: e.tensor_tensor(out=aT[:, fb, :], in0=psu[:, :], in1=sg[:, :], op=ALU.mult), reads=[rpsu, rsg], writes=[raT])
                for nch in range(4):
                    (w0, rw0) = wbuf[nw % 3]
                    (w1, rw1) = wbuf[(nw + 1) % 3]
                    nw += 2
                    self.load_wtile(w0, rw0, self.wb["w_ffn_down"][l], 0, 22, nch * 512, 512)
                    self.load_wtile(w1, rw1, self.wb["w_ffn_down"][l], 22, 22, nch * 512, 512)
                    for s in range(4):
                        ps, rps = self.bank()
                        pairs = [(aT[:, kc, s * 128:(s + 1) * 128], w0[:, kc, :]) for kc in range(22)]
                        pairs += [(aT[:, 22 + kc, s * 128:(s + 1) * 128], w1[:, kc, :]) for kc in range(22)]
                        self.mm_acc(ps[:, :], rps, pairs, [raT, rw0, rw1])
                        P.op("dve", lambda e, ps=ps, s=s, nch=nch: e.tensor_tensor(out=xt[:, s, nch * 512:(nch + 1) * 512], in0=ps[:, :], in1=xt[:, s, nch * 512:(nch + 1) * 512], op=ALU.add),
                             reads=[rps, rx], writes=[rx])
                P.dma("pool", x_dst[t0:t0 + 512, :].rearrange("(s p) d -> p s d", p=128), xt[:, :, :], rx, reads=[rx])
            P.barrier()


_CACHE = {}


def kernel(**inputs):
    x = np.ascontiguousarray(np.asarray(inputs["x"], dtype=np.float32))
    B, L, _ = x.shape
    Lh = L // 2
    key = (Lh,)
    if key not in _CACHE:
        _CACHE[key] = Builder(Lh).build()
    nc = _CACHE[key]
    base = {k: np.ascontiguousarray(np.asarray(v, dtype=np.float32)) for k, v in inputs.items() if k != "x"}
    base.update(host_consts())
    in_maps = []
    for b in range(B):
        for half in range(2):
            m = dict(base)
            m["x"] = np.ascontiguousarray(x[b, half * Lh:(half + 1) * Lh])
            m["c_flag"] = np.full((128, 1), float(half), np.float32)
            in_maps.append(m)
    res = run_bass_kernel_spmd(nc, in_maps, core_ids=list(range(2 * B)))
    out = np.empty((B, L, x.shape[2]), np.float32)
    for b in range(B):
        for half in range(2):
            out[b, half * Lh:(half + 1) * Lh] = np.asarray(res.results[2 * b + half]["out"], dtype=np.float32)
    return out
```

```python
import contextlib
import math
import numpy as np
import concourse.bass as bass
import concourse.mybir as mybir
from concourse.bass_utils import run_bass_kernel_spmd

F32 = mybir.dt.float32
BF16 = mybir.dt.bfloat16
I32 = mybir.dt.int32
AF = mybir.ActivationFunctionType
ALU = mybir.AluOpType
AX = mybir.AxisListType

D = 2048
DEPTH = 2
SSMW = 1024
NG = 64
NS = 64
ATW = 1024
NH = 8
INW = 8192
DFF = 5632
RMS_EPS = 1e-6
SUBLN_EPS = 1e-5


class Res:
    __slots__ = ("name", "w", "r", "dsem", "dcnt", "excl")

    def __init__(self, name):
        self.name = name
        self.excl = False
        self.w = None
        self.r = {}
        self.dsem = None
        self.dcnt = 0


class Prog:
    ENG = ("pe", "act", "dve", "pool", "sp")

    def __init__(self, nc, stack):
        self.nc = nc
        self.stack = stack
        self.sems = []
        self.ops = {e: [] for e in self.ENG}
        self.cnt = {e: 0 for e in self.ENG}
        self.seen = {e: {} for e in self.ENG}
        self.esem = {}
        self.latest = {}
        for e in self.ENG:
            self.esem[e] = self.new_sem("s_" + e)
        self.nres = 0
        self.ccsem = None
        self.named = {}
        self.nalloc = 0

    def new_sem(self, name):
        s = self.stack.enter_context(self.nc.semaphore(name))
        self.sems.append(s)
        return len(self.sems) - 1

    def res(self, name=None):
        self.nres += 1
        return Res(name or ("r%d" % self.nres))

    def nres_(self, name):
        if name not in self.named:
            self.named[name] = Res(name)
        return self.named[name]

    def sbuf(self, name, shape, dtype, stack=None):
        self.nalloc += 1
        st = stack if stack is not None else self.stack
        t = st.enter_context(self.nc.sbuf_tensor("%s_%d" % (name, self.nalloc), list(shape), dtype))
        return t

    def psum(self, name, shape, dtype, stack=None):
        st = stack if stack is not None else self.stack
        return st.enter_context(self.nc.psum_tensor(name, list(shape), dtype))

    def _deps(self, eng, reads, writes):
        deps = {}
        for r in reads:
            if r.w is not None:
                s, v = r.w
                if deps.get(s, 0) < v:
                    deps[s] = v
        for w in writes:
            if w.w is not None:
                s, v = w.w
                if deps.get(s, 0) < v:
                    deps[s] = v
            for (s, v) in w.r.values():
                if deps.get(s, 0) < v:
                    deps[s] = v
        waits = []
        seen = self.seen[eng]
        own = self.esem[eng]
        for s, v in deps.items():
            if s == own and v > self.cnt[eng]:
                continue
            if seen.get(s, 0) < v:
                seen[s] = v
                waits.append((s, v))
        return waits

    def op(self, eng, fn, reads=(), writes=(), signal=True):
        if any(r.excl for r in reads):
            writes = list(writes) + [r for r in reads if r.excl and r not in writes]
            reads = [r for r in reads if not r.excl]
        waits = self._deps(eng, reads, writes)
        idx = self.cnt[eng] + 1
        if signal:
            self.cnt[eng] = idx
        ev = (self.esem[eng], idx)
        self.latest[ev[0]] = idx
        self.ops[eng].append((waits, fn, signal))
        for w in writes:
            w.w = ev
            w.r = {}
        for r in reads:
            r.r[ev[0]] = ev

    def dma(self, q, out, in_, sres, reads=(), writes=(), **kw):
        waits = self._deps(q, reads, writes)
        qk = "sw" if q == "pool" else "hw"
        if sres.dsem is None:
            sres.dsem = {}
            sres.dcnt = {}
        if qk not in sres.dsem:
            sres.dsem[qk] = self.new_sem("d%s_%s" % (qk, sres.name))
            sres.dcnt[qk] = 0
        sres.dcnt[qk] += 16
        ev = (sres.dsem[qk], sres.dcnt[qk])
        self.latest[ev[0]] = ev[1]
        sem = self.sems[ev[0]]

        def fn(e, out=out, in_=in_, sem=sem, kw=kw):
            e.dma_start(out=out, in_=in_, **kw).then_inc(sem, 16)
            return None

        self.ops[q].append((waits, fn, False))
        for w in writes:
            w.w = ev
            w.r = {}
        for r in reads:
            r.r[ev[0]] = ev

    def coll(self, kind, in_ap, out_ap, groups, reads=(), writes=()):
        waits = self._deps("pool", reads, writes)
        if self.ccsem is None:
            self.ccsem = self.new_sem("s_cc")
            self.cccnt = 0
        self.cccnt += 1
        ev = (self.ccsem, self.cccnt)
        self.latest[ev[0]] = ev[1]
        sem = self.sems[ev[0]]

        def fn(e):
            e.collective_compute(kind, ALU.bypass, replica_groups=groups, ins=[in_ap], outs=[out_ap]).then_inc(sem, 1)
            return None

        self.ops["pool"].append((waits, fn, False))
        for w in writes:
            w.w = ev
            w.r = {}
        for r in reads:
            r.r[ev[0]] = ev

    def barrier(self):
        for e in self.ENG:
            waits = []
            seen = self.seen[e]
            for s, v in self.latest.items():
                if seen.get(s, 0) < v:
                    seen[s] = v
                    waits.append((s, v))
            if waits:
                self.ops[e].append((waits, None, False))

    def emit(self):
        nc = self.nc
        sems = self.sems
        with nc.Block() as block:
            def run(name, e):
                own = sems[self.esem[name]]
                for waits, fn, signal in self.ops[name]:
                    for s, v in waits:
                        e.wait_ge(sems[s], v)
                    if fn is None:
                        continue
                    ins = fn(e)
                    if signal:
                        ins.then_inc(own, 1)

            @block.tensor
            def _(e):
                run("pe", e)

            @block.scalar
            def _(e):
                run("act", e)

            @block.vector
            def _(e):
                run("dve", e)

            @block.gpsimd
            def _(e):
                run("pool", e)

            @block.sync
            def _(e):
                run("sp", e)


WSPEC = [
    ("w_in", D, INW), ("w_glu", SSMW, SSMW), ("w_proj_ssm", SSMW, D), ("w_proj_attn", ATW, D),
    ("w_out", D, D), ("w_ffn_gate", D, DFF), ("w_ffn_up", D, DFF), ("w_ffn_down", DFF, D),
]
SMALL = [("norm1_w", [DEPTH, D]), ("lam_re", [DEPTH, NG, NS]), ("lam_im", [DEPTH, NG, NS]),
         ("log_step", [DEPTH, NG]), ("ssm_b_re", [DEPTH, NG, NS, 16]), ("ssm_b_im", [DEPTH, NG, NS, 16]),
         ("ssm_c_re", [DEPTH, NG, 16, NS]), ("ssm_c_im", [DEPTH, NG, 16, NS]), ("ssm_d", [DEPTH, SSMW]),
         ("b_glu", [DEPTH, SSMW]), ("q_norm_w", [DEPTH, 64]), ("k_norm_w", [DEPTH, 64]),
         ("lambda_q1", [DEPTH, 64]), ("lambda_k1", [DEPTH, 64]), ("lambda_q2", [DEPTH, 64]),
         ("lambda_k2", [DEPTH, 64]), ("subln_w", [DEPTH, 128]), ("rel_bias", [32, NH]),
         ("norm2_w", [DEPTH, D])]
CONSTS = [("c_ident", [128, 128]), ("c_bones", [128, 128]), ("c_tmask", [128, 128]), ("c_reld", [128, 128]), ("c_swap", [128, 128]), ("c_flag", [128, 1])]


def host_consts():
    ident = np.eye(128, dtype=np.float32)
    bones = np.kron(np.eye(2, dtype=np.float32), np.ones((64, 64), np.float32))
    jj = np.arange(128) // 16
    tmask = (jj[None, :] >= jj[:, None]).astype(np.float32)
    reld = (np.arange(128)[:, None] - np.arange(128)[None, :]).astype(np.float32)
    swap = np.roll(np.eye(128, dtype=np.float32), 64, axis=1)
    return {"c_ident": ident, "c_bones": bones, "c_tmask": tmask, "c_reld": reld, "c_swap": swap,
            "c_flag": np.zeros((128, 1), np.float32)}


class Builder:
    def __init__(self, L, nlayers=DEPTH, dbg=False, phases=None, nsh=8):
        self.L = L
        self.NT = L // 512
        self.nlayers = nlayers
        self.dbg = dbg
        self.phases = phases
        self.nc = bass.Bass("TRN2", target_bir_lowering=False)
        self.stack = contextlib.ExitStack()
        self.P = Prog(self.nc, self.stack)
        nc = self.nc
        self.i = {}
        self.i["x"] = self.din("x", [L, D])
        for n, k, m in WSPEC:
            self.i[n] = self.din(n, [DEPTH, k, m])
        for n, sh in SMALL + CONSTS:
            self.i[n] = self.din(n, sh)
        self.out = self.dout("out", [L, D])
        self.wb = {n: [self.dscr("wb_%s_%d" % (n, l), [k, m], BF16) for l in range(nlayers)] for n, k, m in WSPEC}
        sk = self.dout if dbg else self.dscr
        self.u_d = sk("u_d", [L, SSMW], BF16)
        self.nch = max(1, L // 1024)
        self.krows = ATW // self.nch
        self.ug = [self.dscr("ug%d" % c, [2 * 1024, SSMW], BF16) for c in range(self.nch)]
        self.vg = [self.dscr("vg%d" % c, [2 * 1024, ATW], BF16) for c in range(self.nch)]
        self.kTg = [self.dscr("kTg%d" % c, [2 * self.krows, L], BF16) for c in range(self.nch)]
        self.groups = [[0, 1], [2, 3], [4, 5], [6, 7]]
        skq = self.din if dbg == "attin" else sk
        self.qT_d = skq("qT_d", [ATW, L], BF16)
        self.kT_d = skq("kT_d", [ATW, L], BF16)
        self.v_d = skq("v_d", [L, ATW], BF16)
        self.gsT_d = sk("gsT_d", [D, L], BF16)
        self.gaT_d = sk("gaT_d", [D, L], BF16)
        self.y_d = sk("y_d", [L, SSMW], F32)
        self.yaT_d = sk("yaT_d", [ATW, L], BF16)
        self.xa_d = sk("xa_d", [L, D], F32)
        self.xb_d = self.dscr("xb_d", [L, D], F32)
        self.rdram = {}
        self.wq = []
        self.wit = 0

    def din(self, name, shape, dtype=F32):
        return self.nc.dram_tensor(name, list(shape), dtype, kind="ExternalInput").ap()

    def dscr(self, name, shape, dtype):
        return self.nc.dram_tensor(name, list(shape), dtype, kind="Internal").ap()

    def dout(self, name, shape, dtype=F32):
        return self.nc.dram_tensor(name, list(shape), dtype, kind="ExternalOutput").ap()

    def setup_globals(self):
        P = self.P
        g = self.g = {}
        self.psb = []
        for b in range(8):
            self.psb.append((P.psum("psb%d" % b, [128, 512], F32), P.nres_("psb%d" % b)))
            self.psb[-1][1].excl = True
        self.psi = 0
        for n in ("c_ident", "c_bones", "c_tmask", "c_reld", "c_swap"):
            t = P.sbuf(n, [128, 128], F32)
            r = P.nres_(n)
            P.dma("sp", t[:, :], self.i[n][:, :], r, writes=[r])
            g[n] = (t, r)
        t = P.sbuf("c_flag", [128, 1], F32)
        r = P.nres_("c_flag")
        P.dma("sp", t[:, :], self.i["c_flag"][:, :], r, writes=[r])
        g["c_flag"] = (t, r)
        t2 = P.sbuf("mflag", [128, 1], F32)
        P.op("dve", lambda e, t=t, t2=t2: e.tensor_scalar(out=t2[:, :], in0=t[:, :], scalar1=-1.0, scalar2=30000.0, op0=ALU.add, op1=ALU.mult), reads=[r], writes=[r])
        g["mflag"] = (t2, r)
        t = P.sbuf("bones_b", [128, 128], BF16)
        r = P.nres_("bones_b")
        P.op("dve", lambda e, t=t: e.tensor_copy(out=t[:, :], in_=g["c_bones"][0][:, :]), reads=[g["c_bones"][1]], writes=[r])
        g["bones_b"] = (t, r)
        for nm, val in (("eps_rms", RMS_EPS), ("eps_sub", SUBLN_EPS), ("halfpi", math.pi / 2), ("zero", 0.0)):
            t = P.sbuf(nm, [128, 1], F32)
            r = P.nres_(nm)
            P.op("pool", lambda e, t=t, val=val: e.memset(t[:, :], val), writes=[r])
            g[nm] = (t, r)

    def bank(self):
        b = self.psb[self.psi % 8]
        self.psi += 1
        return b

    def w_specs(self):
        out = []
        for l in range(self.nlayers):
            for n, K, N in WSPEC:
                if n not in self.wb:
                    continue
                CW = 4096
                ncc = (N + CW - 1) // CW
                cw = N // ncc
                for kt in range(K // 128):
                    for c in range(ncc):
                        out.append((l, n, kt, c, cw))
        return out

    def w_chunk(self, bufs, spec, cast_eng, store_q):
        P = self.P
        l, n, kt, c, cw = spec
        (f, rf), (b, rb) = bufs
        src = self.i[n]
        dst = self.wb[n][l]
        P.dma("sp", f[:, 0:cw], src[l, kt * 128:(kt + 1) * 128, c * cw:(c + 1) * cw], rf, writes=[rf])
        if cast_eng == "act":
            P.op("act", lambda e, f=f, b=b, cw=cw: e.activation(out=b[:, 0:cw], in_=f[:, 0:cw], func=AF.Copy), reads=[rf], writes=[rb])
        else:
            P.op(cast_eng, lambda e, f=f, b=b, cw=cw: e.tensor_copy(out=b[:, 0:cw], in_=f[:, 0:cw]), reads=[rf], writes=[rb])
        P.dma(store_q, dst[kt * 128:(kt + 1) * 128, c * cw:(c + 1) * cw], b[:, 0:cw], rb, reads=[rb], writes=[])

    def w_bufs(self, st, tag):
        P = self.P
        NB = 3
        fb = [(P.sbuf("wf", [128, 4096], F32, st), P.nres_("wf%s%d" % (tag, i))) for i in range(NB)]
        bb = [(P.sbuf("wbb", [128, 4096], BF16, st), P.nres_("wbb%s%d" % (tag, i))) for i in range(NB)]
        return list(zip(fb, bb))

    def phase_w(self):
        P = self.P
        specs = self.w_specs()
        first = [sp_ for sp_ in specs if sp_[0] == 0 and sp_[1] == "w_in"]
        self.wq = [sp_ for sp_ in specs if not (sp_[0] == 0 and sp_[1] == "w_in")]
        with contextlib.ExitStack() as st:
            bufs = self.w_bufs(st, "a")
            ce = ("dve", "pool", "act")
            for it, sp_ in enumerate(first):
                self.w_chunk(bufs[it % 3], sp_, ce[it % 3], "act")
            P.barrier()

    def w_pump(self, bufs, n):
        while n > 0 and self.wq:
            sp_ = self.wq.pop(0)
            self.w_chunk(bufs[self.wit % 3], sp_, "pool", "pool")
            self.wit += 1
            n -= 1

    def norm_T(self, st_bufs, xt, rx, wT, rwT, hT, rhT):
        P, g = self.P, self.g
        junk, rjunk, ssq, rssq, rstd, rrstd, xs = st_bufs
        for s in range(4):
            P.op("act", lambda e, s=s: e.activation(out=junk[:, :], in_=xt[:, s, :], func=AF.Square, accum_out=ssq[:, s:s + 1]),
                 reads=[rx], writes=[rjunk, rssq])
        P.op("act", lambda e: e.activation(out=rstd[:, :], in_=ssq[:, :], func=AF.Sqrt, bias=g["eps_rms"][0][:, :], scale=1.0 / D),
             reads=[rssq, g["eps_rms"][1]], writes=[rrstd])
        P.op("dve", lambda e: e.reciprocal(out=rstd[:, :], in_=rstd[:, :]), reads=[rrstd], writes=[rrstd])
        idt, rid = g["c_ident"]
        for s in range(4):
            xs_t, rxs = xs[s % 2]
            P.op("act", lambda e, s=s, xs_t=xs_t: e.activation(out=xs_t[:, :], in_=xt[:, s, :], func=AF.Copy, scale=rstd[:, s:s + 1]),
                 reads=[rx, rrstd], writes=[rxs])
            for k4 in range(4):
                ps, rps = self.bank()
                for j in range(4):
                    kc = k4 * 4 + j
                    P.op("pe", lambda e, ps=ps, xs_t=xs_t, kc=kc, j=j: e.transpose(out=ps[:, j * 128:(j + 1) * 128], in_=xs_t[:, kc * 128:(kc + 1) * 128], identity=idt[:, :]),
                         reads=[rxs, rid], writes=[rps], signal=(j == 3))
                P.op("dve", lambda e, ps=ps, k4=k4, s=s: e.tensor_tensor(
                    out=hT[:, k4 * 4:k4 * 4 + 4, s * 128:(s + 1) * 128],
                    in0=ps[:, :].rearrange("p (k t) -> p k t", k=4),
                    in1=wT[:, k4 * 4:k4 * 4 + 4].unsqueeze(2).broadcast_to([128, 4, 128]), op=ALU.mult),
                    reads=[rps, rwT], writes=[rhT])

    def norm_bufs(self, st):
        P = self.P
        junk = P.sbuf("junk", [128, D], BF16, st)
        ssq = P.sbuf("ssq", [128, 4], F32, st)
        rstd = P.sbuf("rstd", [128, 4], F32, st)
        xs = [(P.sbuf("xs", [128, D], F32, st), P.nres_("xs%d" % i)) for i in range(2)]
        return (junk, P.nres_("junk"), ssq, P.nres_("ssq"), rstd, P.nres_("rstd"), xs)

    def load_wtile(self, wt, rwt, src, k0, kcn, n0, ncols, q="sp"):
        self.P.dma(q, wt[:, 0:kcn, 0:ncols],
                   src[k0 * 128:(k0 + kcn) * 128, n0:n0 + ncols].rearrange("(kc p) n -> p kc n", p=128),
                   rwt, writes=[rwt])

    def mm_acc(self, ps_ap, rps, pairs, reads):
        n = len(pairs)
        for i, (lhsT, rhs) in enumerate(pairs):
            self.P.op("pe", lambda e, lhsT=lhsT, rhs=rhs, i=i: e.matmul(ps_ap, lhsT=lhsT, rhs=rhs, start=(i == 0), stop=(i == n - 1)),
                      reads=reads, writes=[rps], signal=(i == n - 1))

    def phase_a(self, l, x_src):
        P, g = self.P, self.g
        L = self.L
        with contextlib.ExitStack() as st:
            xt = P.sbuf("xt", [128, 4, D], F32, st)
            rx = P.nres_("xt")
            nbufs = self.norm_bufs(st)
            w1T = P.sbuf("w1T", [128, 16], F32, st)
            rw1 = P.nres_("w1T")
            P.dma("sp", w1T[:, :], self.i["norm1_w"][l].rearrange("(kc p) -> p kc", p=128), rw1, writes=[rw1],
                  allow_slow_non_contiguous=True)
            wqk = P.sbuf("wqk", [128, 2], F32, st)
            rwqk = P.nres_("wqk")
            for ci, nm in enumerate(("q_norm_w", "k_norm_w")):
                for m in range(2):
                    P.dma("sp", wqk[m * 64:(m + 1) * 64, ci:ci + 1], self.i[nm][l:l + 1, :].rearrange("o d -> d o"), rwqk,
                          writes=[rwqk], allow_slow_non_contiguous=True)
            P.op("dve", lambda e: e.tensor_scalar(out=wqk[:, 0:1], in0=wqk[:, 0:1], scalar1=0.125, scalar2=None, op0=ALU.mult),
                 reads=[rwqk], writes=[rwqk])
            hT = P.sbuf("hT", [128, 16, 512], BF16, st)
            rhT = P.nres_("hT")
            wbuf = [(P.sbuf("wA", [128, 16, 512], BF16, st), P.nres_("wA%d" % i)) for i in range(3)]
            ut = [(P.sbuf("ut", [128, 4, 512], BF16, st), P.nres_("ut%d" % i)) for i in range(2)]
            vt = [(P.sbuf("vt", [128, 4, 512], BF16, st), P.nres_("vt%d" % i)) for i in range(2)]
            ot = [(P.sbuf("ot", [128, 512], BF16, st), P.nres_("ot%d" % i)) for i in range(3)]
            sq = [(P.sbuf("sq", [128, 512], BF16, st), P.nres_("sq%d" % i)) for i in range(2)]
            rt = [(P.sbuf("rt", [128, 512], F32, st), P.nres_("rt%d" % i)) for i in range(2)]
            bones, rbones = g["bones_b"]
            wsrc = self.wb["w_in"][l]
            nwl = 0
            oi = 0
            for t in range(self.NT):
                t0 = t * 512
                P.dma("sp", xt[:, :, :], x_src[t0:t0 + 512, :].rearrange("(s p) d -> p s d", p=128), rx, writes=[rx])
                self.norm_T(nbufs, xt, rx, w1T, rw1, hT, rhT)
                for c in range(16):
                    wt, rwt = wbuf[nwl % 3]
                    nwl += 1
                    self.load_wtile(wt, rwt, wsrc, 0, 16, c * 512, 512)
                    if c in (0, 1, 6, 7):
                        isu = c < 2
                        stg, rstg = (ut if isu else vt)[c % 2]
                        for s in range(4):
                            ps, rps = self.bank()
                            self.mm_acc(ps[:, :], rps, [(hT[:, kc, s * 128:(s + 1) * 128], wt[:, kc, :]) for kc in range(16)], [rhT, rwt])
                            if s % 2 == 0:
                                P.op("act", lambda e, ps=ps, stg=stg, s=s: e.activation(out=stg[:, s, :], in_=ps[:, :], func=AF.Copy),
                                     reads=[rps], writes=[rstg])
                            else:
                                P.op("dve", lambda e, ps=ps, stg=stg, s=s: e.tensor_copy(out=stg[:, s, :], in_=ps[:, :]),
                                     reads=[rps], writes=[rstg])
                        dst = self.u_d if isu else self.v_d
                        cc = c if isu else c - 6
                        P.dma("pool", dst[t0:t0 + 512, cc * 512:(cc + 1) * 512].rearrange("(s p) n -> p s n", p=128), stg[:, :, :], rstg,
                              reads=[rstg])
                    else:
                        for j in range(4):
                            ps, rps = self.bank()
                            self.mm_acc(ps[:, :], rps, [(wt[:, kc, j * 128:(j + 1) * 128], hT[:, kc, :]) for kc in range(16)], [rhT, rwt])
                            o, ro = ot[oi % 3]
                            oi += 1
                            if c < 6:
                                isq = c < 4
                                sqt, rsq = sq[oi % 2]
                                rtt, rrt = rt[oi % 2]
                                P.op("act", lambda e, ps=ps, sqt=sqt: e.activation(out=sqt[:, :], in_=ps[:, :], func=AF.Square), reads=[rps], writes=[rsq])
                                ps2, rps2 = self.bank()
                                self.mm_acc(ps2[:, :], rps2, [(bones[:, :], sqt[:, :])], [rbones, rsq])
                                P.op("act", lambda e, ps2=ps2, rtt=rtt: e.activation(out=rtt[:, :], in_=ps2[:, :], func=AF.Sqrt, bias=g["eps_rms"][0][:, :], scale=1.0 / 64),
                                     reads=[rps2, g["eps_rms"][1]], writes=[rrt])
                                P.op("dve", lambda e, rtt=rtt: e.reciprocal(out=rtt[:, :], in_=rtt[:, :]), reads=[rrt], writes=[rrt])
                                ci = 0 if isq else 1
                                P.op("dve", lambda e, ps=ps, rtt=rtt, o=o, ci=ci: e.scalar_tensor_tensor(
                                    out=o[:, :], in0=ps[:, :], scalar=wqk[:, ci:ci + 1], in1=rtt[:, :], op0=ALU.mult, op1=ALU.mult),
                                    reads=[rps, rrt, rwqk], writes=[ro])
                                dst = self.qT_d if isq else self.kT_d
                                r0 = ((c - 2) if isq else (c - 4)) * 512 + j * 128
                            else:
                                P.op("act", lambda e, ps=ps, o=o: e.activation(out=o[:, :], in_=ps[:, :], func=AF.Sigmoid), reads=[rps], writes=[ro])
                                dst = self.gsT_d if c < 12 else self.gaT_d
                                r0 = ((c - 8) if c < 12 else (c - 12)) * 512 + j * 128
                            P.dma("pool", dst[r0:r0 + 128, t0:t0 + 512], o[:, :], ro, reads=[ro])
            P.barrier()

    def build(self):
        ph = self.phases
        self.setup_globals()
        if ph is None or "att" in ph:
            self.setup_attn()
        if ph is None or "w" in ph:
            self.phase_w()
        for l in range(self.nlayers):
            x_src = self.i["x"] if l == 0 else self.xb_d
            x_dst = self.out if l == self.nlayers - 1 else self.xb_d
            if ph is None or "a" in ph:
                self.phase_a(l, x_src)
            if ph is None or "xch" in ph:
                self.phase_xch()
            if ph is None or "ssm" in ph:
                self.phase_ssm(l)
            if ph is None or "att" in ph:
                self.phase_att(l)
            if ph is None or "b1" in ph:
                self.phase_b1(l, x_src)
            if ph is None or "b2" in ph:
                self.phase_b2(l, x_dst)
        self.P.barrier()
        self.P.emit()
        return self.nc

    def phase_xch(self):
        P = self.P
        r = P.nres_("xch")
        tr = min(1024, self.L)
        for c in range(self.nch):
            P.coll("AllGather", self.u_d[c * tr:(c + 1) * tr, :], self.ug[c][:, :], self.groups, reads=[r], writes=[r])
            P.coll("AllGather", self.v_d[c * tr:(c + 1) * tr, :], self.vg[c][:, :], self.groups, reads=[r], writes=[r])
            P.coll("AllGather", self.kT_d[c * self.krows:(c + 1) * self.krows, :], self.kTg[c][:, :], self.groups, reads=[r], writes=[r])
        P.barrier()

    def cmul(self, eng, out_re, out_im, a_re, a_im, b_re, b_im, tmp, rres, wres, neg_im=False):
        P = self.P
        t1, t2 = tmp
        ops = [
            (t1, a_re, b_re, ALU.mult), (t2, a_im, b_im, ALU.mult), (out_re, t1, t2, ALU.subtract),
            (t1, a_re, b_im, ALU.mult), (t2, a_im, b_re, ALU.mult), (out_im, t1, t2, ALU.add),
        ]
        for o, x, y, op in ops:
            P.op(eng, lambda e, o=o, x=x, y=y, op=op: e.tensor_tensor(out=o, in0=x, in1=y, op=op), reads=rres, writes=wres)
        if neg_im:
            P.op(eng, lambda e, o=out_im: e.tensor_scalar(out=o, in0=o, scalar1=-1.0, scalar2=None, op0=ALU.mult), reads=wres, writes=wres)

    def phase_ssm(self, l):
        P, g = self.P, self.g
        L = self.L
        NBLK = 2 * L // 8
        NB2 = NBLK // 2
        bp = min(128, NBLK)
        nbt = NBLK // bp
        nbt2 = nbt // 2
        assert nbt % 2 == 0
        flag, rflag = g["c_flag"]
        nsteps = int(math.ceil(math.log2(NBLK)))
        idt, rid = g["c_ident"]
        swp, rswp = g["c_swap"]
        tmask, rtm = g["c_tmask"]
        with contextlib.ExitStack() as st:
            rp = P.nres_("ssm_par")
            def T(name, shape, dt=F32):
                return P.sbuf(name, shape, dt, st)
            lre, lim, dtt = T("lre", [128, 64]), T("lim", [128, 64]), T("dtt", [128, 64])
            for hf in range(2):
                hs = slice(hf * 64, hf * 64 + 64)
                P.dma("sp", lre[hs, :], self.i["lam_re"][l].rearrange("g p -> p g"), rp, writes=[rp], allow_slow_non_contiguous=True)
                P.dma("sp", lim[hs, :], self.i["lam_im"][l].rearrange("g p -> p g"), rp, writes=[rp], allow_slow_non_contiguous=True)
            P.dma("sp", dtt[:, :], self.i["log_step"][l].partition_broadcast(128), rp, writes=[rp])
            Bre, Bim = T("Bre", [128, 64, 16]), T("Bim", [128, 64, 16])
            for hf in range(2):
                hs = slice(hf * 64, hf * 64 + 64)
                P.dma("sp", Bre[hs, :, :], self.i["ssm_b_re"][l].rearrange("g p c -> p g c"), rp, writes=[rp])
                P.dma("sp", Bim[hs, :, :], self.i["ssm_b_im"][l].rearrange("g p c -> p g c"), rp, writes=[rp])
            Dcol = T("Dcol", [128, 64])
            for j in range(8):
                P.dma("sp", Dcol[16 * j:16 * j + 16, :], self.i["ssm_d"][l].rearrange("(g c) -> c g", c=16), rp, writes=[rp],
                      allow_slow_non_contiguous=True)
            Cre, Cim = T("Cre", [128, 64, 16]), T("Cim", [128, 64, 16])
            crow = T("crow", [128, 128])
            for nm, dstt in (("ssm_c_re", Cre), ("ssm_c_im", Cim)):
                src = self.i[nm][l].rearrange("g c p -> (g c) p")
                for k in range(8):
                    P.dma("sp", crow[:, 0:64], src[k * 128:(k + 1) * 128, :], rp, reads=[rp], writes=[rp])
                    P.dma("sp", crow[:, 64:128], src[k * 128:(k + 1) * 128, :], rp, reads=[rp], writes=[rp])
                    ps, rps = self.bank()
                    P.op("pe", lambda e, ps=ps: e.transpose(out=ps[:, 0:128], in_=crow[:, :], identity=idt[:, :]), reads=[rp, rid], writes=[rps])
                    P.op("dve", lambda e, ps=ps, dstt=dstt, k=k: e.tensor_copy(out=dstt[:, k * 8:(k + 1) * 8, :], in_=ps[:, 0:128].rearrange("p (g c) -> p g c", c=16)),
                         reads=[rps], writes=[rp])
            V = lambda fn, rd=(rp,), wr=(rp,): P.op("dve", fn, reads=list(rd), writes=list(wr))
            A = lambda fn: P.op("act", fn, reads=[rp, g["halfpi"][1]], writes=[rp])
            lr, x1, mag, ang, tq, r_, m1 = [T(n, [128, 64]) for n in ("lr", "x1", "mag", "ang", "tq", "r_", "m1")]
            ti = T("ti", [128, 64], I32)
            sn, cs, ar, ai = [T(n, [128, 64]) for n in ("sn", "cs", "ar", "ai")]
            V(lambda e: e.tensor_scalar(out=lr[:, :], in0=lre[:, :], scalar1=-1e-4, scalar2=None, op0=ALU.min))
            A(lambda e: e.activation(out=dtt[:, :], in_=dtt[:, :], func=AF.Exp))
            V(lambda e: e.tensor_tensor(out=x1[:, :], in0=lr[:, :], in1=dtt[:, :], op=ALU.mult))
            A(lambda e: e.activation(out=mag[:, :], in_=x1[:, :], func=AF.Exp))
            V(lambda e: e.tensor_tensor(out=ang[:, :], in0=lim[:, :], in1=dtt[:, :], op=ALU.mult))
            V(lambda e: e.tensor_scalar(out=tq[:, :], in0=ang[:, :], scalar1=1.0 / (2 * math.pi), scalar2=0.5, op0=ALU.mult, op1=ALU.add))
            V(lambda e: e.tensor_copy(out=ti[:, :], in_=tq[:, :]))
            V(lambda e: e.tensor_copy(out=tq[:, :], in_=ti[:, :]))
            V(lambda e: e.scalar_tensor_tensor(out=r_[:, :], in0=tq[:, :], scalar=-2 * math.pi, in1=ang[:, :], op0=ALU.mult, op1=ALU.add))
            for thr, opc, add in ((-math.pi, ALU.is_lt, 2 * math.pi), (math.pi, ALU.is_gt, -2 * math.pi),
                                  (-math.pi, ALU.is_lt, 2 * math.pi), (math.pi, ALU.is_gt, -2 * math.pi)):
                V(lambda e, thr=thr, opc=opc: e.tensor_single_scalar(out=m1[:, :], in_=r_[:, :], scalar=thr, op=opc))
                V(lambda e, add=add: e.scalar_tensor_tensor(out=r_[:, :], in0=m1[:, :], scalar=add, in1=r_[:, :], op0=ALU.mult, op1=ALU.add))
            V(lambda e: e.tensor_scalar(out=r_[:, :], in0=r_[:, :], scalar1=math.pi, scalar2=-math.pi, op0=ALU.min, op1=ALU.max))
            A(lambda e: e.activation(out=sn[:, :], in_=r_[:, :], func=AF.Sin))
            V(lambda e: e.tensor_scalar(out=m1[:, :], in0=r_[:, :], scalar1=-1.0, scalar2=None, op0=ALU.mult))
            V(lambda e: e.tensor_tensor(out=m1[:, :], in0=m1[:, :], in1=r_[:, :], op=ALU.max))
            A(lambda e: e.activation(out=cs[:, :], in_=m1[:, :], func=AF.Sin, bias=g["halfpi"][0][:, :], scale=-1.0))
            V(lambda e: e.tensor_tensor(out=ar[:, :], in0=mag[:, :], in1=cs[:, :], op=ALU.mult))
            V(lambda e: e.tensor_tensor(out=ai[:, :], in0=mag[:, :], in1=sn[:, :], op=ALU.mult))
            den, nr, fr, fi, t1, t2 = [T(n, [128, 64]) for n in ("den", "nr", "fr", "fi", "t1", "t2")]
            V(lambda e: e.tensor_tensor(out=den[:, :], in0=lr[:, :], in1=lr[:, :], op=ALU.mult))
            V(lambda e: e.tensor_tensor(out=t1[:, :], in0=lim[:, :], in1=lim[:, :], op=ALU.mult))
            V(lambda e: e.tensor_tensor(out=den[:, :], in0=den[:, :], in1=t1[:, :], op=ALU.add))
            V(lambda e: e.reciprocal(out=den[:, :], in_=den[:, :]))
            V(lambda e: e.tensor_scalar(out=nr[:, :], in0=ar[:, :], scalar1=-1.0, scalar2=None, op0=ALU.add))
            V(lambda e: e.tensor_tensor(out=t1[:, :], in0=nr[:, :], in1=lr[:, :], op=ALU.mult))
            V(lambda e: e.tensor_tensor(out=t2[:, :], in0=ai[:, :], in1=lim[:, :], op=ALU.mult))
            V(lambda e: e.tensor_tensor(out=t1[:, :], in0=t1[:, :], in1=t2[:, :], op=ALU.add))
            V(lambda e: e.tensor_tensor(out=fr[:, :], in0=t1[:, :], in1=den[:, :], op=ALU.mult))
            V(lambda e: e.tensor_tensor(out=t1[:, :], in0=ai[:, :], in1=lr[:, :], op=ALU.mult))
            V(lambda e: e.tensor_tensor(out=t2[:, :], in0=nr[:, :], in1=lim[:, :], op=ALU.mult))
            V(lambda e: e.tensor_tensor(out=t1[:, :], in0=t1[:, :], in1=t2[:, :], op=ALU.subtract))
            V(lambda e: e.tensor_tensor(out=fi[:, :], in0=t1[:, :], in1=den[:, :], op=ALU.mult))
            Bbr, Bbi = T("Bbr", [128, 64, 16]), T("Bbi", [128, 64, 16])
            tb1, tb2 = T("tb1", [128, 64, 16]), T("tb2", [128, 64, 16])
            bc16 = lambda a: a[:, :].unsqueeze(2).broadcast_to([128, 64, 16])
            self.cmul("dve", Bbr[:, :, :], Bbi[:, :, :], bc16(fr), bc16(fi), Bre[:, :, :], Bim[:, :, :], (tb1[:, :, :], tb2[:, :, :]), [rp], [rp])
            PWr, PWi = T("PWr", [128, 64, 9]), T("PWi", [128, 64, 9])
            PIr, PIi = T("PIr", [128, 64, 8]), T("PIi", [128, 64, 8])
            PRr, PRi = T("PRr", [128, 64, 8]), T("PRi", [128, 64, 8])
            air, aii = T("air", [128, 64]), T("aii", [128, 64])
            V(lambda e: e.tensor_tensor(out=t1[:, :], in0=ar[:, :], in1=ar[:, :], op=ALU.mult))
            V(lambda e: e.tensor_tensor(out=t2[:, :], in0=ai[:, :], in1=ai[:, :], op=ALU.mult))
            V(lambda e: e.tensor_tensor(out=t1[:, :], in0=t1[:, :], in1=t2[:, :], op=ALU.add))
            V(lambda e: e.reciprocal(out=t1[:, :], in_=t1[:, :]))
            V(lambda e: e.tensor_tensor(out=air[:, :], in0=ar[:, :], in1=t1[:, :], op=ALU.mult))
            V(lambda e: e.scalar_tensor_tensor(out=aii[:, :], in0=ai[:, :], scalar=-1.0, in1=t1[:, :], op0=ALU.mult, op1=ALU.mult))
            for (Xr, Xi, br_, bi_, n) in ((PWr, PWi, ar, ai, 9), (PIr, PIi, air, aii, 8)):
                V(lambda e, Xr=Xr: e.memset(Xr[:, :, 0:1], 1.0))
                V(lambda e, Xi=Xi: e.memset(Xi[:, :, 0:1], 0.0))
                for k in range(1, n):
                    self.cmul("dve", Xr[:, :, k], Xi[:, :, k], Xr[:, :, k - 1], Xi[:, :, k - 1], br_[:, :], bi_[:, :], (t1[:, :], t2[:, :]), [rp], [rp])
            for j in range(8):
                V(lambda e, j=j: e.tensor_copy(out=PRr[:, :, j], in_=PWr[:, :, 7 - j]))
                V(lambda e, j=j: e.tensor_copy(out=PRi[:, :, j], in_=PWi[:, :, 7 - j]))
            APr, APi = T("APr", [128, 64, nsteps]), T("APi", [128, 64, nsteps])
            V(lambda e: e.tensor_copy(out=APr[:, :, 0], in_=PWr[:, :, 8]))
            V(lambda e: e.tensor_copy(out=APi[:, :, 0], in_=PWi[:, :, 8]))
            for k in range(1, nsteps):
                self.cmul("dve", APr[:, :, k], APi[:, :, k], APr[:, :, k - 1], APi[:, :, k - 1], APr[:, :, k - 1], APi[:, :, k - 1], (t1[:, :], t2[:, :]), [rp], [rp])
            V(lambda e: e.tensor_scalar(out=APi[64:128, :, :], in0=APi[64:128, :, :], scalar1=-1.0, scalar2=None, op0=ALU.mult))
            GB = 8
            FW = GB * 16
            Bjr, Bji, BEr, BEi = [T(n, [128, GB, 8, 16]) for n in ("Bjr", "Bji", "BEr", "BEi")]
            Crr, Cri = T("Crr", [128, GB, 9, 16]), T("Cri", [128, GB, 9, 16])
            tg1, tg2 = T("tg1", [128, GB, 9, 16]), T("tg2", [128, GB, 9, 16])
            rgen = P.nres_("ssm_gen")
            Z = T("Z", [128, nbt, 8, FW], BF16)
            rZ = P.nres_("ssm_Z")
            Zc = T("Zc", [128, nbt, GB, 8, 16])
            rZc = P.nres_("ssm_Zc")
            U8 = T("U8", [128, GB, NBLK])
            rU8 = P.nres_("ssm_U8")
            LE = [(T("LE", [128, 128]), P.nres_("ssm_LE%d" % i)) for i in range(2)]
            LS = [(T("LS", [128, 128]), P.nres_("ssm_LS%d" % i)) for i in range(2)]
            MK = [(T("MK", [128, 128]), P.nres_("ssm_MK%d" % i)) for i in range(16)]
            Tt = T("Tt", [128, GB, 128])
            rTt = P.nres_("ssm_Tt")
            tmpT = [(T("tmpT", [128, 128]), P.nres_("ssm_tmpT%d" % i)) for i in range(2)]
            W = NBLK + 1
            S = T("S", [128, GB, W])
            rS = [P.nres_("ssm_S%d" % i) for i in range(GB)]
            Y8 = [(T("Y8", [128, NB2]), P.nres_("ssm_Y8%d" % i)) for i in range(2)]
            Yt = T("Yt", [128, nbt2, 8, FW])
            rYt = P.nres_("ssm_Yt")
            P.op("pool", lambda e: e.memset(S[:, :, 0:1], 0.0), writes=rS)
            nmk = 0
            for gb in range(NG // GB):
                g0 = gb * GB
                gs_ = slice(g0, g0 + GB)
                bj = lambda a: a[:, gs_, :].unsqueeze(3).broadcast_to([128, GB, a.shape[2], 16])
                bb = lambda a, n: a[:, gs_, :].unsqueeze(2).broadcast_to([128, GB, n, 16])
                t8 = (tg1[:, :, 0:8, :], tg2[:, :, 0:8, :])
                self.cmul("pool", Bjr[:, :, :, :], Bji[:, :, :, :], bj(PIr), bj(PIi), bb(Bbr, 8), bb(Bbi, 8), t8, [rp], [rgen])
                self.cmul("pool", BEr[:, :, :, :], BEi[:, :, :, :], bj(PRr), bj(PRi), bb(Bbr, 8), bb(Bbi, 8), t8, [rp], [rgen])
                self.cmul("pool", Crr[:, :, :, :], Cri[:, :, :, :], bj(PWr), bj(PWi), bb(Cre, 9), bb(Cim, 9), (tg1[:, :, :, :], tg2[:, :, :, :]), [rp], [rgen], neg_im=True)
                for bt in range(nbt):
                    usrc, b2 = (self.ug[bt], 0) if bt < nbt2 else (self.u_d, bt - nbt2)
                    P.dma("sp", Z[0:bp, bt, :, :], usrc[b2 * bp * 8:(b2 + 1) * bp * 8, g0 * 16:g0 * 16 + FW].rearrange("(b j) f -> b j f", j=8), rZ, writes=[rZ])
                P.op("pool", lambda e: e.tensor_copy(out=Zc[0:bp, :, :, :, :], in_=Z[0:bp, :, :, :].rearrange("p b j (g c) -> p b g j c", c=16)), reads=[rZ], writes=[rZc])
                for gi in range(GB):
                    gg = g0 + gi
                    ps, rps = self.bank()
                    for bt in range(nbt):
                        P.op("pe", lambda e, ps=ps, bt=bt, gi=gi: e.transpose(out=ps[:, bt * bp:(bt + 1) * bp], in_=Zc[0:bp, bt, gi, :, :].rearrange("p j c -> p (j c)"), identity=idt[0:bp, 0:bp]),
                             reads=[rZc, rid], writes=[rps], signal=(bt == nbt - 1))
                    P.op("act", lambda e, ps=ps, gi=gi: e.activation(out=U8[:, gi, 0:NB2], in_=ps[:, 0:NB2], func=AF.Copy, scale=flag[:, 0:1]), reads=[rps, rflag], writes=[rU8])
                    P.op("act", lambda e, ps=ps, gi=gi: e.activation(out=U8[:, gi, NB2:NBLK], in_=ps[:, NB2:NBLK], func=AF.Copy), reads=[rps], writes=[rU8])
                    le, rle = LE[gi % 2]
                    ps, rps = self.bank()
                    P.op("pe", lambda e, ps=ps, gi=gi: e.transpose(out=ps[:, 0:64], in_=BEr[0:64, gi, :, :].rearrange("p j c -> p (j c)"), identity=idt[0:64, 0:64]), reads=[rgen, rid], writes=[rps], signal=False)
                    P.op("pe", lambda e, ps=ps, gi=gi: e.transpose(out=ps[:, 64:128], in_=BEi[0:64, gi, :, :].rearrange("p j c -> p (j c)"), identity=idt[0:64, 0:64]), reads=[rgen, rid], writes=[rps])
                    P.op("dve", lambda e, ps=ps, le=le: e.tensor_copy(out=le[:, :], in_=ps[:, 0:128]), reads=[rps], writes=[rle])
                    ps, rps = self.bank()
                    self.mm_acc(ps[:, 0:NBLK], rps, [(le[:, :], U8[:, gi, :])], [rle, rU8])
                    P.op("act", lambda e, ps=ps, gi=gi: e.activation(out=S[:, gi, 1:W], in_=ps[:, 0:NBLK], func=AF.Copy), reads=[rps], writes=[rS[gi]])
                    ps, rps = self.bank()
                    self.mm_acc(ps[:, 0:128], rps, [(Bjr[0:64, gi, :, :].rearrange("p j c -> p (j c)"), Crr[0:64, gi, 0:8, :].rearrange("p j c -> p (j c)")),
                                                    (Bji[0:64, gi, :, :].rearrange("p j c -> p (j c)"), Cri[0:64, gi, 0:8, :].rearrange("p j c -> p (j c)"))], [rgen])
                    tt_, rtt_ = tmpT[gi % 2]
                    P.op("dve", lambda e, ps=ps, tt_=tt_: e.tensor_tensor(out=tt_[:, :], in0=ps[:, 0:128], in1=tmask[:, :], op=ALU.mult), reads=[rps, rtm], writes=[rtt_])
                    P.op("dve", lambda e, tt_=tt_, gi=gi, gg=gg: e.scalar_tensor_tensor(out=Tt[:, gi, :], in0=idt[:, :], scalar=Dcol[:, gg:gg + 1], in1=tt_[:, :], op0=ALU.mult, op1=ALU.add),
                         reads=[rtt_, rid, rp], writes=[rTt])
                def build_mk(k):
                    out = []
                    for gi in range(GB):
                        gg = g0 + gi
                        mk, rmk = MK[(k * GB + gi) % len(MK)]
                        P.op("dve", lambda e, mk=mk, gg=gg, k=k: e.tensor_scalar(out=mk[:, :], in0=idt[:, :], scalar1=APr[:, gg, k:k + 1], scalar2=None, op0=ALU.mult),
                             reads=[rid, rp], writes=[rmk])
                        P.op("dve", lambda e, mk=mk, gg=gg, k=k: e.scalar_tensor_tensor(out=mk[:, :], in0=swp[:, :], scalar=APi[:, gg, k:k + 1], in1=mk[:, :], op0=ALU.mult, op1=ALU.add),
                             reads=[rswp, rp, rmk], writes=[rmk])
                        out.append((mk, rmk))
                    return out

                mks = build_mk(0)
                for k in range(nsteps):
                    sh = 1 << k
                    n = NBLK - sh
                    pss = []
                    for gi in range(GB):
                        mk, rmk = mks[gi]
                        ps, rps = self.bank()
                        self.mm_acc(ps[:, 0:n], rps, [(mk[:, :], S[:, gi, 1:1 + n])], [rmk, rS[gi]])
                        pss.append((ps, rps))
                    if k + 1 < nsteps:
                        mks = build_mk(k + 1)
                    for gi in range(GB):
                        ps, rps = pss[gi]
                        P.op("dve", lambda e, ps=ps, gi=gi, sh=sh, n=n: e.tensor_tensor(out=S[:, gi, 1 + sh:1 + sh + n], in0=ps[:, 0:n], in1=S[:, gi, 1 + sh:1 + sh + n], op=ALU.add),
                             reads=[rps], writes=[rS[gi]])
                for gi in range(GB):
                    ls, rls = LS[gi % 2]
                    P.op("dve", lambda e, ls=ls, gi=gi: e.tensor_copy(out=ls[0:64, :], in_=Crr[0:64, gi, 1:9, :].rearrange("p j c -> p (j c)")), reads=[rgen], writes=[rls])
                    P.op("dve", lambda e, ls=ls, gi=gi: e.tensor_copy(out=ls[64:128, :], in_=Cri[64:128, gi, 1:9, :].rearrange("p j c -> p (j c)")), reads=[rgen], writes=[rls])
                    ps, rps = self.bank()
                    self.mm_acc(ps[:, 0:NB2], rps, [(Tt[:, gi, :], U8[:, gi, NB2:NBLK]), (ls[:, :], S[:, gi, NB2:NBLK])], [rTt, rU8, rls, rS[gi]])
                    y8, ry8 = Y8[gi % 2]
                    P.op("act", lambda e, ps=ps, y8=y8: e.activation(out=y8[:, :], in_=ps[:, 0:NB2], func=AF.Copy), reads=[rps], writes=[ry8])
                    ps, rps = self.bank()
                    for bt in range(nbt2):
                        P.op("pe", lambda e, ps=ps, bt=bt, y8=y8: e.transpose(out=ps[0:bp, bt * 128:(bt + 1) * 128], in_=y8[:, bt * bp:(bt + 1) * bp], identity=idt[:, :]),
                             reads=[ry8, rid], writes=[rps], signal=(bt == nbt2 - 1))
                    P.op("dve", lambda e, ps=ps, gi=gi: e.tensor_copy(out=Yt[0:bp, :, :, gi * 16:(gi + 1) * 16], in_=ps[0:bp, 0:nbt2 * 128].rearrange("p (b j c) -> p b j c", b=nbt2, j=8)),
                         reads=[rps], writes=[rYt])
                for bt in range(nbt2):
                    P.dma("pool", self.y_d[bt * bp * 8:(bt + 1) * bp * 8, g0 * 16:g0 * 16 + FW].rearrange("(b j) f -> b j f", j=8), Yt[0:bp, bt, :, :], rYt, reads=[rYt])
            P.barrier()

    def setup_attn(self):
        P, g = self.P, self.g
        reld, rreld = g["c_reld"]
        tab = P.sbuf("tab", [128, 256], F32)
        rtab = P.nres_("tab")
        P.dma("sp", tab[:, :], self.i["rel_bias"].rearrange("b h -> (b h)").partition_broadcast(128), rtab, writes=[rtab])
        steps = [(-90, 15, 14), (-63, 14, 13), (-45, 13, 12), (-31, 12, 11), (-22, 11, 10), (-15, 10, 9), (-11, 9, 8)]
        steps += [(-n, n + 1, n) for n in range(7, -1, -1)]
        steps += [(1, 0, 17)] + [(n, 15 + n, 16 + n) for n in range(2, 8)]
        steps += [(8, 23, 24), (12, 24, 25), (16, 25, 26), (23, 26, 27), (32, 27, 28), (46, 28, 29), (64, 29, 30), (91, 30, 31)]
        ns = len(steps)
        dl = P.sbuf("dl", [128, ns, 8], F32)
        rdl = P.nres_("dl")
        for s, (thr, fb, tb) in enumerate(steps):
            P.op("dve", lambda e, s=s, fb=fb, tb=tb: e.tensor_tensor(out=dl[:, s, :], in0=tab[:, tb * 8:tb * 8 + 8], in1=tab[:, fb * 8:fb * 8 + 8], op=ALU.subtract),
                 reads=[rtab], writes=[rdl])
        bias = P.sbuf("biasT", [128, NH, 2, 128], F32)
        rbias = P.nres_("biasT")
        mk = P.sbuf("mk", [128, 128], F32)
        rmk = P.nres_("mk")
        for kind in range(2):
            off = -128.0 * kind
            for h in range(NH):
                P.op("dve", lambda e, h=h, kind=kind: e.tensor_scalar(out=bias[:, h, kind, :], in0=reld[:, :], scalar1=0.0, scalar2=tab[:, 15 * 8 + h:15 * 8 + h + 1], op0=ALU.mult, op1=ALU.add),
                     reads=[rreld, rtab], writes=[rbias])
            for s, (thr, fb, tb) in enumerate(steps):
                if kind == 1 and thr > -1:
                    continue
                if thr > 64:
                    continue
                P.op("dve", lambda e, thr=thr, off=off: e.tensor_single_scalar(out=mk[:, :], in_=reld[:, :], scalar=float(thr) - off, op=ALU.is_ge), reads=[rreld], writes=[rmk])
                for h in range(NH):
                    P.op("dve", lambda e, h=h, kind=kind, s=s: e.scalar_tensor_tensor(out=bias[:, h, kind, :], in0=mk[:, :], scalar=dl[:, s, h:h + 1], in1=bias[:, h, kind, :], op0=ALU.mult, op1=ALU.add),
                         reads=[rmk, rdl], writes=[rbias])
        for h in range(NH):
            P.op("pool", lambda e, h=h: e.memset(bias[64:128, h, 0, 0:64], -30000.0), reads=[rbias], writes=[rbias])
        g["tab"] = (tab, rtab)
        g["biasT"] = (bias, rbias)

    def bank2(self, lo, n, key):
        c = self.bctr.get(key, 0)
        self.bctr[key] = c + 1
        return self.psb[lo + c % n]

    def phase_att(self, l):
        P, g = self.P, self.g
        L = self.L
        nq = L // 128
        lam_init = 0.8 - 0.6 * math.exp(-0.3 * l)
        idt, rid = g["c_ident"]
        tab, rtab = g["tab"]
        bias, rbias = g["biasT"]
        self.bctr = {}
        with contextlib.ExitStack() as st:
            T = lambda name, shape, dt=F32: P.sbuf(name, shape, dt, st)
            rpar = P.nres_("att_par")
            lq = T("lq", [128, 4, 64])
            for ci, nm in enumerate(("lambda_q1", "lambda_k1", "lambda_q2", "lambda_k2")):
                P.dma("sp", lq[:, ci, :], self.i[nm][l].partition_broadcast(128), rpar, writes=[rpar])
            subw = T("subw", [128, 128])
            P.dma("sp", subw[:, :], self.i["subln_w"][l].partition_broadcast(128), rpar, writes=[rpar])
            e12 = T("e12", [128, 2])
            nlam = T("nlam", [128, 1])
            V = lambda fn: P.op("dve", fn, reads=[rpar], writes=[rpar])
            V(lambda e: e.tensor_tensor(out=lq[:, 0, :], in0=lq[:, 0, :], in1=lq[:, 1, :], op=ALU.mult))
            V(lambda e: e.tensor_tensor(out=lq[:, 2, :], in0=lq[:, 2, :], in1=lq[:, 3, :], op=ALU.mult))
            V(lambda e: e.reduce_sum(out=e12[:, 0:1], in_=lq[:, 0, :], axis=AX.X))
            V(lambda e: e.reduce_sum(out=e12[:, 1:2], in_=lq[:, 2, :], axis=AX.X))
            P.op("act", lambda e: e.activation(out=e12[:, :], in_=e12[:, :], func=AF.Exp), reads=[rpar], writes=[rpar])
            V(lambda e: e.tensor_tensor(out=nlam[:, :], in0=e12[:, 1:2], in1=e12[:, 0:1], op=ALU.subtract))
            V(lambda e: e.tensor_scalar(out=nlam[:, :], in0=nlam[:, :], scalar1=-lam_init, scalar2=None, op0=ALU.add))
            V(lambda e: e.tensor_scalar(out=subw[:, :], in0=subw[:, :], scalar1=1.0 - lam_init, scalar2=None, op0=ALU.mult))
            nb = nq
            NKP = nq
            mflag = g["mflag"][0]
            rflag = g["mflag"][1]
            cbp = T("cbp", [128, 8])
            P.op("dve", lambda e: e.tensor_scalar(out=cbp[:, :], in0=tab[:, 120:128], scalar1=mflag[:, 0:1], scalar2=None, op0=ALU.add), reads=[rtab, rflag], writes=[rpar])
            KT = [(T("KT", [128, 2 * L], BF16), P.nres_("att_KT%d" % i)) for i in range(2)]
            QT = [[(T("QT", [128, L], BF16), P.nres_("att_QT%d_%d" % (i, m))) for m in range(2)] for i in range(2)]
            V1 = [(T("V1", [128, 2 * nb, 132], BF16), P.nres_("att_V1%d" % i)) for i in range(2)]
            for v1, rv1 in V1:
                P.op("pool", lambda e, v1=v1: e.memset(v1[:, :, 128:132], 0.0), writes=[rv1])
                P.op("pool", lambda e, v1=v1: e.memset(v1[:, :, 128:129], 1.0), writes=[rv1])
            for i in range(2):
                for m in range(2):
                    qz, rqz = QT[i][m]
                    P.op("pool", lambda e, qz=qz: e.memset(qz[:, :], 0.0), writes=[rqz])
            PT = [(T("PT", [128, 4, 128], BF16), P.nres_("att_PT%d" % i)) for i in range(5)]
            tS = [(T("tS", [128, 128]), P.nres_("att_tS%d" % i)) for i in range(3)]
            rc = [(T("rc", [128, 2]), P.nres_("att_rc%d" % i)) for i in range(2)]
            oT = [(T("oT", [128, 128]), P.nres_("att_oT%d" % i)) for i in range(2)]
            on = [(T("on", [128, 128]), P.nres_("att_on%d" % i)) for i in range(2)]
            sj = [(T("sj", [128, 128], BF16), P.nres_("att_sj%d" % i)) for i in range(2)]
            ssq = [(T("ssq", [128, 1]), P.nres_("att_ssq%d" % i)) for i in range(2)]
            yo = [(T("yo", [128, 512], BF16), P.nres_("att_yo%d" % i)) for i in range(2)]
            npt = 0
            nts = 0
            wbufs = self.w_bufs(st, "b") if self.wq else None
            import os as _os
            STOP = int(_os.environ.get("ATT_STOP", "4"))
            for h in range(NH if STOP > 0 else 0):
                (kt, rkt), (v1, rv1) = KT[h % 2], V1[h % 2]
                qts = QT[h % 2]
                hpc = self.krows // 128
                P.dma("sp", kt[:, 0:L], self.kTg[h // hpc][(h % hpc) * 128:(h % hpc + 1) * 128, :], rkt, writes=[rkt])
                P.dma("sp", kt[:, L:2 * L], self.kT_d[h * 128:(h + 1) * 128, :], rkt, writes=[rkt])
                for m in range(2):
                    P.dma("sp", qts[m][0][m * 64:(m + 1) * 64, :], self.qT_d[h * 128 + m * 64:h * 128 + (m + 1) * 64, :], qts[m][1], writes=[qts[m][1]])
                for c in range(self.nch):
                    nbc = nb // self.nch
                    P.dma("sp", v1[:, c * nbc:(c + 1) * nbc, 0:128], self.vg[c][0:nbc * 128, h * 128:(h + 1) * 128].rearrange("(j p) e -> p j e", p=128), rv1, writes=[rv1])
                P.dma("sp", v1[:, nb:2 * nb, 0:128], self.v_d[:, h * 128:(h + 1) * 128].rearrange("(j p) e -> p j e", p=128), rv1, writes=[rv1])
                cb = tab[:, 15 * 8 + h:15 * 8 + h + 1]
                cbprev = cbp[:, h:h + 1]
                tps = None
                pending = []

                def emit_pv(o_ps, ro, pt, rpt, grp, i, v1=None, rv1=None):
                    for idx, j in enumerate(grp):
                        P.op("pe", lambda e, o_ps=o_ps, pt=pt, idx=idx, j=j, i=i, v1=v1: e.matmul(
                            o_ps[:, 0:130], lhsT=pt[:, idx, :], rhs=v1[:, j, 0:130], start=(j == 0), stop=(j == NKP + i)),
                            reads=[rpt, rv1], writes=[ro], signal=(j == NKP + i))

                def emit_fin(i, ops_, tps_box):
                    (o0, ro0), (o1, ro1) = ops_
                    rct, rrc = rc[i % 2]
                    ot_, rot = oT[i % 2]
                    on_, ron = on[i % 2]
                    sj_, rsj = sj[i % 2]
                    sq_, rsq = ssq[i % 2]
                    P.op("dve", lambda e, rct=rct, o0=o0: e.reciprocal(out=rct[:, 0:1], in_=o0[:, 128:129]), reads=[ro0], writes=[rrc])
                    P.op("dve", lambda e, rct=rct, o1=o1: e.reciprocal(out=rct[:, 1:2], in_=o1[:, 128:129]), reads=[ro1], writes=[rrc])
                    P.op("dve", lambda e, rct=rct: e.tensor_tensor(out=rct[:, 1:2], in0=rct[:, 1:2], in1=nlam[:, :], op=ALU.mult), reads=[rrc, rpar], writes=[rrc])
                    P.op("dve", lambda e, rct=rct, o0=o0, ot_=ot_: e.tensor_scalar(out=ot_[:, :], in0=o0[:, 0:128], scalar1=rct[:, 0:1], scalar2=None, op0=ALU.mult),
                         reads=[ro0, rrc], writes=[rot])
                    P.op("dve", lambda e, rct=rct, o1=o1, ot_=ot_: e.scalar_tensor_tensor(out=ot_[:, :], in0=o1[:, 0:128], scalar=rct[:, 1:2], in1=ot_[:, :], op0=ALU.mult, op1=ALU.add),
                         reads=[ro1, rrc, rot], writes=[rot])
                    P.op("act", lambda e, ot_=ot_, sj_=sj_, sq_=sq_: e.activation(out=sj_[:, :], in_=ot_[:, :], func=AF.Square, accum_out=sq_[:, :]),
                         reads=[rot], writes=[rsj, rsq])
                    P.op("act", lambda e, sq_=sq_: e.activation(out=sq_[:, :], in_=sq_[:, :], func=AF.Sqrt, bias=g["eps_sub"][0][:, :], scale=1.0 / 128),
                         reads=[rsq, g["eps_sub"][1]], writes=[rsq])
                    P.op("dve", lambda e, sq_=sq_: e.reciprocal(out=sq_[:, :], in_=sq_[:, :]), reads=[rsq], writes=[rsq])
                    P.op("dve", lambda e, ot_=ot_, sq_=sq_, on_=on_: e.scalar_tensor_tensor(out=on_[:, :], in0=ot_[:, :], scalar=sq_[:, 0:1], in1=subw[:, :], op0=ALU.mult, op1=ALU.mult),
                         reads=[rot, rsq, rpar], writes=[ron])
                    tps, rtps = self.psb[7]
                    P.op("pe", lambda e, tps=tps, on_=on_, i=i: e.transpose(out=tps[:, (i % 4) * 128:(i % 4 + 1) * 128], in_=on_[:, :], identity=idt[:, :]),
                         reads=[ron, rid], writes=[rtps])
                    if i % 4 == 3 or i == nq - 1:
                        nblk = i % 4 + 1
                        y_, ry = yo[(i // 4) % 2]
                        P.op("act", lambda e, tps=tps, y_=y_, nblk=nblk: e.activation(out=y_[:, 0:nblk * 128], in_=tps[:, 0:nblk * 128], func=AF.Copy), reads=[rtps], writes=[ry])
                        q0 = (i - nblk + 1) * 128
                        P.dma("pool", self.yaT_d[h * 128:(h + 1) * 128, q0:q0 + nblk * 128], y_[:, 0:nblk * 128], ry, reads=[ry])

                def flush(keep):
                    while sum(1 for kd, _ in pending if kd == "pv") > keep:
                        kd, fn = pending.pop(0)
                        fn()
                    while pending and pending[0][0] == "fin" and keep == 0:
                        kd, fn = pending.pop(0)
                        fn()

                for i in range(nq if STOP > 1 else 0):
                    ops_ = []
                    if wbufs is not None:
                        self.w_pump(wbufs, 3)
                    for m in range(2):
                        qt, rqt = qts[m]
                        o_ps, ro = self.bank2(0, 4, "O")
                        ops_.append((o_ps, ro))
                        gi_ = NKP + i
                        for j0 in range(0, gi_ + 1, 4):
                            grp = list(range(j0, min(j0 + 4, gi_ + 1)))
                            isprev = j0 < NKP
                            s_ps, rs = self.bank2(4, 3, "S")
                            for idx, j in enumerate(grp):
                                P.op("pe", lambda e, s_ps=s_ps, idx=idx, j=j, i=i, kt=kt, qt=qt: e.matmul(
                                    s_ps[:, idx * 128:(idx + 1) * 128], lhsT=kt[:, j * 128:(j + 1) * 128],
                                    rhs=qt[:, i * 128:(i + 1) * 128], start=True, stop=True),
                                    reads=[rkt, rqt], writes=[rs], signal=(idx == len(grp) - 1))
                            pt, rpt = PT[npt % 5]
                            npt += 1
                            nfar = len([j for j in grp if j <= gi_ - 2])
                            if nfar:
                                fb_ = cbprev if isprev else cb
                                P.op("act", lambda e, s_ps=s_ps, pt=pt, nfar=nfar, fb_=fb_: e.activation(
                                    out=pt[:, 0:nfar, :], in_=s_ps[:, 0:nfar * 128].rearrange("p (a b) -> p a b", a=nfar), func=AF.Exp, bias=fb_),
                                    reads=[rs, rtab, rpar], writes=[rpt])
                            for idx, j in enumerate(grp):
                                if j <= gi_ - 2:
                                    continue
                                kind = 0 if j == gi_ else 1
                                ts_, rts = tS[nts % 3]
                                nts += 1
                                if j < NKP:
                                    P.op("dve", lambda e, s_ps=s_ps, idx=idx, ts_=ts_, kind=kind, h=h: e.scalar_tensor_tensor(
                                        out=ts_[:, :], in0=s_ps[:, idx * 128:(idx + 1) * 128], scalar=mflag[:, 0:1], in1=bias[:, h, kind, :], op0=ALU.add, op1=ALU.add),
                                        reads=[rs, rbias, rflag], writes=[rts])
                                else:
                                    P.op("dve", lambda e, s_ps=s_ps, idx=idx, ts_=ts_, kind=kind, h=h: e.tensor_tensor(
                                        out=ts_[:, :], in0=s_ps[:, idx * 128:(idx + 1) * 128], in1=bias[:, h, kind, :], op=ALU.add),
                                        reads=[rs, rbias], writes=[rts])
                                P.op("act", lambda e, ts_=ts_, pt=pt, idx=idx: e.activation(out=pt[:, idx, :], in_=ts_[:, :], func=AF.Exp),
                                     reads=[rts], writes=[rpt])
                            flush(1)
                            pending.append(("pv", lambda o_ps=o_ps, ro=ro, pt=pt, rpt=rpt, grp=grp, i=i: emit_pv(o_ps, ro, pt, rpt, grp, i, v1, rv1)))
                    pending.append(("fin", lambda i=i, ops_=ops_: emit_fin(i, ops_, None)))
                flush(0)
            if wbufs is not None:
                self.w_pump(wbufs, 10 ** 9)
            P.barrier()

    def phase_b1(self, l, x_src):
        P, g = self.P, self.g
        idt, rid = g["c_ident"]
        with contextlib.ExitStack() as st:
            T = lambda name, shape, dt=F32: P.sbuf(name, shape, dt, st)
            xt, rx = T("xt", [128, 4, D]), P.nres_("xt")
            ytoks = [(T("ytok", [128, SSMW]), P.nres_("b1_ytok%d" % i)) for i in range(2)]
            tmp = [(T("gt", [128, SSMW]), P.nres_("b1_gt%d" % i)) for i in range(2)]
            yg = [(T("yg", [128, SSMW]), P.nres_("b1_yg%d" % i)) for i in range(2)]
            ygf, rygf = T("ygf", [128, 8, 512]), P.nres_("b1_ygf")
            ygb, rygb = T("ygb", [128, 8, 512], BF16), P.nres_("b1_ygb")
            yss, ryss = T("yss", [128, 8, 512], BF16), P.nres_("b1_yss")
            yat, ryat = T("yat", [128, 8, 512], BF16), P.nres_("b1_yat")
            gsTs = [(T("gsT", [128, 4, 512], BF16), P.nres_("b1_gs%d" % i)) for i in range(2)]
            gaTs = [(T("gaT", [128, 4, 512], BF16), P.nres_("b1_ga%d" % i)) for i in range(2)]
            mT, rmT = T("mT", [128, 16, 512], BF16), P.nres_("b1_mT")
            sgt = [(T("sgt", [128, 512]), P.nres_("b1_sgt%d" % i)) for i in range(2)]
            m1 = [(T("m1", [128, 512]), P.nres_("b1_m1%d" % i)) for i in range(2)]
            m2 = [(T("m2", [128, 512]), P.nres_("b1_m2%d" % i)) for i in range(2)]
            wbuf = [(T("wB", [128, 16, 512], BF16), P.nres_("wA%d" % i)) for i in range(2)]
            wglu, rwglu = T("wglu", [128, 8, SSMW], BF16), P.nres_("b1_wglu")
            bglu, rbglu = T("bglu", [128, 8]), P.nres_("b1_bglu")
            self.load_wtile(wglu, rwglu, self.wb["w_glu"][l], 0, 8, 0, SSMW)
            P.dma("sp", bglu[:, :], self.i["b_glu"][l].rearrange("(c p) -> p c", p=128), rbglu, writes=[rbglu], allow_slow_non_contiguous=True)
            nw = 0
            for t in range(self.NT):
                t0 = t * 512
                P.dma("sp", xt[:, :, :], x_src[t0:t0 + 512, :].rearrange("(s p) d -> p s d", p=128), rx, writes=[rx])
                P.dma("sp", yat[:, :, :], self.yaT_d.rearrange("(fc p) t -> p fc t", p=128)[:, :, t0:t0 + 512], ryat, writes=[ryat])
                for s in range(4):
                    (tm, rtm), (ygs, rygs) = tmp[s % 2], yg[s % 2]
                    ytk, rytok = ytoks[s % 2]
                    P.dma("sp", ytk[:, :], self.y_d[t0 + s * 128:t0 + (s + 1) * 128, :], rytok, writes=[rytok])
                    P.op("pool", lambda e, tm=tm, ytk=ytk: e.tensor_tensor(out=tm[:, :], in0=ytk[:, :], in1=ytk[:, :], op=ALU.mult), reads=[rytok], writes=[rtm])
                    P.op("dve", lambda e, tm=tm: e.tensor_scalar(out=tm[:, :], in0=tm[:, :], scalar1=0.044715, scalar2=1.0, op0=ALU.mult, op1=ALU.add), reads=[rtm], writes=[rtm])
                    P.op("pool", lambda e, tm=tm, ytk=ytk: e.tensor_tensor(out=tm[:, :], in0=tm[:, :], in1=ytk[:, :], op=ALU.mult), reads=[rtm, rytok], writes=[rtm])
                    P.op("act", lambda e, tm=tm: e.activation(out=tm[:, :], in_=tm[:, :], func=AF.Sigmoid, scale=2.0 * math.sqrt(2.0 / math.pi)), reads=[rtm], writes=[rtm])
                    P.op("dve", lambda e, tm=tm, ygs=ygs, ytk=ytk: e.tensor_tensor(out=ygs[:, :], in0=tm[:, :], in1=ytk[:, :], op=ALU.mult), reads=[rtm, rytok], writes=[rygs])
                    for k4 in range(2):
                        ps, rps = self.bank()
                        for j in range(4):
                            fc = k4 * 4 + j
                            P.op("pe", lambda e, ps=ps, ygs=ygs, fc=fc, j=j: e.transpose(out=ps[:, j * 128:(j + 1) * 128], in_=ygs[:, fc * 128:(fc + 1) * 128], identity=idt[:, :]),
                                 reads=[rygs, rid], writes=[rps], signal=(j == 3))
                        P.op("act", lambda e, ps=ps, k4=k4, s=s: e.activation(out=ygf[:, k4 * 4:k4 * 4 + 4, s * 128:(s + 1) * 128], in_=ps[:, :].rearrange("p (k t) -> p k t", k=4), func=AF.Copy),
                             reads=[rps], writes=[rygf])
                        P.op("dve", lambda e, ps=ps, k4=k4, s=s: e.tensor_copy(out=ygb[:, k4 * 4:k4 * 4 + 4, s * 128:(s + 1) * 128], in_=ps[:, :].rearrange("p (k t) -> p k t", k=4)),
                             reads=[rps], writes=[rygb])
                for cb in range(8):
                    ps, rps = self.bank()
                    self.mm_acc(ps[:, :], rps, [(wglu[:, kc, cb * 128:(cb + 1) * 128], ygb[:, kc, :]) for kc in range(8)], [rwglu, rygb])
                    sg, rsg = sgt[cb % 2]
                    P.op("act", lambda e, ps=ps, sg=sg, cb=cb: e.activation(out=sg[:, :], in_=ps[:, :], func=AF.Sigmoid, bias=bglu[:, cb:cb + 1]), reads=[rps, rbglu], writes=[rsg])
                    P.op("dve", lambda e, sg=sg, cb=cb: e.tensor_tensor(out=yss[:, cb, :], in0=ygf[:, cb, :], in1=sg[:, :], op=ALU.mult), reads=[rsg, rygf], writes=[ryss])
                for c4 in range(4):
                    (ws, rws), (wa, rwa) = wbuf[0], wbuf[1]
                    self.load_wtile(ws, rws, self.wb["w_proj_ssm"][l], 0, 8, c4 * 512, 512)
                    self.load_wtile(wa, rwa, self.wb["w_proj_attn"][l], 0, 8, c4 * 512, 512)
                    (gsT, rgs), (gaT, rga) = gsTs[c4 % 2], gaTs[c4 % 2]
                    P.dma("sp", gsT[:, :, :], self.gsT_d[c4 * 512:(c4 + 1) * 512, t0:t0 + 512].rearrange("(fc p) t -> p fc t", p=128), rgs, writes=[rgs])
                    P.dma("sp", gaT[:, :, :], self.gaT_d[c4 * 512:(c4 + 1) * 512, t0:t0 + 512].rearrange("(fc p) t -> p fc t", p=128), rga, writes=[rga])
                    for j in range(4):
                        cb = c4 * 4 + j
                        ps1, rps1 = self.bank()
                        self.mm_acc(ps1[:, :], rps1, [(ws[:, kc, j * 128:(j + 1) * 128], yss[:, kc, :]) for kc in range(8)], [rws, ryss])
                        ps2, rps2 = self.bank()
                        self.mm_acc(ps2[:, :], rps2, [(wa[:, kc, j * 128:(j + 1) * 128], yat[:, kc, :]) for kc in range(8)], [rwa, ryat])
                        (a1, ra1), (a2, ra2) = m1[cb % 2], m2[cb % 2]
                        P.op("dve", lambda e, ps1=ps1, a1=a1, j=j, gsT=gsT: e.tensor_tensor(out=a1[:, :], in0=ps1[:, :], in1=gsT[:, j, :], op=ALU.mult), reads=[rps1, rgs], writes=[ra1])
                        P.op("dve", lambda e, ps2=ps2, a2=a2, j=j, gaT=gaT: e.tensor_tensor(out=a2[:, :], in0=ps2[:, :], in1=gaT[:, j, :], op=ALU.mult), reads=[rps2, rga], writes=[ra2])
                        P.op("pool", lambda e, a1=a1, a2=a2, cb=cb: e.tensor_tensor(out=mT[:, cb, :], in0=a1[:, :], in1=a2[:, :], op=ALU.add), reads=[ra1, ra2], writes=[rmT])
                for nch in range(4):
                    wt, rwt = wbuf[nch % 2]
                    self.load_wtile(wt, rwt, self.wb["w_out"][l], 0, 16, nch * 512, 512)
                    for s in range(4):
                        ps, rps = self.bank()
                        self.mm_acc(ps[:, :], rps, [(mT[:, kc, s * 128:(s + 1) * 128], wt[:, kc, :]) for kc in range(16)], [rmT, rwt])
                        P.op("dve", lambda e, ps=ps, s=s, nch=nch: e.tensor_tensor(out=xt[:, s, nch * 512:(nch + 1) * 512], in0=ps[:, :], in1=xt[:, s, nch * 512:(nch + 1) * 512], op=ALU.add),
                             reads=[rps, rx], writes=[rx])
                P.dma("pool", self.xa_d[t0:t0 + 512, :].rearrange("(s p) d -> p s d", p=128), xt[:, :, :], rx, reads=[rx])
            P.barrier()

    def phase_b2(self, l, x_dst):
        P, g = self.P, self.g
        with contextlib.ExitStack() as st:
            T = lambda name, shape, dt=F32: P.sbuf(name, shape, dt, st)
            xt, rx = T("xt", [128, 4, D]), P.nres_("xt")
            nbufs = self.norm_bufs(st)
            w2T, rw2 = T("w2T", [128, 16]), P.nres_("w1T")
            P.dma("sp", w2T[:, :], self.i["norm2_w"][l].rearrange("(kc p) -> p kc", p=128), rw2, writes=[rw2], allow_slow_non_contiguous=True)
            hT, rhT = T("hT", [128, 16, 512], BF16), P.nres_("hT")
            aT, raT = T("aT", [128, 44, 512], BF16), P.nres_("b2_aT")
            wbuf = [(T("wF", [128, 22, 512], BF16), P.nres_("wA%d" % i)) for i in range(3)]
            sgt = [(T("sgt", [128, 512]), P.nres_("b1_sgt%d" % i)) for i in range(2)]
            nw = 0
            for t in range(self.NT):
                t0 = t * 512
                P.dma("sp", xt[:, :, :], self.xa_d[t0:t0 + 512, :].rearrange("(s p) d -> p s d", p=128), rx, writes=[rx])
                self.norm_T(nbufs, xt, rx, w2T, rw2, hT, rhT)
                for c in range(DFF // 512):
                    (wg, rwg) = wbuf[nw % 3]
                    (wu, rwu) = wbuf[(nw + 1) % 3]
                    nw += 2
                    self.load_wtile(wg, rwg, self.wb["w_ffn_gate"][l], 0, 16, c * 512, 512)
                    self.load_wtile(wu, rwu, self.wb["w_ffn_up"][l], 0, 16, c * 512, 512)
                    for j in range(4):
                        fb = c * 4 + j
                        psg, rpsg = self.bank()
                        self.mm_acc(psg[:, :], rpsg, [(wg[:, kc, j * 128:(j + 1) * 128], hT[:, kc, :]) for kc in range(16)], [rwg, rhT])
                        psu, rpsu = self.bank()
                        self.mm_acc(psu[:, :], rpsu, [(wu[:, kc, j * 128:(j + 1) * 128], hT[:, kc, :]) for kc in range(16)], [rwu, rhT])
                        sg, rsg = sgt[fb % 2]
                        P.op("act", lambda e, psg=psg, sg=sg: e.activation(out=sg[:, :], in_=psg[:, :], func=AF.Silu), reads=[rpsg], writes=[rsg])
                        P.op("dve", lambda e, psu=psu, sg=sg, fb=fb: e.tensor_tensor(out=aT[:, fb, :], in0=psu[:, :], in1=sg[:, :], op=ALU.mult), reads=[rpsu, rsg], writes=[raT])
                for nch in range(4):
                    (w0, rw0) = wbuf[nw % 3]
                    (w1, rw1) = wbuf[(nw + 1) % 3]
                    nw += 2
                    self.load_wtile(w0, rw0, self.wb["w_ffn_down"][l], 0, 22, nch * 512, 512)
                    self.load_wtile(w1, rw1, self.wb["w_ffn_down"][l], 22, 22, nch * 512, 512)
                    for s in range(4):
                        ps, rps = self.bank()
                        pairs = [(aT[:, kc, s * 128:(s + 1) * 128], w0[:, kc, :]) for kc in range(22)]
                        pairs += [(aT[:, 22 + kc, s * 128:(s + 1) * 128], w1[:, kc, :]) for kc in range(22)]
                        self.mm_acc(ps[:, :], rps, pairs, [raT, rw0, rw1])
                        P.op("dve", lambda e, ps=ps, s=s, nch=nch: e.tensor_tensor(out=xt[:, s, nch * 512:(nch + 1) * 512], in0=ps[:, :], in1=xt[:, s, nch * 512:(nch + 1) * 512], op=ALU.add),
                             reads=[rps, rx], writes=[rx])
                P.dma("pool", x_dst[t0:t0 + 512, :].rearrange("(s p) d -> p s d", p=128), xt[:, :, :], rx, reads=[rx])
            P.barrier()


_CACHE = {}


def kernel(**inputs):
    x = np.ascontiguousarray(np.asarray(inputs["x"], dtype=np.float32))
    B, L, _ = x.shape
    Lh = L // 2
    key = (Lh,)
    if key not in _CACHE:
        _CACHE[key] = Builder(Lh).build()
    nc = _CACHE[key]
    base = {k: np.ascontiguousarray(np.asarray(v, dtype=np.float32)) for k, v in inputs.items() if k != "x"}
    base.update(host_consts())
    in_maps = []
    for b in range(B):
        for half in range(2):
            m = dict(base)
            m["x"] = np.ascontiguousarray(x[b, half * Lh:(half + 1) * Lh])
            m["c_flag"] = np.full((128, 1), float(half), np.float32)
            in_maps.append(m)
    res = run_bass_kernel_spmd(nc, in_maps, core_ids=list(range(2 * B)))
    out = np.empty((B, L, x.shape[2]), np.float32)
    for b in range(B):
        for half in range(2):
            out[b, half * Lh:(half + 1) * Lh] = np.asarray(res.results[2 * b + half]["out"], dtype=np.float32)
    return out
```

```python
import contextlib
import math
import numpy as np
import concourse.bass as bass
import concourse.mybir as mybir
from concourse.bass_utils import run_bass_kernel_spmd

F32 = mybir.dt.float32
BF16 = mybir.dt.bfloat16
I32 = mybir.dt.int32
AF = mybir.ActivationFunctionType
ALU = mybir.AluOpType
AX = mybir.AxisListType

D = 2048
DEPTH = 2
SSMW = 1024
NG = 64
NS = 64
ATW = 1024
NH = 8
INW = 8192
DFF = 5632
RMS_EPS = 1e-6
SUBLN_EPS = 1e-5


class Res:
    __slots__ = ("name", "w", "r", "dsem", "dcnt", "excl")

    def __init__(self, name):
        self.name = name
        self.excl = False
        self.w = None
        self.r = {}
        self.dsem = None
        self.dcnt = 0


class Prog:
    ENG = ("pe", "act", "dve", "pool", "sp")

    def __init__(self, nc, stack):
        self.nc = nc
        self.stack = stack
        self.sems = []
        self.ops = {e: [] for e in self.ENG}
        self.cnt = {e: 0 for e in self.ENG}
        self.seen = {e: {} for e in self.ENG}
        self.esem = {}
        self.latest = {}
        for e in self.ENG:
            self.esem[e] = self.new_sem("s_" + e)
        self.nres = 0
        self.ccsem = None
        self.named = {}
        self.nalloc = 0

    def new_sem(self, name):
        s = self.stack.enter_context(self.nc.semaphore(name))
        self.sems.append(s)
        return len(self.sems) - 1

    def res(self, name=None):
        self.nres += 1
        return Res(name or ("r%d" % self.nres))

    def nres_(self, name):
        if name not in self.named:
            self.named[name] = Res(name)
        return self.named[name]

    def sbuf(self, name, shape, dtype, stack=None):
        self.nalloc += 1
        st = stack if stack is not None else self.stack
        t = st.enter_context(self.nc.sbuf_tensor("%s_%d" % (name, self.nalloc), list(shape), dtype))
        return t

    def psum(self, name, shape, dtype, stack=None):
        st = stack if stack is not None else self.stack
        return st.enter_context(self.nc.psum_tensor(name, list(shape), dtype))

    def _deps(self, eng, reads, writes):
        deps = {}
        for r in reads:
            if r.w is not None:
                s, v = r.w
                if deps.get(s, 0) < v:
                    deps[s] = v
        for w in writes:
            if w.w is not None:
                s, v = w.w
                if deps.get(s, 0) < v:
                    deps[s] = v
            for (s, v) in w.r.values():
                if deps.get(s, 0) < v:
                    deps[s] = v
        waits = []
        seen = self.seen[eng]
        own = self.esem[eng]
        for s, v in deps.items():
            if s == own and v > self.cnt[eng]:
                continue
            if seen.get(s, 0) < v:
                seen[s] = v
                waits.append((s, v))
        return waits

    def op(self, eng, fn, reads=(), writes=(), signal=True):
        if any(r.excl for r in reads):
            writes = list(writes) + [r for r in reads if r.excl and r not in writes]
            reads = [r for r in reads if not r.excl]
        waits = self._deps(eng, reads, writes)
        idx = self.cnt[eng] + 1
        if signal:
            self.cnt[eng] = idx
        ev = (self.esem[eng], idx)
        self.latest[ev[0]] = idx
        self.ops[eng].append((waits, fn, signal))
        for w in writes:
            w.w = ev
            w.r = {}
        for r in reads:
            r.r[ev[0]] = ev

    def dma(self, q, out, in_, sres, reads=(), writes=(), **kw):
        waits = self._deps(q, reads, writes)
        qk = "sw" if q == "pool" else "hw"
        if sres.dsem is None:
            sres.dsem = {}
            sres.dcnt = {}
        if qk not in sres.dsem:
            sres.dsem[qk] = self.new_sem("d%s_%s" % (qk, sres.name))
            sres.dcnt[qk] = 0
        sres.dcnt[qk] += 16
        ev = (sres.dsem[qk], sres.dcnt[qk])
        self.latest[ev[0]] = ev[1]
        sem = self.sems[ev[0]]

        def fn(e, out=out, in_=in_, sem=sem, kw=kw):
            e.dma_start(out=out, in_=in_, **kw).then_inc(sem, 16)
            return None

        self.ops[q].append((waits, fn, False))
        for w in writes:
            w.w = ev
            w.r = {}
        for r in reads:
            r.r[ev[0]] = ev

    def coll(self, kind, in_ap, out_ap, groups, reads=(), writes=()):
        waits = self._deps("pool", reads, writes)
        if self.ccsem is None:
            self.ccsem = self.new_sem("s_cc")
            self.cccnt = 0
        self.cccnt += 1
        ev = (self.ccsem, self.cccnt)
        self.latest[ev[0]] = ev[1]
        sem = self.sems[ev[0]]

        def fn(e):
            e.collective_compute(kind, ALU.bypass, replica_groups=groups, ins=[in_ap], outs=[out_ap]).then_inc(sem, 1)
            return None

        self.ops["pool"].append((waits, fn, False))
        for w in writes:
            w.w = ev
            w.r = {}
        for r in reads:
            r.r[ev[0]] = ev

    def barrier(self):
        for e in self.ENG:
            waits = []
            seen = self.seen[e]
            for s, v in self.latest.items():
                if seen.get(s, 0) < v:
                    seen[s] = v
                    waits.append((s, v))
            if waits:
                self.ops[e].append((waits, None, False))

    def emit(self):
        nc = self.nc
        sems = self.sems
        with nc.Block() as block:
            def run(name, e):
                own = sems[self.esem[name]]
                for waits, fn, signal in self.ops[name]:
                    for s, v in waits:
                        e.wait_ge(sems[s], v)
                    if fn is None:
                        continue
                    ins = fn(e)
                    if signal:
                        ins.then_inc(own, 1)

            @block.tensor
            def _(e):
                run("pe", e)

            @block.scalar
            def _(e):
                run("act", e)

            @block.vector
            def _(e):
                run("dve", e)

            @block.gpsimd
            def _(e):
                run("pool", e)

            @block.sync
            def _(e):
                run("sp", e)


WSPEC = [
    ("w_in", D, INW), ("w_glu", SSMW, SSMW), ("w_proj_ssm", SSMW, D), ("w_proj_attn", ATW, D),
    ("w_out", D, D), ("w_ffn_gate", D, DFF), ("w_ffn_up", D, DFF), ("w_ffn_down", DFF, D),
]
SMALL = [("norm1_w", [DEPTH, D]), ("lam_re", [DEPTH, NG, NS]), ("lam_im", [DEPTH, NG, NS]),
         ("log_step", [DEPTH, NG]), ("ssm_b_re", [DEPTH, NG, NS, 16]), ("ssm_b_im", [DEPTH, NG, NS, 16]),
         ("ssm_c_re", [DEPTH, NG, 16, NS]), ("ssm_c_im", [DEPTH, NG, 16, NS]), ("ssm_d", [DEPTH, SSMW]),
         ("b_glu", [DEPTH, SSMW]), ("q_norm_w", [DEPTH, 64]), ("k_norm_w", [DEPTH, 64]),
         ("lambda_q1", [DEPTH, 64]), ("lambda_k1", [DEPTH, 64]), ("lambda_q2", [DEPTH, 64]),
         ("lambda_k2", [DEPTH, 64]), ("subln_w", [DEPTH, 128]), ("rel_bias", [32, NH]),
         ("norm2_w", [DEPTH, D])]
CONSTS = [("c_ident", [128, 128]), ("c_bones", [128, 128]), ("c_tmask", [128, 128]), ("c_reld", [128, 128]), ("c_swap", [128, 128]), ("c_flag", [128, 1])]


def host_consts():
    ident = np.eye(128, dtype=np.float32)
    bones = np.kron(np.eye(2, dtype=np.float32), np.ones((64, 64), np.float32))
    jj = np.arange(128) // 16
    tmask = (jj[None, :] >= jj[:, None]).astype(np.float32)
    reld = (np.arange(128)[:, None] - np.arange(128)[None, :]).astype(np.float32)
    swap = np.roll(np.eye(128, dtype=np.float32), 64, axis=1)
    return {"c_ident": ident, "c_bones": bones, "c_tmask": tmask, "c_reld": reld, "c_swap": swap,
            "c_flag": np.zeros((128, 1), np.float32)}


class Builder:
    def __init__(self, L, nlayers=DEPTH, dbg=False, phases=None, nsh=8):
        self.L = L
        self.NT = L // 512
        self.nlayers = nlayers
        self.dbg = dbg
        self.phases = phases
        self.nc = bass.Bass("TRN2", target_bir_lowering=False)
        self.stack = contextlib.ExitStack()
        self.P = Prog(self.nc, self.stack)
        nc = self.nc
        self.i = {}
        self.i["x"] = self.din("x", [L, D])
        for n, k, m in WSPEC:
            self.i[n] = self.din(n, [DEPTH, k, m])
        for n, sh in SMALL + CONSTS:
            self.i[n] = self.din(n, sh)
        self.out = self.dout("out", [L, D])
        self.wb = {n: [self.dscr("wb_%s_%d" % (n, l), [k, m], BF16) for l in range(nlayers)] for n, k, m in WSPEC}
        sk = self.dout if dbg else self.dscr
        self.u_d = sk("u_d", [L, SSMW], BF16)
        self.nch = max(1, L // 1024)
        self.krows = ATW // self.nch
        self.ug = [self.dscr("ug%d" % c, [2 * 1024, SSMW], BF16) for c in range(self.nch)]
        self.vg = [self.dscr("vg%d" % c, [2 * 1024, ATW], BF16) for c in range(self.nch)]
        self.kTg = [self.dscr("kTg%d" % c, [2 * self.krows, L], BF16) for c in range(self.nch)]
        self.groups = [[0, 1], [2, 3], [4, 5], [6, 7]]
        skq = self.din if dbg == "attin" else sk
        self.qT_d = skq("qT_d", [ATW, L], BF16)
        self.kT_d = skq("kT_d", [ATW, L], BF16)
        self.v_d = skq("v_d", [L, ATW], BF16)
        self.gsT_d = sk("gsT_d", [D, L], BF16)
        self.gaT_d = sk("gaT_d", [D, L], BF16)
        self.y_d = sk("y_d", [L, SSMW], F32)
        self.yaT_d = sk("yaT_d", [ATW, L], BF16)
        self.xa_d = sk("xa_d", [L, D], F32)
        self.xb_d = self.dscr("xb_d", [L, D], F32)
        self.rdram = {}
        self.wq = []
        self.wit = 0

    def din(self, name, shape, dtype=F32):
        return self.nc.dram_tensor(name, list(shape), dtype, kind="ExternalInput").ap()

    def dscr(self, name, shape, dtype):
        return self.nc.dram_tensor(name, list(shape), dtype, kind="Internal").ap()

    def dout(self, name, shape, dtype=F32):
        return self.nc.dram_tensor(name, list(shape), dtype, kind="ExternalOutput").ap()

    def setup_globals(self):
        P = self.P
        g = self.g = {}
        self.psb = []
        for b in range(8):
            self.psb.append((P.psum("psb%d" % b, [128, 512], F32), P.nres_("psb%d" % b)))
            self.psb[-1][1].excl = True
        self.psi = 0
        for n in ("c_ident", "c_bones", "c_tmask", "c_reld", "c_swap"):
            t = P.sbuf(n, [128, 128], F32)
            r = P.nres_(n)
            P.dma("sp", t[:, :], self.i[n][:, :], r, writes=[r])
            g[n] = (t, r)
        t = P.sbuf("c_flag", [128, 1], F32)
        r = P.nres_("c_flag")
        P.dma("sp", t[:, :], self.i["c_flag"][:, :], r, writes=[r])
        g["c_flag"] = (t, r)
        t2 = P.sbuf("mflag", [128, 1], F32)
        P.op("dve", lambda e, t=t, t2=t2: e.tensor_scalar(out=t2[:, :], in0=t[:, :], scalar1=-1.0, scalar2=30000.0, op0=ALU.add, op1=ALU.mult), reads=[r], writes=[r])
        g["mflag"] = (t2, r)
        t = P.sbuf("bones_b", [128, 128], BF16)
        r = P.nres_("bones_b")
        P.op("dve", lambda e, t=t: e.tensor_copy(out=t[:, :], in_=g["c_bones"][0][:, :]), reads=[g["c_bones"][1]], writes=[r])
        g["bones_b"] = (t, r)
        for nm, val in (("eps_rms", RMS_EPS), ("eps_sub", SUBLN_EPS), ("halfpi", math.pi / 2), ("zero", 0.0)):
            t = P.sbuf(nm, [128, 1], F32)
            r = P.nres_(nm)
            P.op("pool", lambda e, t=t, val=val: e.memset(t[:, :], val), writes=[r])
            g[nm] = (t, r)

    def bank(self):
        b = self.psb[self.psi % 8]
        self.psi += 1
        return b

    def w_specs(self):
        out = []
        for l in range(self.nlayers):
            for n, K, N in WSPEC:
                if n not in self.wb:
                    continue
                CW = 4096
                ncc = (N + CW - 1) // CW
                cw = N // ncc
                for kt in range(K // 128):
                    for c in range(ncc):
                        out.append((l, n, kt, c, cw))
        return out

    def w_chunk(self, bufs, spec, cast_eng, store_q):
        P = self.P
        l, n, kt, c, cw = spec
        (f, rf), (b, rb) = bufs
        src = self.i[n]
        dst = self.wb[n][l]
        P.dma("sp", f[:, 0:cw], src[l, kt * 128:(kt + 1) * 128, c * cw:(c + 1) * cw], rf, writes=[rf])
        if cast_eng == "act":
            P.op("act", lambda e, f=f, b=b, cw=cw: e.activation(out=b[:, 0:cw], in_=f[:, 0:cw], func=AF.Copy), reads=[rf], writes=[rb])
        else:
            P.op(cast_eng, lambda e, f=f, b=b, cw=cw: e.tensor_copy(out=b[:, 0:cw], in_=f[:, 0:cw]), reads=[rf], writes=[rb])
        P.dma(store_q, dst[kt * 128:(kt + 1) * 128, c * cw:(c + 1) * cw], b[:, 0:cw], rb, reads=[rb], writes=[])

    def w_bufs(self, st, tag):
        P = self.P
        NB = 3
        fb = [(P.sbuf("wf", [128, 4096], F32, st), P.nres_("wf%s%d" % (tag, i))) for i in range(NB)]
        bb = [(P.sbuf("wbb", [128, 4096], BF16, st), P.nres_("wbb%s%d" % (tag, i))) for i in range(NB)]
        return list(zip(fb, bb))

    def phase_w(self):
        P = self.P
        specs = self.w_specs()
        first = [sp_ for sp_ in specs if sp_[0] == 0 and sp_[1] == "w_in"]
        self.wq = [sp_ for sp_ in specs if not (sp_[0] == 0 and sp_[1] == "w_in")]
        with contextlib.ExitStack() as st:
            bufs = self.w_bufs(st, "a")
            ce = ("dve", "pool", "act")
            for it, sp_ in enumerate(first):
                self.w_chunk(bufs[it % 3], sp_, ce[it % 3], "act")
            P.barrier()

    def w_pump(self, bufs, n):
        while n > 0 and self.wq:
            sp_ = self.wq.pop(0)
            self.w_chunk(bufs[self.wit % 3], sp_, "pool" if self.wit % 2 == 0 else "dve", "pool")
            self.wit += 1
            n -= 1

    def norm_T(self, st_bufs, xt, rx, wT, rwT, hT, rhT):
        P, g = self.P, self.g
        junk, rjunk, ssq, rssq, rstd, rrstd, xs = st_bufs
        for s in range(4):
            P.op("act", lambda e, s=s: e.activation(out=junk[:, :], in_=xt[:, s, :], func=AF.Square, accum_out=ssq[:, s:s + 1]),
                 reads=[rx], writes=[rjunk, rssq])
        P.op("act", lambda e: e.activation(out=rstd[:, :], in_=ssq[:, :], func=AF.Sqrt, bias=g["eps_rms"][0][:, :], scale=1.0 / D),
             reads=[rssq, g["eps_rms"][1]], writes=[rrstd])
        P.op("dve", lambda e: e.reciprocal(out=rstd[:, :], in_=rstd[:, :]), reads=[rrstd], writes=[rrstd])
        idt, rid = g["c_ident"]
        for s in range(4):
            xs_t, rxs = xs[s % 2]
            P.op("act", lambda e, s=s, xs_t=xs_t: e.activation(out=xs_t[:, :], in_=xt[:, s, :], func=AF.Copy, scale=rstd[:, s:s + 1]),
                 reads=[rx, rrstd], writes=[rxs])
            for k4 in range(4):
                ps, rps = self.bank()
                for j in range(4):
                    kc = k4 * 4 + j
                    P.op("pe", lambda e, ps=ps, xs_t=xs_t, kc=kc, j=j: e.transpose(out=ps[:, j * 128:(j + 1) * 128], in_=xs_t[:, kc * 128:(kc + 1) * 128], identity=idt[:, :]),
                         reads=[rxs, rid], writes=[rps], signal=(j == 3))
                P.op("dve", lambda e, ps=ps, k4=k4, s=s: e.tensor_tensor(
                    out=hT[:, k4 * 4:k4 * 4 + 4, s * 128:(s + 1) * 128],
                    in0=ps[:, :].rearrange("p (k t) -> p k t", k=4),
                    in1=wT[:, k4 * 4:k4 * 4 + 4].unsqueeze(2).broadcast_to([128, 4, 128]), op=ALU.mult),
                    reads=[rps, rwT], writes=[rhT])

    def norm_bufs(self, st):
        P = self.P
        junk = P.sbuf("junk", [128, D], BF16, st)
        ssq = P.sbuf("ssq", [128, 4], F32, st)
        rstd = P.sbuf("rstd", [128, 4], F32, st)
        xs = [(P.sbuf("xs", [128, D], F32, st), P.nres_("xs%d" % i)) for i in range(2)]
        return (junk, P.nres_("junk"), ssq, P.nres_("ssq"), rstd, P.nres_("rstd"), xs)

    def load_wtile(self, wt, rwt, src, k0, kcn, n0, ncols, q="sp"):
        self.P.dma(q, wt[:, 0:kcn, 0:ncols],
                   src[k0 * 128:(k0 + kcn) * 128, n0:n0 + ncols].rearrange("(kc p) n -> p kc n", p=128),
                   rwt, writes=[rwt])

    def mm_acc(self, ps_ap, rps, pairs, reads):
        n = len(pairs)
        for i, (lhsT, rhs) in enumerate(pairs):
            self.P.op("pe", lambda e, lhsT=lhsT, rhs=rhs, i=i: e.matmul(ps_ap, lhsT=lhsT, rhs=rhs, start=(i == 0), stop=(i == n - 1)),
                      reads=reads, writes=[rps], signal=(i == n - 1))

    def phase_a(self, l, x_src):
        P, g = self.P, self.g
        L = self.L
        with contextlib.ExitStack() as st:
            xt = P.sbuf("xt", [128, 4, D], F32, st)
            rx = P.nres_("xt")
            nbufs = self.norm_bufs(st)
            w1T = P.sbuf("w1T", [128, 16], F32, st)
            rw1 = P.nres_("w1T")
            P.dma("sp", w1T[:, :], self.i["norm1_w"][l].rearrange("(kc p) -> p kc", p=128), rw1, writes=[rw1],
                  allow_slow_non_contiguous=True)
            wqk = P.sbuf("wqk", [128, 2], F32, st)
            rwqk = P.nres_("wqk")
            for ci, nm in enumerate(("q_norm_w", "k_norm_w")):
                for m in range(2):
                    P.dma("sp", wqk[m * 64:(m + 1) * 64, ci:ci + 1], self.i[nm][l:l + 1, :].rearrange("o d -> d o"), rwqk,
                          writes=[rwqk], allow_slow_non_contiguous=True)
            P.op("dve", lambda e: e.tensor_scalar(out=wqk[:, 0:1], in0=wqk[:, 0:1], scalar1=0.125, scalar2=None, op0=ALU.mult),
                 reads=[rwqk], writes=[rwqk])
            hT = P.sbuf("hT", [128, 16, 512], BF16, st)
            rhT = P.nres_("hT")
            wbuf = [(P.sbuf("wA", [128, 16, 512], BF16, st), P.nres_("wA%d" % i)) for i in range(3)]
            ut = [(P.sbuf("ut", [128, 4, 512], BF16, st), P.nres_("ut%d" % i)) for i in range(2)]
            vt = [(P.sbuf("vt", [128, 4, 512], BF16, st), P.nres_("vt%d" % i)) for i in range(2)]
            ot = [(P.sbuf("ot", [128, 512], BF16, st), P.nres_("ot%d" % i)) for i in range(3)]
            sq = [(P.sbuf("sq", [128, 512], BF16, st), P.nres_("sq%d" % i)) for i in range(2)]
            rt = [(P.sbuf("rt", [128, 512], F32, st), P.nres_("rt%d" % i)) for i in range(2)]
            bones, rbones = g["bones_b"]
            wsrc = self.wb["w_in"][l]
            nwl = 0
            oi = 0
            for t in range(self.NT):
                t0 = t * 512
                P.dma("sp", xt[:, :, :], x_src[t0:t0 + 512, :].rearrange("(s p) d -> p s d", p=128), rx, writes=[rx])
                self.norm_T(nbufs, xt, rx, w1T, rw1, hT, rhT)
                for c in range(16):
                    wt, rwt = wbuf[nwl % 3]
                    nwl += 1
                    self.load_wtile(wt, rwt, wsrc, 0, 16, c * 512, 512)
                    if c in (0, 1, 6, 7):
                        isu = c < 2
                        stg, rstg = (ut if isu else vt)[c % 2]
                        for s in range(4):
                            ps, rps = self.bank()
                            self.mm_acc(ps[:, :], rps, [(hT[:, kc, s * 128:(s + 1) * 128], wt[:, kc, :]) for kc in range(16)], [rhT, rwt])
                            if s % 2 == 0:
                                P.op("act", lambda e, ps=ps, stg=stg, s=s: e.activation(out=stg[:, s, :], in_=ps[:, :], func=AF.Copy),
                                     reads=[rps], writes=[rstg])
                            else:
                                P.op("dve", lambda e, ps=ps, stg=stg, s=s: e.tensor_copy(out=stg[:, s, :], in_=ps[:, :]),
                                     reads=[rps], writes=[rstg])
                        dst = self.u_d if isu else self.v_d
                        cc = c if isu else c - 6
                        P.dma("pool", dst[t0:t0 + 512, cc * 512:(cc + 1) * 512].rearrange("(s p) n -> p s n", p=128), stg[:, :, :], rstg,
                              reads=[rstg])
                    else:
                        for j in range(4):
                            ps, rps = self.bank()
                            self.mm_acc(ps[:, :], rps, [(wt[:, kc, j * 128:(j + 1) * 128], hT[:, kc, :]) for kc in range(16)], [rhT, rwt])
                            o, ro = ot[oi % 3]
                            oi += 1
                            if c < 6:
                                isq = c < 4
                                sqt, rsq = sq[oi % 2]
                                rtt, rrt = rt[oi % 2]
                                P.op("act", lambda e, ps=ps, sqt=sqt: e.activation(out=sqt[:, :], in_=ps[:, :], func=AF.Square), reads=[rps], writes=[rsq])
                                ps2, rps2 = self.bank()
                                self.mm_acc(ps2[:, :], rps2, [(bones[:, :], sqt[:, :])], [rbones, rsq])
                                P.op("act", lambda e, ps2=ps2, rtt=rtt: e.activation(out=rtt[:, :], in_=ps2[:, :], func=AF.Sqrt, bias=g["eps_rms"][0][:, :], scale=1.0 / 64),
                                     reads=[rps2, g["eps_rms"][1]], writes=[rrt])
                                P.op("dve", lambda e, rtt=rtt: e.reciprocal(out=rtt[:, :], in_=rtt[:, :]), reads=[rrt], writes=[rrt])
                                ci = 0 if isq else 1
                                P.op("dve", lambda e, ps=ps, rtt=rtt, o=o, ci=ci: e.scalar_tensor_tensor(
                                    out=o[:, :], in0=ps[:, :], scalar=wqk[:, ci:ci + 1], in1=rtt[:, :], op0=ALU.mult, op1=ALU.mult),
                                    reads=[rps, rrt, rwqk], writes=[ro])
                                dst = self.qT_d if isq else self.kT_d
                                r0 = ((c - 2) if isq else (c - 4)) * 512 + j * 128
                            else:
                                P.op("act", lambda e, ps=ps, o=o: e.activation(out=o[:, :], in_=ps[:, :], func=AF.Sigmoid), reads=[rps], writes=[ro])
                                dst = self.gsT_d if c < 12 else self.gaT_d
                                r0 = ((c - 8) if c < 12 else (c - 12)) * 512 + j * 128
                            P.dma("pool", dst[r0:r0 + 128, t0:t0 + 512], o[:, :], ro, reads=[ro])
            P.barrier()

    def build(self):
        ph = self.phases
        self.setup_globals()
        if ph is None or "att" in ph:
            self.setup_attn()
        if ph is None or "w" in ph:
            self.phase_w()
        for l in range(self.nlayers):
            x_src = self.i["x"] if l == 0 else self.xb_d
            x_dst = self.out if l == self.nlayers - 1 else self.xb_d
            if ph is None or "a" in ph:
                self.phase_a(l, x_src)
            if ph is None or "xch" in ph:
                self.phase_xch()
            if ph is None or "ssm" in ph:
                self.phase_ssm(l)
            if ph is None or "att" in ph:
                self.phase_att(l)
            if ph is None or "b1" in ph:
                self.phase_b1(l, x_src)
            if ph is None or "b2" in ph:
                self.phase_b2(l, x_dst)
        self.P.barrier()
        self.P.emit()
        return self.nc

    def phase_xch(self):
        P = self.P
        r = P.nres_("xch")
        tr = min(1024, self.L)
        for c in range(self.nch):
            P.coll("AllGather", self.u_d[c * tr:(c + 1) * tr, :], self.ug[c][:, :], self.groups, reads=[r], writes=[r])
            P.coll("AllGather", self.v_d[c * tr:(c + 1) * tr, :], self.vg[c][:, :], self.groups, reads=[r], writes=[r])
            P.coll("AllGather", self.kT_d[c * self.krows:(c + 1) * self.krows, :], self.kTg[c][:, :], self.groups, reads=[r], writes=[r])
        P.barrier()

    def cmul(self, eng, out_re, out_im, a_re, a_im, b_re, b_im, tmp, rres, wres, neg_im=False):
        P = self.P
        t1, t2 = tmp
        ops = [
            (t1, a_re, b_re, ALU.mult), (t2, a_im, b_im, ALU.mult), (out_re, t1, t2, ALU.subtract),
            (t1, a_re, b_im, ALU.mult), (t2, a_im, b_re, ALU.mult), (out_im, t1, t2, ALU.add),
        ]
        for o, x, y, op in ops:
            P.op(eng, lambda e, o=o, x=x, y=y, op=op: e.tensor_tensor(out=o, in0=x, in1=y, op=op), reads=rres, writes=wres)
        if neg_im:
            P.op(eng, lambda e, o=out_im: e.tensor_scalar(out=o, in0=o, scalar1=-1.0, scalar2=None, op0=ALU.mult), reads=wres, writes=wres)

    def phase_ssm(self, l):
        P, g = self.P, self.g
        L = self.L
        NBLK = 2 * L // 8
        NB2 = NBLK // 2
        bp = min(128, NBLK)
        nbt = NBLK // bp
        nbt2 = nbt // 2
        assert nbt % 2 == 0
        flag, rflag = g["c_flag"]
        nsteps = int(math.ceil(math.log2(NBLK)))
        idt, rid = g["c_ident"]
        swp, rswp = g["c_swap"]
        tmask, rtm = g["c_tmask"]
        with contextlib.ExitStack() as st:
            rp = P.nres_("ssm_par")
            def T(name, shape, dt=F32):
                return P.sbuf(name, shape, dt, st)
            lre, lim, dtt = T("lre", [128, 64]), T("lim", [128, 64]), T("dtt", [128, 64])
            for hf in range(2):
                hs = slice(hf * 64, hf * 64 + 64)
                P.dma("sp", lre[hs, :], self.i["lam_re"][l].rearrange("g p -> p g"), rp, writes=[rp], allow_slow_non_contiguous=True)
                P.dma("sp", lim[hs, :], self.i["lam_im"][l].rearrange("g p -> p g"), rp, writes=[rp], allow_slow_non_contiguous=True)
            P.dma("sp", dtt[:, :], self.i["log_step"][l].partition_broadcast(128), rp, writes=[rp])
            Bre, Bim = T("Bre", [128, 64, 16]), T("Bim", [128, 64, 16])
            for hf in range(2):
                hs = slice(hf * 64, hf * 64 + 64)
                P.dma("sp", Bre[hs, :, :], self.i["ssm_b_re"][l].rearrange("g p c -> p g c"), rp, writes=[rp])
                P.dma("sp", Bim[hs, :, :], self.i["ssm_b_im"][l].rearrange("g p c -> p g c"), rp, writes=[rp])
            Dcol = T("Dcol", [128, 64])
            for j in range(8):
                P.dma("sp", Dcol[16 * j:16 * j + 16, :], self.i["ssm_d"][l].rearrange("(g c) -> c g", c=16), rp, writes=[rp],
                      allow_slow_non_contiguous=True)
            Cre, Cim = T("Cre", [128, 64, 16]), T("Cim", [128, 64, 16])
            crow = T("crow", [128, 128])
            for nm, dstt in (("ssm_c_re", Cre), ("ssm_c_im", Cim)):
                src = self.i[nm][l].rearrange("g c p -> (g c) p")
                for k in range(8):
                    P.dma("sp", crow[:, 0:64], src[k * 128:(k + 1) * 128, :], rp, reads=[rp], writes=[rp])
                    P.dma("sp", crow[:, 64:128], src[k * 128:(k + 1) * 128, :], rp, reads=[rp], writes=[rp])
                    ps, rps = self.bank()
                    P.op("pe", lambda e, ps=ps: e.transpose(out=ps[:, 0:128], in_=crow[:, :], identity=idt[:, :]), reads=[rp, rid], writes=[rps])
                    P.op("dve", lambda e, ps=ps, dstt=dstt, k=k: e.tensor_copy(out=dstt[:, k * 8:(k + 1) * 8, :], in_=ps[:, 0:128].rearrange("p (g c) -> p g c", c=16)),
                         reads=[rps], writes=[rp])
            V = lambda fn, rd=(rp,), wr=(rp,): P.op("dve", fn, reads=list(rd), writes=list(wr))
            A = lambda fn: P.op("act", fn, reads=[rp, g["halfpi"][1]], writes=[rp])
            lr, x1, mag, ang, tq, r_, m1 = [T(n, [128, 64]) for n in ("lr", "x1", "mag", "ang", "tq", "r_", "m1")]
            ti = T("ti", [128, 64], I32)
            sn, cs, ar, ai = [T(n, [128, 64]) for n in ("sn", "cs", "ar", "ai")]
            V(lambda e: e.tensor_scalar(out=lr[:, :], in0=lre[:, :], scalar1=-1e-4, scalar2=None, op0=ALU.min))
            A(lambda e: e.activation(out=dtt[:, :], in_=dtt[:, :], func=AF.Exp))
            V(lambda e: e.tensor_tensor(out=x1[:, :], in0=lr[:, :], in1=dtt[:, :], op=ALU.mult))
            A(lambda e: e.activation(out=mag[:, :], in_=x1[:, :], func=AF.Exp))
            V(lambda e: e.tensor_tensor(out=ang[:, :], in0=lim[:, :], in1=dtt[:, :], op=ALU.mult))
            V(lambda e: e.tensor_scalar(out=tq[:, :], in0=ang[:, :], scalar1=1.0 / (2 * math.pi), scalar2=0.5, op0=ALU.mult, op1=ALU.add))
            V(lambda e: e.tensor_copy(out=ti[:, :], in_=tq[:, :]))
            V(lambda e: e.tensor_copy(out=tq[:, :], in_=ti[:, :]))
            V(lambda e: e.scalar_tensor_tensor(out=r_[:, :], in0=tq[:, :], scalar=-2 * math.pi, in1=ang[:, :], op0=ALU.mult, op1=ALU.add))
            for thr, opc, add in ((-math.pi, ALU.is_lt, 2 * math.pi), (math.pi, ALU.is_gt, -2 * math.pi),
                                  (-math.pi, ALU.is_lt, 2 * math.pi), (math.pi, ALU.is_gt, -2 * math.pi)):
                V(lambda e, thr=thr, opc=opc: e.tensor_single_scalar(out=m1[:, :], in_=r_[:, :], scalar=thr, op=opc))
                V(lambda e, add=add: e.scalar_tensor_tensor(out=r_[:, :], in0=m1[:, :], scalar=add, in1=r_[:, :], op0=ALU.mult, op1=ALU.add))
            V(lambda e: e.tensor_scalar(out=r_[:, :], in0=r_[:, :], scalar1=math.pi, scalar2=-math.pi, op0=ALU.min, op1=ALU.max))
            A(lambda e: e.activation(out=sn[:, :], in_=r_[:, :], func=AF.Sin))
            V(lambda e: e.tensor_scalar(out=m1[:, :], in0=r_[:, :], scalar1=-1.0, scalar2=None, op0=ALU.mult))
            V(lambda e: e.tensor_tensor(out=m1[:, :], in0=m1[:, :], in1=r_[:, :], op=ALU.max))
            A(lambda e: e.activation(out=cs[:, :], in_=m1[:, :], func=AF.Sin, bias=g["halfpi"][0][:, :], scale=-1.0))
            V(lambda e: e.tensor_tensor(out=ar[:, :], in0=mag[:, :], in1=cs[:, :], op=ALU.mult))
            V(lambda e: e.tensor_tensor(out=ai[:, :], in0=mag[:, :], in1=sn[:, :], op=ALU.mult))
            den, nr, fr, fi, t1, t2 = [T(n, [128, 64]) for n in ("den", "nr", "fr", "fi", "t1", "t2")]
            V(lambda e: e.tensor_tensor(out=den[:, :], in0=lr[:, :], in1=lr[:, :], op=ALU.mult))
            V(lambda e: e.tensor_tensor(out=t1[:, :], in0=lim[:, :], in1=lim[:, :], op=ALU.mult))
            V(lambda e: e.tensor_tensor(out=den[:, :], in0=den[:, :], in1=t1[:, :], op=ALU.add))
            V(lambda e: e.reciprocal(out=den[:, :], in_=den[:, :]))
            V(lambda e: e.tensor_scalar(out=nr[:, :], in0=ar[:, :], scalar1=-1.0, scalar2=None, op0=ALU.add))
            V(lambda e: e.tensor_tensor(out=t1[:, :], in0=nr[:, :], in1=lr[:, :], op=ALU.mult))
            V(lambda e: e.tensor_tensor(out=t2[:, :], in0=ai[:, :], in1=lim[:, :], op=ALU.mult))
            V(lambda e: e.tensor_tensor(out=t1[:, :], in0=t1[:, :], in1=t2[:, :], op=ALU.add))
            V(lambda e: e.tensor_tensor(out=fr[:, :], in0=t1[:, :], in1=den[:, :], op=ALU.mult))
            V(lambda e: e.tensor_tensor(out=t1[:, :], in0=ai[:, :], in1=lr[:, :], op=ALU.mult))
            V(lambda e: e.tensor_tensor(out=t2[:, :], in0=nr[:, :], in1=lim[:, :], op=ALU.mult))
            V(lambda e: e.tensor_tensor(out=t1[:, :], in0=t1[:, :], in1=t2[:, :], op=ALU.subtract))
            V(lambda e: e.tensor_tensor(out=fi[:, :], in0=t1[:, :], in1=den[:, :], op=ALU.mult))
            Bbr, Bbi = T("Bbr", [128, 64, 16]), T("Bbi", [128, 64, 16])
            tb1, tb2 = T("tb1", [128, 64, 16]), T("tb2", [128, 64, 16])
            bc16 = lambda a: a[:, :].unsqueeze(2).broadcast_to([128, 64, 16])
            self.cmul("dve", Bbr[:, :, :], Bbi[:, :, :], bc16(fr), bc16(fi), Bre[:, :, :], Bim[:, :, :], (tb1[:, :, :], tb2[:, :, :]), [rp], [rp])
            PWr, PWi = T("PWr", [128, 64, 9]), T("PWi", [128, 64, 9])
            PIr, PIi = T("PIr", [128, 64, 8]), T("PIi", [128, 64, 8])
            PRr, PRi = T("PRr", [128, 64, 8]), T("PRi", [128, 64, 8])
            air, aii = T("air", [128, 64]), T("aii", [128, 64])
            V(lambda e: e.tensor_tensor(out=t1[:, :], in0=ar[:, :], in1=ar[:, :], op=ALU.mult))
            V(lambda e: e.tensor_tensor(out=t2[:, :], in0=ai[:, :], in1=ai[:, :], op=ALU.mult))
            V(lambda e: e.tensor_tensor(out=t1[:, :], in0=t1[:, :], in1=t2[:, :], op=ALU.add))
            V(lambda e: e.reciprocal(out=t1[:, :], in_=t1[:, :]))
            V(lambda e: e.tensor_tensor(out=air[:, :], in0=ar[:, :], in1=t1[:, :], op=ALU.mult))
            V(lambda e: e.scalar_tensor_tensor(out=aii[:, :], in0=ai[:, :], scalar=-1.0, in1=t1[:, :], op0=ALU.mult, op1=ALU.mult))
            for (Xr, Xi, br_, bi_, n) in ((PWr, PWi, ar, ai, 9), (PIr, PIi, air, aii, 8)):
                V(lambda e, Xr=Xr: e.memset(Xr[:, :, 0:1], 1.0))
                V(lambda e, Xi=Xi: e.memset(Xi[:, :, 0:1], 0.0))
                for k in range(1, n):
                    self.cmul("dve", Xr[:, :, k], Xi[:, :, k], Xr[:, :, k - 1], Xi[:, :, k - 1], br_[:, :], bi_[:, :], (t1[:, :], t2[:, :]), [rp], [rp])
            for j in range(8):
                V(lambda e, j=j: e.tensor_copy(out=PRr[:, :, j], in_=PWr[:, :, 7 - j]))
                V(lambda e, j=j: e.tensor_copy(out=PRi[:, :, j], in_=PWi[:, :, 7 - j]))
            APr, APi = T("APr", [128, 64, nsteps]), T("APi", [128, 64, nsteps])
            V(lambda e: e.tensor_copy(out=APr[:, :, 0], in_=PWr[:, :, 8]))
            V(lambda e: e.tensor_copy(out=APi[:, :, 0], in_=PWi[:, :, 8]))
            for k in range(1, nsteps):
                self.cmul("dve", APr[:, :, k], APi[:, :, k], APr[:, :, k - 1], APi[:, :, k - 1], APr[:, :, k - 1], APi[:, :, k - 1], (t1[:, :], t2[:, :]), [rp], [rp])
            V(lambda e: e.tensor_scalar(out=APi[64:128, :, :], in0=APi[64:128, :, :], scalar1=-1.0, scalar2=None, op0=ALU.mult))
            GB = 8
            FW = GB * 16
            Bjr, Bji, BEr, BEi = [T(n, [128, GB, 8, 16]) for n in ("Bjr", "Bji", "BEr", "BEi")]
            Crr, Cri = T("Crr", [128, GB, 9, 16]), T("Cri", [128, GB, 9, 16])
            tg1, tg2 = T("tg1", [128, GB, 9, 16]), T("tg2", [128, GB, 9, 16])
            rgen = P.nres_("ssm_gen")
            Z = T("Z", [128, nbt, 8, FW], BF16)
            rZ = P.nres_("ssm_Z")
            Zc = T("Zc", [128, nbt, GB, 8, 16])
            rZc = P.nres_("ssm_Zc")
            U8 = T("U8", [128, GB, NBLK])
            rU8 = P.nres_("ssm_U8")
            LE = [(T("LE", [128, 128]), P.nres_("ssm_LE%d" % i)) for i in range(2)]
            LS = [(T("LS", [128, 128]), P.nres_("ssm_LS%d" % i)) for i in range(2)]
            MK = [(T("MK", [128, 128]), P.nres_("ssm_MK%d" % i)) for i in range(16)]
            Tt = T("Tt", [128, GB, 128])
            rTt = P.nres_("ssm_Tt")
            tmpT = [(T("tmpT", [128, 128]), P.nres_("ssm_tmpT%d" % i)) for i in range(2)]
            W = NBLK + 1
            S = T("S", [128, GB, W])
            rS = [P.nres_("ssm_S%d" % i) for i in range(GB)]
            Y8 = [(T("Y8", [128, NB2]), P.nres_("ssm_Y8%d" % i)) for i in range(2)]
            Yt = T("Yt", [128, nbt2, 8, FW])
            rYt = P.nres_("ssm_Yt")
            P.op("pool", lambda e: e.memset(S[:, :, 0:1], 0.0), writes=rS)
            nmk = 0
            for gb in range(NG // GB):
                g0 = gb * GB
                gs_ = slice(g0, g0 + GB)
                bj = lambda a: a[:, gs_, :].unsqueeze(3).broadcast_to([128, GB, a.shape[2], 16])
                bb = lambda a, n: a[:, gs_, :].unsqueeze(2).broadcast_to([128, GB, n, 16])
                t8 = (tg1[:, :, 0:8, :], tg2[:, :, 0:8, :])
                self.cmul("pool", Bjr[:, :, :, :], Bji[:, :, :, :], bj(PIr), bj(PIi), bb(Bbr, 8), bb(Bbi, 8), t8, [rp], [rgen])
                self.cmul("pool", BEr[:, :, :, :], BEi[:, :, :, :], bj(PRr), bj(PRi), bb(Bbr, 8), bb(Bbi, 8), t8, [rp], [rgen])
                self.cmul("pool", Crr[:, :, :, :], Cri[:, :, :, :], bj(PWr), bj(PWi), bb(Cre, 9), bb(Cim, 9), (tg1[:, :, :, :], tg2[:, :, :, :]), [rp], [rgen], neg_im=True)
                for bt in range(nbt):
                    usrc, b2 = (self.ug[bt], 0) if bt < nbt2 else (self.u_d, bt - nbt2)
                    P.dma("sp", Z[0:bp, bt, :, :], usrc[b2 * bp * 8:(b2 + 1) * bp * 8, g0 * 16:g0 * 16 + FW].rearrange("(b j) f -> b j f", j=8), rZ, writes=[rZ])
                P.op("pool", lambda e: e.tensor_copy(out=Zc[0:bp, :, :, :, :], in_=Z[0:bp, :, :, :].rearrange("p b j (g c) -> p b g j c", c=16)), reads=[rZ], writes=[rZc])
                for gi in range(GB):
                    gg = g0 + gi
                    ps, rps = self.bank()
                    for bt in range(nbt):
                        P.op("pe", lambda e, ps=ps, bt=bt, gi=gi: e.transpose(out=ps[:, bt * bp:(bt + 1) * bp], in_=Zc[0:bp, bt, gi, :, :].rearrange("p j c -> p (j c)"), identity=idt[0:bp, 0:bp]),
                             reads=[rZc, rid], writes=[rps], signal=(bt == nbt - 1))
                    P.op("act", lambda e, ps=ps, gi=gi: e.activation(out=U8[:, gi, 0:NB2], in_=ps[:, 0:NB2], func=AF.Copy, scale=flag[:, 0:1]), reads=[rps, rflag], writes=[rU8])
                    P.op("act", lambda e, ps=ps, gi=gi: e.activation(out=U8[:, gi, NB2:NBLK], in_=ps[:, NB2:NBLK], func=AF.Copy), reads=[rps], writes=[rU8])
                    le, rle = LE[gi % 2]
                    ps, rps = self.bank()
                    P.op("pe", lambda e, ps=ps, gi=gi: e.transpose(out=ps[:, 0:64], in_=BEr[0:64, gi, :, :].rearrange("p j c -> p (j c)"), identity=idt[0:64, 0:64]), reads=[rgen, rid], writes=[rps], signal=False)
                    P.op("pe", lambda e, ps=ps, gi=gi: e.transpose(out=ps[:, 64:128], in_=BEi[0:64, gi, :, :].rearrange("p j c -> p (j c)"), identity=idt[0:64, 0:64]), reads=[rgen, rid], writes=[rps])
                    P.op("dve", lambda e, ps=ps, le=le: e.tensor_copy(out=le[:, :], in_=ps[:, 0:128]), reads=[rps], writes=[rle])
                    ps, rps = self.bank()
                    self.mm_acc(ps[:, 0:NBLK], rps, [(le[:, :], U8[:, gi, :])], [rle, rU8])
                    P.op("act", lambda e, ps=ps, gi=gi: e.activation(out=S[:, gi, 1:W], in_=ps[:, 0:NBLK], func=AF.Copy), reads=[rps], writes=[rS[gi]])
                    ps, rps = self.bank()
                    self.mm_acc(ps[:, 0:128], rps, [(Bjr[0:64, gi, :, :].rearrange("p j c -> p (j c)"), Crr[0:64, gi, 0:8, :].rearrange("p j c -> p (j c)")),
                                                    (Bji[0:64, gi, :, :].rearrange("p j c -> p (j c)"), Cri[0:64, gi, 0:8, :].rearrange("p j c -> p (j c)"))], [rgen])
                    tt_, rtt_ = tmpT[gi % 2]
                    P.op("dve", lambda e, ps=ps, tt_=tt_: e.tensor_tensor(out=tt_[:, :], in0=ps[:, 0:128], in1=tmask[:, :], op=ALU.mult), reads=[rps, rtm], writes=[rtt_])
                    P.op("dve", lambda e, tt_=tt_, gi=gi, gg=gg: e.scalar_tensor_tensor(out=Tt[:, gi, :], in0=idt[:, :], scalar=Dcol[:, gg:gg + 1], in1=tt_[:, :], op0=ALU.mult, op1=ALU.add),
                         reads=[rtt_, rid, rp], writes=[rTt])
                def build_mk(k):
                    out = []
                    for gi in range(GB):
                        gg = g0 + gi
                        mk, rmk = MK[(k * GB + gi) % len(MK)]
                        P.op("dve", lambda e, mk=mk, gg=gg, k=k: e.tensor_scalar(out=mk[:, :], in0=idt[:, :], scalar1=APr[:, gg, k:k + 1], scalar2=None, op0=ALU.mult),
                             reads=[rid, rp], writes=[rmk])
                        P.op("dve", lambda e, mk=mk, gg=gg, k=k: e.scalar_tensor_tensor(out=mk[:, :], in0=swp[:, :], scalar=APi[:, gg, k:k + 1], in1=mk[:, :], op0=ALU.mult, op1=ALU.add),
                             reads=[rswp, rp, rmk], writes=[rmk])
                        out.append((mk, rmk))
                    return out

                mks = build_mk(0)
                for k in range(nsteps):
                    sh = 1 << k
                    n = NBLK - sh
                    pss = []
                    for gi in range(GB):
                        mk, rmk = mks[gi]
                        ps, rps = self.bank()
                        self.mm_acc(ps[:, 0:n], rps, [(mk[:, :], S[:, gi, 1:1 + n])], [rmk, rS[gi]])
                        pss.append((ps, rps))
                    if k + 1 < nsteps:
                        mks = build_mk(k + 1)
                    for gi in range(GB):
                        ps, rps = pss[gi]
                        P.op("dve", lambda e, ps=ps, gi=gi, sh=sh, n=n: e.tensor_tensor(out=S[:, gi, 1 + sh:1 + sh + n], in0=ps[:, 0:n], in1=S[:, gi, 1 + sh:1 + sh + n], op=ALU.add),
                             reads=[rps], writes=[rS[gi]])
                for gi in range(GB):
                    ls, rls = LS[gi % 2]
                    P.op("dve", lambda e, ls=ls, gi=gi: e.tensor_copy(out=ls[0:64, :], in_=Crr[0:64, gi, 1:9, :].rearrange("p j c -> p (j c)")), reads=[rgen], writes=[rls])
                    P.op("dve", lambda e, ls=ls, gi=gi: e.tensor_copy(out=ls[64:128, :], in_=Cri[64:128, gi, 1:9, :].rearrange("p j c -> p (j c)")), reads=[rgen], writes=[rls])
                    ps, rps = self.bank()
                    self.mm_acc(ps[:, 0:NB2], rps, [(Tt[:, gi, :], U8[:, gi, NB2:NBLK]), (ls[:, :], S[:, gi, NB2:NBLK])], [rTt, rU8, rls, rS[gi]])
                    y8, ry8 = Y8[gi % 2]
                    P.op("act", lambda e, ps=ps, y8=y8: e.activation(out=y8[:, :], in_=ps[:, 0:NB2], func=AF.Copy), reads=[rps], writes=[ry8])
                    ps, rps = self.bank()
                    for bt in range(nbt2):
                        P.op("pe", lambda e, ps=ps, bt=bt, y8=y8: e.transpose(out=ps[0:bp, bt * 128:(bt + 1) * 128], in_=y8[:, bt * bp:(bt + 1) * bp], identity=idt[:, :]),
                             reads=[ry8, rid], writes=[rps], signal=(bt == nbt2 - 1))
                    P.op("dve", lambda e, ps=ps, gi=gi: e.tensor_copy(out=Yt[0:bp, :, :, gi * 16:(gi + 1) * 16], in_=ps[0:bp, 0:nbt2 * 128].rearrange("p (b j c) -> p b j c", b=nbt2, j=8)),
                         reads=[rps], writes=[rYt])
                for bt in range(nbt2):
                    P.dma("pool", self.y_d[bt * bp * 8:(bt + 1) * bp * 8, g0 * 16:g0 * 16 + FW].rearrange("(b j) f -> b j f", j=8), Yt[0:bp, bt, :, :], rYt, reads=[rYt])
            P.barrier()

    def setup_attn(self):
        P, g = self.P, self.g
        reld, rreld = g["c_reld"]
        tab = P.sbuf("tab", [128, 256], F32)
        rtab = P.nres_("tab")
        P.dma("sp", tab[:, :], self.i["rel_bias"].rearrange("b h -> (b h)").partition_broadcast(128), rtab, writes=[rtab])
        steps = [(-90, 15, 14), (-63, 14, 13), (-45, 13, 12), (-31, 12, 11), (-22, 11, 10), (-15, 10, 9), (-11, 9, 8)]
        steps += [(-n, n + 1, n) for n in range(7, -1, -1)]
        steps += [(1, 0, 17)] + [(n, 15 + n, 16 + n) for n in range(2, 8)]
        steps += [(8, 23, 24), (12, 24, 25), (16, 25, 26), (23, 26, 27), (32, 27, 28), (46, 28, 29), (64, 29, 30), (91, 30, 31)]
        ns = len(steps)
        dl = P.sbuf("dl", [128, ns, 8], F32)
        rdl = P.nres_("dl")
        for s, (thr, fb, tb) in enumerate(steps):
            P.op("dve", lambda e, s=s, fb=fb, tb=tb: e.tensor_tensor(out=dl[:, s, :], in0=tab[:, tb * 8:tb * 8 + 8], in1=tab[:, fb * 8:fb * 8 + 8], op=ALU.subtract),
                 reads=[rtab], writes=[rdl])
        bias = P.sbuf("biasT", [128, NH, 2, 128], F32)
        rbias = P.nres_("biasT")
        mk = P.sbuf("mk", [128, 128], F32)
        rmk = P.nres_("mk")
        for kind in range(2):
            off = -128.0 * kind
            for h in range(NH):
                P.op("dve", lambda e, h=h, kind=kind: e.tensor_scalar(out=bias[:, h, kind, :], in0=reld[:, :], scalar1=0.0, scalar2=tab[:, 15 * 8 + h:15 * 8 + h + 1], op0=ALU.mult, op1=ALU.add),
                     reads=[rreld, rtab], writes=[rbias])
            for s, (thr, fb, tb) in enumerate(steps):
                if kind == 1 and thr > -1:
                    continue
                if thr > 64:
                    continue
                P.op("dve", lambda e, thr=thr, off=off: e.tensor_single_scalar(out=mk[:, :], in_=reld[:, :], scalar=float(thr) - off, op=ALU.is_ge), reads=[rreld], writes=[rmk])
                for h in range(NH):
                    P.op("dve", lambda e, h=h, kind=kind, s=s: e.scalar_tensor_tensor(out=bias[:, h, kind, :], in0=mk[:, :], scalar=dl[:, s, h:h + 1], in1=bias[:, h, kind, :], op0=ALU.mult, op1=ALU.add),
                         reads=[rmk, rdl], writes=[rbias])
        for h in range(NH):
            P.op("pool", lambda e, h=h: e.memset(bias[64:128, h, 0, 0:64], -30000.0), reads=[rbias], writes=[rbias])
        g["tab"] = (tab, rtab)
        g["biasT"] = (bias, rbias)

    def bank2(self, lo, n, key):
        c = self.bctr.get(key, 0)
        self.bctr[key] = c + 1
        return self.psb[lo + c % n]

    def phase_att(self, l):
        P, g = self.P, self.g
        L = self.L
        nq = L // 128
        lam_init = 0.8 - 0.6 * math.exp(-0.3 * l)
        idt, rid = g["c_ident"]
        tab, rtab = g["tab"]
        bias, rbias = g["biasT"]
        self.bctr = {}
        with contextlib.ExitStack() as st:
            T = lambda name, shape, dt=F32: P.sbuf(name, shape, dt, st)
            rpar = P.nres_("att_par")
            lq = T("lq", [128, 4, 64])
            for ci, nm in enumerate(("lambda_q1", "lambda_k1", "lambda_q2", "lambda_k2")):
                P.dma("sp", lq[:, ci, :], self.i[nm][l].partition_broadcast(128), rpar, writes=[rpar])
            subw = T("subw", [128, 128])
            P.dma("sp", subw[:, :], self.i["subln_w"][l].partition_broadcast(128), rpar, writes=[rpar])
            e12 = T("e12", [128, 2])
            nlam = T("nlam", [128, 1])
            V = lambda fn: P.op("dve", fn, reads=[rpar], writes=[rpar])
            V(lambda e: e.tensor_tensor(out=lq[:, 0, :], in0=lq[:, 0, :], in1=lq[:, 1, :], op=ALU.mult))
            V(lambda e: e.tensor_tensor(out=lq[:, 2, :], in0=lq[:, 2, :], in1=lq[:, 3, :], op=ALU.mult))
            V(lambda e: e.reduce_sum(out=e12[:, 0:1], in_=lq[:, 0, :], axis=AX.X))
            V(lambda e: e.reduce_sum(out=e12[:, 1:2], in_=lq[:, 2, :], axis=AX.X))
            P.op("act", lambda e: e.activation(out=e12[:, :], in_=e12[:, :], func=AF.Exp), reads=[rpar], writes=[rpar])
            V(lambda e: e.tensor_tensor(out=nlam[:, :], in0=e12[:, 1:2], in1=e12[:, 0:1], op=ALU.subtract))
            V(lambda e: e.tensor_scalar(out=nlam[:, :], in0=nlam[:, :], scalar1=-lam_init, scalar2=None, op0=ALU.add))
            V(lambda e: e.tensor_scalar(out=subw[:, :], in0=subw[:, :], scalar1=1.0 - lam_init, scalar2=None, op0=ALU.mult))
            nb = nq
            NKP = nq
            mflag = g["mflag"][0]
            rflag = g["mflag"][1]
            cbp = T("cbp", [128, 8])
            P.op("dve", lambda e: e.tensor_scalar(out=cbp[:, :], in0=tab[:, 120:128], scalar1=mflag[:, 0:1], scalar2=None, op0=ALU.add), reads=[rtab, rflag], writes=[rpar])
            KT = [(T("KT", [128, 2 * L], BF16), P.nres_("att_KT%d" % i)) for i in range(2)]
            QT = [[(T("QT", [128, L], BF16), P.nres_("att_QT%d_%d" % (i, m))) for m in range(2)] for i in range(2)]
            V1 = [(T("V1", [128, 2 * nb, 132], BF16), P.nres_("att_V1%d" % i)) for i in range(2)]
            for v1, rv1 in V1:
                P.op("pool", lambda e, v1=v1: e.memset(v1[:, :, 128:132], 0.0), writes=[rv1])
                P.op("pool", lambda e, v1=v1: e.memset(v1[:, :, 128:129], 1.0), writes=[rv1])
            for i in range(2):
                for m in range(2):
                    qz, rqz = QT[i][m]
                    P.op("pool", lambda e, qz=qz: e.memset(qz[:, :], 0.0), writes=[rqz])
            PT = [(T("PT", [128, 4, 128], BF16), P.nres_("att_PT%d" % i)) for i in range(5)]
            tS = [(T("tS", [128, 128]), P.nres_("att_tS%d" % i)) for i in range(3)]
            rc = [(T("rc", [128, 2]), P.nres_("att_rc%d" % i)) for i in range(2)]
            oT = [(T("oT", [128, 128]), P.nres_("att_oT%d" % i)) for i in range(2)]
            on = [(T("on", [128, 128]), P.nres_("att_on%d" % i)) for i in range(2)]
            sj = [(T("sj", [128, 128], BF16), P.nres_("att_sj%d" % i)) for i in range(2)]
            ssq = [(T("ssq", [128, 1]), P.nres_("att_ssq%d" % i)) for i in range(2)]
            yo = [(T("yo", [128, 512], BF16), P.nres_("att_yo%d" % i)) for i in range(2)]
            npt = 0
            nts = 0
            wbufs = self.w_bufs(st, "b") if self.wq else None
            import os as _os
            STOP = int(_os.environ.get("ATT_STOP", "4"))
            for h in range(NH if STOP > 0 else 0):
                (kt, rkt), (v1, rv1) = KT[h % 2], V1[h % 2]
                qts = QT[h % 2]
                hpc = self.krows // 128
                P.dma("sp", kt[:, 0:L], self.kTg[h // hpc][(h % hpc) * 128:(h % hpc + 1) * 128, :], rkt, writes=[rkt])
                P.dma("sp", kt[:, L:2 * L], self.kT_d[h * 128:(h + 1) * 128, :], rkt, writes=[rkt])
                for m in range(2):
                    P.dma("sp", qts[m][0][m * 64:(m + 1) * 64, :], self.qT_d[h * 128 + m * 64:h * 128 + (m + 1) * 64, :], qts[m][1], writes=[qts[m][1]])
                for c in range(self.nch):
                    nbc = nb // self.nch
                    P.dma("sp", v1[:, c * nbc:(c + 1) * nbc, 0:128], self.vg[c][0:nbc * 128, h * 128:(h + 1) * 128].rearrange("(j p) e -> p j e", p=128), rv1, writes=[rv1])
                P.dma("sp", v1[:, nb:2 * nb, 0:128], self.v_d[:, h * 128:(h + 1) * 128].rearrange("(j p) e -> p j e", p=128), rv1, writes=[rv1])
                cb = tab[:, 15 * 8 + h:15 * 8 + h + 1]
                cbprev = cbp[:, h:h + 1]
                tps = None
                pending = []

                def emit_pv(o_ps, ro, pt, rpt, grp, i, v1=None, rv1=None):
                    for idx, j in enumerate(grp):
                        P.op("pe", lambda e, o_ps=o_ps, pt=pt, idx=idx, j=j, i=i, v1=v1: e.matmul(
                            o_ps[:, 0:130], lhsT=pt[:, idx, :], rhs=v1[:, j, 0:130], start=(j == 0), stop=(j == NKP + i)),
                            reads=[rpt, rv1], writes=[ro], signal=(j == NKP + i))

                def emit_fin(i, ops_, tps_box):
                    (o0, ro0), (o1, ro1) = ops_
                    rct, rrc = rc[i % 2]
                    ot_, rot = oT[i % 2]
                    on_, ron = on[i % 2]
                    sj_, rsj = sj[i % 2]
                    sq_, rsq = ssq[i % 2]
                    P.op("dve", lambda e, rct=rct, o0=o0: e.reciprocal(out=rct[:, 0:1], in_=o0[:, 128:129]), reads=[ro0], writes=[rrc])
                    P.op("dve", lambda e, rct=rct, o1=o1: e.reciprocal(out=rct[:, 1:2], in_=o1[:, 128:129]), reads=[ro1], writes=[rrc])
                    P.op("dve", lambda e, rct=rct: e.tensor_tensor(out=rct[:, 1:2], in0=rct[:, 1:2], in1=nlam[:, :], op=ALU.mult), reads=[rrc, rpar], writes=[rrc])
                    P.op("dve", lambda e, rct=rct, o0=o0, ot_=ot_: e.tensor_scalar(out=ot_[:, :], in0=o0[:, 0:128], scalar1=rct[:, 0:1], scalar2=None, op0=ALU.mult),
                         reads=[ro0, rrc], writes=[rot])
                    P.op("dve", lambda e, rct=rct, o1=o1, ot_=ot_: e.scalar_tensor_tensor(out=ot_[:, :], in0=o1[:, 0:128], scalar=rct[:, 1:2], in1=ot_[:, :], op0=ALU.mult, op1=ALU.add),
                         reads=[ro1, rrc, rot], writes=[rot])
                    P.op("act", lambda e, ot_=ot_, sj_=sj_, sq_=sq_: e.activation(out=sj_[:, :], in_=ot_[:, :], func=AF.Square, accum_out=sq_[:, :]),
                         reads=[rot], writes=[rsj, rsq])
                    P.op("act", lambda e, sq_=sq_: e.activation(out=sq_[:, :], in_=sq_[:, :], func=AF.Sqrt, bias=g["eps_sub"][0][:, :], scale=1.0 / 128),
                         reads=[rsq, g["eps_sub"][1]], writes=[rsq])
                    P.op("dve", lambda e, sq_=sq_: e.reciprocal(out=sq_[:, :], in_=sq_[:, :]), reads=[rsq], writes=[rsq])
                    P.op("dve", lambda e, ot_=ot_, sq_=sq_, on_=on_: e.scalar_tensor_tensor(out=on_[:, :], in0=ot_[:, :], scalar=sq_[:, 0:1], in1=subw[:, :], op0=ALU.mult, op1=ALU.mult),
                         reads=[rot, rsq, rpar], writes=[ron])
                    tps, rtps = self.psb[7]
                    P.op("pe", lambda e, tps=tps, on_=on_, i=i: e.transpose(out=tps[:, (i % 4) * 128:(i % 4 + 1) * 128], in_=on_[:, :], identity=idt[:, :]),
                         reads=[ron, rid], writes=[rtps])
                    if i % 4 == 3 or i == nq - 1:
                        nblk = i % 4 + 1
                        y_, ry = yo[(i // 4) % 2]
                        P.op("act", lambda e, tps=tps, y_=y_, nblk=nblk: e.activation(out=y_[:, 0:nblk * 128], in_=tps[:, 0:nblk * 128], func=AF.Copy), reads=[rtps], writes=[ry])
                        q0 = (i - nblk + 1) * 128
                        P.dma("pool", self.yaT_d[h * 128:(h + 1) * 128, q0:q0 + nblk * 128], y_[:, 0:nblk * 128], ry, reads=[ry])

                def flush(keep):
                    while sum(1 for kd, _ in pending if kd == "pv") > keep:
                        kd, fn = pending.pop(0)
                        fn()
                    while pending and pending[0][0] == "fin" and keep == 0:
                        kd, fn = pending.pop(0)
                        fn()

                for i in range(nq if STOP > 1 else 0):
                    ops_ = []
                    if wbufs is not None:
                        self.w_pump(wbufs, 3)
                    for m in range(2):
                        qt, rqt = qts[m]
                        o_ps, ro = self.bank2(0, 4, "O")
                        ops_.append((o_ps, ro))
                        gi_ = NKP + i
                        for j0 in range(0, gi_ + 1, 4):
                            grp = list(range(j0, min(j0 + 4, gi_ + 1)))
                            isprev = j0 < NKP
                            s_ps, rs = self.bank2(4, 3, "S")
                            for idx, j in enumerate(grp):
                                P.op("pe", lambda e, s_ps=s_ps, idx=idx, j=j, i=i, kt=kt, qt=qt: e.matmul(
                                    s_ps[:, idx * 128:(idx + 1) * 128], lhsT=kt[:, j * 128:(j + 1) * 128],
                                    rhs=qt[:, i * 128:(i + 1) * 128], start=True, stop=True),
                                    reads=[rkt, rqt], writes=[rs], signal=(idx == len(grp) - 1))
                            pt, rpt = PT[npt % 5]
                            npt += 1
                            nfar = len([j for j in grp if j <= gi_ - 2])
                            if nfar:
                                fb_ = cbprev if isprev else cb
                                P.op("act", lambda e, s_ps=s_ps, pt=pt, nfar=nfar, fb_=fb_: e.activation(
                                    out=pt[:, 0:nfar, :], in_=s_ps[:, 0:nfar * 128].rearrange("p (a b) -> p a b", a=nfar), func=AF.Exp, bias=fb_),
                                    reads=[rs, rtab, rpar], writes=[rpt])
                            for idx, j in enumerate(grp):
                                if j <= gi_ - 2:
                                    continue
                                kind = 0 if j == gi_ else 1
                                ts_, rts = tS[nts % 3]
                                nts += 1
                                if j < NKP:
                                    P.op("dve", lambda e, s_ps=s_ps, idx=idx, ts_=ts_, kind=kind, h=h: e.scalar_tensor_tensor(
                                        out=ts_[:, :], in0=s_ps[:, idx * 128:(idx + 1) * 128], scalar=mflag[:, 0:1], in1=bias[:, h, kind, :], op0=ALU.add, op1=ALU.add),
                                        reads=[rs, rbias, rflag], writes=[rts])
                                else:
                                    P.op("dve", lambda e, s_ps=s_ps, idx=idx, ts_=ts_, kind=kind, h=h: e.tensor_tensor(
                                        out=ts_[:, :], in0=s_ps[:, idx * 128:(idx + 1) * 128], in1=bias[:, h, kind, :], op=ALU.add),
                                        reads=[rs, rbias], writes=[rts])
                                P.op("act", lambda e, ts_=ts_, pt=pt, idx=idx: e.activation(out=pt[:, idx, :], in_=ts_[:, :], func=AF.Exp),
                                     reads=[rts], writes=[rpt])
                            flush(1)
                            pending.append(("pv", lambda o_ps=o_ps, ro=ro, pt=pt, rpt=rpt, grp=grp, i=i: emit_pv(o_ps, ro, pt, rpt, grp, i, v1, rv1)))
                    pending.append(("fin", lambda i=i, ops_=ops_: emit_fin(i, ops_, None)))
                flush(0)
            if wbufs is not None:
                self.w_pump(wbufs, 10 ** 9)
            P.barrier()

    def phase_b1(self, l, x_src):
        P, g = self.P, self.g
        idt, rid = g["c_ident"]
        with contextlib.ExitStack() as st:
            T = lambda name, shape, dt=F32: P.sbuf(name, shape, dt, st)
            xt, rx = T("xt", [128, 4, D]), P.nres_("xt")
            ytoks = [(T("ytok", [128, SSMW]), P.nres_("b1_ytok%d" % i)) for i in range(2)]
            tmp = [(T("gt", [128, SSMW]), P.nres_("b1_gt%d" % i)) for i in range(2)]
            yg = [(T("yg", [128, SSMW]), P.nres_("b1_yg%d" % i)) for i in range(2)]
            ygf, rygf = T("ygf", [128, 8, 512]), P.nres_("b1_ygf")
            ygb, rygb = T("ygb", [128, 8, 512], BF16), P.nres_("b1_ygb")
            yss, ryss = T("yss", [128, 8, 512], BF16), P.nres_("b1_yss")
            yat, ryat = T("yat", [128, 8, 512], BF16), P.nres_("b1_yat")
            gsTs = [(T("gsT", [128, 4, 512], BF16), P.nres_("b1_gs%d" % i)) for i in range(2)]
            gaTs = [(T("gaT", [128, 4, 512], BF16), P.nres_("b1_ga%d" % i)) for i in range(2)]
            mT, rmT = T("mT", [128, 16, 512], BF16), P.nres_("b1_mT")
            sgt = [(T("sgt", [128, 512]), P.nres_("b1_sgt%d" % i)) for i in range(2)]
            m1 = [(T("m1", [128, 512]), P.nres_("b1_m1%d" % i)) for i in range(2)]
            m2 = [(T("m2", [128, 512]), P.nres_("b1_m2%d" % i)) for i in range(2)]
            wbuf = [(T("wB", [128, 16, 512], BF16), P.nres_("wA%d" % i)) for i in range(2)]
            wglu, rwglu = T("wglu", [128, 8, SSMW], BF16), P.nres_("b1_wglu")
            bglu, rbglu = T("bglu", [128, 8]), P.nres_("b1_bglu")
            self.load_wtile(wglu, rwglu, self.wb["w_glu"][l], 0, 8, 0, SSMW)
            P.dma("sp", bglu[:, :], self.i["b_glu"][l].rearrange("(c p) -> p c", p=128), rbglu, writes=[rbglu], allow_slow_non_contiguous=True)
            nw = 0
            for t in range(self.NT):
                t0 = t * 512
                P.dma("sp", xt[:, :, :], x_src[t0:t0 + 512, :].rearrange("(s p) d -> p s d", p=128), rx, writes=[rx])
                P.dma("sp", yat[:, :, :], self.yaT_d.rearrange("(fc p) t -> p fc t", p=128)[:, :, t0:t0 + 512], ryat, writes=[ryat])
                for s in range(4):
                    (tm, rtm), (ygs, rygs) = tmp[s % 2], yg[s % 2]
                    ytk, rytok = ytoks[s % 2]
                    P.dma("sp", ytk[:, :], self.y_d[t0 + s * 128:t0 + (s + 1) * 128, :], rytok, writes=[rytok])
                    P.op("pool", lambda e, tm=tm, ytk=ytk: e.tensor_tensor(out=tm[:, :], in0=ytk[:, :], in1=ytk[:, :], op=ALU.mult), reads=[rytok], writes=[rtm])
                    P.op("dve", lambda e, tm=tm: e.tensor_scalar(out=tm[:, :], in0=tm[:, :], scalar1=0.044715, scalar2=1.0, op0=ALU.mult, op1=ALU.add), reads=[rtm], writes=[rtm])
                    P.op("pool", lambda e, tm=tm, ytk=ytk: e.tensor_tensor(out=tm[:, :], in0=tm[:, :], in1=ytk[:, :], op=ALU.mult), reads=[rtm, rytok], writes=[rtm])
                    P.op("act", lambda e, tm=tm: e.activation(out=tm[:, :], in_=tm[:, :], func=AF.Sigmoid, scale=2.0 * math.sqrt(2.0 / math.pi)), reads=[rtm], writes=[rtm])
                    P.op("dve", lambda e, tm=tm, ygs=ygs, ytk=ytk: e.tensor_tensor(out=ygs[:, :], in0=tm[:, :], in1=ytk[:, :], op=ALU.mult), reads=[rtm, rytok], writes=[rygs])
                    for k4 in range(2):
                        ps, rps = self.bank()
                        for j in range(4):
                            fc = k4 * 4 + j
                            P.op("pe", lambda e, ps=ps, ygs=ygs, fc=fc, j=j: e.transpose(out=ps[:, j * 128:(j + 1) * 128], in_=ygs[:, fc * 128:(fc + 1) * 128], identity=idt[:, :]),
                                 reads=[rygs, rid], writes=[rps], signal=(j == 3))
                        P.op("act", lambda e, ps=ps, k4=k4, s=s: e.activation(out=ygf[:, k4 * 4:k4 * 4 + 4, s * 128:(s + 1) * 128], in_=ps[:, :].rearrange("p (k t) -> p k t", k=4), func=AF.Copy),
                             reads=[rps], writes=[rygf])
                        P.op("dve", lambda e, ps=ps, k4=k4, s=s: e.tensor_copy(out=ygb[:, k4 * 4:k4 * 4 + 4, s * 128:(s + 1) * 128], in_=ps[:, :].rearrange("p (k t) -> p k t", k=4)),
                             reads=[rps], writes=[rygb])
                for cb in range(8):
                    ps, rps = self.bank()
                    self.mm_acc(ps[:, :], rps, [(wglu[:, kc, cb * 128:(cb + 1) * 128], ygb[:, kc, :]) for kc in range(8)], [rwglu, rygb])
                    sg, rsg = sgt[cb % 2]
                    P.op("act", lambda e, ps=ps, sg=sg, cb=cb: e.activation(out=sg[:, :], in_=ps[:, :], func=AF.Sigmoid, bias=bglu[:, cb:cb + 1]), reads=[rps, rbglu], writes=[rsg])
                    P.op("dve", lambda e, sg=sg, cb=cb: e.tensor_tensor(out=yss[:, cb, :], in0=ygf[:, cb, :], in1=sg[:, :], op=ALU.mult), reads=[rsg, rygf], writes=[ryss])
                for c4 in range(4):
                    (ws, rws), (wa, rwa) = wbuf[0], wbuf[1]
                    self.load_wtile(ws, rws, self.wb["w_proj_ssm"][l], 0, 8, c4 * 512, 512)
                    self.load_wtile(wa, rwa, self.wb["w_proj_attn"][l], 0, 8, c4 * 512, 512)
                    (gsT, rgs), (gaT, rga) = gsTs[c4 % 2], gaTs[c4 % 2]
                    P.dma("sp", gsT[:, :, :], self.gsT_d[c4 * 512:(c4 + 1) * 512, t0:t0 + 512].rearrange("(fc p) t -> p fc t", p=128), rgs, writes=[rgs])
                    P.dma("sp", gaT[:, :, :], self.gaT_d[c4 * 512:(c4 + 1) * 512, t0:t0 + 512].rearrange("(fc p) t -> p fc t", p=128), rga, writes=[rga])
                    for j in range(4):
                        cb = c4 * 4 + j
                        ps1, rps1 = self.bank()
                        self.mm_acc(ps1[:, :], rps1, [(ws[:, kc, j * 128:(j + 1) * 128], yss[:, kc, :]) for kc in range(8)], [rws, ryss])
                        ps2, rps2 = self.bank()
                        self.mm_acc(ps2[:, :], rps2, [(wa[:, kc, j * 128:(j + 1) * 128], yat[:, kc, :]) for kc in range(8)], [rwa, ryat])
                        (a1, ra1), (a2, ra2) = m1[cb % 2], m2[cb % 2]
                        P.op("dve", lambda e, ps1=ps1, a1=a1, j=j, gsT=gsT: e.tensor_tensor(out=a1[:, :], in0=ps1[:, :], in1=gsT[:, j, :], op=ALU.mult), reads=[rps1, rgs], writes=[ra1])
                        P.op("dve", lambda e, ps2=ps2, a2=a2, j=j, gaT=gaT: e.tensor_tensor(out=a2[:, :], in0=ps2[:, :], in1=gaT[:, j, :], op=ALU.mult), reads=[rps2, rga], writes=[ra2])
                        P.op("pool", lambda e, a1=a1, a2=a2, cb=cb: e.tensor_tensor(out=mT[:, cb, :], in0=a1[:, :], in1=a2[:, :], op=ALU.add), reads=[ra1, ra2], writes=[rmT])
                for nch in range(4):
                    wt, rwt = wbuf[nch % 2]
                    self.load_wtile(wt, rwt, self.wb["w_out"][l], 0, 16, nch * 512, 512)
                    for s in range(4):
                        ps, rps = self.bank()
                        self.mm_acc(ps[:, :], rps, [(mT[:, kc, s * 128:(s + 1) * 128], wt[:, kc, :]) for kc in range(16)], [rmT, rwt])
                        P.op("dve", lambda e, ps=ps, s=s, nch=nch: e.tensor_tensor(out=xt[:, s, nch * 512:(nch + 1) * 512], in0=ps[:, :], in1=xt[:, s, nch * 512:(nch + 1) * 512], op=ALU.add),
                             reads=[rps, rx], writes=[rx])
                P.dma("pool", self.xa_d[t0:t0 + 512, :].rearrange("(s p) d -> p s d", p=128), xt[:, :, :], rx, reads=[rx])
            P.barrier()

    def phase_b2(self, l, x_dst):
        P, g = self.P, self.g
        with contextlib.ExitStack() as st:
            T = lambda name, shape, dt=F32: P.sbuf(name, shape, dt, st)
            xt, rx = T("xt", [128, 4, D]), P.nres_("xt")
            nbufs = self.norm_bufs(st)
            w2T, rw2 = T("w2T", [128, 16]), P.nres_("w1T")
            P.dma("sp", w2T[:, :], self.i["norm2_w"][l].rearrange("(kc p) -> p kc", p=128), rw2, writes=[rw2], allow_slow_non_contiguous=True)
            hT, rhT = T("hT", [128, 16, 512], BF16), P.nres_("hT")
            aT, raT = T("aT", [128, 44, 512], BF16), P.nres_("b2_aT")
            wbuf = [(T("wF", [128, 22, 512], BF16), P.nres_("wA%d" % i)) for i in range(3)]
            sgt = [(T("sgt", [128, 512]), P.nres_("b1_sgt%d" % i)) for i in range(2)]
            nw = 0
            for t in range(self.NT):
                t0 = t * 512
                P.dma("sp", xt[:, :, :], self.xa_d[t0:t0 + 512, :].rearrange("(s p) d -> p s d", p=128), rx, writes=[rx])
                self.norm_T(nbufs, xt, rx, w2T, rw2, hT, rhT)
                for c in range(DFF // 512):
                    (wg, rwg) = wbuf[nw % 3]
                    (wu, rwu) = wbuf[(nw + 1) % 3]
                    nw += 2
                    self.load_wtile(wg, rwg, self.wb["w_ffn_gate"][l], 0, 16, c * 512, 512)
                    self.load_wtile(wu, rwu, self.wb["w_ffn_up"][l], 0, 16, c * 512, 512)
                    for j in range(4):
                        fb = c * 4 + j
                        psg, rpsg = self.bank()
                        self.mm_acc(psg[:, :], rpsg, [(wg[:, kc, j * 128:(j + 1) * 128], hT[:, kc, :]) for kc in range(16)], [rwg, rhT])
                        psu, rpsu = self.bank()
                        self.mm_acc(psu[:, :], rpsu, [(wu[:, kc, j * 128:(j + 1) * 128], hT[:, kc, :]) for kc in range(16)], [rwu, rhT])
                        sg, rsg = sgt[fb % 2]
                        P.op("act", lambda e, psg=psg, sg=sg: e.activation(out=sg[:, :], in_=psg[:, :], func=AF.Silu), reads=[rpsg], writes=[rsg])
                        P.op("dve", lambda e, psu=psu, sg=sg, fb=fb: e.tensor_tensor(out=aT[:, fb, :], in0=psu[:, :], in1=sg[:, :], op=ALU.mult), reads=[rpsu, rsg], writes=[raT])
                for nch in range(4):
                    (w0, rw0) = wbuf[nw % 3]
                    (w1, rw1) = wbuf[(nw + 1) % 3]
                    nw += 2
                    self.load_wtile(w0, rw0, self.wb["w_ffn_down"][l], 0, 22, nch * 512, 512)
                    self.load_wtile(w1, rw1, self.wb["w_ffn_down"][l], 22, 22, nch * 512, 512)
                    for s in range(4):
                        ps, rps = self.bank()
                        pairs = [(aT[:, kc, s * 128:(s + 1) * 128], w0[:, kc, :]) for kc in range(22)]
                        pairs += [(aT[:, 22 + kc, s * 128:(s + 1) * 128], w1[:, kc, :]) for kc in range(22)]
                        self.mm_acc(ps[:, :], rps, pairs, [raT, rw0, rw1])
                        P.op("dve", lambda e, ps=ps, s=s, nch=nch: e.tensor_tensor(out=xt[:, s, nch * 512:(nch + 1) * 512], in0=ps[:, :], in1=xt[:, s, nch * 512:(nch + 1) * 512], op=ALU.add),
                             reads=[rps, rx], writes=[rx])
                P.dma("pool", x_dst[t0:t0 + 512, :].rearrange("(s p) d -> p s d", p=128), xt[:, :, :], rx, reads=[rx])
            P.barrier()


_CACHE = {}


def kernel(**inputs):
    x = np.ascontiguousarray(np.asarray(inputs["x"], dtype=np.float32))
    B, L, _ = x.shape
    Lh = L // 2
    key = (Lh,)
    if key not in _CACHE:
        _CACHE[key] = Builder(Lh).build()
    nc = _CACHE[key]
    base = {k: np.ascontiguousarray(np.asarray(v, dtype=np.float32)) for k, v in inputs.items() if k != "x"}
    base.update(host_consts())
    in_maps = []
    for b in range(B):
        for half in range(2):
            m = dict(base)
            m["x"] = np.ascontiguousarray(x[b, half * Lh:(half + 1) * Lh])
            m["c_flag"] = np.full((128, 1), float(half), np.float32)
            in_maps.append(m)
    res = run_bass_kernel_spmd(nc, in_maps, core_ids=list(range(2 * B)))
    out = np.empty((B, L, x.shape[2]), np.float32)
    for b in range(B):
        for half in range(2):
            out[b, half * Lh:(half + 1) * Lh] = np.asarray(res.results[2 * b + half]["out"], dtype=np.float32)
    return out
```
